# Optimizing a Trainium2 kernel written in Bass

```python
import math
import jax, jax.numpy as jnp
from jax import lax
import numpy as np


D_MODEL = 1024
BATCH = 2
SEQ = 16384
DEPTH = 2

CHUNK = 64
Q_BLOCK = 128
F32 = jnp.float32
RMS_EPS = 1e-6
ROPE_BASE = 10000.0

DA_HEADS = 4
DA_QK_DIM = 64
DA_V_DIM = 128
DA_WIDTH = DA_HEADS * DA_V_DIM
RET_HEADS = 4
RET_QK_DIM = 64
RET_V_DIM = 128
RET_WIDTH = RET_HEADS * RET_V_DIM
S5_WIDTH = 256
S5_GROUP = 16
S5_GROUPS = S5_WIDTH // S5_GROUP
S5_STATE = 64
S5_DT_MIN = 0.001
S5_DT_MAX = 0.1
GLA_HEADS = 6
GLA_QK_DIM = 64
GLA_V_DIM = 128
GLA_WIDTH = GLA_HEADS * GLA_V_DIM
GLA_GATE_RANK = 16
GLA_TAU = 16.0
MLP_HIDDEN = 4 * D_MODEL

AB_SIZES = (DA_HEADS * 2 * DA_QK_DIM, DA_HEADS * 2 * DA_QK_DIM, DA_WIDTH,
            RET_HEADS * RET_QK_DIM, RET_HEADS * RET_QK_DIM, RET_WIDTH, RET_WIDTH)
AB_IN = sum(AB_SIZES)
CD_SIZES = (S5_WIDTH, GLA_HEADS * GLA_QK_DIM, GLA_HEADS * GLA_QK_DIM, GLA_WIDTH, GLA_WIDTH, GLA_GATE_RANK)
CD_IN = sum(CD_SIZES)

kernel_name = 'hybrid_diffattn_retnet_s5_gla_block'


def _rmsnorm(x, gain):
    xf = x.astype(F32)
    y = xf * lax.rsqrt(jnp.mean(xf * xf, axis=-1, keepdims=True) + RMS_EPS)
    return (y * gain.astype(F32)).astype(x.dtype)


def _split_cols(h, sizes):
    out = []
    start = 0
    for size in sizes:
        out.append(h[..., start:start + size])
        start += size
    return out


def _rotary(x):
    seq, d = x.shape[1], x.shape[-1]
    half = d // 2
    inv_freq = 1.0 / (ROPE_BASE ** jnp.linspace(0.0, 1.0, half, dtype=F32))
    ang = jnp.arange(seq, dtype=F32)[:, None] * inv_freq[None, :]
    cos = jnp.cos(ang)[None, :, None, :]
    sin = jnp.sin(ang)[None, :, None, :]
    x1, x2 = x[..., :half], x[..., half:]
    return jnp.concatenate([x1 * cos - x2 * sin, x1 * sin + x2 * cos], axis=-1)


def _chunk_states(decay, kv):
    def step(state, inp):
        d, u = inp
        return d * state + u, state
    init = jnp.zeros(kv.shape[:1] + kv.shape[2:], F32)
    _, prev = lax.scan(step, init, (jnp.moveaxis(decay, 1, 0), jnp.moveaxis(kv, 1, 0)))
    return jnp.moveaxis(prev, 0, 1)


def _diff_attention(q, k, v, lam):
    bsz, seq, heads, _, dq = q.shape
    n_blocks = seq // Q_BLOCK
    scale = dq ** -0.5
    key_chunk = jnp.arange(seq) // CHUNK
    q_blocks = jnp.moveaxis(q.reshape(bsz, n_blocks, Q_BLOCK, heads, 2, dq), 1, 0)

    def block(args):
        q_blk, blk = args
        q_chunk = (blk * Q_BLOCK + jnp.arange(Q_BLOCK)) // CHUNK
        mask = key_chunk[None, :] <= q_chunk[:, None]
        s = jnp.einsum('bqhmd,bkhmd->bhmqk', q_blk, k).astype(F32) * scale
        p = jax.nn.softmax(jnp.where(mask, s, -jnp.inf), axis=-1)
        w = p[:, :, 0] - lam * p[:, :, 1]
        return jnp.einsum('bhqk,bkhe->bqhe', w.astype(v.dtype), v)

    out = lax.map(block, (q_blocks, jnp.arange(n_blocks)))
    return jnp.moveaxis(out, 0, 1).reshape(bsz, seq, heads, v.shape[-1])


def _retention(q, k, v):
    bsz, seq, heads, dk = q.shape
    dv = v.shape[-1]
    nc = seq // CHUNK
    shp = (bsz, nc, CHUNK, heads)
    q = _rotary(q.astype(F32)).reshape(shp + (dk,))
    k = (_rotary(k.astype(F32)) * dk ** -0.5).reshape(shp + (dk,))
    v = v.astype(F32).reshape(shp + (dv,))
    log_gamma = jnp.log(1.0 - 2.0 ** (-5.0 - jnp.arange(heads, dtype=F32)))
    idx = jnp.arange(CHUNK, dtype=F32)
    intra_decay = jnp.exp(log_gamma[:, None, None] * jnp.abs(idx[:, None] - idx[None, :]))
    scores = jnp.einsum('bnihd,bnjhd->bnhij', q, k) * intra_decay
    o_intra = jnp.einsum('bnhij,bnjhe->bnihe', scores, v)
    k_w = jnp.exp((CHUNK - 1.0 - idx)[:, None] * log_gamma[None, :])[..., None]
    kv = jnp.einsum('bnjhd,bnjhe->bnhde', k * k_w, v)
    decay = jnp.broadcast_to(jnp.exp(CHUNK * log_gamma)[:, None, None], (bsz, nc, heads, 1, 1))
    prev = _chunk_states(decay, kv)
    q_w = jnp.exp((idx + 1.0)[:, None] * log_gamma[None, :])[..., None]
    o_cross = jnp.einsum('bnihd,bnhde->bnihe', q * q_w, prev)
    return (o_intra + o_cross).reshape(bsz, seq, heads, dv)


def _gla(q, k, v, log_a):
    bsz, seq, heads, dk = q.shape
    dv = v.shape[-1]
    nc = seq // CHUNK
    shp = (bsz, nc, CHUNK, heads)
    q = (q.astype(F32) * dk ** -0.5).reshape(shp + (dk,))
    k = k.astype(F32).reshape(shp + (dk,))
    v = v.astype(F32).reshape(shp + (dv,))
    b = jnp.cumsum(log_a.astype(F32).reshape(shp + (dk,)), axis=2)
    b_last = b[:, :, -1:]
    e_pos, e_neg = jnp.exp(b), jnp.exp(-b)
    fwd = jnp.einsum('bnihd,bnjhd->bnhij', q * e_pos, k * e_neg)
    bwd = jnp.einsum('bnihd,bnjhd->bnhij', q * e_neg, k * e_pos)
    idx = jnp.arange(CHUNK)
    scores = jnp.where(idx[:, None] >= idx[None, :], fwd, bwd)
    o_intra = jnp.einsum('bnhij,bnjhe->bnihe', scores, v)
    kv = jnp.einsum('bnjhd,bnjhe->bnhde', k * jnp.exp(b_last - b), v)
    prev = _chunk_states(jnp.exp(b_last[:, :, 0])[..., None], kv)
    o_cross = jnp.einsum('bnihd,bnhde->bnihe', q * e_pos, prev)
    return (o_intra + o_cross).reshape(bsz, seq, heads, dv)


def _ssm_combine(left, right):
    a_l, b_l = left
    a_r, b_r = right
    return a_r * a_l, a_r * b_l + b_r


def _s5(u, a_re, a_im, log_step, b_re, b_im, c_re, c_im, d_skip, w_glu):
    bsz, seq, _ = u.shape
    uf = u.astype(F32).reshape(bsz, seq, S5_GROUPS, S5_GROUP)
    lam = lax.complex(a_re.astype(F32), a_im.astype(F32))
    delta = jnp.exp(log_step.astype(F32))[:, None]
    lam_bar = jnp.exp(lam * delta)
    b_bar = ((lam_bar - 1.0) / lam)[:, :, None] * lax.complex(b_re.astype(F32), b_im.astype(F32))
    bu = lax.complex(jnp.einsum('bsgc,gpc->bsgp', uf, jnp.real(b_bar)),
                     jnp.einsum('bsgc,gpc->bsgp', uf, jnp.imag(b_bar)))
    a = jnp.broadcast_to(lam_bar, (1, seq) + lam_bar.shape)
    _, states = lax.associative_scan(_ssm_combine, (a, bu), axis=1)
    y = (jnp.einsum('gcp,bsgp->bsgc', c_re.astype(F32), jnp.real(states))
         - jnp.einsum('gcp,bsgp->bsgc', c_im.astype(F32), jnp.imag(states)))
    y = (y + d_skip.astype(F32).reshape(S5_GROUPS, S5_GROUP) * uf).reshape(bsz, seq, S5_WIDTH)
    z = jax.nn.gelu(y)
    out = z * jax.nn.sigmoid(z @ w_glu.astype(F32))
    return out.astype(u.dtype)


def setup_inputs(seed: int = 0) -> dict:
    key = jax.random.key(seed)
    ks = jax.random.split(key, 32)
    n_even = (DEPTH + 1) // 2
    n_odd = DEPTH // 2

    def normal(i, shape, scale):
        return scale * jax.random.normal(ks[i], shape, F32)

    def gain(i, shape):
        return 1.0 + 0.02 * jax.random.normal(ks[i], shape, F32)

    x = normal(0, (BATCH, SEQ, D_MODEL), 1.0)
    norm_mix_g = gain(1, (DEPTH, D_MODEL))
    norm_mlp_g = gain(2, (DEPTH, D_MODEL))
    w_up = normal(3, (DEPTH, D_MODEL, MLP_HIDDEN), D_MODEL ** -0.5)
    w_down = normal(4, (DEPTH, MLP_HIDDEN, D_MODEL), MLP_HIDDEN ** -0.5)
    ab_w_in = normal(5, (n_even, D_MODEL, AB_IN), D_MODEL ** -0.5)
    ab_w_out = normal(6, (n_even, DA_WIDTH + RET_WIDTH, D_MODEL), (DA_WIDTH + RET_WIDTH) ** -0.5)
    da_q_norm = gain(7, (n_even, DA_QK_DIM))
    da_k_norm = gain(8, (n_even, DA_QK_DIM))
    da_lam_q1 = normal(9, (n_even, DA_QK_DIM), 0.1)
    da_lam_k1 = normal(10, (n_even, DA_QK_DIM), 0.1)
    da_lam_q2 = normal(11, (n_even, DA_QK_DIM), 0.1)
    da_lam_k2 = normal(12, (n_even, DA_QK_DIM), 0.1)
    da_out_norm = gain(13, (n_even, DA_V_DIM))
    ret_out_norm = gain(14, (n_even, RET_V_DIM))
    cd_w_in = normal(15, (n_odd, D_MODEL, CD_IN), D_MODEL ** -0.5)
    cd_w_out = normal(16, (n_odd, S5_WIDTH + GLA_WIDTH, D_MODEL), (S5_WIDTH + GLA_WIDTH) ** -0.5)
    s5_a_re = -0.5 + normal(17, (n_odd, S5_GROUPS, S5_STATE), 0.01)
    s5_a_im = math.pi * jnp.arange(S5_STATE, dtype=F32) + normal(18, (n_odd, S5_GROUPS, S5_STATE), 0.01)
    s5_log_step = jax.random.uniform(ks[19], (n_odd, S5_GROUPS), F32, math.log(S5_DT_MIN), math.log(S5_DT_MAX))
    s5_b_re = normal(20, (n_odd, S5_GROUPS, S5_STATE, S5_GROUP), (2.0 * S5_GROUP) ** -0.5)
    s5_b_im = normal(21, (n_odd, S5_GROUPS, S5_STATE, S5_GROUP), (2.0 * S5_GROUP) ** -0.5)
    s5_c_re = normal(22, (n_odd, S5_GROUPS, S5_GROUP, S5_STATE), (2.0 * S5_STATE) ** -0.5)
    s5_c_im = normal(23, (n_odd, S5_GROUPS, S5_GROUP, S5_STATE), (2.0 * S5_STATE) ** -0.5)
    s5_d = normal(24, (n_odd, S5_WIDTH), 1.0)
    s5_w_glu = normal(25, (n_odd, S5_WIDTH, S5_WIDTH), S5_WIDTH ** -0.5)
    gla_w_a2 = normal(26, (n_odd, GLA_GATE_RANK, GLA_HEADS * GLA_QK_DIM), GLA_GATE_RANK ** -0.5)
    gla_b_a2 = normal(27, (n_odd, GLA_HEADS * GLA_QK_DIM), 0.1)
    gla_out_norm = gain(28, (n_odd, GLA_V_DIM))
    return {'x': x, 'norm_mix_g': norm_mix_g, 'norm_mlp_g': norm_mlp_g, 'w_up': w_up, 'w_down': w_down,
            'ab_w_in': ab_w_in, 'ab_w_out': ab_w_out, 'da_q_norm': da_q_norm, 'da_k_norm': da_k_norm,
            'da_lam_q1': da_lam_q1, 'da_lam_k1': da_lam_k1, 'da_lam_q2': da_lam_q2, 'da_lam_k2': da_lam_k2,
            'da_out_norm': da_out_norm, 'ret_out_norm': ret_out_norm, 'cd_w_in': cd_w_in, 'cd_w_out': cd_w_out,
            's5_a_re': s5_a_re, 's5_a_im': s5_a_im, 's5_log_step': s5_log_step, 's5_b_re': s5_b_re,
            's5_b_im': s5_b_im, 's5_c_re': s5_c_re, 's5_c_im': s5_c_im, 's5_d': s5_d, 's5_w_glu': s5_w_glu,
            'gla_w_a2': gla_w_a2, 'gla_b_a2': gla_b_a2, 'gla_out_norm': gla_out_norm}


def reference(x, norm_mix_g, norm_mlp_g, w_up, w_down, ab_w_in, ab_w_out, da_q_norm, da_k_norm,
              da_lam_q1, da_lam_k1, da_lam_q2, da_lam_k2, da_out_norm, ret_out_norm, cd_w_in, cd_w_out,
              s5_a_re, s5_a_im, s5_log_step, s5_b_re, s5_b_im, s5_c_re, s5_c_im, s5_d, s5_w_glu,
              gla_w_a2, gla_b_a2, gla_out_norm):
    bsz, seq, _ = x.shape
    for layer in range(DEPTH):
        h = _rmsnorm(x, norm_mix_g[layer])
        if layer % 2 == 0:
            j = layer // 2
            q_a, k_a, v_a, q_r, k_r, v_r, g_r = _split_cols(h @ ab_w_in[j], AB_SIZES)
            lam_init = 0.8 - 0.6 * math.exp(-0.3 * layer)
            lam = (jnp.exp(jnp.sum(da_lam_q1[j].astype(F32) * da_lam_k1[j].astype(F32)))
                   - jnp.exp(jnp.sum(da_lam_q2[j].astype(F32) * da_lam_k2[j].astype(F32))) + lam_init)
            qa = _rmsnorm(q_a.reshape(bsz, seq, DA_HEADS, 2, DA_QK_DIM), da_q_norm[j])
            ka = _rmsnorm(k_a.reshape(bsz, seq, DA_HEADS, 2, DA_QK_DIM), da_k_norm[j])
            o_a = _diff_attention(qa, ka, v_a.reshape(bsz, seq, DA_HEADS, DA_V_DIM), lam)
            o_a = (_rmsnorm(o_a, da_out_norm[j]) * (1.0 - lam_init)).reshape(bsz, seq, DA_WIDTH)
            o_r = _retention(q_r.reshape(bsz, seq, RET_HEADS, RET_QK_DIM),
                             k_r.reshape(bsz, seq, RET_HEADS, RET_QK_DIM),
                             v_r.reshape(bsz, seq, RET_HEADS, RET_V_DIM)).astype(x.dtype)
            o_r = _rmsnorm(o_r, ret_out_norm[j]).reshape(bsz, seq, RET_WIDTH) * jax.nn.silu(g_r)
            mixed = jnp.concatenate([o_a, o_r], axis=-1) @ ab_w_out[j]
        else:
            j = layer // 2
            u, q_g, k_g, v_g, r_g, a_lr = _split_cols(h @ cd_w_in[j], CD_SIZES)
            o_c = _s5(u, s5_a_re[j], s5_a_im[j], s5_log_step[j], s5_b_re[j], s5_b_im[j],
                      s5_c_re[j], s5_c_im[j], s5_d[j], s5_w_glu[j])
            log_a = jax.nn.log_sigmoid((a_lr @ gla_w_a2[j] + gla_b_a2[j]).astype(F32)) / GLA_TAU
            o_d = _gla(q_g.reshape(bsz, seq, GLA_HEADS, GLA_QK_DIM),
                       k_g.reshape(bsz, seq, GLA_HEADS, GLA_QK_DIM),
                       v_g.reshape(bsz, seq, GLA_HEADS, GLA_V_DIM),
                       log_a.reshape(bsz, seq, GLA_HEADS, GLA_QK_DIM)).astype(x.dtype)
            o_d = _rmsnorm(o_d, gla_out_norm[j]).reshape(bsz, seq, GLA_WIDTH) * jax.nn.silu(r_g)
            mixed = jnp.concatenate([o_c, o_d], axis=-1) @ cd_w_out[j]
        x = x + mixed
        h = _rmsnorm(x, norm_mlp_g[layer])
        x = x + jnp.square(jax.nn.relu(h @ w_up[layer])) @ w_down[layer]
    return x
```

```python
import numpy as np
from contextlib import ExitStack
import concourse.bass as bass
import concourse.mybir as mybir
from concourse.bass_utils import run_bass_kernel_spmd

F32 = mybir.dt.float32
BF16 = mybir.dt.bfloat16
ALU = mybir.AluOpType
AF = mybir.ActivationFunctionType
AX = mybir.AxisListType

EPOCH = 16000
NDMA = 8


class Buf:
    __slots__ = ("name", "w", "r")

    def __init__(self, name=""):
        self.name = name
        self.w = None
        self.r = {}


class Prog:
    CE = ("tensor", "vector", "scalar", "gpsimd")
    DQ = ("sync", "gpsimd")

    def __init__(self, nc, es, nepoch=4):
        self.nc = nc
        self.es = es
        self.q = {e: [] for e in ("tensor", "vector", "scalar", "gpsimd", "sync")}
        self.cnt = {e: 0 for e in self.CE}
        self.vc = {e: {} for e in self.q}
        self.tokvc = {}
        self.tokeng = {}
        self.sems = {}
        for e in self.CE:
            for k in range(nepoch):
                self.sems[(e, k)] = es.enter_context(nc.semaphore(f"s_{e}_{k}"))
        self.dsem = {}
        self.dcnt = {}
        for qn in self.DQ:
            for k in range(NDMA):
                self.dsem[(qn, k)] = es.enter_context(nc.semaphore(f"d_{qn}_{k}"))
            self.dcnt[qn] = 0
        self.nbuf = 0

    def buf(self, name=""):
        self.nbuf += 1
        return Buf(name or f"b{self.nbuf}")

    def _need(self, eng, tok, waits):
        if tok is None:
            return
        sem, val = tok
        if self.vc[eng].get(sem, 0) >= val:
            return
        waits.append(tok)
        vc = self.vc[eng]
        for s, v in self.tokvc[tok].items():
            if vc.get(s, 0) < v:
                vc[s] = v

    def _deps(self, eng, reads, writes):
        waits = []
        for b in reads:
            if b.w is not None:
                if not (eng == "tensor" and self.tokeng.get(b.w) == "tensor"):
                    self._need(eng, b.w, waits)
        for b in writes:
            if b.w is not None and self.tokeng.get(b.w) != eng:
                self._need(eng, b.w, waits)
            for e2, t2 in b.r.items():
                if e2 != eng:
                    self._need(eng, t2, waits)
        return waits

    def _commit(self, eng, tok, reads, writes):
        vc = dict(self.vc[eng])
        vc[tok[0]] = max(vc.get(tok[0], 0), tok[1])
        self.tokvc[tok] = vc
        self.tokeng[tok] = eng
        for b in reads:
            b.r[eng] = tok
        for b in writes:
            b.w = tok
            b.r = {}

    def op(self, eng, fn, reads=(), writes=()):
        waits = self._deps(eng, reads, writes)
        self.cnt[eng] += 1
        c = self.cnt[eng] - 1
        sem = self.sems[(eng, c // EPOCH)]
        tok = (sem, c % EPOCH + 1)
        self.q[eng].append((waits, fn, (sem, 1)))
        self._commit(eng, tok, reads, writes)
        return tok

    def dma(self, qn, out, in_, reads=(), writes=(), **kw):
        eng = qn
        waits = self._deps(eng, reads, writes)
        n = self.dcnt[qn]
        self.dcnt[qn] += 1
        sem = self.dsem[(qn, n % NDMA)]
        k = n // NDMA
        if k > 0:
            prev = (sem, 16 * k)
            if self.vc[eng].get(sem, 0) < 16 * k:
                waits.append(prev)
                self.vc[eng][sem] = 16 * k
        tok = (sem, 16 * (k + 1))
        self.q[eng].append((waits, lambda e: e.dma_start(out=out, in_=in_, **kw), (sem, 16)))
        vc = dict(self.vc[eng])
        vc[sem] = 16 * (k + 1)
        self.tokvc[tok] = vc
        self.tokeng[tok] = "dma_" + qn
        for b in reads:
            b.r["dma_" + qn + str(n % NDMA)] = tok
        for b in writes:
            b.w = tok
            b.r = {}
        return tok

    def finish_all(self):
        waits = []
        for e in self.CE:
            cnt = self.cnt[e]
            if cnt:
                cc = cnt - 1
                self._need("sync", (self.sems[(e, cc // EPOCH)], cc % EPOCH + 1), waits)
        for qn in self.DQ:
            n = self.dcnt[qn]
            for j in range(max(0, n - NDMA), n):
                tok = (self.dsem[(qn, j % NDMA)], 16 * (j // NDMA + 1))
                self._need("sync", tok, waits)
        self.q["sync"].append((waits, None, None))

    def finish(self, bufs):
        waits = []
        for b in bufs:
            if b.w is not None:
                self._need("sync", b.w, waits)
        self.q["sync"].append((waits, None, None))

    def emit(self):
        nc = self.nc
        q = self.q

        def replay(e, name):
            for waits, fn, inc in q[name]:
                for sem, val in waits:
                    e.wait_ge(sem, val)
                if fn is not None:
                    ins = fn(e)
                    ins.then_inc(inc[0], inc[1])

        with nc.Block() as block:
            @block.tensor
            def _(e):
                replay(e, "tensor")

            @block.vector
            def _(e):
                replay(e, "vector")

            @block.scalar
            def _(e):
                replay(e, "scalar")

            @block.gpsimd
            def _(e):
                replay(e, "gpsimd")

            @block.sync
            def _(e):
                replay(e, "sync")


NCORE = 8
TOK = 4096
D = 1024
SEQ = 16384
EPS = 1e-6


def _mk(nc_name="TRN2"):
    return bass.Bass(nc_name, target_bir_lowering=False)


class Ctx:
    def __init__(self):
        self.nc = _mk()
        self.es = ExitStack()
        self.p = None
        self.nps = 0

    def start(self):
        self.p = Prog(self.nc, self.es)
        return self.p

    def din(self, name, shape, dt=F32):
        return self.nc.dram_tensor(name, list(shape), dt, kind="ExternalInput").ap()

    def dout(self, name, shape, dt=F32):
        return self.nc.dram_tensor(name, list(shape), dt, kind="ExternalOutput").ap()

    def sb(self, name, shape, dt=F32):
        return self.es.enter_context(self.nc.sbuf_tensor("sb_" + name, list(shape), dt))

    def ps(self, name, shape, dt=F32):
        return self.es.enter_context(self.nc.psum_tensor("ps_" + name, list(shape), dt))


class Rot:
    def __init__(self, items):
        self.items = items
        self.i = 0

    def next(self):
        it = self.items[self.i % len(self.items)]
        self.i += 1
        return it


def load_weight_bf16(c, p, wd, wb, wbuf, nk, ncol, gpk=None, bgpk=None, stage=None, colchunk=None):
    cc = colchunk or ncol
    engs = ("vector", "gpsimd")
    n = 0
    for k in range(nk):
        for c0 in range(0, ncol, cc):
            c1 = min(ncol, c0 + cc)
            st, bst = stage.next()
            p.dma("sync", st[:, 0:c1 - c0], wd[k * 128:(k + 1) * 128, c0:c1], writes=[bst])
            eng = engs[n % 2]
            n += 1
            if gpk is not None:
                p.op(eng, lambda e, st=st, k=k, c0=c0, c1=c1: e.tensor_scalar(
                    out=wb[:, k, c0:c1], in0=st[:, 0:c1 - c0], scalar1=gpk[:, k:k + 1], scalar2=None, op0=ALU.mult),
                    reads=[bst, bgpk], writes=[wbuf])
            else:
                p.op(eng, lambda e, st=st, k=k, c0=c0, c1=c1: e.tensor_copy(out=wb[:, k, c0:c1], in_=st[:, 0:c1 - c0]),
                     reads=[bst], writes=[wbuf])


def build_pin(layer):
    c = Ctx()
    nc = c.nc
    NCOL = 3072 if layer == 0 else 2944
    NOUT = 3584 if layer == 0 else 2560
    x = c.din("x", [TOK, D])
    xT = c.din("xT", [D, TOK])
    w = c.din("w", [D, NCOL if layer == 0 else 2560])
    gpk_d = c.din("gpk", [128, 8])
    out = c.dout("out", [TOK, NOUT], BF16)
    if layer == 0:
        gq_d = c.din("gqk", [128, 1024])
        cs_d = c.din("cs", [128, 32, 512])
        t12_d = c.din("t12", [128, 1024])
    else:
        walrT_d = c.din("walrT", [16, D])
        wa2_d = c.din("wa2", [16, 384])
        ba2_d = c.din("ba2", [128, 384])
        la_out = c.dout("la", [TOK, 384], F32)
    p = c.start()
    sb = c.sb
    wb = sb("wb", [128, 8, NCOL], BF16); bwb = p.buf()
    gpk = sb("gpk", [128, 8]); bgpk = p.buf()
    p.dma("gpsimd", gpk[:], gpk_d[:, :], writes=[bgpk])
    stage = Rot([(sb(f"st{i}", [128, 1024]), p.buf()) for i in range(3)])
    if layer == 0:
        gq = sb("gq", [128, 1024]); bgq = p.buf()
        t12 = sb("t12", [128, 1024]); bt12 = p.buf()
        p.dma("gpsimd", gq[:], gq_d[:, :], writes=[bgq])
        p.dma("gpsimd", t12[:], t12_d[:, :], writes=[bt12])
        load_weight_bf16(c, p, w, wb, bwb, 8, NCOL, gpk, bgpk, stage, 1024)
    else:
        ba2 = sb("ba2", [128, 384]); bba2 = p.buf()
        p.dma("gpsimd", ba2[:], ba2_d[:, :], writes=[bba2])
        walrT = sb("walrT", [16, D]); bwalr = p.buf()
        wa2 = sb("wa2", [16, 384]); bwa2 = p.buf()
        p.dma("gpsimd", walrT[:], walrT_d[:, :], writes=[bwalr])
        p.dma("gpsimd", wa2[:], wa2_d[:, :], writes=[bwa2])
        load_weight_bf16(c, p, w, wb[:, :, 0:2560], bwb, 8, 2560, gpk, bgpk, stage, 1024)
        pw = c.ps("pw", [128, 512]); bpw = p.buf()
        for k in range(8):
            p.op("tensor", lambda e, k=k: e.matmul(pw[:, 0:384], lhsT=walrT[:, k * 128:(k + 1) * 128], rhs=wa2[:, :],
                                                   start=True, stop=True), reads=[bwalr, bwa2], writes=[bpw])
            p.op("vector", lambda e, k=k: e.tensor_scalar(out=wb[:, k, 2560:2944], in0=pw[:, 0:384], scalar1=gpk[:, k:k + 1],
                                                          scalar2=None, op0=ALU.mult), reads=[bpw, bgpk], writes=[bwb])
    nbank = (NCOL + 511) // 512
    banks = Rot([(c.ps(f"pb{i}", [128, 512]), p.buf()) for i in range(7 if layer == 1 else 8)])
    GT = 512
    xin = Rot([(sb(f"xin{i}", [128, 4, D]), p.buf()) for i in range(2)])
    xTin = Rot([(sb(f"xTin{i}", [128, 8, GT]), p.buf()) for i in range(2)])
    xTb = Rot([(sb(f"xTb{i}", [128, 8, GT], BF16), p.buf()) for i in range(2)])
    junk = sb("junk", [128, D]); bjunk = p.buf()
    ss = Rot([(sb(f"ss{i}", [128, 1]), p.buf()) for i in range(4)])
    hq = Rot([(sb(f"hq{i}", [128, 1024]), p.buf()) for i in range(2)])
    sq = sb("sq", [128, 1024]); bsq = p.buf()
    qs = Rot([(sb(f"qs{i}", [128, 16]), p.buf()) for i in range(2)])
    ob = Rot([(sb(f"ob{i}", [128, NOUT], BF16), p.buf()) for i in range(2)])
    if layer == 0:
        cs = Rot([(sb(f"cs{i}", [128, 4, 512]), p.buf()) for i in range(2)])
        rq = Rot([(sb(f"rq{i}", [128, 512]), p.buf()) for i in range(2)])
        rt = [(sb(f"rt{i}", [128, 256]), p.buf()) for i in range(4)]
        ro = Rot([(sb(f"ro{i}", [128, 512]), p.buf()) for i in range(2)])
    else:
        la = Rot([(sb(f"la{i}", [128, 384]), p.buf()) for i in range(2)])
        le = Rot([(sb(f"le{i}", [128, 384]), p.buf()) for i in range(2)])
    xTv = xT.rearrange("(k p) t -> p k t", p=128)
    xv = x.rearrange("(n p) d -> p n d", p=128)
    for g in range(TOK // GT):
        xi, bxi = xin.next()
        xti, bxti = xTin.next()
        xtb, bxtb = xTb.next()
        p.dma("sync", xi[:], xv[:, g * 4:(g + 1) * 4, :], writes=[bxi])
        p.dma("gpsimd", xti[:], xTv[:, :, g * GT:(g + 1) * GT], writes=[bxti])
        p.op("gpsimd", lambda e, xtb=xtb, xti=xti: e.tensor_copy(out=xtb[:, 0:4, :], in_=xti[:, 0:4, :]), reads=[bxti], writes=[bxtb])
        p.op("vector", lambda e, xtb=xtb, xti=xti: e.tensor_copy(out=xtb[:, 4:8, :], in_=xti[:, 4:8, :]), reads=[bxti], writes=[bxtb])
        if layer == 0:
            csg, bcsg = cs.next()
            p.dma("gpsimd", csg[:], cs_d[:, g * 4:(g + 1) * 4, :], writes=[bcsg])
        for j in range(4):
            tt = g * 4 + j
            s1, bs1 = ss.next()
            p.op("scalar", lambda e, xi=xi, j=j, s1=s1: e.activation(out=junk[:], in_=xi[:, j, :], func=AF.Square, accum_out=s1[:]),
                 reads=[bxi], writes=[bjunk, bs1])
            p.op("scalar", lambda e, s1=s1: e.activation(out=s1[:], in_=s1[:], func=AF.Sqrt, scale=1.0 / D, bias=EPS), reads=[bs1], writes=[bs1])
            p.op("vector", lambda e, s1=s1: e.reciprocal(out=s1[:], in_=s1[:]), reads=[bs1], writes=[bs1])
            pbs = []
            for b in range(nbank):
                pb, bpb = banks.next()
                c0, c1 = b * 512, min(NCOL, (b + 1) * 512)
                for k in range(8):
                    p.op("tensor", lambda e, pb=pb, xtb=xtb, k=k, j=j, c0=c0, c1=c1: e.matmul(
                        pb[:, 0:c1 - c0], lhsT=xtb[:, k, j * 128:(j + 1) * 128], rhs=wb[:, k, c0:c1], start=(k == 0), stop=(k == 7)),
                        reads=[bxtb, bwb], writes=[bpb])
                pbs.append((pb, bpb))
            o, bo = ob.next()
            if layer == 0:
                h, bh = hq.next()
                for b in range(2):
                    p.op("scalar", lambda e, b=b, h=h, s1=s1, pb=pbs[b][0]: e.activation(out=h[:, b * 512:(b + 1) * 512], in_=pb[:, :], func=AF.Copy, scale=s1[:, 0:1]),
                         reads=[pbs[b][1], bs1], writes=[bh])
                p.op("vector", lambda e, o=o, s1=s1, pb=pbs[2][0]: e.tensor_scalar(out=o[:, 1024:1536], in0=pb[:, :], scalar1=s1[:, 0:1], scalar2=None, op0=ALU.mult),
                     reads=[pbs[2][1], bs1], writes=[bo])
                p.op("vector", lambda e, o=o, s1=s1, pb=pbs[4][0]: e.tensor_scalar(out=o[:, 2560:3072], in0=pb[:, :], scalar1=s1[:, 0:1], scalar2=None, op0=ALU.mult),
                     reads=[pbs[4][1], bs1], writes=[bo])
                p.op("scalar", lambda e, o=o, s1=s1, pb=pbs[5][0]: e.activation(out=o[:, 3072:3584], in_=pb[:, :], func=AF.Silu, scale=s1[:, 0:1]),
                     reads=[pbs[5][1], bs1], writes=[bo])
                r, br = rq.next()
                p.op("scalar", lambda e, r=r, s1=s1, pb=pbs[3][0]: e.activation(out=r[:], in_=pb[:, :], func=AF.Copy, scale=s1[:, 0:1]),
                     reads=[pbs[3][1], bs1], writes=[br])
                q1, bq1 = qs.next()
                h3 = h[:].rearrange("p (g d) -> p g d", d=64)
                p.op("gpsimd", lambda e, h=h: e.tensor_tensor(out=sq[:], in0=h[:], in1=h[:], op=ALU.mult), reads=[bh], writes=[bsq])
                p.op("vector", lambda e, q1=q1: e.tensor_reduce(out=q1[:], in_=sq[:].rearrange("p (g d) -> p g d", d=64), axis=AX.X, op=ALU.add),
                     reads=[bsq], writes=[bq1])
                p.op("scalar", lambda e, q1=q1: e.activation(out=q1[:], in_=q1[:], func=AF.Sqrt, scale=1.0 / 64, bias=EPS), reads=[bq1], writes=[bq1])
                p.op("vector", lambda e, q1=q1: e.reciprocal(out=q1[:], in_=q1[:]), reads=[bq1], writes=[bq1])
                p.op("vector", lambda e, h3=h3, q1=q1: e.tensor_tensor(out=h3, in0=h3, in1=q1[:].unsqueeze(2).to_broadcast([128, 16, 64]), op=ALU.mult),
                     reads=[bh, bq1], writes=[bh])
                p.op("gpsimd", lambda e, h=h, o=o: e.tensor_tensor(out=o[:, 0:1024], in0=h[:], in1=gq[:], op=ALU.mult), reads=[bh, bgq], writes=[bo])
                r4 = r[:].rearrange("p (a two d) -> p a two d", two=2, d=32)
                x1, x2 = r4[:, :, 0, :], r4[:, :, 1, :]
                cosv = csg[:, j, 0:256].rearrange("p (a d) -> p a d", d=32)
                sinv = csg[:, j, 256:512].rearrange("p (a d) -> p a d", d=32)
                (ta, bta), (tb_, btb), (tc, btc), (td, btd) = rt
                v3 = lambda t: t[:].rearrange("p (a d) -> p a d", d=32)
                rr, brr = ro.next()
                rr4 = rr[:].rearrange("p (a two d) -> p a two d", two=2, d=32)
                p.op("vector", lambda e, x1=x1, cosv=cosv: e.tensor_tensor(out=v3(ta), in0=x1, in1=cosv, op=ALU.mult), reads=[br, bcsg], writes=[bta])
                p.op("gpsimd", lambda e, x2=x2, sinv=sinv: e.tensor_tensor(out=v3(tb_), in0=x2, in1=sinv, op=ALU.mult), reads=[br, bcsg], writes=[btb])
                p.op("vector", lambda e, x1=x1, sinv=sinv: e.tensor_tensor(out=v3(tc), in0=x1, in1=sinv, op=ALU.mult), reads=[br, bcsg], writes=[btc])
                p.op("gpsimd", lambda e, x2=x2, cosv=cosv: e.tensor_tensor(out=v3(td), in0=x2, in1=cosv, op=ALU.mult), reads=[br, bcsg], writes=[btd])
                p.op("vector", lambda e, rr4=rr4: e.tensor_tensor(out=rr4[:, :, 0, :], in0=v3(ta), in1=v3(tb_), op=ALU.subtract), reads=[bta, btb], writes=[brr])
                p.op("gpsimd", lambda e, rr4=rr4: e.tensor_tensor(out=rr4[:, :, 1, :], in0=v3(tc), in1=v3(td), op=ALU.add), reads=[btc, btd], writes=[brr])
                p.op("vector", lambda e, rr=rr, o=o: e.tensor_tensor(out=o[:, 1536:2048], in0=rr[:], in1=t12[:, 0:512], op=ALU.mult), reads=[brr, bt12], writes=[bo])
                p.op("gpsimd", lambda e, rr=rr, o=o: e.tensor_tensor(out=o[:, 2048:2560], in0=rr[:], in1=t12[:, 512:1024], op=ALU.mult), reads=[brr, bt12], writes=[bo])
            else:
                for b in range(4):
                    eng = ("vector", "scalar")[b % 2]
                    cw = 256 if b == 3 else 512
                    if eng == "vector":
                        p.op("vector", lambda e, o=o, s1=s1, b=b, cw=cw, pb=pbs[b][0]: e.tensor_scalar(out=o[:, b * 512:b * 512 + cw], in0=pb[:, 0:cw], scalar1=s1[:, 0:1], scalar2=None, op0=ALU.mult),
                             reads=[pbs[b][1], bs1], writes=[bo])
                    else:
                        p.op("scalar", lambda e, o=o, s1=s1, b=b, cw=cw, pb=pbs[b][0]: e.activation(out=o[:, b * 512:b * 512 + cw], in_=pb[:, 0:cw], func=AF.Copy, scale=s1[:, 0:1]),
                             reads=[pbs[b][1], bs1], writes=[bo])
                p.op("gpsimd", lambda e, o=o: e.tensor_scalar(out=o[:, 256:640], in0=o[:, 256:640], scalar1=0.125, scalar2=None, op0=ALU.mult), reads=[bo], writes=[bo])
                p.op("scalar", lambda e, o=o, s1=s1, pb=pbs[3][0]: e.activation(out=o[:, 1792:2048], in_=pb[:, 256:512], func=AF.Silu, scale=s1[:, 0:1]),
                     reads=[pbs[3][1], bs1], writes=[bo])
                p.op("scalar", lambda e, o=o, s1=s1, pb=pbs[4][0]: e.activation(out=o[:, 2048:2560], in_=pb[:, :], func=AF.Silu, scale=s1[:, 0:1]),
                     reads=[pbs[4][1], bs1], writes=[bo])
                l1, bl1 = la.next()
                l2, bl2 = le.next()
                p.op("vector", lambda e, l1=l1, s1=s1, pb=pbs[5][0]: e.scalar_tensor_tensor(out=l1[:], in0=pb[:, 0:384], scalar=s1[:, 0:1], in1=ba2[:], op0=ALU.mult, op1=ALU.add),
                     reads=[pbs[5][1], bs1, bba2], writes=[bl1])
                p.op("scalar", lambda e, l1=l1: e.activation(out=l1[:], in_=l1[:], func=AF.Exp, scale=-1.0), reads=[bl1], writes=[bl1])
                p.op("scalar", lambda e, l1=l1: e.activation(out=l1[:], in_=l1[:], func=AF.Ln, bias=1.0), reads=[bl1], writes=[bl1])
                p.op("gpsimd", lambda e, l1=l1, l2=l2: e.tensor_scalar(out=l2[:], in0=l1[:], scalar1=-1.0 / 16, scalar2=None, op0=ALU.mult), reads=[bl1], writes=[bl2])
                p.dma("gpsimd", la_out[tt * 128:(tt + 1) * 128, :], l2[:], reads=[bl2], writes=[p.buf()])
            p.dma("sync", out[tt * 128:(tt + 1) * 128, :], o[:], reads=[bo], writes=[p.buf()])
    p.finish_all()
    p.emit()
    return c


def _rot_tables():
    half = 32
    inv_freq = (1.0 / (10000.0 ** np.linspace(0.0, 1.0, half, dtype=np.float32))).astype(np.float32)
    ang = (np.arange(SEQ, dtype=np.float32)[:, None] * inv_freq[None, :]).astype(np.float32)
    return np.cos(ang).astype(np.float32), np.sin(ang).astype(np.float32)


def _ret_log_gamma():
    return np.log(1.0 - 2.0 ** (-5.0 - np.arange(4, dtype=np.float32))).astype(np.float32)


def _gpk(g):
    return np.ascontiguousarray(g.reshape(8, 128).T)


def pin0_maps(inp, x):
    cos, sin = _rot_tables()
    lg = _ret_log_gamma()
    i = np.arange(128, dtype=np.float32)
    qw = np.exp((i + 1.0)[:, None] * lg[None, :])
    kw = np.exp((127.0 - i)[:, None] * lg[None, :]) * 0.125
    t12 = np.empty((128, 1024), np.float32)
    t12[:, 0:256] = 1.0
    t12[:, 256:512] = 0.125
    t12[:, 512:768] = np.repeat(qw, 64, axis=1)
    t12[:, 768:1024] = np.repeat(kw, 64, axis=1)
    gqk = np.concatenate([np.tile(inp['da_q_norm'][0], 8), np.tile(inp['da_k_norm'][0], 8)])
    gqk = np.ascontiguousarray(np.broadcast_to(gqk[None, :], (128, 1024)))
    gpk = _gpk(inp['norm_mix_g'][0])
    w = np.ascontiguousarray(inp['ab_w_in'][0])
    maps = []
    for c in range(NCORE):
        xs = x[c * TOK:(c + 1) * TOK]
        pos0 = (c % 4) * TOK
        cc = cos[pos0:pos0 + TOK].reshape(32, 128, 32).transpose(1, 0, 2)
        sn = sin[pos0:pos0 + TOK].reshape(32, 128, 32).transpose(1, 0, 2)
        cs = np.concatenate([np.tile(cc, (1, 1, 8)), np.tile(sn, (1, 1, 8))], axis=2)
        maps.append({"x": np.ascontiguousarray(xs), "xT": np.ascontiguousarray(xs.T), "w": w, "gpk": gpk,
                     "gqk": gqk, "cs": np.ascontiguousarray(cs), "t12": t12})
    return maps


NT = SEQ // 128


def build_mix0(nqg=32):
    c = Ctx()
    nc = c.nc
    qaT_d = c.din("qaT", [128, SEQ], BF16)
    kaT_d = c.din("kaT", [128, SEQ], BF16)
    va_d = c.din("va", [SEQ, 128], BF16)
    qrT_d = c.din("qrT", [64, SEQ], BF16)
    krT_d = c.din("krT", [64, SEQ], BF16)
    qwrT_d = c.din("qwrT", [64, SEQ], BF16)
    kwr_d = c.din("kwr", [SEQ, 64], BF16)
    vr_d = c.din("vr", [SEQ, 128], BF16)
    gr_d = c.din("gr", [SEQ, 128], BF16)
    cst_d = c.din("cst", [128, 512])
    lam_d = c.din("lamv", [128, 256])
    oa_out = c.dout("oa", [128, SEQ], BF16)
    or_out = c.dout("or", [SEQ, 128], BF16)
    p = c.start()
    sb = c.sb
    kaT = sb("kaT", [128, SEQ], BF16); bkaT = [p.buf() for _ in range(8)]
    va = sb("va", [128, NT, 132], BF16); bva = [p.buf() for _ in range(8)]
    cst = sb("cst", [128, 512]); bcst = p.buf()
    lamv = sb("lamv", [128, 256]); blam = p.buf()
    p.dma("gpsimd", cst[:], cst_d[:, :], writes=[bcst])
    p.dma("gpsimd", lamv[:], lam_d[:, :], writes=[blam])
    vav = va_d.rearrange("(n p) e -> p n e", p=128)
    for i in range(8):
        p.dma("sync", kaT[:, i * 2048:(i + 1) * 2048], kaT_d[:, i * 2048:(i + 1) * 2048], writes=[bkaT[i]])
        p.dma("sync", va[:, i * 16:(i + 1) * 16, 0:128], vav[:, i * 16:(i + 1) * 16, :], writes=[bva[i]])
        p.op("gpsimd", lambda e, i=i: e.memset(va[:, i * 16:(i + 1) * 16, 128:129], 1.0), writes=[bva[i]])
    lt = sb("lt", [128, 128]); blt = p.buf()
    l2 = sb("l2", [128, 2]); bl2 = p.buf()
    lam = sb("lam", [128, 1]); blamS = p.buf()
    lv = lamv[:].rearrange("p (a two d) -> p a two d", two=2, d=64)
    p.op("vector", lambda e: e.tensor_tensor(out=lt[:].rearrange("p (a d) -> p a d", d=64), in0=lv[:, :, 0, :], in1=lv[:, :, 1, :], op=ALU.mult),
         reads=[blam], writes=[blt])
    p.op("vector", lambda e: e.tensor_reduce(out=l2[:], in_=lt[:].rearrange("p (a d) -> p a d", d=64), axis=AX.X, op=ALU.add), reads=[blt], writes=[bl2])
    p.op("scalar", lambda e: e.activation(out=l2[:], in_=l2[:], func=AF.Exp), reads=[bl2], writes=[bl2])
    p.op("vector", lambda e: e.tensor_tensor(out=lam[:], in0=l2[:, 0:1], in1=l2[:, 1:2], op=ALU.subtract), reads=[bl2], writes=[blamS])
    p.op("vector", lambda e: e.tensor_scalar(out=lam[:], in0=lam[:], scalar1=0.2, scalar2=None, op0=ALU.add), reads=[blamS], writes=[blamS])

    psS = Rot([(c.ps(f"pS{i}", [128, 512]), p.buf()) for i in range(3)])
    accO = [(c.ps(f"pAO{i}", [128, 512]), p.buf()) for i in range(2)]
    accL = [(c.ps(f"pAL{i}", [128, 512]), p.buf()) for i in range(2)]
    onesb = sb("onesb", [128, 128], BF16); bonesb = p.buf()
    p.op("vector", lambda e: e.memset(onesb[:], 1.0), writes=[bonesb])
    pR = c.ps("pR", [128, 512])
    bpRs, bpRo, bpRkv = p.buf(), p.buf(), p.buf()
    qin = Rot([(sb(f"qin{i}", [128, 512], BF16), p.buf()) for i in range(2)])
    pT = Rot([(sb(f"pT{i}", [128, 512], BF16), p.buf()) for i in range(4)])
    fin = {k: Rot([(sb(f"{k}{i}", shp, dt), p.buf()) for i in range(nb)]) for k, shp, dt, nb in
           (("fr0", [128, 512], F32, 1), ("fr1", [128, 512], F32, 1), ("ft0", [128, 512], F32, 1), ("ft1", [128, 512], F32, 1),
            ("fo", [128, 512], F32, 1), ("fq", [128, 512], BF16, 1), ("fn", [128, 512], F32, 1), ("foa", [128, 512], BF16, 2))}
    junk = sb("junk", [128, 128]); bjunk = p.buf()
    obuf = Rot([(sb(f"ob{i}", [128, 128], BF16), p.buf()) for i in range(4)])
    CH = 2048
    rq = Rot([(sb(f"rq{i}", [64, CH], BF16), p.buf()) for i in range(2)])
    rk = Rot([(sb(f"rk{i}", [64, CH], BF16), p.buf()) for i in range(2)])
    rqw = Rot([(sb(f"rqw{i}", [64, CH], BF16), p.buf()) for i in range(2)])
    rkw = Rot([(sb(f"rkw{i}", [128, 16, 64], BF16), p.buf()) for i in range(2)])
    rv = Rot([(sb(f"rv{i}", [128, 16, 128], BF16), p.buf()) for i in range(2)])
    rg = Rot([(sb(f"rg{i}", [128, 16, 128], BF16), p.buf()) for i in range(2)])
    Sf = sb("Sf", [64, 128]); bSf = p.buf()
    Sb = Rot([(sb(f"Sb{i}", [64, 128], BF16), p.buf()) for i in range(2)])
    sm = Rot([(sb(f"sm{i}", [128, 128], BF16), p.buf()) for i in range(2)])
    gt = Rot([(sb(f"gt{i}", [128, 128]), p.buf()) for i in range(2)])
    rs_ = Rot([(sb(f"rs{i}", [128, 1]), p.buf()) for i in range(2)])
    p.op("vector", lambda e: e.memset(Sf[:], 0.0), writes=[bSf])
    sb0, bsb0 = Sb.next()
    p.op("vector", lambda e: e.memset(sb0[:], 0.0), writes=[bsb0])
    ret_state = {"cur": None, "sb": (sb0, bsb0)}
    otiles = {}

    def ret_tile(t):
        ci, ti = t // 16, t % 16
        if ti == 0:
            cur = (rq.next(), rk.next(), rqw.next(), rkw.next(), rv.next(), rg.next())
            sl = slice(ci * CH, (ci + 1) * CH)
            p.dma("sync", cur[0][0][:], qrT_d[:, sl], writes=[cur[0][1]])
            p.dma("sync", cur[1][0][:], krT_d[:, sl], writes=[cur[1][1]])
            p.dma("sync", cur[2][0][:], qwrT_d[:, sl], writes=[cur[2][1]])
            p.dma("sync", cur[3][0][:], kwr_d.rearrange("(n p) e -> p n e", p=128)[:, ci * 16:(ci + 1) * 16, :], writes=[cur[3][1]])
            p.dma("sync", cur[4][0][:], vr_d.rearrange("(n p) e -> p n e", p=128)[:, ci * 16:(ci + 1) * 16, :], writes=[cur[4][1]])
            p.dma("sync", cur[5][0][:], gr_d.rearrange("(n p) e -> p n e", p=128)[:, ci * 16:(ci + 1) * 16, :], writes=[cur[5][1]])
            ret_state["cur"] = cur
        (q_, bq_), (k_, bk_), (qw_, bqw_), (kw_, bkw_), (v_, bv_), (g_, bg_) = ret_state["cur"]
        ts = slice(ti * 128, (ti + 1) * 128)
        Sbc, bSbc = ret_state["sb"]
        p.op("tensor", lambda e: e.matmul(pR[:, 0:128], lhsT=k_[:, ts], rhs=q_[:, ts], start=True, stop=True), reads=[bk_, bq_], writes=[bpRs])
        s_, bs_ = sm.next()
        p.op("vector", lambda e: e.tensor_tensor(out=s_[:], in0=pR[:, 0:128], in1=cst[:, 0:128], op=ALU.mult), reads=[bpRs, bcst], writes=[bs_])
        p.op("tensor", lambda e: e.matmul(pR[:, 128:256], lhsT=s_[:], rhs=v_[:, ti, :], start=True, stop=False), reads=[bs_, bv_], writes=[bpRo])
        p.op("tensor", lambda e: e.matmul(pR[:, 128:256], lhsT=qw_[:, ts], rhs=Sbc[:], start=False, stop=True), reads=[bqw_, bSbc], writes=[bpRo])
        p.op("tensor", lambda e: e.matmul(pR[0:64, 256:384], lhsT=kw_[:, ti, :], rhs=v_[:, ti, :], start=True, stop=True), reads=[bkw_, bv_], writes=[bpRkv])
        p.op("vector", lambda e: e.scalar_tensor_tensor(out=Sf[:], in0=Sf[:], scalar=cst[0:64, 384:385], in1=pR[0:64, 256:384], op0=ALU.mult, op1=ALU.add),
             reads=[bSf, bpRkv, bcst], writes=[bSf])
        Sbn, bSbn = Sb.next()
        p.op("scalar", lambda e: e.activation(out=Sbn[:], in_=Sf[:], func=AF.Copy), reads=[bSf], writes=[bSbn])
        ret_state["sb"] = (Sbn, bSbn)
        r1, br1 = rs_.next()
        p.op("scalar", lambda e: e.activation(out=junk[:], in_=pR[:, 128:256], func=AF.Square, accum_out=r1[:]), reads=[bpRo], writes=[bjunk, br1])
        p.op("scalar", lambda e: e.activation(out=r1[:], in_=r1[:], func=AF.Sqrt, scale=1.0 / 128, bias=EPS), reads=[br1], writes=[br1])
        p.op("vector", lambda e: e.reciprocal(out=r1[:], in_=r1[:]), reads=[br1], writes=[br1])
        g1, bg1 = gt.next()
        p.op("gpsimd", lambda e: e.tensor_tensor(out=g1[:], in0=g_[:, ti, :], in1=cst[:, 256:384], op=ALU.mult), reads=[bg_, bcst], writes=[bg1])
        o, bo = obuf.next()
        p.op("vector", lambda e: e.scalar_tensor_tensor(out=o[:], in0=pR[:, 128:256], scalar=r1[:, 0:1], in1=g1[:], op0=ALU.mult, op1=ALU.mult),
             reads=[bpRo, br1, bg1], writes=[bo])
        p.dma("gpsimd", or_out[t * 128:(t + 1) * 128, :], o[:], reads=[bo], writes=[p.buf()])

    def da_final(qg):
        (O0, bO0), (O1, bO1) = accO
        (L0, bL0), (L1, bL1) = accL
        (r0, br0), (r1_, br1_), (t0, bt0), (t1, bt1) = fin["fr0"].next(), fin["fr1"].next(), fin["ft0"].next(), fin["ft1"].next()
        (fo, bfo), (fq, bfq), (fn, bfn), (foa, bfoa) = fin["fo"].next(), fin["fq"].next(), fin["fn"].next(), fin["foa"].next()
        p.op("vector", lambda e: e.reciprocal(out=r0[:], in_=L0[:, :]), reads=[bL0], writes=[br0])
        p.op("vector", lambda e: e.reciprocal(out=r1_[:], in_=L1[:, :]), reads=[bL1], writes=[br1_])
        p.op("vector", lambda e: e.tensor_tensor(out=t0[:], in0=O0[:, :], in1=r0[:], op=ALU.mult), reads=[bO0, br0], writes=[bt0])
        p.op("vector", lambda e: e.scalar_tensor_tensor(out=t1[:], in0=O1[:, :], scalar=lam[:, 0:1], in1=r1_[:], op0=ALU.mult, op1=ALU.mult),
             reads=[bO1, blamS, br1_], writes=[bt1])
        p.op("gpsimd", lambda e: e.tensor_tensor(out=fo[:], in0=t0[:], in1=t1[:], op=ALU.subtract), reads=[bt0, bt1], writes=[bfo])
        p.op("gpsimd", lambda e: e.tensor_tensor(out=fq[:], in0=fo[:], in1=fo[:], op=ALU.mult), reads=[bfo], writes=[bfq])
        pn, bpn = psS.next()
        p.op("tensor", lambda e: e.matmul(pn[:, :], lhsT=onesb[:], rhs=fq[:], start=True, stop=True), reads=[bonesb, bfq], writes=[bpn])
        p.op("scalar", lambda e: e.activation(out=fn[:], in_=pn[:, :], func=AF.Sqrt, scale=1.0 / 128, bias=EPS), reads=[bpn], writes=[bfn])
        p.op("vector", lambda e: e.reciprocal(out=fn[:], in_=fn[:]), reads=[bfn], writes=[bfn])
        p.op("vector", lambda e: e.scalar_tensor_tensor(out=foa[:], in0=fo[:], scalar=cst[:, 385:386], in1=fn[:], op0=ALU.mult, op1=ALU.mult),
             reads=[bfo, bcst, bfn], writes=[bfoa])
        p.dma("gpsimd", oa_out[:, qg * 512:(qg + 1) * 512], foa[:], reads=[bfoa], writes=[p.buf()])

    for qg in range(nqg):
        qi, bqi = qin.next()
        p.dma("sync", qi[:], qaT_d[:, qg * 512:(qg + 1) * 512], writes=[bqi])
        nk = 4 * qg + 4
        for kt in range(nk):
            j = kt - 4 * qg
            q0 = max(0, j)
            qs = q0 * 128
            kb = bkaT[kt // 16]
            pts = []
            for m in range(2):
                pS, bpS = psS.next()
                p.op("tensor", lambda e, pS=pS, m=m, kt=kt, qs=qs, qi=qi: e.matmul(pS[:, qs:512], lhsT=kaT[m * 64:(m + 1) * 64, kt * 128:(kt + 1) * 128],
                                                                      rhs=qi[m * 64:(m + 1) * 64, qs:512], start=True, stop=True) if True else None,
                     reads=[kb, bqi], writes=[bpS])
                pt, bpt = pT.next()
                p.op("scalar", lambda e, pS=pS, pt=pt, qs=qs: e.activation(out=pt[:, qs:512], in_=pS[:, qs:512], func=AF.Exp, scale=0.125, bias=-8.0),
                     reads=[bpS], writes=[bpt])
                if j >= 0:
                    p.op("gpsimd", lambda e, pt=pt, qs=qs: e.memset(pt[64:128, qs:qs + 64], 0.0), writes=[bpt])
                pts.append((pt, bpt))
            for m in range(2):
                pt, bpt = pts[m]
                (AO, bAO), (AL, bAL) = accO[m], accL[m]
                p.op("tensor", lambda e, AO=AO, pt=pt, kt=kt, qs=qs, st=(kt == 0), sp=(kt == nk - 1): e.matmul(
                    AO[:, qs:512], lhsT=va[:, kt, 0:128], rhs=pt[:, qs:512], start=st, stop=sp), reads=[bpt, bva[kt // 16]], writes=[bAO])
                p.op("tensor", lambda e, AL=AL, pt=pt, kt=kt, qs=qs, st=(kt == 0), sp=(kt == nk - 1): e.matmul(
                    AL[:, qs:512], lhsT=onesb[:], rhs=pt[:, qs:512], start=st, stop=sp), reads=[bpt, bonesb], writes=[bAL])
        da_final(qg)
        for t in range(qg * 4, qg * 4 + 4):
            ret_tile(t)
    p.finish_all()
    p.emit()
    return c


def mix0_maps(inp, pre):
    lg = _ret_log_gamma()
    i = np.arange(128)
    maps = []
    for c in range(NCORE):
        b, h = c // 4, c % 4
        rows = pre[b * SEQ:(b + 1) * SEQ]
        gam = np.exp(lg[h]).astype(np.float32)
        dist = np.abs(i[:, None] - i[None, :]).astype(np.float32)
        mret = np.exp(lg[h] * dist) * ((i[:, None] // 64) <= (i[None, :] // 64))
        cst = np.zeros((128, 512), np.float32)
        cst[:, 0:128] = mret
        cst[:, 128:256] = inp['da_out_norm'][0][None, :] * np.float32(0.8)
        cst[:, 256:384] = inp['ret_out_norm'][0][None, :]
        cst[:, 384] = np.exp(np.float32(128.0) * lg[h])
        cst[:, 385] = inp['da_out_norm'][0] * np.float32(0.8)
        lamv = np.concatenate([inp['da_lam_q1'][0], inp['da_lam_k1'][0], inp['da_lam_q2'][0], inp['da_lam_k2'][0]])
        lamv = np.ascontiguousarray(np.broadcast_to(lamv[None, :], (128, 256))).astype(np.float32)
        ct = lambda a: np.ascontiguousarray(a)
        maps.append({
            "qaT": ct(rows[:, h * 128:(h + 1) * 128].T), "kaT": ct(rows[:, 512 + h * 128:512 + (h + 1) * 128].T),
            "va": ct(rows[:, 1024 + h * 128:1024 + (h + 1) * 128]),
            "qrT": ct(rows[:, 1536 + h * 64:1536 + (h + 1) * 64].T), "krT": ct(rows[:, 1792 + h * 64:1792 + (h + 1) * 64].T),
            "qwrT": ct(rows[:, 2048 + h * 64:2048 + (h + 1) * 64].T), "kwr": ct(rows[:, 2304 + h * 64:2304 + (h + 1) * 64]),
            "vr": ct(rows[:, 2560 + h * 128:2560 + (h + 1) * 128]), "gr": ct(rows[:, 3072 + h * 128:3072 + (h + 1) * 128]),
            "cst": cst, "lamv": lamv})
    return maps


def build_pout(layer, ngrp=None):
    c = Ctx()
    nc = c.nc
    GT = 256
    NG = TOK // GT if ngrp is None else ngrp
    x = c.din("x", [TOK, D])
    nko = 8 if layer == 0 else 6
    oT_d = c.din("oT", [nko * 128, TOK], BF16)
    wout_d = c.din("wout", [D, D])
    wup_d = c.din("wup", [D, 4096])
    wdn_d = c.din("wdn", [4096, D])
    gpk_d = c.din("gpk", [128, 8])
    ident_d = c.din("ident", [128, 128])
    if layer == 1:
        yT_d = c.din("yT", [256, TOK])
        wglu_d = c.din("wglu", [256, 256])
        rT_d = c.din("rT", [768, TOK], BF16)
        gng_d = c.din("gng", [128, 1])
    out = c.dout("xo", [TOK, D])
    p = c.start()
    sb = c.sb
    woutb = sb("woutb", [128, 8, D], BF16); bwout = p.buf()
    wupb = sb("wupb", [128, 8, 4096], BF16); bwup = p.buf()
    wdnb = sb("wdnb", [128, 32, D], BF16); bwdn = p.buf()
    gpk = sb("gpk", [128, 8]); bgpk = p.buf()
    identf = sb("identf", [128, 128]); bidf = p.buf()
    ident = sb("ident", [128, 128], BF16); bid = p.buf()
    p.dma("gpsimd", gpk[:], gpk_d[:, :], writes=[bgpk])
    p.dma("gpsimd", identf[:], ident_d[:, :], writes=[bidf])
    p.op("vector", lambda e: e.tensor_copy(out=ident[:], in_=identf[:]), reads=[bidf], writes=[bid])
    xin = Rot([(sb(f"xin{i}", [128, D]), p.buf()) for i in range(2)])
    stage = xin
    load_weight_bf16(c, p, wout_d, woutb, bwout, 8, D, None, None, stage, 1024)
    if layer == 1:
        wglub = sb("wglub", [128, 2, 256], BF16); bwglu = p.buf()
        load_weight_bf16(c, p, wglu_d, wglub, bwglu, 2, 256, None, None, stage, 256)
    load_weight_bf16(c, p, wup_d, wupb, bwup, 8, 4096, gpk, bgpk, stage, 1024)
    load_weight_bf16(c, p, wdn_d, wdnb, bwdn, 32, D, None, None, stage, 1024)
    accD = [(c.ps(f"pD{i}", [128, 512]), p.buf()) for i in range(4)]
    pH = Rot([(c.ps(f"pH{i}", [128, 512]), p.buf()) for i in range(4)])
    oin = Rot([(sb(f"oin{i}", [128, 8, GT], BF16), p.buf()) for i in range(2)])
    x1 = Rot([(sb(f"x1_{i}", [128, 2, D]), p.buf()) for i in range(2)])
    x1b = Rot([(sb(f"x1b{i}", [128, D], BF16), p.buf()) for i in range(2)])
    x1T = Rot([(sb(f"x1T{i}", [128, 8, GT], BF16), p.buf()) for i in range(1)])
    rsd = Rot([(sb(f"rsd{i}", [128, 2]), p.buf()) for i in range(4)])
    junk = sb("junk", [128, D], BF16); bjunk = p.buf()
    hr = Rot([(sb(f"hr{i}", [128, GT], BF16), p.buf()) for i in range(3)])
    h2 = Rot([(sb(f"h2{i}", [128, GT], BF16), p.buf()) for i in range(4)])
    if layer == 1:
        yin = Rot([(sb(f"yin{i}", [128, 2, GT]), p.buf()) for i in range(1)])
        ga = sb("ga", [128, 2, GT]); bga = p.buf()
        gb = sb("gb", [128, 2, GT]); bgb = p.buf()
        gz, bgz = ga, bga
        gng = sb("gng", [128, 1]); bgng = p.buf()
        p.dma("gpsimd", gng[:], gng_d[:, :], writes=[bgng])
        onesb = sb("onesb", [128, 128], BF16); bonesb = p.buf()
        p.op("vector", lambda e: e.memset(onesb[:], 1.0), writes=[bonesb])
        rin = Rot([(sb(f"rin{i}", [128, 6, GT], BF16), p.buf()) for i in range(1)])
        nsq = Rot([(sb(f"nsq{i}", [128, GT], BF16), p.buf()) for i in range(2)])
        nrs = Rot([(sb(f"nrs{i}", [128, GT]), p.buf()) for i in range(2)])
        ntm = Rot([(sb(f"ntm{i}", [128, GT]), p.buf()) for i in range(1)])
        gzb = sb("gzb", [128, 2, GT], BF16); bgzb = p.buf()
        gs = sb("gs", [128, GT]); bgs = p.buf()
    xv = x.rearrange("(n p) d -> p n d", p=128)
    ov = oT_d.rearrange("(k p) t -> p k t", p=128)
    for g in range(NG):
        oi, boi = oin.next()
        k0 = 8 - nko
        p.dma("gpsimd", oi[:, k0:8, :], ov[:, :, g * GT:(g + 1) * GT], writes=[boi])
        if layer == 1:
            yi, byi = yin.next()
            p.dma("sync", yi[:], yT_d.rearrange("(k p) t -> p k t", p=128)[:, :, g * GT:(g + 1) * GT], writes=[byi])
            p.op("gpsimd", lambda e, yi=yi: e.tensor_tensor(out=ga[:], in0=yi[:], in1=yi[:], op=ALU.mult), reads=[byi], writes=[bga])
            p.op("vector", lambda e: e.tensor_scalar(out=ga[:], in0=ga[:], scalar1=0.044715, scalar2=1.0, op0=ALU.mult, op1=ALU.add), reads=[bga], writes=[bga])
            p.op("gpsimd", lambda e, yi=yi: e.tensor_tensor(out=gb[:], in0=ga[:], in1=yi[:], op=ALU.mult), reads=[bga, byi], writes=[bgb])
            p.op("scalar", lambda e: e.activation(out=gb[:], in_=gb[:], func=AF.Tanh, scale=0.7978845608028654), reads=[bgb], writes=[bgb])
            p.op("vector", lambda e: e.tensor_scalar(out=gb[:], in0=gb[:], scalar1=1.0, scalar2=0.5, op0=ALU.add, op1=ALU.mult), reads=[bgb], writes=[bgb])
            p.op("gpsimd", lambda e, yi=yi: e.tensor_tensor(out=gz[:], in0=gb[:], in1=yi[:], op=ALU.mult), reads=[bgb, byi], writes=[bgz])
            p.op("vector", lambda e: e.tensor_copy(out=gzb[:], in_=gz[:]), reads=[bgz], writes=[bgzb])
            for jc in range(2):
                pg, bpg = pH.next()
                for ic in range(2):
                    p.op("tensor", lambda e, pg=pg, ic=ic, jc=jc: e.matmul(pg[:, 0:GT], lhsT=wglub[:, ic, jc * 128:(jc + 1) * 128], rhs=gzb[:, ic, :],
                                                                      start=(ic == 0), stop=(ic == 1)), reads=[bwglu, bgzb], writes=[bpg])
                p.op("scalar", lambda e, pg=pg: e.activation(out=gs[:], in_=pg[:, 0:GT], func=AF.Sigmoid), reads=[bpg], writes=[bgs])
                p.op("vector", lambda e, jc=jc, oi=oi: e.tensor_tensor(out=oi[:, jc, :], in0=gz[:, jc, :], in1=gs[:], op=ALU.mult), reads=[bgz, bgs], writes=[boi])
            ri, bri = rin.next()
            p.dma("sync", ri[:], rT_d.rearrange("(k p) t -> p k t", p=128)[:, :, g * GT:(g + 1) * GT], writes=[bri])
            for kk in range(6):
                q_, bq_ = nsq.next()
                p.op("gpsimd", lambda e, q_=q_, oi=oi, kk=kk: e.tensor_tensor(out=q_[:], in0=oi[:, 2 + kk, :], in1=oi[:, 2 + kk, :], op=ALU.mult), reads=[boi], writes=[bq_])
                pn, bpn = pH.next()
                p.op("tensor", lambda e, pn=pn, q_=q_: e.matmul(pn[:, 0:GT], lhsT=onesb[:], rhs=q_[:], start=True, stop=True), reads=[bonesb, bq_], writes=[bpn])
                r_, br_ = nrs.next()
                p.op("scalar", lambda e, pn=pn, r_=r_: e.activation(out=r_[:], in_=pn[:, 0:GT], func=AF.Sqrt, scale=1.0 / 128, bias=EPS), reads=[bpn], writes=[br_])
                p.op("vector", lambda e, r_=r_: e.reciprocal(out=r_[:], in_=r_[:]), reads=[br_], writes=[br_])
                t_, bt_ = ntm.next()
                p.op("vector", lambda e, t_=t_, r_=r_, oi=oi, kk=kk: e.scalar_tensor_tensor(out=t_[:], in0=oi[:, 2 + kk, :], scalar=gng[:, 0:1], in1=r_[:], op0=ALU.mult, op1=ALU.mult),
                     reads=[boi, bgng, br_], writes=[bt_])
                p.op("gpsimd", lambda e, t_=t_, ri=ri, oi=oi, kk=kk: e.tensor_tensor(out=oi[:, 2 + kk, :], in0=t_[:], in1=ri[:, kk, :], op=ALU.mult), reads=[bt_, bri], writes=[boi])
        x1g, bx1g = x1.next()
        x1Tg, bx1Tg = x1T.next()
        rs2, brs2 = rsd.next()
        rq2, brq2 = rsd.next()
        for j in range(2):
            xi, bxi = xin.next()
            p.dma("sync", xi[:], x[(g * 2 + j) * 128:(g * 2 + j + 1) * 128, :], writes=[bxi])
            for hh in range(2):
                pd, bpd = accD[j * 2 + hh]
                for k in range(8):
                    p.op("tensor", lambda e, pd=pd, oi=oi, k=k, j=j, hh=hh: e.matmul(pd[:, :], lhsT=oi[:, k, j * 128:(j + 1) * 128],
                                                                                rhs=woutb[:, k, hh * 512:(hh + 1) * 512], start=(k == 0), stop=(k == 7)),
                         reads=[boi, bwout], writes=[bpd])
                p.op("vector", lambda e, pd=pd, xi=xi, x1g=x1g, j=j, hh=hh: e.tensor_tensor(out=x1g[:, j, hh * 512:(hh + 1) * 512], in0=pd[:, :],
                                                                                       in1=xi[:, hh * 512:(hh + 1) * 512], op=ALU.add),
                     reads=[bpd, bxi], writes=[bx1g])
            p.op("scalar", lambda e, x1g=x1g, j=j, rs2=rs2: e.activation(out=junk[:], in_=x1g[:, j, :], func=AF.Square, accum_out=rs2[:, j:j + 1]),
                 reads=[bx1g], writes=[bjunk, brs2])
            xb, bxb = x1b.next()
            p.op("gpsimd", lambda e, xb=xb, x1g=x1g, j=j: e.tensor_copy(out=xb[:], in_=x1g[:, j, :]), reads=[bx1g], writes=[bxb])
            pt, bpt = pH.next()
            ptb = pt[:, :].bitcast(BF16)
            for k in range(8):
                p.op("tensor", lambda e, ptb=ptb, xb=xb, k=k: e.transpose(ptb[:, k * 128:(k + 1) * 128], xb[:, k * 128:(k + 1) * 128], ident[:]),
                     reads=[bxb, bid], writes=[bpt])
            p.op("scalar", lambda e, ptb=ptb, x1Tg=x1Tg, j=j: e.activation(out=x1Tg[:, :, j * 128:(j + 1) * 128], in_=ptb.rearrange("p (k t) -> p k t", t=128), func=AF.Copy),
                 reads=[bpt], writes=[bx1Tg])
        p.op("scalar", lambda e, rs2=rs2: e.activation(out=rs2[:], in_=rs2[:], func=AF.Sqrt, scale=1.0 / D, bias=EPS), reads=[brs2], writes=[brs2])
        p.op("vector", lambda e, rs2=rs2: e.reciprocal(out=rs2[:], in_=rs2[:]), reads=[brs2], writes=[brs2])
        p.op("vector", lambda e, rs2=rs2, rq2=rq2: e.tensor_tensor(out=rq2[:], in0=rs2[:], in1=rs2[:], op=ALU.mult), reads=[brs2], writes=[brq2])
        def up(f):
            ph, bph = pH.next()
            for k in range(8):
                p.op("tensor", lambda e, ph=ph, k=k, f=f, x1Tg=x1Tg: e.matmul(ph[:, 0:GT], lhsT=wupb[:, k, f * 128:(f + 1) * 128], rhs=x1Tg[:, k, :],
                                                                         start=(k == 0), stop=(k == 7)), reads=[bwup, bx1Tg], writes=[bph])
            r, br = hr.next()
            p.op("scalar", lambda e, ph=ph, r=r: e.activation(out=r[:], in_=ph[:, 0:GT], func=AF.Relu), reads=[bph], writes=[br])
            hq, bhq = h2.next()
            eng = "gpsimd" if f % 2 == 0 else "vector"
            p.op(eng, lambda e, r=r, hq=hq: e.tensor_tensor(out=hq[:], in0=r[:], in1=r[:], op=ALU.mult), reads=[br], writes=[bhq])
            return hq, bhq

        def down(f, hq, bhq):
            for j in range(2):
                for hh in range(2):
                    pd, bpd = accD[j * 2 + hh]
                    p.op("tensor", lambda e, pd=pd, hq=hq, j=j, hh=hh, f=f: e.matmul(pd[:, :], lhsT=hq[:, j * 128:(j + 1) * 128],
                                                                                rhs=wdnb[:, f, hh * 512:(hh + 1) * 512], start=(f == 0), stop=(f == 31)),
                         reads=[bhq, bwdn], writes=[bpd])
        cur = up(0)
        for f in range(32):
            nxt = up(f + 1) if f < 31 else None
            down(f, *cur)
            cur = nxt
        for j in range(2):
            for hh in range(2):
                pd, bpd = accD[j * 2 + hh]
                p.op("vector", lambda e, pd=pd, x1g=x1g, rq2=rq2, j=j, hh=hh: e.scalar_tensor_tensor(
                    out=x1g[:, j, hh * 512:(hh + 1) * 512], in0=pd[:, :], scalar=rq2[:, j:j + 1], in1=x1g[:, j, hh * 512:(hh + 1) * 512], op0=ALU.mult, op1=ALU.add),
                    reads=[bpd, brq2, bx1g], writes=[bx1g])
            tt = g * 2 + j
            p.dma("sync", out[tt * 128:(tt + 1) * 128, :], x1g[:, j, :], reads=[bx1g], writes=[p.buf()])
    p.finish_all()
    p.emit()
    return c


def pout_maps(inp, layer, x, oT_full, yT_full=None, rT_full=None):
    j = layer // 2
    wout = np.ascontiguousarray(inp['ab_w_out'][j] if layer == 0 else inp['cd_w_out'][j])
    maps = []
    ident = np.eye(128, dtype=np.float32)
    for c in range(NCORE):
        sl = slice(c * TOK, (c + 1) * TOK)
        m = {"x": np.ascontiguousarray(x[sl]), "oT": np.ascontiguousarray(oT_full[:, sl]), "wout": wout,
             "wup": np.ascontiguousarray(inp['w_up'][layer]), "wdn": np.ascontiguousarray(inp['w_down'][layer]),
             "gpk": _gpk(inp['norm_mlp_g'][layer]), "ident": ident}
        if layer == 1:
            m["yT"] = np.ascontiguousarray(yT_full[:, sl])
            m["wglu"] = np.ascontiguousarray(inp['s5_w_glu'][j])
            m["rT"] = np.ascontiguousarray(rT_full[:, sl])
            m["gng"] = np.ascontiguousarray(inp['gla_out_norm'][j][:, None])
        maps.append(m)
    return maps


def assemble_oT0(r):
    oT = np.empty((1024, 2 * SEQ), r[0]["oa"].dtype)
    for c in range(NCORE):
        b, h = c // 4, c % 4
        oT[h * 128:(h + 1) * 128, b * SEQ:(b + 1) * SEQ] = r[c]["oa"]
        oT[512 + h * 128:512 + (h + 1) * 128, b * SEQ:(b + 1) * SEQ] = r[c]["or"].T
    return oT


def pin1_maps(inp, x):
    w = np.ascontiguousarray(inp['cd_w_in'][0][:, 0:2560])
    walrT = np.ascontiguousarray(inp['cd_w_in'][0][:, 2560:2576].T)
    wa2 = np.ascontiguousarray(inp['gla_w_a2'][0])
    ba2 = np.ascontiguousarray(np.broadcast_to(inp['gla_b_a2'][0][None, :], (128, 384)))
    gpk = _gpk(inp['norm_mix_g'][1])
    maps = []
    for c in range(NCORE):
        xs = x[c * TOK:(c + 1) * TOK]
        maps.append({"x": np.ascontiguousarray(xs), "xT": np.ascontiguousarray(xs.T), "w": w, "gpk": gpk,
                     "walrT": walrT, "wa2": wa2, "ba2": ba2})
    return maps


TWO_PI = float(2.0 * np.pi)
PI = float(np.pi)
I32 = mybir.dt.int32


def emit_sincos(p, sb, name, ang, bang, shape, s_out, c_out, bouts, eng_pool="gpsimd"):
    P_ = shape[0]
    kf = sb(name + "_kf", shape); ki = sb(name + "_ki", shape, I32); r = sb(name + "_r", shape); ab = sb(name + "_ab", shape)
    bkf, bki, br, bab = p.buf(), p.buf(), p.buf(), p.buf()
    p.op("vector", lambda e: e.tensor_scalar(out=kf[:], in0=ang, scalar1=1.0 / TWO_PI, scalar2=None, op0=ALU.mult), reads=[bang], writes=[bkf])
    p.op("vector", lambda e: e.tensor_copy(out=ki[:], in_=kf[:]), reads=[bkf], writes=[bki])
    p.op("vector", lambda e: e.tensor_copy(out=kf[:], in_=ki[:]), reads=[bki], writes=[bkf])
    p.op("vector", lambda e: e.scalar_tensor_tensor(out=r[:], in0=kf[:], scalar=-TWO_PI, in1=ang, op0=ALU.mult, op1=ALU.add), reads=[bkf, bang], writes=[br])
    p.op("vector", lambda e: e.tensor_scalar(out=r[:], in0=r[:], scalar1=-PI, scalar2=PI, op0=ALU.max, op1=ALU.min), reads=[br], writes=[br])
    p.op("scalar", lambda e: e.activation(out=s_out, in_=r[:], func=AF.Sin), reads=[br], writes=[bouts[0]])
    p.op("scalar", lambda e: e.activation(out=ab[:], in_=r[:], func=AF.Abs), reads=[br], writes=[bab])
    p.op("scalar", lambda e: e.activation(out=c_out, in_=ab[:], func=AF.Sin, scale=-1.0, bias=PI / 2), reads=[bab], writes=[bouts[1]])


def build_mix1(nwin=32):
    c = Ctx()
    nc = c.nc
    W = 512
    uT_d = c.din("uT", [64, SEQ], BF16)
    prmC_d = c.din("prmC", [128, 6])
    prmR_d = c.din("prmR", [64, 3, 128])
    bpad_d = c.din("bpad", [64, 2, 128])
    ct_d = c.din("ct", [128, 2, 2, 64])
    dsk_d = c.din("dsk", [64, 1])
    iota_d = c.din("iota", [128, W])
    gm_d = c.din("gm", [128, 256])
    idn_d = c.din("idn", [128, 128])
    gq_d = c.din("gq", [3, 64, SEQ], BF16)
    gk_d = c.din("gk", [3, 64, SEQ], BF16)
    gla_d = c.din("gla", [3, 64, SEQ])
    gv_d = c.din("gv", [SEQ, 192], BF16)
    yT_o = c.dout("yT", [64, SEQ])
    go_o = c.dout("go", [SEQ, 192], BF16)
    p = c.start()
    sb = c.sb

    def ld(name, shape, src, dt=F32, q="gpsimd"):
        t = sb(name, shape, dt); b = p.buf()
        p.dma(q, t[:], src, writes=[b])
        return t, b
    prmC, bprmC = ld("prmC", [128, 6], prmC_d[:, :])
    prmR, bprmR = ld("prmR", [64, 3, 128], prmR_d[:, :, :])
    bpad, bbpad = ld("bpad", [64, 2, 128], bpad_d[:, :, :])
    ctf, bctf = ld("ctf", [128, 2, 2, 64], ct_d[:, :, :, :])
    dsk, bdsk = ld("dsk", [64, 1], dsk_d[:, :])
    iota, biota = ld("iota", [128, W], iota_d[:, :])
    gm, bgm = ld("gm", [128, 256], gm_d[:, :])
    idf, bidf = ld("idf", [128, 128], idn_d[:, :])
    ident = sb("ident", [128, 128], BF16); bid = p.buf()
    p.op("vector", lambda e: e.tensor_copy(out=ident[:], in_=idf[:]), reads=[bidf], writes=[bid])
    ones = sb("ones", [128, 1]); bones = p.buf()
    p.op("vector", lambda e: e.memset(ones[:], 1.0), writes=[bones])

    pc = prmC[:].rearrange("p (a k) -> p a k", k=3)
    dl = sb("dl", [128, 2]); bdl = p.buf()
    rr = sb("rr", [128, 2]); brr = p.buf()
    th = sb("th", [128, 2]); bth = p.buf()
    p.op("scalar", lambda e: e.activation(out=dl[:], in_=pc[:, :, 2], func=AF.Exp), reads=[bprmC], writes=[bdl])
    p.op("vector", lambda e: e.tensor_tensor(out=rr[:], in0=pc[:, :, 0], in1=dl[:], op=ALU.mult), reads=[bprmC, bdl], writes=[brr])
    p.op("scalar", lambda e: e.activation(out=rr[:], in_=rr[:], func=AF.Exp), reads=[brr], writes=[brr])
    p.op("vector", lambda e: e.tensor_tensor(out=th[:], in0=pc[:, :, 1], in1=dl[:], op=ALU.mult), reads=[bprmC, bdl], writes=[bth])
    cosT, sinT, bcs = [], [], []
    ang = sb("ang", [128, W]); bang = p.buf()
    for pr in range(2):
        ct_ = sb(f"cosT{pr}", [128, W]); st_ = sb(f"sinT{pr}", [128, W]); b1, b2 = p.buf(), p.buf()
        p.op("vector", lambda e, pr=pr: e.tensor_scalar(out=ang[:], in0=iota[:], scalar1=th[:, pr:pr + 1], scalar2=None, op0=ALU.mult), reads=[biota, bth], writes=[bang])
        emit_sincos(p, sb, f"sc{pr}", ang[:], bang, [128, W], st_[:], ct_[:], (b1, b2))
        cosT.append(ct_); sinT.append(st_); bcs.append((b2, b1))
    angW = sb("angW", [128, 2]); bangW = p.buf()
    cW = sb("cW", [128, 2]); sW = sb("sW", [128, 2]); nsW = sb("nsW", [128, 2]); bcW, bsW, bnsW = p.buf(), p.buf(), p.buf()
    p.op("vector", lambda e: e.tensor_scalar(out=angW[:], in0=th[:], scalar1=float(W), scalar2=None, op0=ALU.mult), reads=[bth], writes=[bangW])
    emit_sincos(p, sb, "scW", angW[:], bangW, [128, 2], sW[:], cW[:], (bsW, bcW))
    p.op("vector", lambda e: e.tensor_scalar(out=nsW[:], in0=sW[:], scalar1=-1.0, scalar2=None, op0=ALU.mult), reads=[bsW], writes=[bnsW])
    R = lambda k: prmR[:, k, :]
    def t64(name):
        return sb(name, [64, 128]), p.buf()
    (dR, bdR), (x1, bx1), (er, ber), (thR, bthR), (sR, bsR), (cR, bcR) = [t64(n) for n in ("dR", "x1R", "erR", "thR", "sR", "cR")]
    (lbr, blbr), (lbi, blbi), (den, bden), (t1, bt1), (t2, bt2), (cfr, bcfr), (cfi, bcfi) = [t64(n) for n in ("lbr", "lbi", "den", "t1R", "t2R", "cfr", "cfi")]
    V = "vector"
    p.op("scalar", lambda e: e.activation(out=dR[:], in_=R(2), func=AF.Exp), reads=[bprmR], writes=[bdR])
    p.op(V, lambda e: e.tensor_tensor(out=x1[:], in0=R(0), in1=dR[:], op=ALU.mult), reads=[bprmR, bdR], writes=[bx1])
    p.op("scalar", lambda e: e.activation(out=er[:], in_=x1[:], func=AF.Exp), reads=[bx1], writes=[ber])
    p.op(V, lambda e: e.tensor_tensor(out=thR[:], in0=R(1), in1=dR[:], op=ALU.mult), reads=[bprmR, bdR], writes=[bthR])
    emit_sincos(p, sb, "scR", thR[:], bthR, [64, 128], sR[:], cR[:], (bsR, bcR))
    p.op(V, lambda e: e.tensor_tensor(out=lbr[:], in0=er[:], in1=cR[:], op=ALU.mult), reads=[ber, bcR], writes=[blbr])
    p.op(V, lambda e: e.tensor_scalar(out=lbr[:], in0=lbr[:], scalar1=-1.0, scalar2=None, op0=ALU.add), reads=[blbr], writes=[blbr])
    p.op(V, lambda e: e.tensor_tensor(out=lbi[:], in0=er[:], in1=sR[:], op=ALU.mult), reads=[ber, bsR], writes=[blbi])
    p.op(V, lambda e: e.tensor_tensor(out=den[:], in0=R(0), in1=R(0), op=ALU.mult), reads=[bprmR], writes=[bden])
    p.op(V, lambda e: e.tensor_tensor(out=t1[:], in0=R(1), in1=R(1), op=ALU.mult), reads=[bprmR], writes=[bt1])
    p.op(V, lambda e: e.tensor_tensor(out=den[:], in0=den[:], in1=t1[:], op=ALU.add), reads=[bden, bt1], writes=[bden])
    p.op(V, lambda e: e.reciprocal(out=den[:], in_=den[:]), reads=[bden], writes=[bden])
    p.op(V, lambda e: e.tensor_tensor(out=t1[:], in0=lbr[:], in1=R(0), op=ALU.mult), reads=[blbr, bprmR, bden], writes=[bt1])
    p.op(V, lambda e: e.tensor_tensor(out=t2[:], in0=lbi[:], in1=R(1), op=ALU.mult), reads=[blbi, bprmR], writes=[bt2])
    p.op(V, lambda e: e.tensor_tensor(out=cfr[:], in0=t1[:], in1=t2[:], op=ALU.add), reads=[bt1, bt2], writes=[bcfr])
    p.op(V, lambda e: e.tensor_tensor(out=cfr[:], in0=cfr[:], in1=den[:], op=ALU.mult), reads=[bcfr, bden], writes=[bcfr])
    p.op(V, lambda e: e.tensor_tensor(out=t1[:], in0=lbi[:], in1=R(0), op=ALU.mult), reads=[blbi, bprmR, bcfr], writes=[bt1])
    p.op(V, lambda e: e.tensor_tensor(out=t2[:], in0=lbr[:], in1=R(1), op=ALU.mult), reads=[blbr, bprmR, bcfr], writes=[bt2])
    p.op(V, lambda e: e.tensor_tensor(out=cfi[:], in0=t1[:], in1=t2[:], op=ALU.subtract), reads=[bt1, bt2], writes=[bcfi])
    p.op(V, lambda e: e.tensor_tensor(out=cfi[:], in0=cfi[:], in1=den[:], op=ALU.mult), reads=[bcfi, bden], writes=[bcfi])
    bbre = sb("bbre", [64, 128], BF16); bbim = sb("bbim", [64, 128], BF16); bbbre, bbbim = p.buf(), p.buf()
    Bre, Bim = bpad[:, 0, :], bpad[:, 1, :]
    p.op(V, lambda e: e.tensor_tensor(out=t1[:], in0=cfr[:], in1=Bre, op=ALU.mult), reads=[bcfr, bbpad, bcfi], writes=[bt1])
    p.op(V, lambda e: e.tensor_tensor(out=t2[:], in0=cfi[:], in1=Bim, op=ALU.mult), reads=[bcfi, bbpad], writes=[bt2])
    p.op(V, lambda e: e.tensor_tensor(out=bbre[:], in0=t1[:], in1=t2[:], op=ALU.subtract), reads=[bt1, bt2], writes=[bbbre])
    p.op(V, lambda e: e.tensor_tensor(out=t1[:], in0=cfr[:], in1=Bim, op=ALU.mult), reads=[bcfr, bbpad, bbbre], writes=[bt1])
    p.op(V, lambda e: e.tensor_tensor(out=t2[:], in0=cfi[:], in1=Bre, op=ALU.mult), reads=[bcfi, bbpad, bbbre], writes=[bt2])
    p.op(V, lambda e: e.tensor_tensor(out=bbim[:], in0=t1[:], in1=t2[:], op=ALU.add), reads=[bt1, bt2], writes=[bbbim])
    ctb = sb("ctb", [128, 2, 2, 64], BF16); bctb = p.buf()
    p.op(V, lambda e: e.tensor_copy(out=ctb[:, :, 0, :], in_=ctf[:, :, 0, :]), reads=[bctf], writes=[bctb])
    p.op(V, lambda e: e.tensor_scalar(out=ctb[:, :, 1, :], in0=ctf[:, :, 1, :], scalar1=-1.0, scalar2=None, op0=ALU.mult), reads=[bctf], writes=[bctb])

    pBU = [(c.ps(f"pBU{i}", [128, 512]), p.buf()) for i in range(2)]
    pY = (c.ps("pY", [128, 512]), p.buf())
    pS = Rot([(c.ps(f"pS{i}", [128, 512]), p.buf()) for i in range(2)])
    pO = Rot([(c.ps(f"pO{i}", [128, 512]), p.buf()) for i in range(2)])
    pT = (c.ps("pT", [128, 512]), p.buf())
    uin = Rot([(sb(f"uin{i}", [64, W], BF16), p.buf()) for i in range(2)])
    def T(name, dt=F32, n=1, shape=None):
        return Rot([(sb(f"{name}{i}", shape or [128, W], dt), p.buf()) for i in range(n)])
    ta, tb_, tc, td = T("s5a"), T("s5b"), T("s5c"), T("s5d")
    kre, kim = T("kre"), T("kim")
    wre = [T(f"wre{pr}", n=2) for pr in range(2)]
    wim = [T(f"wim{pr}", n=2) for pr in range(2)]
    xre, xim = T("xre", BF16, 2), T("xim", BF16, 2)
    w0 = [[(sb(f"w0_{pr}_{i}", [128, 2]), p.buf()) for i in range(2)] for pr in range(2)]
    for pr in range(2):
        p.op("vector", lambda e, pr=pr: e.memset(w0[pr][0][0][:], 0.0), writes=[w0[pr][0][1]])
    yo = T("yo", n=2, shape=[64, W])
    CH = 1024
    gq = Rot([(sb(f"gq{i}", [64, 3, CH], BF16), p.buf()) for i in range(2)])
    gk = Rot([(sb(f"gk{i}", [64, 3, CH], BF16), p.buf()) for i in range(2)])
    gla = Rot([(sb(f"gla{i}", [64, 3, CH]), p.buf()) for i in range(2)])
    gv = Rot([(sb(f"gv{i}", [128, 8, 192], BF16), p.buf()) for i in range(2)])
    G64 = lambda name, dt=F32, n=2: Rot([(sb(f"{name}{i}", [64, 128], dt), p.buf()) for i in range(n)])
    Bt, ep, en, el = G64("Bt"), G64("ep"), G64("en"), G64("el")
    qf, kf, qb, kb, ks = [G64(n, BF16) for n in ("qf", "kf", "qb", "kb", "ks")]
    s1 = Rot([(sb(f"gs1{i}", [128, 128]), p.buf()) for i in range(2)])
    s2 = Rot([(sb(f"gs2{i}", [128, 128]), p.buf()) for i in range(2)])
    Sm = Rot([(sb(f"gSm{i}", [128, 128], BF16), p.buf()) for i in range(2)])
    kst = Rot([(sb(f"kst{i}", [128, 64], BF16), p.buf()) for i in range(2)])
    Sst = [(sb(f"Sst{i}", [64, 64]), p.buf()) for i in range(3)]
    Sstb = [Rot([(sb(f"Sstb{i}_{k}", [64, 64], BF16), p.buf()) for k in range(2)]) for i in range(3)]
    cur_sb = []
    for i in range(3):
        p.op("vector", lambda e, i=i: e.memset(Sst[i][0][:], 0.0), writes=[Sst[i][1]])
        t_, b_ = Sstb[i].next()
        p.op("vector", lambda e, t_=t_: e.memset(t_[:], 0.0), writes=[b_])
        cur_sb.append((t_, b_))
    otile = Rot([(sb(f"got{i}", [128, 192], BF16), p.buf()) for i in range(3)])
    gst = {"cur": None}

    def s5_window(m):
        ui, bui = uin.next()
        p.dma("sync", ui[:], uT_d[:, m * W:(m + 1) * W], writes=[bui])
        py, bpy = pY
        for pr in range(2):
            (pre_, bpre), (pim_, bpim) = pBU
            rows = slice(32 * pr, 32 * pr + 32)
            p.op("tensor", lambda e, pr=pr, rows=rows, ui=ui, pre_=pre_: e.matmul(pre_[:, :], lhsT=bbre[rows, :], rhs=ui[rows, :], start=True, stop=True),
                 reads=[bbbre, bui], writes=[bpre])
            p.op("tensor", lambda e, pr=pr, rows=rows, ui=ui, pim_=pim_: e.matmul(pim_[:, :], lhsT=bbim[rows, :], rhs=ui[rows, :], start=True, stop=True),
                 reads=[bbbim, bui], writes=[bpim])
            bcos, bsin = bcs[pr]
            (a_, ba_), (b_, bb_), (c_, bc_), (d_, bd_) = ta.next(), tb_.next(), tc.next(), td.next()
            (kr_, bkr_), (ki_, bki_) = kre.next(), kim.next()
            CT, ST = cosT[pr], sinT[pr]
            p.op("vector", lambda e, a_=a_, pre_=pre_, CT=CT: e.tensor_tensor(out=a_[:], in0=pre_[:, :], in1=CT[:], op=ALU.mult), reads=[bpre, bcos], writes=[ba_])
            p.op("vector", lambda e, b_=b_, pim_=pim_, ST=ST: e.tensor_tensor(out=b_[:], in0=pim_[:, :], in1=ST[:], op=ALU.mult), reads=[bpim, bsin], writes=[bb_])
            p.op("vector", lambda e, c_=c_, pim_=pim_, CT=CT: e.tensor_tensor(out=c_[:], in0=pim_[:, :], in1=CT[:], op=ALU.mult), reads=[bpim, bcos], writes=[bc_])
            p.op("vector", lambda e, d_=d_, pre_=pre_, ST=ST: e.tensor_tensor(out=d_[:], in0=pre_[:, :], in1=ST[:], op=ALU.mult), reads=[bpre, bsin], writes=[bd_])
            p.op("gpsimd", lambda e, kr_=kr_, a_=a_, b_=b_: e.tensor_tensor(out=kr_[:], in0=a_[:], in1=b_[:], op=ALU.add), reads=[ba_, bb_], writes=[bkr_])
            p.op("gpsimd", lambda e, ki_=ki_, c_=c_, d_=d_: e.tensor_tensor(out=ki_[:], in0=c_[:], in1=d_[:], op=ALU.subtract), reads=[bc_, bd_], writes=[bki_])
            (wr_, bwr_), (wi_, bwi_) = wre[pr].next(), wim[pr].next()
            w0c, bw0c = w0[pr][m % 2]
            w0n, bw0n = w0[pr][(m + 1) % 2]
            rbc = rr[:, pr:pr + 1].to_broadcast([128, W])
            p.op("vector", lambda e, wr_=wr_, kr_=kr_, w0c=w0c, rbc=rbc: e.tensor_tensor_scan(out=wr_[:], data0=rbc, data1=kr_[:], initial=w0c[:, 0:1], op0=ALU.mult, op1=ALU.add),
                 reads=[brr, bkr_, bw0c], writes=[bwr_])
            p.op("vector", lambda e, wi_=wi_, ki_=ki_, w0c=w0c, rbc=rbc: e.tensor_tensor_scan(out=wi_[:], data0=rbc, data1=ki_[:], initial=w0c[:, 1:2], op0=ALU.mult, op1=ALU.add),
                 reads=[brr, bki_, bw0c], writes=[bwi_])
            p.op("vector", lambda e, w0n=w0n, wr_=wr_, pr=pr: e.tensor_tensor(out=w0n[:, 0:1], in0=wr_[:, W - 1:W], in1=cW[:, pr:pr + 1], op=ALU.mult), reads=[bwr_, bcW], writes=[bw0n])
            p.op("vector", lambda e, w0n=w0n, wi_=wi_, pr=pr: e.scalar_tensor_tensor(out=w0n[:, 0:1], in0=wi_[:, W - 1:W], scalar=nsW[:, pr:pr + 1], in1=w0n[:, 0:1], op0=ALU.mult, op1=ALU.add),
                 reads=[bwi_, bnsW, bw0n], writes=[bw0n])
            p.op("vector", lambda e, w0n=w0n, wi_=wi_, pr=pr: e.tensor_tensor(out=w0n[:, 1:2], in0=wi_[:, W - 1:W], in1=cW[:, pr:pr + 1], op=ALU.mult), reads=[bwi_, bcW], writes=[bw0n])
            p.op("vector", lambda e, w0n=w0n, wr_=wr_, pr=pr: e.scalar_tensor_tensor(out=w0n[:, 1:2], in0=wr_[:, W - 1:W], scalar=sW[:, pr:pr + 1], in1=w0n[:, 1:2], op0=ALU.mult, op1=ALU.add),
                 reads=[bwr_, bsW, bw0n], writes=[bw0n])
            (a2, ba2), (b2, bb2), (c2, bc2), (d2, bd2) = ta.next(), tb_.next(), tc.next(), td.next()
            (xr_, bxr_), (xi_, bxi_) = xre.next(), xim.next()
            p.op("gpsimd", lambda e, a2=a2, wr_=wr_, CT=CT: e.tensor_tensor(out=a2[:], in0=wr_[:], in1=CT[:], op=ALU.mult), reads=[bwr_, bcos], writes=[ba2])
            p.op("gpsimd", lambda e, b2=b2, wi_=wi_, ST=ST: e.tensor_tensor(out=b2[:], in0=wi_[:], in1=ST[:], op=ALU.mult), reads=[bwi_, bsin], writes=[bb2])
            p.op("vector", lambda e, c2=c2, wi_=wi_, CT=CT: e.tensor_tensor(out=c2[:], in0=wi_[:], in1=CT[:], op=ALU.mult), reads=[bwi_, bcos], writes=[bc2])
            p.op("gpsimd", lambda e, d2=d2, wr_=wr_, ST=ST: e.tensor_tensor(out=d2[:], in0=wr_[:], in1=ST[:], op=ALU.mult), reads=[bwr_, bsin], writes=[bd2])
            p.op("gpsimd", lambda e, xr_=xr_, a2=a2, b2=b2: e.tensor_tensor(out=xr_[:], in0=a2[:], in1=b2[:], op=ALU.subtract), reads=[ba2, bb2], writes=[bxr_])
            p.op("gpsimd", lambda e, xi_=xi_, c2=c2, d2=d2: e.tensor_tensor(out=xi_[:], in0=c2[:], in1=d2[:], op=ALU.add), reads=[bc2, bd2], writes=[bxi_])
            p.op("tensor", lambda e, pr=pr, xr_=xr_, py=py: e.matmul(py[0:64, :], lhsT=ctb[:, pr, 0, :], rhs=xr_[:], start=(pr == 0), stop=False), reads=[bctb, bxr_], writes=[bpy])
            p.op("tensor", lambda e, pr=pr, xi_=xi_, py=py: e.matmul(py[0:64, :], lhsT=ctb[:, pr, 1, :], rhs=xi_[:], start=False, stop=(pr == 1)), reads=[bctb, bxi_], writes=[bpy])
        yo_, byo = yo.next()
        p.op("vector", lambda e, yo_=yo_, ui=ui, py=py: e.scalar_tensor_tensor(out=yo_[:], in0=ui[:], scalar=dsk[:, 0:1], in1=py[0:64, :], op0=ALU.mult, op1=ALU.add),
             reads=[bui, bdsk, bpy], writes=[byo])
        p.dma("sync", yT_o[:, m * W:(m + 1) * W], yo_[:], reads=[byo], writes=[p.buf()])

    def gla_tile(t):
        ci, ti = t // 8, t % 8
        if ti == 0:
            cur = (gq.next(), gk.next(), gla.next(), gv.next())
            sl = slice(ci * CH, (ci + 1) * CH)
            p.dma("gpsimd", cur[0][0][:], gq_d[:, :, sl].rearrange("u d t -> d u t"), writes=[cur[0][1]])
            p.dma("gpsimd", cur[1][0][:], gk_d[:, :, sl].rearrange("u d t -> d u t"), writes=[cur[1][1]])
            p.dma("gpsimd", cur[2][0][:], gla_d[:, :, sl].rearrange("u d t -> d u t"), writes=[cur[2][1]])
            p.dma("gpsimd", cur[3][0][:], gv_d.rearrange("(n p) e -> p n e", p=128)[:, ci * 8:(ci + 1) * 8, :], writes=[cur[3][1]])
            gst["cur"] = cur
        (q_, bq_), (k_, bk_), (la_, bla_), (v_, bv_) = gst["cur"]
        ts = slice(ti * 128, (ti + 1) * 128)
        ot, bot = otile.next()
        for i in range(3):
            (B_, bB_), (ep_, bep_), (en_, ben_), (el_, bel_) = Bt.next(), ep.next(), en.next(), el.next()
            p.op("vector", lambda e, B_=B_, i=i: e.tensor_tensor_scan(out=B_[:], data0=ones[0:64, 0:1].to_broadcast([64, 128]), data1=la_[:, i, ts], initial=0.0, op0=ALU.mult, op1=ALU.add),
                 reads=[bones, bla_], writes=[bB_])
            p.op("scalar", lambda e, B_=B_, ep_=ep_: e.activation(out=ep_[:], in_=B_[:], func=AF.Exp), reads=[bB_], writes=[bep_])
            p.op("scalar", lambda e, B_=B_, en_=en_: e.activation(out=en_[:], in_=B_[:], func=AF.Exp, scale=-1.0), reads=[bB_], writes=[ben_])
            p.op("scalar", lambda e, B_=B_, el_=el_: e.activation(out=el_[:], in_=B_[:], func=AF.Exp, scale=-1.0, bias=B_[:, 127:128]), reads=[bB_], writes=[bel_])
            (qf_, bqf_), (kf_, bkf_), (qb_, bqb_), (kb_, bkb_), (ks_, bks_) = qf.next(), kf.next(), qb.next(), kb.next(), ks.next()
            p.op("vector", lambda e, qf_=qf_, ep_=ep_, i=i: e.tensor_tensor(out=qf_[:], in0=q_[:, i, ts], in1=ep_[:], op=ALU.mult), reads=[bq_, bep_], writes=[bqf_])
            p.op("gpsimd", lambda e, kf_=kf_, en_=en_, i=i: e.tensor_tensor(out=kf_[:], in0=k_[:, i, ts], in1=en_[:], op=ALU.mult), reads=[bk_, ben_], writes=[bkf_])
            p.op("vector", lambda e, qb_=qb_, en_=en_, i=i: e.tensor_tensor(out=qb_[:], in0=q_[:, i, ts], in1=en_[:], op=ALU.mult), reads=[bq_, ben_], writes=[bqb_])
            p.op("gpsimd", lambda e, kb_=kb_, ep_=ep_, i=i: e.tensor_tensor(out=kb_[:], in0=k_[:, i, ts], in1=ep_[:], op=ALU.mult), reads=[bk_, bep_], writes=[bkb_])
            p.op("gpsimd", lambda e, ks_=ks_, el_=el_, i=i: e.tensor_tensor(out=ks_[:], in0=k_[:, i, ts], in1=el_[:], op=ALU.mult), reads=[bk_, bel_], writes=[bks_])
            ps_, bps_ = pS.next()
            p.op("tensor", lambda e, ps_=ps_, kf_=kf_, qf_=qf_: e.matmul(ps_[:, 0:128], lhsT=kf_[:], rhs=qf_[:], start=True, stop=True), reads=[bkf_, bqf_], writes=[bps_])
            p.op("tensor", lambda e, ps_=ps_, kb_=kb_, qb_=qb_: e.matmul(ps_[:, 128:256], lhsT=kb_[:], rhs=qb_[:], start=True, stop=True), reads=[bkb_, bqb_], writes=[bps_])
            (s1_, bs1_), (s2_, bs2_), (S_, bS_) = s1.next(), s2.next(), Sm.next()
            p.op("vector", lambda e, s1_=s1_, ps_=ps_: e.tensor_tensor(out=s1_[:], in0=ps_[:, 0:128], in1=gm[:, 0:128], op=ALU.mult), reads=[bps_, bgm], writes=[bs1_])
            p.op("vector", lambda e, s2_=s2_, ps_=ps_: e.tensor_tensor(out=s2_[:], in0=ps_[:, 128:256], in1=gm[:, 128:256], op=ALU.mult), reads=[bps_, bgm], writes=[bs2_])
            p.op("gpsimd", lambda e, S_=S_, s1_=s1_, s2_=s2_: e.tensor_tensor(out=S_[:], in0=s1_[:], in1=s2_[:], op=ALU.add), reads=[bs1_, bs2_], writes=[bS_])
            ptt, bptt = pT
            ptb = ptt[:, :].bitcast(BF16)
            p.op("tensor", lambda e, ptb=ptb, ks_=ks_: e.transpose(ptb[:, 0:64], ks_[:], ident[0:64, 0:64]), reads=[bks_, bid], writes=[bptt])
            kt_, bkt_ = kst.next()
            p.op("scalar", lambda e, kt_=kt_, ptb=ptb: e.activation(out=kt_[:], in_=ptb[:, 0:64], func=AF.Copy), reads=[bptt], writes=[bkt_])
            po_, bpo_ = pO.next()
            sbc, bsbc = cur_sb[i]
            vv = v_[:, ti, i * 64:(i + 1) * 64]
            p.op("tensor", lambda e, po_=po_, S_=S_, vv=vv: e.matmul(po_[:, 0:64], lhsT=S_[:], rhs=vv, start=True, stop=False), reads=[bS_, bv_], writes=[bpo_])
            p.op("tensor", lambda e, po_=po_, qf_=qf_, sbc=sbc: e.matmul(po_[:, 0:64], lhsT=qf_[:], rhs=sbc[:], start=False, stop=True), reads=[bqf_, bsbc], writes=[bpo_])
            p.op("tensor", lambda e, po_=po_, kt_=kt_, vv=vv: e.matmul(po_[0:64, 64:128], lhsT=kt_[:], rhs=vv, start=True, stop=True), reads=[bkt_, bv_], writes=[bpo_])
            st_, bst_ = Sst[i]
            p.op("vector", lambda e, st_=st_, ep_=ep_, po_=po_: e.scalar_tensor_tensor(out=st_[:], in0=st_[:], scalar=ep_[:, 127:128], in1=po_[0:64, 64:128], op0=ALU.mult, op1=ALU.add),
                 reads=[bst_, bep_, bpo_], writes=[bst_])
            sbn, bsbn = Sstb[i].next()
            p.op("scalar", lambda e, sbn=sbn, st_=st_: e.activation(out=sbn[:], in_=st_[:], func=AF.Copy), reads=[bst_], writes=[bsbn])
            cur_sb[i] = (sbn, bsbn)
            p.op("scalar", lambda e, ot=ot, po_=po_, i=i: e.activation(out=ot[:, i * 64:(i + 1) * 64], in_=po_[:, 0:64], func=AF.Copy), reads=[bpo_], writes=[bot])
        p.dma("gpsimd", go_o[t * 128:(t + 1) * 128, :], ot[:], reads=[bot], writes=[p.buf()])

    for m in range(nwin):
        s5_window(m)
        for t in range(4 * m, 4 * m + 4):
            gla_tile(t)
    p.finish_all()
    p.emit()
    return c


def mix1_maps(inp, pre1, la1):
    a_re, a_im, ls = inp['s5_a_re'][0], inp['s5_a_im'][0], inp['s5_log_step'][0]
    b_re, b_im, c_re, c_im = inp['s5_b_re'][0], inp['s5_b_im'][0], inp['s5_c_re'][0], inp['s5_c_im'][0]
    iota = np.ascontiguousarray(np.broadcast_to(np.arange(512, dtype=np.float32)[None, :], (128, 512)))
    i = np.arange(128)
    mf = (i[None, :] >= i[:, None]).astype(np.float32)
    mb = ((i[None, :] < i[:, None]) & ((i[None, :] // 64) == (i[:, None] // 64))).astype(np.float32)
    gm = np.ascontiguousarray(np.concatenate([mf, mb], axis=1))
    idn = np.eye(128, dtype=np.float32)
    maps = []
    for c in range(NCORE):
        b, c4 = c // 4, c % 4
        rows = pre1[b * SEQ:(b + 1) * SEQ]
        lar = la1[b * SEQ:(b + 1) * SEQ]
        prmC = np.zeros((128, 6), np.float32)
        prmR = np.zeros((64, 3, 128), np.float32)
        bpad = np.zeros((64, 2, 128), np.float32)
        ct = np.zeros((128, 2, 2, 64), np.float32)
        for pr in range(2):
            for g2 in range(2):
                g = 4 * c4 + 2 * pr + g2
                ps = slice(64 * g2, 64 * g2 + 64)
                prmC[ps, pr * 3 + 0] = a_re[g]
                prmC[ps, pr * 3 + 1] = a_im[g]
                prmC[ps, pr * 3 + 2] = ls[g]
                prmR[32 * pr:32 * pr + 32, 0, ps] = a_re[g][None, :]
                prmR[32 * pr:32 * pr + 32, 1, ps] = a_im[g][None, :]
                prmR[32 * pr:32 * pr + 32, 2, ps] = ls[g]
                r0 = 32 * pr + 16 * g2
                bpad[r0:r0 + 16, 0, ps] = b_re[g].T
                bpad[r0:r0 + 16, 1, ps] = b_im[g].T
                gl = 2 * pr + g2
                ct[ps, pr, 0, 16 * gl:16 * gl + 16] = c_re[g].T
                ct[ps, pr, 1, 16 * gl:16 * gl + 16] = c_im[g].T
        gq = np.empty((3, 64, SEQ), rows.dtype); gk = np.empty((3, 64, SEQ), rows.dtype)
        gla = np.empty((3, 64, SEQ), np.float32); gv = np.empty((SEQ, 192), rows.dtype)
        for i3 in range(3):
            u = 3 * c4 + i3
            hd, half = u // 2, u % 2
            gq[i3] = rows[:, 256 + hd * 64:256 + (hd + 1) * 64].T
            gk[i3] = rows[:, 640 + hd * 64:640 + (hd + 1) * 64].T
            gla[i3] = lar[:, hd * 64:(hd + 1) * 64].T
            gv[:, i3 * 64:(i3 + 1) * 64] = rows[:, 1024 + hd * 128 + half * 64:1024 + hd * 128 + half * 64 + 64]
        maps.append({"uT": np.ascontiguousarray(rows[:, 64 * c4:64 * c4 + 64].T), "prmC": prmC, "prmR": prmR, "bpad": bpad, "ct": ct,
                     "dsk": np.ascontiguousarray(inp['s5_d'][0][64 * c4:64 * c4 + 64, None]), "iota": iota, "gm": gm, "idn": idn,
                     "gq": gq, "gk": gk, "gla": gla, "gv": gv})
    return maps


def assemble_mix1(results):
    yT = np.empty((256, 2 * SEQ), np.float32)
    oT = np.empty((768, 2 * SEQ), results[0]["go"].dtype)
    for c in range(NCORE):
        b, c4 = c // 4, c % 4
        yT[64 * c4:64 * c4 + 64, b * SEQ:(b + 1) * SEQ] = results[c]["yT"]
        go = results[c]["go"]
        for i3 in range(3):
            u = 3 * c4 + i3
            hd, half = u // 2, u % 2
            ch0 = hd * 128 + half * 64
            oT[ch0:ch0 + 64, b * SEQ:(b + 1) * SEQ] = go[:, i3 * 64:(i3 + 1) * 64].T
    return yT, oT


_CACHE = {}


def _prog(key, fn):
    if key not in _CACHE:
        _CACHE[key] = fn()
    return _CACHE[key]


def _run(c, maps):
    res = run_bass_kernel_spmd(c.nc, maps, core_ids=list(range(NCORE)))
    return res.results


def kernel(**inp):
    inp = {k: np.asarray(v) for k, v in inp.items()}
    x = np.ascontiguousarray(inp['x'].reshape(-1, D).astype(np.float32, copy=False))
    r = _run(_prog("pin0", lambda: build_pin(0)), pin0_maps(inp, x))
    pre0 = np.concatenate([q["out"] for q in r], 0)
    r = _run(_prog("mix0", lambda: build_mix0(32)), mix0_maps(inp, pre0))
    oT0 = assemble_oT0(r)
    r = _run(_prog("pout0", lambda: build_pout(0)), pout_maps(inp, 0, x, oT0))
    x1 = np.concatenate([q["xo"] for q in r], 0)
    r = _run(_prog("pin1", lambda: build_pin(1)), pin1_maps(inp, x1))
    pre1 = np.concatenate([q["out"] for q in r], 0)
    la1 = np.concatenate([q["la"] for q in r], 0)
    r = _run(_prog("mix1", lambda: build_mix1(32)), mix1_maps(inp, pre1, la1))
    yT, oT1 = assemble_mix1(r)
    rT = np.ascontiguousarray(pre1[:, 1792:2560].T)
    r = _run(_prog("pout1", lambda: build_pout(1)), pout_maps(inp, 1, x1, oT1, yT, rT))
    x2 = np.concatenate([q["xo"] for q in r], 0)
    return x2.reshape(inp['x'].shape).astype(np.float32, copy=False)
```

```python
import numpy as np
from contextlib import ExitStack
import concourse.bass as bass
import concourse.mybir as mybir
from concourse.bass_utils import run_bass_kernel_spmd

F32 = mybir.dt.float32
BF16 = mybir.dt.bfloat16
ALU = mybir.AluOpType
AF = mybir.ActivationFunctionType
AX = mybir.AxisListType

EPOCH = 16000
NDMA = 8


class Buf:
    __slots__ = ("name", "w", "r")

    def __init__(self, name=""):
        self.name = name
        self.w = None
        self.r = {}


class Prog:
    CE = ("tensor", "vector", "scalar", "gpsimd")
    DQ = ("sync", "gpsimd")

    def __init__(self, nc, es, nepoch=4):
        self.nc = nc
        self.es = es
        self.q = {e: [] for e in ("tensor", "vector", "scalar", "gpsimd", "sync")}
        self.cnt = {e: 0 for e in self.CE}
        self.vc = {e: {} for e in self.q}
        self.tokvc = {}
        self.tokeng = {}
        self.sems = {}
        for e in self.CE:
            for k in range(nepoch):
                self.sems[(e, k)] = es.enter_context(nc.semaphore(f"s_{e}_{k}"))
        self.dsem = {}
        self.dcnt = {}
        for qn in self.DQ:
            for k in range(NDMA):
                self.dsem[(qn, k)] = es.enter_context(nc.semaphore(f"d_{qn}_{k}"))
            self.dcnt[qn] = 0
        self.nbuf = 0

    def buf(self, name=""):
        self.nbuf += 1
        return Buf(name or f"b{self.nbuf}")

    def _need(self, eng, tok, waits):
        if tok is None:
            return
        sem, val = tok
        if self.vc[eng].get(sem, 0) >= val:
            return
        waits.append(tok)
        vc = self.vc[eng]
        for s, v in self.tokvc[tok].items():
            if vc.get(s, 0) < v:
                vc[s] = v

    def _deps(self, eng, reads, writes):
        waits = []
        for b in reads:
            if b.w is not None:
                if not (eng == "tensor" and self.tokeng.get(b.w) == "tensor"):
                    self._need(eng, b.w, waits)
        for b in writes:
            if b.w is not None and self.tokeng.get(b.w) != eng:
                self._need(eng, b.w, waits)
            for e2, t2 in b.r.items():
                if e2 != eng:
                    self._need(eng, t2, waits)
        return waits

    def _commit(self, eng, tok, reads, writes):
        vc = dict(self.vc[eng])
        vc[tok[0]] = max(vc.get(tok[0], 0), tok[1])
        self.tokvc[tok] = vc
        self.tokeng[tok] = eng
        for b in reads:
            b.r[eng] = tok
        for b in writes:
            b.w = tok
            b.r = {}

    def op(self, eng, fn, reads=(), writes=()):
        waits = self._deps(eng, reads, writes)
        self.cnt[eng] += 1
        c = self.cnt[eng] - 1
        sem = self.sems[(eng, c // EPOCH)]
        tok = (sem, c % EPOCH + 1)
        self.q[eng].append((waits, fn, (sem, 1)))
        self._commit(eng, tok, reads, writes)
        return tok

    def dma(self, qn, out, in_, reads=(), writes=(), **kw):
        eng = qn
        waits = self._deps(eng, reads, writes)
        n = self.dcnt[qn]
        self.dcnt[qn] += 1
        sem = self.dsem[(qn, n % NDMA)]
        k = n // NDMA
        if k > 0:
            prev = (sem, 16 * k)
            if self.vc[eng].get(sem, 0) < 16 * k:
                waits.append(prev)
                self.vc[eng][sem] = 16 * k
        tok = (sem, 16 * (k + 1))
        self.q[eng].append((waits, lambda e: e.dma_start(out=out, in_=in_, **kw), (sem, 16)))
        vc = dict(self.vc[eng])
        vc[sem] = 16 * (k + 1)
        self.tokvc[tok] = vc
        self.tokeng[tok] = "dma_" + qn
        for b in reads:
            b.r["dma_" + qn + str(n % NDMA)] = tok
        for b in writes:
            b.w = tok
            b.r = {}
        return tok

    def finish_all(self):
        waits = []
        for e in self.CE:
            cnt = self.cnt[e]
            if cnt:
                cc = cnt - 1
                self._need("sync", (self.sems[(e, cc // EPOCH)], cc % EPOCH + 1), waits)
        for qn in self.DQ:
            n = self.dcnt[qn]
            for j in range(max(0, n - NDMA), n):
                tok = (self.dsem[(qn, j % NDMA)], 16 * (j // NDMA + 1))
                self._need("sync", tok, waits)
        self.q["sync"].append((waits, None, None))

    def finish(self, bufs):
        waits = []
        for b in bufs:
            if b.w is not None:
                self._need("sync", b.w, waits)
        self.q["sync"].append((waits, None, None))

    def emit(self):
        nc = self.nc
        q = self.q

        def replay(e, name):
            for waits, fn, inc in q[name]:
                for sem, val in waits:
                    e.wait_ge(sem, val)
                if fn is not None:
                    ins = fn(e)
                    ins.then_inc(inc[0], inc[1])

        with nc.Block() as block:
            @block.tensor
            def _(e):
                replay(e, "tensor")

            @block.vector
            def _(e):
                replay(e, "vector")

            @block.scalar
            def _(e):
                replay(e, "scalar")

            @block.gpsimd
            def _(e):
                replay(e, "gpsimd")

            @block.sync
            def _(e):
                replay(e, "sync")


NCORE = 8
TOK = 4096
D = 1024
SEQ = 16384
EPS = 1e-6


def _mk(nc_name="TRN2"):
    return bass.Bass(nc_name, target_bir_lowering=False)


class Ctx:
    def __init__(self):
        self.nc = _mk()
        self.es = ExitStack()
        self.p = None
        self.nps = 0

    def start(self):
        self.p = Prog(self.nc, self.es)
        return self.p

    def din(self, name, shape, dt=F32):
        return self.nc.dram_tensor(name, list(shape), dt, kind="ExternalInput").ap()

    def dout(self, name, shape, dt=F32):
        return self.nc.dram_tensor(name, list(shape), dt, kind="ExternalOutput").ap()

    def sb(self, name, shape, dt=F32):
        return self.es.enter_context(self.nc.sbuf_tensor("sb_" + name, list(shape), dt))

    def ps(self, name, shape, dt=F32):
        return self.es.enter_context(self.nc.psum_tensor("ps_" + name, list(shape), dt))


class Rot:
    def __init__(self, items):
        self.items = items
        self.i = 0

    def next(self):
        it = self.items[self.i % len(self.items)]
        self.i += 1
        return it


def load_weight_bf16(c, p, wd, wb, wbuf, nk, ncol, gpk=None, bgpk=None, stage=None, colchunk=None):
    cc = colchunk or ncol
    engs = ("vector", "gpsimd")
    n = 0
    for k in range(nk):
        for c0 in range(0, ncol, cc):
            c1 = min(ncol, c0 + cc)
            st, bst = stage.next()
            p.dma("sync", st[:, 0:c1 - c0], wd[k * 128:(k + 1) * 128, c0:c1], writes=[bst])
            eng = engs[n % 2]
            n += 1
            if gpk is not None:
                p.op(eng, lambda e, st=st, k=k, c0=c0, c1=c1: e.tensor_scalar(
                    out=wb[:, k, c0:c1], in0=st[:, 0:c1 - c0], scalar1=gpk[:, k:k + 1], scalar2=None, op0=ALU.mult),
                    reads=[bst, bgpk], writes=[wbuf])
            else:
                p.op(eng, lambda e, st=st, k=k, c0=c0, c1=c1: e.tensor_copy(out=wb[:, k, c0:c1], in_=st[:, 0:c1 - c0]),
                     reads=[bst], writes=[wbuf])


def build_pin(layer):
    c = Ctx()
    nc = c.nc
    NCOL = 3072 if layer == 0 else 2944
    NOUT = 3584 if layer == 0 else 2560
    x = c.din("x", [TOK, D])
    xT = c.din("xT", [D, TOK])
    w = c.din("w", [D, NCOL if layer == 0 else 2560])
    gpk_d = c.din("gpk", [128, 8])
    out = c.dout("out", [TOK, NOUT], BF16)
    if layer == 0:
        gq_d = c.din("gqk", [128, 1024])
        cs_d = c.din("cs", [128, 32, 512])
        t12_d = c.din("t12", [128, 1024])
    else:
        walrT_d = c.din("walrT", [16, D])
        wa2_d = c.din("wa2", [16, 384])
        ba2_d = c.din("ba2", [128, 384])
        la_out = c.dout("la", [TOK, 384], F32)
    p = c.start()
    sb = c.sb
    wb = sb("wb", [128, 8, NCOL], BF16); bwb = p.buf()
    gpk = sb("gpk", [128, 8]); bgpk = p.buf()
    p.dma("gpsimd", gpk[:], gpk_d[:, :], writes=[bgpk])
    stage = Rot([(sb(f"st{i}", [128, 1024]), p.buf()) for i in range(3)])
    if layer == 0:
        gq = sb("gq", [128, 1024]); bgq = p.buf()
        t12 = sb("t12", [128, 1024]); bt12 = p.buf()
        p.dma("gpsimd", gq[:], gq_d[:, :], writes=[bgq])
        p.dma("gpsimd", t12[:], t12_d[:, :], writes=[bt12])
        load_weight_bf16(c, p, w, wb, bwb, 8, NCOL, gpk, bgpk, stage, 1024)
    else:
        ba2 = sb("ba2", [128, 384]); bba2 = p.buf()
        p.dma("gpsimd", ba2[:], ba2_d[:, :], writes=[bba2])
        walrT = sb("walrT", [16, D]); bwalr = p.buf()
        wa2 = sb("wa2", [16, 384]); bwa2 = p.buf()
        p.dma("gpsimd", walrT[:], walrT_d[:, :], writes=[bwalr])
        p.dma("gpsimd", wa2[:], wa2_d[:, :], writes=[bwa2])
        load_weight_bf16(c, p, w, wb[:, :, 0:2560], bwb, 8, 2560, gpk, bgpk, stage, 1024)
        pw = c.ps("pw", [128, 512]); bpw = p.buf()
        for k in range(8):
            p.op("tensor", lambda e, k=k: e.matmul(pw[:, 0:384], lhsT=walrT[:, k * 128:(k + 1) * 128], rhs=wa2[:, :],
                                                   start=True, stop=True), reads=[bwalr, bwa2], writes=[bpw])
            p.op("vector", lambda e, k=k: e.tensor_scalar(out=wb[:, k, 2560:2944], in0=pw[:, 0:384], scalar1=gpk[:, k:k + 1],
                                                          scalar2=None, op0=ALU.mult), reads=[bpw, bgpk], writes=[bwb])
    nbank = (NCOL + 511) // 512
    banks = Rot([(c.ps(f"pb{i}", [128, 512]), p.buf()) for i in range(7 if layer == 1 else 8)])
    GT = 512
    xin = Rot([(sb(f"xin{i}", [128, 4, D]), p.buf()) for i in range(2)])
    xTin = Rot([(sb(f"xTin{i}", [128, 8, GT]), p.buf()) for i in range(2)])
    xTb = Rot([(sb(f"xTb{i}", [128, 8, GT], BF16), p.buf()) for i in range(2)])
    junk = sb("junk", [128, D]); bjunk = p.buf()
    ss = Rot([(sb(f"ss{i}", [128, 1]), p.buf()) for i in range(4)])
    hq = Rot([(sb(f"hq{i}", [128, 1024]), p.buf()) for i in range(2)])
    sq = sb("sq", [128, 1024]); bsq = p.buf()
    qs = Rot([(sb(f"qs{i}", [128, 16]), p.buf()) for i in range(2)])
    ob = Rot([(sb(f"ob{i}", [128, NOUT], BF16), p.buf()) for i in range(2)])
    if layer == 0:
        cs = Rot([(sb(f"cs{i}", [128, 4, 512]), p.buf()) for i in range(2)])
        rq = Rot([(sb(f"rq{i}", [128, 512]), p.buf()) for i in range(2)])
        rt = [(sb(f"rt{i}", [128, 256]), p.buf()) for i in range(4)]
        ro = Rot([(sb(f"ro{i}", [128, 512]), p.buf()) for i in range(2)])
    else:
        la = Rot([(sb(f"la{i}", [128, 384]), p.buf()) for i in range(2)])
        le = Rot([(sb(f"le{i}", [128, 384]), p.buf()) for i in range(2)])
    xTv = xT.rearrange("(k p) t -> p k t", p=128)
    xv = x.rearrange("(n p) d -> p n d", p=128)
    for g in range(TOK // GT):
        xi, bxi = xin.next()
        xti, bxti = xTin.next()
        xtb, bxtb = xTb.next()
        p.dma("sync", xi[:], xv[:, g * 4:(g + 1) * 4, :], writes=[bxi])
        p.dma("gpsimd", xti[:], xTv[:, :, g * GT:(g + 1) * GT], writes=[bxti])
        p.op("gpsimd", lambda e, xtb=xtb, xti=xti: e.tensor_copy(out=xtb[:, 0:4, :], in_=xti[:, 0:4, :]), reads=[bxti], writes=[bxtb])
        p.op("vector", lambda e, xtb=xtb, xti=xti: e.tensor_copy(out=xtb[:, 4:8, :], in_=xti[:, 4:8, :]), reads=[bxti], writes=[bxtb])
        if layer == 0:
            csg, bcsg = cs.next()
            p.dma("gpsimd", csg[:], cs_d[:, g * 4:(g + 1) * 4, :], writes=[bcsg])
        for j in range(4):
            tt = g * 4 + j
            s1, bs1 = ss.next()
            p.op("scalar", lambda e, xi=xi, j=j, s1=s1: e.activation(out=junk[:], in_=xi[:, j, :], func=AF.Square, accum_out=s1[:]),
                 reads=[bxi], writes=[bjunk, bs1])
            p.op("scalar", lambda e, s1=s1: e.activation(out=s1[:], in_=s1[:], func=AF.Sqrt, scale=1.0 / D, bias=EPS), reads=[bs1], writes=[bs1])
            p.op("vector", lambda e, s1=s1: e.reciprocal(out=s1[:], in_=s1[:]), reads=[bs1], writes=[bs1])
            pbs = []
            for b in range(nbank):
                pb, bpb = banks.next()
                c0, c1 = b * 512, min(NCOL, (b + 1) * 512)
                for k in range(8):
                    p.op("tensor", lambda e, pb=pb, xtb=xtb, k=k, j=j, c0=c0, c1=c1: e.matmul(
                        pb[:, 0:c1 - c0], lhsT=xtb[:, k, j * 128:(j + 1) * 128], rhs=wb[:, k, c0:c1], start=(k == 0), stop=(k == 7)),
                        reads=[bxtb, bwb], writes=[bpb])
                pbs.append((pb, bpb))
            o, bo = ob.next()
            if layer == 0:
                h, bh = hq.next()
                for b in range(2):
                    p.op("scalar", lambda e, b=b, h=h, s1=s1, pb=pbs[b][0]: e.activation(out=h[:, b * 512:(b + 1) * 512], in_=pb[:, :], func=AF.Copy, scale=s1[:, 0:1]),
                         reads=[pbs[b][1], bs1], writes=[bh])
                p.op("vector", lambda e, o=o, s1=s1, pb=pbs[2][0]: e.tensor_scalar(out=o[:, 1024:1536], in0=pb[:, :], scalar1=s1[:, 0:1], scalar2=None, op0=ALU.mult),
                     reads=[pbs[2][1], bs1], writes=[bo])
                p.op("vector", lambda e, o=o, s1=s1, pb=pbs[4][0]: e.tensor_scalar(out=o[:, 2560:3072], in0=pb[:, :], scalar1=s1[:, 0:1], scalar2=None, op0=ALU.mult),
                     reads=[pbs[4][1], bs1], writes=[bo])
                p.op("scalar", lambda e, o=o, s1=s1, pb=pbs[5][0]: e.activation(out=o[:, 3072:3584], in_=pb[:, :], func=AF.Silu, scale=s1[:, 0:1]),
                     reads=[pbs[5][1], bs1], writes=[bo])
                r, br = rq.next()
                p.op("scalar", lambda e, r=r, s1=s1, pb=pbs[3][0]: e.activation(out=r[:], in_=pb[:, :], func=AF.Copy, scale=s1[:, 0:1]),
                     reads=[pbs[3][1], bs1], writes=[br])
                q1, bq1 = qs.next()
                h3 = h[:].rearrange("p (g d) -> p g d", d=64)
                p.op("gpsimd", lambda e, h=h: e.tensor_tensor(out=sq[:], in0=h[:], in1=h[:], op=ALU.mult), reads=[bh], writes=[bsq])
                p.op("vector", lambda e, q1=q1: e.tensor_reduce(out=q1[:], in_=sq[:].rearrange("p (g d) -> p g d", d=64), axis=AX.X, op=ALU.add),
                     reads=[bsq], writes=[bq1])
                p.op("scalar", lambda e, q1=q1: e.activation(out=q1[:], in_=q1[:], func=AF.Sqrt, scale=1.0 / 64, bias=EPS), reads=[bq1], writes=[bq1])
                p.op("vector", lambda e, q1=q1: e.reciprocal(out=q1[:], in_=q1[:]), reads=[bq1], writes=[bq1])
                p.op("vector", lambda e, h3=h3, q1=q1: e.tensor_tensor(out=h3, in0=h3, in1=q1[:].unsqueeze(2).to_broadcast([128, 16, 64]), op=ALU.mult),
                     reads=[bh, bq1], writes=[bh])
                p.op("gpsimd", lambda e, h=h, o=o: e.tensor_tensor(out=o[:, 0:1024], in0=h[:], in1=gq[:], op=ALU.mult), reads=[bh, bgq], writes=[bo])
                r4 = r[:].rearrange("p (a two d) -> p a two d", two=2, d=32)
                x1, x2 = r4[:, :, 0, :], r4[:, :, 1, :]
                cosv = csg[:, j, 0:256].rearrange("p (a d) -> p a d", d=32)
                sinv = csg[:, j, 256:512].rearrange("p (a d) -> p a d", d=32)
                (ta, bta), (tb_, btb), (tc, btc), (td, btd) = rt
                v3 = lambda t: t[:].rearrange("p (a d) -> p a d", d=32)
                rr, brr = ro.next()
                rr4 = rr[:].rearrange("p (a two d) -> p a two d", two=2, d=32)
                p.op("vector", lambda e, x1=x1, cosv=cosv: e.tensor_tensor(out=v3(ta), in0=x1, in1=cosv, op=ALU.mult), reads=[br, bcsg], writes=[bta])
                p.op("gpsimd", lambda e, x2=x2, sinv=sinv: e.tensor_tensor(out=v3(tb_), in0=x2, in1=sinv, op=ALU.mult), reads=[br, bcsg], writes=[btb])
                p.op("vector", lambda e, x1=x1, sinv=sinv: e.tensor_tensor(out=v3(tc), in0=x1, in1=sinv, op=ALU.mult), reads=[br, bcsg], writes=[btc])
                p.op("gpsimd", lambda e, x2=x2, cosv=cosv: e.tensor_tensor(out=v3(td), in0=x2, in1=cosv, op=ALU.mult), reads=[br, bcsg], writes=[btd])
                p.op("vector", lambda e, rr4=rr4: e.tensor_tensor(out=rr4[:, :, 0, :], in0=v3(ta), in1=v3(tb_), op=ALU.subtract), reads=[bta, btb], writes=[brr])
                p.op("gpsimd", lambda e, rr4=rr4: e.tensor_tensor(out=rr4[:, :, 1, :], in0=v3(tc), in1=v3(td), op=ALU.add), reads=[btc, btd], writes=[brr])
                p.op("vector", lambda e, rr=rr, o=o: e.tensor_tensor(out=o[:, 1536:2048], in0=rr[:], in1=t12[:, 0:512], op=ALU.mult), reads=[brr, bt12], writes=[bo])
                p.op("gpsimd", lambda e, rr=rr, o=o: e.tensor_tensor(out=o[:, 2048:2560], in0=rr[:], in1=t12[:, 512:1024], op=ALU.mult), reads=[brr, bt12], writes=[bo])
            else:
                for b in range(4):
                    eng = ("vector", "scalar")[b % 2]
                    cw = 256 if b == 3 else 512
                    if eng == "vector":
                        p.op("vector", lambda e, o=o, s1=s1, b=b, cw=cw, pb=pbs[b][0]: e.tensor_scalar(out=o[:, b * 512:b * 512 + cw], in0=pb[:, 0:cw], scalar1=s1[:, 0:1], scalar2=None, op0=ALU.mult),
                             reads=[pbs[b][1], bs1], writes=[bo])
                    else:
                        p.op("scalar", lambda e, o=o, s1=s1, b=b, cw=cw, pb=pbs[b][0]: e.activation(out=o[:, b * 512:b * 512 + cw], in_=pb[:, 0:cw], func=AF.Copy, scale=s1[:, 0:1]),
                             reads=[pbs[b][1], bs1], writes=[bo])
                p.op("gpsimd", lambda e, o=o: e.tensor_scalar(out=o[:, 256:640], in0=o[:, 256:640], scalar1=0.125, scalar2=None, op0=ALU.mult), reads=[bo], writes=[bo])
                p.op("scalar", lambda e, o=o, s1=s1, pb=pbs[3][0]: e.activation(out=o[:, 1792:2048], in_=pb[:, 256:512], func=AF.Silu, scale=s1[:, 0:1]),
                     reads=[pbs[3][1], bs1], writes=[bo])
                p.op("scalar", lambda e, o=o, s1=s1, pb=pbs[4][0]: e.activation(out=o[:, 2048:2560], in_=pb[:, :], func=AF.Silu, scale=s1[:, 0:1]),
                     reads=[pbs[4][1], bs1], writes=[bo])
                l1, bl1 = la.next()
                l2, bl2 = le.next()
                p.op("vector", lambda e, l1=l1, s1=s1, pb=pbs[5][0]: e.scalar_tensor_tensor(out=l1[:], in0=pb[:, 0:384], scalar=s1[:, 0:1], in1=ba2[:], op0=ALU.mult, op1=ALU.add),
                     reads=[pbs[5][1], bs1, bba2], writes=[bl1])
                p.op("scalar", lambda e, l1=l1: e.activation(out=l1[:], in_=l1[:], func=AF.Exp, scale=-1.0), reads=[bl1], writes=[bl1])
                p.op("scalar", lambda e, l1=l1: e.activation(out=l1[:], in_=l1[:], func=AF.Ln, bias=1.0), reads=[bl1], writes=[bl1])
                p.op("gpsimd", lambda e, l1=l1, l2=l2: e.tensor_scalar(out=l2[:], in0=l1[:], scalar1=-1.0 / 16, scalar2=None, op0=ALU.mult), reads=[bl1], writes=[bl2])
                p.dma("gpsimd", la_out[tt * 128:(tt + 1) * 128, :], l2[:], reads=[bl2], writes=[p.buf()])
            p.dma("sync", out[tt * 128:(tt + 1) * 128, :], o[:], reads=[bo], writes=[p.buf()])
    p.finish_all()
    p.emit()
    return c


def _rot_tables():
    half = 32
    inv_freq = (1.0 / (10000.0 ** np.linspace(0.0, 1.0, half, dtype=np.float32))).astype(np.float32)
    ang = (np.arange(SEQ, dtype=np.float32)[:, None] * inv_freq[None, :]).astype(np.float32)
    return np.cos(ang).astype(np.float32), np.sin(ang).astype(np.float32)


def _ret_log_gamma():
    return np.log(1.0 - 2.0 ** (-5.0 - np.arange(4, dtype=np.float32))).astype(np.float32)


def _gpk(g):
    return np.ascontiguousarray(g.reshape(8, 128).T)


def pin0_maps(inp, x):
    cos, sin = _rot_tables()
    lg = _ret_log_gamma()
    i = np.arange(128, dtype=np.float32)
    qw = np.exp((i + 1.0)[:, None] * lg[None, :])
    kw = np.exp((127.0 - i)[:, None] * lg[None, :]) * 0.125
    t12 = np.empty((128, 1024), np.float32)
    t12[:, 0:256] = 1.0
    t12[:, 256:512] = 0.125
    t12[:, 512:768] = np.repeat(qw, 64, axis=1)
    t12[:, 768:1024] = np.repeat(kw, 64, axis=1)
    gqk = np.concatenate([np.tile(inp['da_q_norm'][0], 8), np.tile(inp['da_k_norm'][0], 8)])
    gqk = np.ascontiguousarray(np.broadcast_to(gqk[None, :], (128, 1024)))
    gpk = _gpk(inp['norm_mix_g'][0])
    w = np.ascontiguousarray(inp['ab_w_in'][0])
    maps = []
    for c in range(NCORE):
        xs = x[c * TOK:(c + 1) * TOK]
        pos0 = (c % 4) * TOK
        cc = cos[pos0:pos0 + TOK].reshape(32, 128, 32).transpose(1, 0, 2)
        sn = sin[pos0:pos0 + TOK].reshape(32, 128, 32).transpose(1, 0, 2)
        cs = np.concatenate([np.tile(cc, (1, 1, 8)), np.tile(sn, (1, 1, 8))], axis=2)
        maps.append({"x": np.ascontiguousarray(xs), "xT": np.ascontiguousarray(xs.T), "w": w, "gpk": gpk,
                     "gqk": gqk, "cs": np.ascontiguousarray(cs), "t12": t12})
    return maps


NT = SEQ // 128


def build_mix0(nqg=32):
    c = Ctx()
    nc = c.nc
    qaT_d = c.din("qaT", [128, SEQ], BF16)
    kaT_d = c.din("kaT", [128, SEQ], BF16)
    va_d = c.din("va", [SEQ, 128], BF16)
    qrT_d = c.din("qrT", [64, SEQ], BF16)
    krT_d = c.din("krT", [64, SEQ], BF16)
    qwrT_d = c.din("qwrT", [64, SEQ], BF16)
    kwr_d = c.din("kwr", [SEQ, 64], BF16)
    vr_d = c.din("vr", [SEQ, 128], BF16)
    gr_d = c.din("gr", [SEQ, 128], BF16)
    cst_d = c.din("cst", [128, 512])
    lam_d = c.din("lamv", [128, 256])
    out = c.dout("o", [SEQ, 256], BF16)
    p = c.start()
    sb = c.sb
    kaT = sb("kaT", [128, SEQ], BF16); bkaT = [p.buf() for _ in range(8)]
    va = sb("va", [128, NT, 132], BF16); bva = [p.buf() for _ in range(8)]
    cst = sb("cst", [128, 512]); bcst = p.buf()
    lamv = sb("lamv", [128, 256]); blam = p.buf()
    p.dma("gpsimd", cst[:], cst_d[:, :], writes=[bcst])
    p.dma("gpsimd", lamv[:], lam_d[:, :], writes=[blam])
    vav = va_d.rearrange("(n p) e -> p n e", p=128)
    for i in range(8):
        p.dma("sync", kaT[:, i * 2048:(i + 1) * 2048], kaT_d[:, i * 2048:(i + 1) * 2048], writes=[bkaT[i]])
        p.dma("sync", va[:, i * 16:(i + 1) * 16, 0:128], vav[:, i * 16:(i + 1) * 16, :], writes=[bva[i]])
        p.op("gpsimd", lambda e, i=i: e.memset(va[:, i * 16:(i + 1) * 16, 128:129], 1.0), writes=[bva[i]])
    lt = sb("lt", [128, 128]); blt = p.buf()
    l2 = sb("l2", [128, 2]); bl2 = p.buf()
    lam = sb("lam", [128, 1]); blamS = p.buf()
    lv = lamv[:].rearrange("p (a two d) -> p a two d", two=2, d=64)
    p.op("vector", lambda e: e.tensor_tensor(out=lt[:].rearrange("p (a d) -> p a d", d=64), in0=lv[:, :, 0, :], in1=lv[:, :, 1, :], op=ALU.mult),
         reads=[blam], writes=[blt])
    p.op("vector", lambda e: e.tensor_reduce(out=l2[:], in_=lt[:].rearrange("p (a d) -> p a d", d=64), axis=AX.X, op=ALU.add), reads=[blt], writes=[bl2])
    p.op("scalar", lambda e: e.activation(out=l2[:], in_=l2[:], func=AF.Exp), reads=[bl2], writes=[bl2])
    p.op("vector", lambda e: e.tensor_tensor(out=lam[:], in0=l2[:, 0:1], in1=l2[:, 1:2], op=ALU.subtract), reads=[bl2], writes=[blamS])
    p.op("vector", lambda e: e.tensor_scalar(out=lam[:], in0=lam[:], scalar1=0.2, scalar2=None, op0=ALU.add), reads=[blamS], writes=[blamS])

    psS = Rot([(c.ps(f"pS{i}", [128, 512]), p.buf()) for i in range(4)])
    accb = [c.ps(f"pA{i}", [128, 512]) for i in range(3)]
    accbuf = [p.buf() for _ in range(3)]
    acc = []
    for i in range(8):
        bk, off = i // 3, (i % 3) * 136
        acc.append((accb[bk][:, off:off + 129], accbuf[bk]))
    pR = c.ps("pR", [128, 512])
    bpRs = bpRo = bpRkv = p.buf()
    qin = Rot([(sb(f"qin{i}", [128, 512], BF16), p.buf()) for i in range(2)])
    pT = Rot([(sb(f"pT{i}", [128, 512], BF16), p.buf()) for i in range(4)])
    fin = {k: Rot([(sb(f"{k}{i}", shp), p.buf()) for i in range(2)]) for k, shp in
           (("fr", [128, 4]), ("ft", [128, 128]), ("fo", [128, 128]), ("fs", [128, 1]))}
    junk = sb("junk", [128, 128]); bjunk = p.buf()
    obuf = Rot([(sb(f"ob{i}", [128, 256], BF16), p.buf()) for i in range(8)])
    CH = 2048
    rq = Rot([(sb(f"rq{i}", [64, CH], BF16), p.buf()) for i in range(2)])
    rk = Rot([(sb(f"rk{i}", [64, CH], BF16), p.buf()) for i in range(2)])
    rqw = Rot([(sb(f"rqw{i}", [64, CH], BF16), p.buf()) for i in range(2)])
    rkw = Rot([(sb(f"rkw{i}", [128, 16, 64], BF16), p.buf()) for i in range(2)])
    rv = Rot([(sb(f"rv{i}", [128, 16, 128], BF16), p.buf()) for i in range(2)])
    rg = Rot([(sb(f"rg{i}", [128, 16, 128], BF16), p.buf()) for i in range(2)])
    Sf = sb("Sf", [64, 128]); bSf = p.buf()
    Sb = Rot([(sb(f"Sb{i}", [64, 128], BF16), p.buf()) for i in range(2)])
    sm = Rot([(sb(f"sm{i}", [128, 128], BF16), p.buf()) for i in range(2)])
    gt = Rot([(sb(f"gt{i}", [128, 128]), p.buf()) for i in range(2)])
    rs_ = Rot([(sb(f"rs{i}", [128, 1]), p.buf()) for i in range(2)])
    p.op("vector", lambda e: e.memset(Sf[:], 0.0), writes=[bSf])
    sb0, bsb0 = Sb.next()
    p.op("vector", lambda e: e.memset(sb0[:], 0.0), writes=[bsb0])
    ret_state = {"cur": None, "sb": (sb0, bsb0)}
    otiles = {}

    def get_ot(t):
        if t not in otiles:
            otiles[t] = obuf.next() + ([0],)
        return otiles[t]

    def done_half(t):
        o, bo, cnt = otiles[t]
        cnt[0] += 1
        if cnt[0] == 2:
            p.dma("gpsimd", out[t * 128:(t + 1) * 128, :], o[:], reads=[bo], writes=[p.buf()])
            del otiles[t]

    def ret_tile(t):
        ci, ti = t // 16, t % 16
        if ti == 0:
            cur = (rq.next(), rk.next(), rqw.next(), rkw.next(), rv.next(), rg.next())
            sl = slice(ci * CH, (ci + 1) * CH)
            p.dma("sync", cur[0][0][:], qrT_d[:, sl], writes=[cur[0][1]])
            p.dma("sync", cur[1][0][:], krT_d[:, sl], writes=[cur[1][1]])
            p.dma("sync", cur[2][0][:], qwrT_d[:, sl], writes=[cur[2][1]])
            p.dma("sync", cur[3][0][:], kwr_d.rearrange("(n p) e -> p n e", p=128)[:, ci * 16:(ci + 1) * 16, :], writes=[cur[3][1]])
            p.dma("sync", cur[4][0][:], vr_d.rearrange("(n p) e -> p n e", p=128)[:, ci * 16:(ci + 1) * 16, :], writes=[cur[4][1]])
            p.dma("sync", cur[5][0][:], gr_d.rearrange("(n p) e -> p n e", p=128)[:, ci * 16:(ci + 1) * 16, :], writes=[cur[5][1]])
            ret_state["cur"] = cur
        (q_, bq_), (k_, bk_), (qw_, bqw_), (kw_, bkw_), (v_, bv_), (g_, bg_) = ret_state["cur"]
        ts = slice(ti * 128, (ti + 1) * 128)
        Sbc, bSbc = ret_state["sb"]
        p.op("tensor", lambda e: e.matmul(pR[:, 0:128], lhsT=k_[:, ts], rhs=q_[:, ts], start=True, stop=True), reads=[bk_, bq_], writes=[bpRs])
        s_, bs_ = sm.next()
        p.op("vector", lambda e: e.tensor_tensor(out=s_[:], in0=pR[:, 0:128], in1=cst[:, 0:128], op=ALU.mult), reads=[bpRs, bcst], writes=[bs_])
        p.op("tensor", lambda e: e.matmul(pR[:, 128:256], lhsT=s_[:], rhs=v_[:, ti, :], start=True, stop=False), reads=[bs_, bv_], writes=[bpRo])
        p.op("tensor", lambda e: e.matmul(pR[:, 128:256], lhsT=qw_[:, ts], rhs=Sbc[:], start=False, stop=True), reads=[bqw_, bSbc], writes=[bpRo])
        p.op("tensor", lambda e: e.matmul(pR[0:64, 256:384], lhsT=kw_[:, ti, :], rhs=v_[:, ti, :], start=True, stop=True), reads=[bkw_, bv_], writes=[bpRkv])
        p.op("vector", lambda e: e.scalar_tensor_tensor(out=Sf[:], in0=Sf[:], scalar=cst[0:64, 384:385], in1=pR[0:64, 256:384], op0=ALU.mult, op1=ALU.add),
             reads=[bSf, bpRkv, bcst], writes=[bSf])
        Sbn, bSbn = Sb.next()
        p.op("scalar", lambda e: e.activation(out=Sbn[:], in_=Sf[:], func=AF.Copy), reads=[bSf], writes=[bSbn])
        ret_state["sb"] = (Sbn, bSbn)
        r1, br1 = rs_.next()
        p.op("scalar", lambda e: e.activation(out=junk[:], in_=pR[:, 128:256], func=AF.Square, accum_out=r1[:]), reads=[bpRo], writes=[bjunk, br1])
        p.op("scalar", lambda e: e.activation(out=r1[:], in_=r1[:], func=AF.Sqrt, scale=1.0 / 128, bias=EPS), reads=[br1], writes=[br1])
        p.op("vector", lambda e: e.reciprocal(out=r1[:], in_=r1[:]), reads=[br1], writes=[br1])
        g1, bg1 = gt.next()
        p.op("gpsimd", lambda e: e.tensor_tensor(out=g1[:], in0=g_[:, ti, :], in1=cst[:, 256:384], op=ALU.mult), reads=[bg_, bcst], writes=[bg1])
        o, bo, _ = get_ot(t)
        p.op("vector", lambda e: e.scalar_tensor_tensor(out=o[:, 128:256], in0=pR[:, 128:256], scalar=r1[:, 0:1], in1=g1[:], op0=ALU.mult, op1=ALU.mult),
             reads=[bpRo, br1, bg1], writes=[bo])
        done_half(t)

    def da_final(qt_glob, a0, a1):
        (A0, bA0), (A1, bA1) = a0, a1
        fr, bfr = fin["fr"].next()
        ft, bft = fin["ft"].next()
        fo, bfo = fin["fo"].next()
        fs, bfs = fin["fs"].next()
        p.op("vector", lambda e: e.reciprocal(out=fr[:, 0:1], in_=A0[:, 128:129]), reads=[bA0], writes=[bfr])
        p.op("vector", lambda e: e.reciprocal(out=fr[:, 1:2], in_=A1[:, 128:129]), reads=[bA1], writes=[bfr])
        p.op("vector", lambda e: e.tensor_tensor(out=fr[:, 2:3], in0=fr[:, 1:2], in1=lam[:, 0:1], op=ALU.mult), reads=[bfr, blamS], writes=[bfr])
        p.op("vector", lambda e: e.tensor_scalar(out=ft[:], in0=A1[:, 0:128], scalar1=fr[:, 2:3], scalar2=None, op0=ALU.mult), reads=[bA1, bfr], writes=[bft])
        p.op("vector", lambda e: e.scalar_tensor_tensor(out=fo[:], in0=A0[:, 0:128], scalar=fr[:, 0:1], in1=ft[:], op0=ALU.mult, op1=ALU.subtract),
             reads=[bA0, bfr, bft], writes=[bfo])
        p.op("scalar", lambda e: e.activation(out=junk[:], in_=fo[:], func=AF.Square, accum_out=fs[:]), reads=[bfo], writes=[bjunk, bfs])
        p.op("scalar", lambda e: e.activation(out=fs[:], in_=fs[:], func=AF.Sqrt, scale=1.0 / 128, bias=EPS), reads=[bfs], writes=[bfs])
        p.op("vector", lambda e: e.reciprocal(out=fs[:], in_=fs[:]), reads=[bfs], writes=[bfs])
        o, bo, _ = get_ot(qt_glob)
        p.op("vector", lambda e: e.scalar_tensor_tensor(out=o[:, 0:128], in0=fo[:], scalar=fs[:, 0:1], in1=cst[:, 128:256], op0=ALU.mult, op1=ALU.mult),
             reads=[bfo, bfs, bcst], writes=[bo])
        done_half(qt_glob)

    steps = [(qg, kt) for qg in range(nqg) for kt in range(4 * qg + 4)]
    qcur = {}

    def stage_a(qg, kt):
        if kt == 0:
            qi, bqi = qin.next()
            p.dma("sync", qi[:], qaT_d[:, qg * 512:(qg + 1) * 512], writes=[bqi])
            qcur[qg] = (qi, bqi)
        qi, bqi = qcur[qg]
        j = kt - 4 * qg
        q0 = max(0, j)
        qs = q0 * 128
        kb = bkaT[kt // 16]
        pts = []
        for m in range(2):
            pS, bpS = psS.next()
            p.op("tensor", lambda e, pS=pS, m=m, kt=kt, qs=qs, qi=qi: e.matmul(pS[:, qs:512], lhsT=kaT[m * 64:(m + 1) * 64, kt * 128:(kt + 1) * 128],
                                                                         rhs=qi[m * 64:(m + 1) * 64, qs:512], start=True, stop=True),
                 reads=[kb, bqi], writes=[bpS])
            pt, bpt = pT.next()
            p.op("scalar", lambda e, pS=pS, pt=pt, qs=qs: e.activation(out=pt[:, qs:512], in_=pS[:, qs:512], func=AF.Exp, scale=0.125, bias=-8.0),
                 reads=[bpS], writes=[bpt])
            if j >= 0:
                p.op("gpsimd", lambda e, pt=pt, qs=qs: e.memset(pt[64:128, qs:qs + 64], 0.0), writes=[bpt])
            pts.append((pt, bpt))
        return pts

    def stage_b(qg, kt, pts):
        j = kt - 4 * qg
        q0 = max(0, j)
        for qt in range(q0, 4):
            for m in range(2):
                A, bA = acc[qt * 2 + m]
                pt, bpt = pts[m]
                p.op("tensor", lambda e, A=A, pt=pt, qt=qt, kt=kt, st=(kt == 0 and (qt * 2 + m) % 3 == 0), sp=(kt == 4 * qg + qt): e.matmul(
                    A, lhsT=pt[:, qt * 128:(qt + 1) * 128], rhs=va[:, kt, 0:129], start=st, stop=sp, skip_group_check=True),
                    reads=[bpt, bva[kt // 16]], writes=[bA])
        if kt == 4 * qg + 3:
            for qt in range(4):
                da_final(qg * 4 + qt, acc[qt * 2], acc[qt * 2 + 1])
            for t in range(qg * 4, qg * 4 + 4):
                ret_tile(t)

    cur = stage_a(*steps[0])
    for i, (qg, kt) in enumerate(steps):
        nxt = stage_a(*steps[i + 1]) if i + 1 < len(steps) else None
        stage_b(qg, kt, cur)
        cur = nxt
    p.finish_all()
    p.emit()
    return c


def mix0_maps(inp, pre):
    lg = _ret_log_gamma()
    i = np.arange(128)
    maps = []
    for c in range(NCORE):
        b, h = c // 4, c % 4
        rows = pre[b * SEQ:(b + 1) * SEQ]
        gam = np.exp(lg[h]).astype(np.float32)
        dist = np.abs(i[:, None] - i[None, :]).astype(np.float32)
        mret = np.exp(lg[h] * dist) * ((i[:, None] // 64) <= (i[None, :] // 64))
        cst = np.zeros((128, 512), np.float32)
        cst[:, 0:128] = mret
        cst[:, 128:256] = inp['da_out_norm'][0][None, :] * np.float32(0.8)
        cst[:, 256:384] = inp['ret_out_norm'][0][None, :]
        cst[:, 384] = np.exp(np.float32(128.0) * lg[h])
        lamv = np.concatenate([inp['da_lam_q1'][0], inp['da_lam_k1'][0], inp['da_lam_q2'][0], inp['da_lam_k2'][0]])
        lamv = np.ascontiguousarray(np.broadcast_to(lamv[None, :], (128, 256))).astype(np.float32)
        ct = lambda a: np.ascontiguousarray(a)
        maps.append({
            "qaT": ct(rows[:, h * 128:(h + 1) * 128].T), "kaT": ct(rows[:, 512 + h * 128:512 + (h + 1) * 128].T),
            "va": ct(rows[:, 1024 + h * 128:1024 + (h + 1) * 128]),
            "qrT": ct(rows[:, 1536 + h * 64:1536 + (h + 1) * 64].T), "krT": ct(rows[:, 1792 + h * 64:1792 + (h + 1) * 64].T),
            "qwrT": ct(rows[:, 2048 + h * 64:2048 + (h + 1) * 64].T), "kwr": ct(rows[:, 2304 + h * 64:2304 + (h + 1) * 64]),
            "vr": ct(rows[:, 2560 + h * 128:2560 + (h + 1) * 128]), "gr": ct(rows[:, 3072 + h * 128:3072 + (h + 1) * 128]),
            "cst": cst, "lamv": lamv})
    return maps


def build_pout(layer, ngrp=None):
    c = Ctx()
    nc = c.nc
    GT = 256
    NG = TOK // GT if ngrp is None else ngrp
    x = c.din("x", [TOK, D])
    nko = 8 if layer == 0 else 6
    oT_d = c.din("oT", [nko * 128, TOK], BF16)
    wout_d = c.din("wout", [D, D])
    wup_d = c.din("wup", [D, 4096])
    wdn_d = c.din("wdn", [4096, D])
    gpk_d = c.din("gpk", [128, 8])
    ident_d = c.din("ident", [128, 128])
    if layer == 1:
        yT_d = c.din("yT", [256, TOK])
        wglu_d = c.din("wglu", [256, 256])
        rT_d = c.din("rT", [768, TOK], BF16)
        gng_d = c.din("gng", [128, 1])
    out = c.dout("xo", [TOK, D])
    p = c.start()
    sb = c.sb
    woutb = sb("woutb", [128, 8, D], BF16); bwout = p.buf()
    wupb = sb("wupb", [128, 8, 4096], BF16); bwup = p.buf()
    wdnb = sb("wdnb", [128, 32, D], BF16); bwdn = p.buf()
    gpk = sb("gpk", [128, 8]); bgpk = p.buf()
    identf = sb("identf", [128, 128]); bidf = p.buf()
    ident = sb("ident", [128, 128], BF16); bid = p.buf()
    p.dma("gpsimd", gpk[:], gpk_d[:, :], writes=[bgpk])
    p.dma("gpsimd", identf[:], ident_d[:, :], writes=[bidf])
    p.op("vector", lambda e: e.tensor_copy(out=ident[:], in_=identf[:]), reads=[bidf], writes=[bid])
    xin = Rot([(sb(f"xin{i}", [128, D]), p.buf()) for i in range(2)])
    stage = xin
    load_weight_bf16(c, p, wout_d, woutb, bwout, 8, D, None, None, stage, 1024)
    if layer == 1:
        wglub = sb("wglub", [128, 2, 256], BF16); bwglu = p.buf()
        load_weight_bf16(c, p, wglu_d, wglub, bwglu, 2, 256, None, None, stage, 256)
    load_weight_bf16(c, p, wup_d, wupb, bwup, 8, 4096, gpk, bgpk, stage, 1024)
    load_weight_bf16(c, p, wdn_d, wdnb, bwdn, 32, D, None, None, stage, 1024)
    accD = [(c.ps(f"pD{i}", [128, 512]), p.buf()) for i in range(4)]
    pH = Rot([(c.ps(f"pH{i}", [128, 512]), p.buf()) for i in range(4)])
    oin = Rot([(sb(f"oin{i}", [128, 8, GT], BF16), p.buf()) for i in range(2)])
    x1 = Rot([(sb(f"x1_{i}", [128, 2, D]), p.buf()) for i in range(2)])
    x1b = Rot([(sb(f"x1b{i}", [128, D], BF16), p.buf()) for i in range(2)])
    x1T = Rot([(sb(f"x1T{i}", [128, 8, GT], BF16), p.buf()) for i in range(1)])
    rsd = Rot([(sb(f"rsd{i}", [128, 2]), p.buf()) for i in range(4)])
    junk = sb("junk", [128, D], BF16); bjunk = p.buf()
    hr = Rot([(sb(f"hr{i}", [128, GT], BF16), p.buf()) for i in range(3)])
    h2 = Rot([(sb(f"h2{i}", [128, GT], BF16), p.buf()) for i in range(4)])
    if layer == 1:
        yin = Rot([(sb(f"yin{i}", [128, 2, GT]), p.buf()) for i in range(1)])
        ga = sb("ga", [128, 2, GT]); bga = p.buf()
        gb = sb("gb", [128, 2, GT]); bgb = p.buf()
        gz, bgz = ga, bga
        gng = sb("gng", [128, 1]); bgng = p.buf()
        p.dma("gpsimd", gng[:], gng_d[:, :], writes=[bgng])
        onesb = sb("onesb", [128, 128], BF16); bonesb = p.buf()
        p.op("vector", lambda e: e.memset(onesb[:], 1.0), writes=[bonesb])
        rin = Rot([(sb(f"rin{i}", [128, 6, GT], BF16), p.buf()) for i in range(1)])
        nsq = Rot([(sb(f"nsq{i}", [128, GT], BF16), p.buf()) for i in range(2)])
        nrs = Rot([(sb(f"nrs{i}", [128, GT]), p.buf()) for i in range(2)])
        ntm = Rot([(sb(f"ntm{i}", [128, GT]), p.buf()) for i in range(1)])
        gzb = sb("gzb", [128, 2, GT], BF16); bgzb = p.buf()
        gs = sb("gs", [128, GT]); bgs = p.buf()
    xv = x.rearrange("(n p) d -> p n d", p=128)
    ov = oT_d.rearrange("(k p) t -> p k t", p=128)
    for g in range(NG):
        oi, boi = oin.next()
        k0 = 8 - nko
        p.dma("gpsimd", oi[:, k0:8, :], ov[:, :, g * GT:(g + 1) * GT], writes=[boi])
        if layer == 1:
            yi, byi = yin.next()
            p.dma("sync", yi[:], yT_d.rearrange("(k p) t -> p k t", p=128)[:, :, g * GT:(g + 1) * GT], writes=[byi])
            p.op("gpsimd", lambda e, yi=yi: e.tensor_tensor(out=ga[:], in0=yi[:], in1=yi[:], op=ALU.mult), reads=[byi], writes=[bga])
            p.op("vector", lambda e: e.tensor_scalar(out=ga[:], in0=ga[:], scalar1=0.044715, scalar2=1.0, op0=ALU.mult, op1=ALU.add), reads=[bga], writes=[bga])
            p.op("gpsimd", lambda e, yi=yi: e.tensor_tensor(out=gb[:], in0=ga[:], in1=yi[:], op=ALU.mult), reads=[bga, byi], writes=[bgb])
            p.op("scalar", lambda e: e.activation(out=gb[:], in_=gb[:], func=AF.Tanh, scale=0.7978845608028654), reads=[bgb], writes=[bgb])
            p.op("vector", lambda e: e.tensor_scalar(out=gb[:], in0=gb[:], scalar1=1.0, scalar2=0.5, op0=ALU.add, op1=ALU.mult), reads=[bgb], writes=[bgb])
            p.op("gpsimd", lambda e, yi=yi: e.tensor_tensor(out=gz[:], in0=gb[:], in1=yi[:], op=ALU.mult), reads=[bgb, byi], writes=[bgz])
            p.op("vector", lambda e: e.tensor_copy(out=gzb[:], in_=gz[:]), reads=[bgz], writes=[bgzb])
            for jc in range(2):
                pg, bpg = pH.next()
                for ic in range(2):
                    p.op("tensor", lambda e, pg=pg, ic=ic, jc=jc: e.matmul(pg[:, 0:GT], lhsT=wglub[:, ic, jc * 128:(jc + 1) * 128], rhs=gzb[:, ic, :],
                                                                      start=(ic == 0), stop=(ic == 1)), reads=[bwglu, bgzb], writes=[bpg])
                p.op("scalar", lambda e, pg=pg: e.activation(out=gs[:], in_=pg[:, 0:GT], func=AF.Sigmoid), reads=[bpg], writes=[bgs])
                p.op("vector", lambda e, jc=jc, oi=oi: e.tensor_tensor(out=oi[:, jc, :], in0=gz[:, jc, :], in1=gs[:], op=ALU.mult), reads=[bgz, bgs], writes=[boi])
            ri, bri = rin.next()
            p.dma("sync", ri[:], rT_d.rearrange("(k p) t -> p k t", p=128)[:, :, g * GT:(g + 1) * GT], writes=[bri])
            for kk in range(6):
                q_, bq_ = nsq.next()
                p.op("gpsimd", lambda e, q_=q_, oi=oi, kk=kk: e.tensor_tensor(out=q_[:], in0=oi[:, 2 + kk, :], in1=oi[:, 2 + kk, :], op=ALU.mult), reads=[boi], writes=[bq_])
                pn, bpn = pH.next()
                p.op("tensor", lambda e, pn=pn, q_=q_: e.matmul(pn[:, 0:GT], lhsT=onesb[:], rhs=q_[:], start=True, stop=True), reads=[bonesb, bq_], writes=[bpn])
                r_, br_ = nrs.next()
                p.op("scalar", lambda e, pn=pn, r_=r_: e.activation(out=r_[:], in_=pn[:, 0:GT], func=AF.Sqrt, scale=1.0 / 128, bias=EPS), reads=[bpn], writes=[br_])
                p.op("vector", lambda e, r_=r_: e.reciprocal(out=r_[:], in_=r_[:]), reads=[br_], writes=[br_])
                t_, bt_ = ntm.next()
                p.op("vector", lambda e, t_=t_, r_=r_, oi=oi, kk=kk: e.scalar_tensor_tensor(out=t_[:], in0=oi[:, 2 + kk, :], scalar=gng[:, 0:1], in1=r_[:], op0=ALU.mult, op1=ALU.mult),
                     reads=[boi, bgng, br_], writes=[bt_])
                p.op("gpsimd", lambda e, t_=t_, ri=ri, oi=oi, kk=kk: e.tensor_tensor(out=oi[:, 2 + kk, :], in0=t_[:], in1=ri[:, kk, :], op=ALU.mult), reads=[bt_, bri], writes=[boi])
        x1g, bx1g = x1.next()
        x1Tg, bx1Tg = x1T.next()
        rs2, brs2 = rsd.next()
        rq2, brq2 = rsd.next()
        for j in range(2):
            xi, bxi = xin.next()
            p.dma("sync", xi[:], x[(g * 2 + j) * 128:(g * 2 + j + 1) * 128, :], writes=[bxi])
            for hh in range(2):
                pd, bpd = accD[j * 2 + hh]
                for k in range(8):
                    p.op("tensor", lambda e, pd=pd, oi=oi, k=k, j=j, hh=hh: e.matmul(pd[:, :], lhsT=oi[:, k, j * 128:(j + 1) * 128],
                                                                                rhs=woutb[:, k, hh * 512:(hh + 1) * 512], start=(k == 0), stop=(k == 7)),
                         reads=[boi, bwout], writes=[bpd])
                p.op("vector", lambda e, pd=pd, xi=xi, x1g=x1g, j=j, hh=hh: e.tensor_tensor(out=x1g[:, j, hh * 512:(hh + 1) * 512], in0=pd[:, :],
                                                                                       in1=xi[:, hh * 512:(hh + 1) * 512], op=ALU.add),
                     reads=[bpd, bxi], writes=[bx1g])
            p.op("scalar", lambda e, x1g=x1g, j=j, rs2=rs2: e.activation(out=junk[:], in_=x1g[:, j, :], func=AF.Square, accum_out=rs2[:, j:j + 1]),
                 reads=[bx1g], writes=[bjunk, brs2])
            xb, bxb = x1b.next()
            p.op("gpsimd", lambda e, xb=xb, x1g=x1g, j=j: e.tensor_copy(out=xb[:], in_=x1g[:, j, :]), reads=[bx1g], writes=[bxb])
            pt, bpt = pH.next()
            ptb = pt[:, :].bitcast(BF16)
            for k in range(8):
                p.op("tensor", lambda e, ptb=ptb, xb=xb, k=k: e.transpose(ptb[:, k * 128:(k + 1) * 128], xb[:, k * 128:(k + 1) * 128], ident[:]),
                     reads=[bxb, bid], writes=[bpt])
            p.op("scalar", lambda e, ptb=ptb, x1Tg=x1Tg, j=j: e.activation(out=x1Tg[:, :, j * 128:(j + 1) * 128], in_=ptb.rearrange("p (k t) -> p k t", t=128), func=AF.Copy),
                 reads=[bpt], writes=[bx1Tg])
        p.op("scalar", lambda e, rs2=rs2: e.activation(out=rs2[:], in_=rs2[:], func=AF.Sqrt, scale=1.0 / D, bias=EPS), reads=[brs2], writes=[brs2])
        p.op("vector", lambda e, rs2=rs2: e.reciprocal(out=rs2[:], in_=rs2[:]), reads=[brs2], writes=[brs2])
        p.op("vector", lambda e, rs2=rs2, rq2=rq2: e.tensor_tensor(out=rq2[:], in0=rs2[:], in1=rs2[:], op=ALU.mult), reads=[brs2], writes=[brq2])
        def up(f):
            ph, bph = pH.next()
            for k in range(8):
                p.op("tensor", lambda e, ph=ph, k=k, f=f, x1Tg=x1Tg: e.matmul(ph[:, 0:GT], lhsT=wupb[:, k, f * 128:(f + 1) * 128], rhs=x1Tg[:, k, :],
                                                                         start=(k == 0), stop=(k == 7)), reads=[bwup, bx1Tg], writes=[bph])
            r, br = hr.next()
            p.op("scalar", lambda e, ph=ph, r=r: e.activation(out=r[:], in_=ph[:, 0:GT], func=AF.Relu), reads=[bph], writes=[br])
            hq, bhq = h2.next()
            eng = "gpsimd" if f % 2 == 0 else "vector"
            p.op(eng, lambda e, r=r, hq=hq: e.tensor_tensor(out=hq[:], in0=r[:], in1=r[:], op=ALU.mult), reads=[br], writes=[bhq])
            return hq, bhq

        def down(f, hq, bhq):
            for j in range(2):
                for hh in range(2):
                    pd, bpd = accD[j * 2 + hh]
                    p.op("tensor", lambda e, pd=pd, hq=hq, j=j, hh=hh, f=f: e.matmul(pd[:, :], lhsT=hq[:, j * 128:(j + 1) * 128],
                                                                                rhs=wdnb[:, f, hh * 512:(hh + 1) * 512], start=(f == 0), stop=(f == 31)),
                         reads=[bhq, bwdn], writes=[bpd])
        cur = up(0)
        for f in range(32):
            nxt = up(f + 1) if f < 31 else None
            down(f, *cur)
            cur = nxt
        for j in range(2):
            for hh in range(2):
                pd, bpd = accD[j * 2 + hh]
                p.op("vector", lambda e, pd=pd, x1g=x1g, rq2=rq2, j=j, hh=hh: e.scalar_tensor_tensor(
                    out=x1g[:, j, hh * 512:(hh + 1) * 512], in0=pd[:, :], scalar=rq2[:, j:j + 1], in1=x1g[:, j, hh * 512:(hh + 1) * 512], op0=ALU.mult, op1=ALU.add),
                    reads=[bpd, brq2, bx1g], writes=[bx1g])
            tt = g * 2 + j
            p.dma("sync", out[tt * 128:(tt + 1) * 128, :], x1g[:, j, :], reads=[bx1g], writes=[p.buf()])
    p.finish_all()
    p.emit()
    return c


def pout_maps(inp, layer, x, oT_full, yT_full=None, rT_full=None):
    j = layer // 2
    wout = np.ascontiguousarray(inp['ab_w_out'][j] if layer == 0 else inp['cd_w_out'][j])
    maps = []
    ident = np.eye(128, dtype=np.float32)
    for c in range(NCORE):
        sl = slice(c * TOK, (c + 1) * TOK)
        m = {"x": np.ascontiguousarray(x[sl]), "oT": np.ascontiguousarray(oT_full[:, sl]), "wout": wout,
             "wup": np.ascontiguousarray(inp['w_up'][layer]), "wdn": np.ascontiguousarray(inp['w_down'][layer]),
             "gpk": _gpk(inp['norm_mlp_g'][layer]), "ident": ident}
        if layer == 1:
            m["yT"] = np.ascontiguousarray(yT_full[:, sl])
            m["wglu"] = np.ascontiguousarray(inp['s5_w_glu'][j])
            m["rT"] = np.ascontiguousarray(rT_full[:, sl])
            m["gng"] = np.ascontiguousarray(inp['gla_out_norm'][j][:, None])
        maps.append(m)
    return maps


def assemble_oT0(o):
    oT = np.empty((1024, 2 * SEQ), o.dtype)
    for c in range(NCORE):
        b, h = c // 4, c % 4
        oT[h * 128:(h + 1) * 128, b * SEQ:(b + 1) * SEQ] = o[c][:, 0:128].T
        oT[512 + h * 128:512 + (h + 1) * 128, b * SEQ:(b + 1) * SEQ] = o[c][:, 128:256].T
    return oT


def pin1_maps(inp, x):
    w = np.ascontiguousarray(inp['cd_w_in'][0][:, 0:2560])
    walrT = np.ascontiguousarray(inp['cd_w_in'][0][:, 2560:2576].T)
    wa2 = np.ascontiguousarray(inp['gla_w_a2'][0])
    ba2 = np.ascontiguousarray(np.broadcast_to(inp['gla_b_a2'][0][None, :], (128, 384)))
    gpk = _gpk(inp['norm_mix_g'][1])
    maps = []
    for c in range(NCORE):
        xs = x[c * TOK:(c + 1) * TOK]
        maps.append({"x": np.ascontiguousarray(xs), "xT": np.ascontiguousarray(xs.T), "w": w, "gpk": gpk,
                     "walrT": walrT, "wa2": wa2, "ba2": ba2})
    return maps


TWO_PI = float(2.0 * np.pi)
PI = float(np.pi)
I32 = mybir.dt.int32


def emit_sincos(p, sb, name, ang, bang, shape, s_out, c_out, bouts, eng_pool="gpsimd"):
    P_ = shape[0]
    kf = sb(name + "_kf", shape); ki = sb(name + "_ki", shape, I32); r = sb(name + "_r", shape); ab = sb(name + "_ab", shape)
    bkf, bki, br, bab = p.buf(), p.buf(), p.buf(), p.buf()
    p.op("vector", lambda e: e.tensor_scalar(out=kf[:], in0=ang, scalar1=1.0 / TWO_PI, scalar2=None, op0=ALU.mult), reads=[bang], writes=[bkf])
    p.op("vector", lambda e: e.tensor_copy(out=ki[:], in_=kf[:]), reads=[bkf], writes=[bki])
    p.op("vector", lambda e: e.tensor_copy(out=kf[:], in_=ki[:]), reads=[bki], writes=[bkf])
    p.op("vector", lambda e: e.scalar_tensor_tensor(out=r[:], in0=kf[:], scalar=-TWO_PI, in1=ang, op0=ALU.mult, op1=ALU.add), reads=[bkf, bang], writes=[br])
    p.op("vector", lambda e: e.tensor_scalar(out=r[:], in0=r[:], scalar1=-PI, scalar2=PI, op0=ALU.max, op1=ALU.min), reads=[br], writes=[br])
    p.op("scalar", lambda e: e.activation(out=s_out, in_=r[:], func=AF.Sin), reads=[br], writes=[bouts[0]])
    p.op("scalar", lambda e: e.activation(out=ab[:], in_=r[:], func=AF.Abs), reads=[br], writes=[bab])
    p.op("scalar", lambda e: e.activation(out=c_out, in_=ab[:], func=AF.Sin, scale=-1.0, bias=PI / 2), reads=[bab], writes=[bouts[1]])


def build_mix1(nwin=32):
    c = Ctx()
    nc = c.nc
    W = 512
    uT_d = c.din("uT", [64, SEQ], BF16)
    prmC_d = c.din("prmC", [128, 6])
    prmR_d = c.din("prmR", [64, 3, 128])
    bpad_d = c.din("bpad", [64, 2, 128])
    ct_d = c.din("ct", [128, 2, 2, 64])
    dsk_d = c.din("dsk", [64, 1])
    iota_d = c.din("iota", [128, W])
    gm_d = c.din("gm", [128, 256])
    idn_d = c.din("idn", [128, 128])
    gq_d = c.din("gq", [3, 64, SEQ], BF16)
    gk_d = c.din("gk", [3, 64, SEQ], BF16)
    gla_d = c.din("gla", [3, 64, SEQ])
    gv_d = c.din("gv", [SEQ, 192], BF16)
    yT_o = c.dout("yT", [64, SEQ])
    go_o = c.dout("go", [SEQ, 192], BF16)
    p = c.start()
    sb = c.sb

    def ld(name, shape, src, dt=F32, q="gpsimd"):
        t = sb(name, shape, dt); b = p.buf()
        p.dma(q, t[:], src, writes=[b])
        return t, b
    prmC, bprmC = ld("prmC", [128, 6], prmC_d[:, :])
    prmR, bprmR = ld("prmR", [64, 3, 128], prmR_d[:, :, :])
    bpad, bbpad = ld("bpad", [64, 2, 128], bpad_d[:, :, :])
    ctf, bctf = ld("ctf", [128, 2, 2, 64], ct_d[:, :, :, :])
    dsk, bdsk = ld("dsk", [64, 1], dsk_d[:, :])
    iota, biota = ld("iota", [128, W], iota_d[:, :])
    gm, bgm = ld("gm", [128, 256], gm_d[:, :])
    idf, bidf = ld("idf", [128, 128], idn_d[:, :])
    ident = sb("ident", [128, 128], BF16); bid = p.buf()
    p.op("vector", lambda e: e.tensor_copy(out=ident[:], in_=idf[:]), reads=[bidf], writes=[bid])
    ones = sb("ones", [128, 1]); bones = p.buf()
    p.op("vector", lambda e: e.memset(ones[:], 1.0), writes=[bones])

    pc = prmC[:].rearrange("p (a k) -> p a k", k=3)
    dl = sb("dl", [128, 2]); bdl = p.buf()
    rr = sb("rr", [128, 2]); brr = p.buf()
    th = sb("th", [128, 2]); bth = p.buf()
    p.op("scalar", lambda e: e.activation(out=dl[:], in_=pc[:, :, 2], func=AF.Exp), reads=[bprmC], writes=[bdl])
    p.op("vector", lambda e: e.tensor_tensor(out=rr[:], in0=pc[:, :, 0], in1=dl[:], op=ALU.mult), reads=[bprmC, bdl], writes=[brr])
    p.op("scalar", lambda e: e.activation(out=rr[:], in_=rr[:], func=AF.Exp), reads=[brr], writes=[brr])
    p.op("vector", lambda e: e.tensor_tensor(out=th[:], in0=pc[:, :, 1], in1=dl[:], op=ALU.mult), reads=[bprmC, bdl], writes=[bth])
    cosT, sinT, bcs = [], [], []
    ang = sb("ang", [128, W]); bang = p.buf()
    for pr in range(2):
        ct_ = sb(f"cosT{pr}", [128, W]); st_ = sb(f"sinT{pr}", [128, W]); b1, b2 = p.buf(), p.buf()
        p.op("vector", lambda e, pr=pr: e.tensor_scalar(out=ang[:], in0=iota[:], scalar1=th[:, pr:pr + 1], scalar2=None, op0=ALU.mult), reads=[biota, bth], writes=[bang])
        emit_sincos(p, sb, f"sc{pr}", ang[:], bang, [128, W], st_[:], ct_[:], (b1, b2))
        cosT.append(ct_); sinT.append(st_); bcs.append((b2, b1))
    angW = sb("angW", [128, 2]); bangW = p.buf()
    cW = sb("cW", [128, 2]); sW = sb("sW", [128, 2]); nsW = sb("nsW", [128, 2]); bcW, bsW, bnsW = p.buf(), p.buf(), p.buf()
    p.op("vector", lambda e: e.tensor_scalar(out=angW[:], in0=th[:], scalar1=float(W), scalar2=None, op0=ALU.mult), reads=[bth], writes=[bangW])
    emit_sincos(p, sb, "scW", angW[:], bangW, [128, 2], sW[:], cW[:], (bsW, bcW))
    p.op("vector", lambda e: e.tensor_scalar(out=nsW[:], in0=sW[:], scalar1=-1.0, scalar2=None, op0=ALU.mult), reads=[bsW], writes=[bnsW])
    R = lambda k: prmR[:, k, :]
    def t64(name):
        return sb(name, [64, 128]), p.buf()
    (dR, bdR), (x1, bx1), (er, ber), (thR, bthR), (sR, bsR), (cR, bcR) = [t64(n) for n in ("dR", "x1R", "erR", "thR", "sR", "cR")]
    (lbr, blbr), (lbi, blbi), (den, bden), (t1, bt1), (t2, bt2), (cfr, bcfr), (cfi, bcfi) = [t64(n) for n in ("lbr", "lbi", "den", "t1R", "t2R", "cfr", "cfi")]
    V = "vector"
    p.op("scalar", lambda e: e.activation(out=dR[:], in_=R(2), func=AF.Exp), reads=[bprmR], writes=[bdR])
    p.op(V, lambda e: e.tensor_tensor(out=x1[:], in0=R(0), in1=dR[:], op=ALU.mult), reads=[bprmR, bdR], writes=[bx1])
    p.op("scalar", lambda e: e.activation(out=er[:], in_=x1[:], func=AF.Exp), reads=[bx1], writes=[ber])
    p.op(V, lambda e: e.tensor_tensor(out=thR[:], in0=R(1), in1=dR[:], op=ALU.mult), reads=[bprmR, bdR], writes=[bthR])
    emit_sincos(p, sb, "scR", thR[:], bthR, [64, 128], sR[:], cR[:], (bsR, bcR))
    p.op(V, lambda e: e.tensor_tensor(out=lbr[:], in0=er[:], in1=cR[:], op=ALU.mult), reads=[ber, bcR], writes=[blbr])
    p.op(V, lambda e: e.tensor_scalar(out=lbr[:], in0=lbr[:], scalar1=-1.0, scalar2=None, op0=ALU.add), reads=[blbr], writes=[blbr])
    p.op(V, lambda e: e.tensor_tensor(out=lbi[:], in0=er[:], in1=sR[:], op=ALU.mult), reads=[ber, bsR], writes=[blbi])
    p.op(V, lambda e: e.tensor_tensor(out=den[:], in0=R(0), in1=R(0), op=ALU.mult), reads=[bprmR], writes=[bden])
    p.op(V, lambda e: e.tensor_tensor(out=t1[:], in0=R(1), in1=R(1), op=ALU.mult), reads=[bprmR], writes=[bt1])
    p.op(V, lambda e: e.tensor_tensor(out=den[:], in0=den[:], in1=t1[:], op=ALU.add), reads=[bden, bt1], writes=[bden])
    p.op(V, lambda e: e.reciprocal(out=den[:], in_=den[:]), reads=[bden], writes=[bden])
    p.op(V, lambda e: e.tensor_tensor(out=t1[:], in0=lbr[:], in1=R(0), op=ALU.mult), reads=[blbr, bprmR, bden], writes=[bt1])
    p.op(V, lambda e: e.tensor_tensor(out=t2[:], in0=lbi[:], in1=R(1), op=ALU.mult), reads=[blbi, bprmR], writes=[bt2])
    p.op(V, lambda e: e.tensor_tensor(out=cfr[:], in0=t1[:], in1=t2[:], op=ALU.add), reads=[bt1, bt2], writes=[bcfr])
    p.op(V, lambda e: e.tensor_tensor(out=cfr[:], in0=cfr[:], in1=den[:], op=ALU.mult), reads=[bcfr, bden], writes=[bcfr])
    p.op(V, lambda e: e.tensor_tensor(out=t1[:], in0=lbi[:], in1=R(0), op=ALU.mult), reads=[blbi, bprmR, bcfr], writes=[bt1])
    p.op(V, lambda e: e.tensor_tensor(out=t2[:], in0=lbr[:], in1=R(1), op=ALU.mult), reads=[blbr, bprmR, bcfr], writes=[bt2])
    p.op(V, lambda e: e.tensor_tensor(out=cfi[:], in0=t1[:], in1=t2[:], op=ALU.subtract), reads=[bt1, bt2], writes=[bcfi])
    p.op(V, lambda e: e.tensor_tensor(out=cfi[:], in0=cfi[:], in1=den[:], op=ALU.mult), reads=[bcfi, bden], writes=[bcfi])
    bbre = sb("bbre", [64, 128], BF16); bbim = sb("bbim", [64, 128], BF16); bbbre, bbbim = p.buf(), p.buf()
    Bre, Bim = bpad[:, 0, :], bpad[:, 1, :]
    p.op(V, lambda e: e.tensor_tensor(out=t1[:], in0=cfr[:], in1=Bre, op=ALU.mult), reads=[bcfr, bbpad, bcfi], writes=[bt1])
    p.op(V, lambda e: e.tensor_tensor(out=t2[:], in0=cfi[:], in1=Bim, op=ALU.mult), reads=[bcfi, bbpad], writes=[bt2])
    p.op(V, lambda e: e.tensor_tensor(out=bbre[:], in0=t1[:], in1=t2[:], op=ALU.subtract), reads=[bt1, bt2], writes=[bbbre])
    p.op(V, lambda e: e.tensor_tensor(out=t1[:], in0=cfr[:], in1=Bim, op=ALU.mult), reads=[bcfr, bbpad, bbbre], writes=[bt1])
    p.op(V, lambda e: e.tensor_tensor(out=t2[:], in0=cfi[:], in1=Bre, op=ALU.mult), reads=[bcfi, bbpad, bbbre], writes=[bt2])
    p.op(V, lambda e: e.tensor_tensor(out=bbim[:], in0=t1[:], in1=t2[:], op=ALU.add), reads=[bt1, bt2], writes=[bbbim])
    ctb = sb("ctb", [128, 2, 2, 64], BF16); bctb = p.buf()
    p.op(V, lambda e: e.tensor_copy(out=ctb[:, :, 0, :], in_=ctf[:, :, 0, :]), reads=[bctf], writes=[bctb])
    p.op(V, lambda e: e.tensor_scalar(out=ctb[:, :, 1, :], in0=ctf[:, :, 1, :], scalar1=-1.0, scalar2=None, op0=ALU.mult), reads=[bctf], writes=[bctb])

    pBU = [(c.ps(f"pBU{i}", [128, 512]), p.buf()) for i in range(2)]
    pY = (c.ps("pY", [128, 512]), p.buf())
    pS = Rot([(c.ps(f"pS{i}", [128, 512]), p.buf()) for i in range(2)])
    pO = Rot([(c.ps(f"pO{i}", [128, 512]), p.buf()) for i in range(2)])
    pT = (c.ps("pT", [128, 512]), p.buf())
    uin = Rot([(sb(f"uin{i}", [64, W], BF16), p.buf()) for i in range(2)])
    def T(name, dt=F32, n=1, shape=None):
        return Rot([(sb(f"{name}{i}", shape or [128, W], dt), p.buf()) for i in range(n)])
    ta, tb_, tc, td = T("s5a"), T("s5b"), T("s5c"), T("s5d")
    kre, kim = T("kre"), T("kim")
    wre = [T(f"wre{pr}", n=2) for pr in range(2)]
    wim = [T(f"wim{pr}", n=2) for pr in range(2)]
    xre, xim = T("xre", BF16, 2), T("xim", BF16, 2)
    w0 = [[(sb(f"w0_{pr}_{i}", [128, 2]), p.buf()) for i in range(2)] for pr in range(2)]
    for pr in range(2):
        p.op("vector", lambda e, pr=pr: e.memset(w0[pr][0][0][:], 0.0), writes=[w0[pr][0][1]])
    yo = T("yo", n=2, shape=[64, W])
    CH = 1024
    gq = Rot([(sb(f"gq{i}", [64, 3, CH], BF16), p.buf()) for i in range(2)])
    gk = Rot([(sb(f"gk{i}", [64, 3, CH], BF16), p.buf()) for i in range(2)])
    gla = Rot([(sb(f"gla{i}", [64, 3, CH]), p.buf()) for i in range(2)])
    gv = Rot([(sb(f"gv{i}", [128, 8, 192], BF16), p.buf()) for i in range(2)])
    G64 = lambda name, dt=F32, n=2: Rot([(sb(f"{name}{i}", [64, 128], dt), p.buf()) for i in range(n)])
    Bt, ep, en, el = G64("Bt"), G64("ep"), G64("en"), G64("el")
    qf, kf, qb, kb, ks = [G64(n, BF16) for n in ("qf", "kf", "qb", "kb", "ks")]
    s1 = Rot([(sb(f"gs1{i}", [128, 128]), p.buf()) for i in range(2)])
    s2 = Rot([(sb(f"gs2{i}", [128, 128]), p.buf()) for i in range(2)])
    Sm = Rot([(sb(f"gSm{i}", [128, 128], BF16), p.buf()) for i in range(2)])
    kst = Rot([(sb(f"kst{i}", [128, 64], BF16), p.buf()) for i in range(2)])
    Sst = [(sb(f"Sst{i}", [64, 64]), p.buf()) for i in range(3)]
    Sstb = [Rot([(sb(f"Sstb{i}_{k}", [64, 64], BF16), p.buf()) for k in range(2)]) for i in range(3)]
    cur_sb = []
    for i in range(3):
        p.op("vector", lambda e, i=i: e.memset(Sst[i][0][:], 0.0), writes=[Sst[i][1]])
        t_, b_ = Sstb[i].next()
        p.op("vector", lambda e, t_=t_: e.memset(t_[:], 0.0), writes=[b_])
        cur_sb.append((t_, b_))
    otile = Rot([(sb(f"got{i}", [128, 192], BF16), p.buf()) for i in range(3)])
    gst = {"cur": None}

    def s5_window(m):
        ui, bui = uin.next()
        p.dma("sync", ui[:], uT_d[:, m * W:(m + 1) * W], writes=[bui])
        py, bpy = pY
        for pr in range(2):
            (pre_, bpre), (pim_, bpim) = pBU
            rows = slice(32 * pr, 32 * pr + 32)
            p.op("tensor", lambda e, pr=pr, rows=rows, ui=ui, pre_=pre_: e.matmul(pre_[:, :], lhsT=bbre[rows, :], rhs=ui[rows, :], start=True, stop=True),
                 reads=[bbbre, bui], writes=[bpre])
            p.op("tensor", lambda e, pr=pr, rows=rows, ui=ui, pim_=pim_: e.matmul(pim_[:, :], lhsT=bbim[rows, :], rhs=ui[rows, :], start=True, stop=True),
                 reads=[bbbim, bui], writes=[bpim])
            bcos, bsin = bcs[pr]
            (a_, ba_), (b_, bb_), (c_, bc_), (d_, bd_) = ta.next(), tb_.next(), tc.next(), td.next()
            (kr_, bkr_), (ki_, bki_) = kre.next(), kim.next()
            CT, ST = cosT[pr], sinT[pr]
            p.op("vector", lambda e, a_=a_, pre_=pre_, CT=CT: e.tensor_tensor(out=a_[:], in0=pre_[:, :], in1=CT[:], op=ALU.mult), reads=[bpre, bcos], writes=[ba_])
            p.op("vector", lambda e, b_=b_, pim_=pim_, ST=ST: e.tensor_tensor(out=b_[:], in0=pim_[:, :], in1=ST[:], op=ALU.mult), reads=[bpim, bsin], writes=[bb_])
            p.op("vector", lambda e, c_=c_, pim_=pim_, CT=CT: e.tensor_tensor(out=c_[:], in0=pim_[:, :], in1=CT[:], op=ALU.mult), reads=[bpim, bcos], writes=[bc_])
            p.op("vector", lambda e, d_=d_, pre_=pre_, ST=ST: e.tensor_tensor(out=d_[:], in0=pre_[:, :], in1=ST[:], op=ALU.mult), reads=[bpre, bsin], writes=[bd_])
            p.op("gpsimd", lambda e, kr_=kr_, a_=a_, b_=b_: e.tensor_tensor(out=kr_[:], in0=a_[:], in1=b_[:], op=ALU.add), reads=[ba_, bb_], writes=[bkr_])
            p.op("gpsimd", lambda e, ki_=ki_, c_=c_, d_=d_: e.tensor_tensor(out=ki_[:], in0=c_[:], in1=d_[:], op=ALU.subtract), reads=[bc_, bd_], writes=[bki_])
            (wr_, bwr_), (wi_, bwi_) = wre[pr].next(), wim[pr].next()
            w0c, bw0c = w0[pr][m % 2]
            w0n, bw0n = w0[pr][(m + 1) % 2]
            rbc = rr[:, pr:pr + 1].to_broadcast([128, W])
            p.op("vector", lambda e, wr_=wr_, kr_=kr_, w0c=w0c, rbc=rbc: e.tensor_tensor_scan(out=wr_[:], data0=rbc, data1=kr_[:], initial=w0c[:, 0:1], op0=ALU.mult, op1=ALU.add),
                 reads=[brr, bkr_, bw0c], writes=[bwr_])
            p.op("vector", lambda e, wi_=wi_, ki_=ki_, w0c=w0c, rbc=rbc: e.tensor_tensor_scan(out=wi_[:], data0=rbc, data1=ki_[:], initial=w0c[:, 1:2], op0=ALU.mult, op1=ALU.add),
                 reads=[brr, bki_, bw0c], writes=[bwi_])
            p.op("vector", lambda e, w0n=w0n, wr_=wr_, pr=pr: e.tensor_tensor(out=w0n[:, 0:1], in0=wr_[:, W - 1:W], in1=cW[:, pr:pr + 1], op=ALU.mult), reads=[bwr_, bcW], writes=[bw0n])
            p.op("vector", lambda e, w0n=w0n, wi_=wi_, pr=pr: e.scalar_tensor_tensor(out=w0n[:, 0:1], in0=wi_[:, W - 1:W], scalar=nsW[:, pr:pr + 1], in1=w0n[:, 0:1], op0=ALU.mult, op1=ALU.add),
                 reads=[bwi_, bnsW, bw0n], writes=[bw0n])
            p.op("vector", lambda e, w0n=w0n, wi_=wi_, pr=pr: e.tensor_tensor(out=w0n[:, 1:2], in0=wi_[:, W - 1:W], in1=cW[:, pr:pr + 1], op=ALU.mult), reads=[bwi_, bcW], writes=[bw0n])
            p.op("vector", lambda e, w0n=w0n, wr_=wr_, pr=pr: e.scalar_tensor_tensor(out=w0n[:, 1:2], in0=wr_[:, W - 1:W], scalar=sW[:, pr:pr + 1], in1=w0n[:, 1:2], op0=ALU.mult, op1=ALU.add),
                 reads=[bwr_, bsW, bw0n], writes=[bw0n])
            (a2, ba2), (b2, bb2), (c2, bc2), (d2, bd2) = ta.next(), tb_.next(), tc.next(), td.next()
            (xr_, bxr_), (xi_, bxi_) = xre.next(), xim.next()
            p.op("gpsimd", lambda e, a2=a2, wr_=wr_, CT=CT: e.tensor_tensor(out=a2[:], in0=wr_[:], in1=CT[:], op=ALU.mult), reads=[bwr_, bcos], writes=[ba2])
            p.op("gpsimd", lambda e, b2=b2, wi_=wi_, ST=ST: e.tensor_tensor(out=b2[:], in0=wi_[:], in1=ST[:], op=ALU.mult), reads=[bwi_, bsin], writes=[bb2])
            p.op("vector", lambda e, c2=c2, wi_=wi_, CT=CT: e.tensor_tensor(out=c2[:], in0=wi_[:], in1=CT[:], op=ALU.mult), reads=[bwi_, bcos], writes=[bc2])
            p.op("gpsimd", lambda e, d2=d2, wr_=wr_, ST=ST: e.tensor_tensor(out=d2[:], in0=wr_[:], in1=ST[:], op=ALU.mult), reads=[bwr_, bsin], writes=[bd2])
            p.op("gpsimd", lambda e, xr_=xr_, a2=a2, b2=b2: e.tensor_tensor(out=xr_[:], in0=a2[:], in1=b2[:], op=ALU.subtract), reads=[ba2, bb2], writes=[bxr_])
            p.op("gpsimd", lambda e, xi_=xi_, c2=c2, d2=d2: e.tensor_tensor(out=xi_[:], in0=c2[:], in1=d2[:], op=ALU.add), reads=[bc2, bd2], writes=[bxi_])
            p.op("tensor", lambda e, pr=pr, xr_=xr_, py=py: e.matmul(py[0:64, :], lhsT=ctb[:, pr, 0, :], rhs=xr_[:], start=(pr == 0), stop=False), reads=[bctb, bxr_], writes=[bpy])
            p.op("tensor", lambda e, pr=pr, xi_=xi_, py=py: e.matmul(py[0:64, :], lhsT=ctb[:, pr, 1, :], rhs=xi_[:], start=False, stop=(pr == 1)), reads=[bctb, bxi_], writes=[bpy])
        yo_, byo = yo.next()
        p.op("vector", lambda e, yo_=yo_, ui=ui, py=py: e.scalar_tensor_tensor(out=yo_[:], in0=ui[:], scalar=dsk[:, 0:1], in1=py[0:64, :], op0=ALU.mult, op1=ALU.add),
             reads=[bui, bdsk, bpy], writes=[byo])
        p.dma("sync", yT_o[:, m * W:(m + 1) * W], yo_[:], reads=[byo], writes=[p.buf()])

    def gla_tile(t):
        ci, ti = t // 8, t % 8
        if ti == 0:
            cur = (gq.next(), gk.next(), gla.next(), gv.next())
            sl = slice(ci * CH, (ci + 1) * CH)
            p.dma("gpsimd", cur[0][0][:], gq_d[:, :, sl].rearrange("u d t -> d u t"), writes=[cur[0][1]])
            p.dma("gpsimd", cur[1][0][:], gk_d[:, :, sl].rearrange("u d t -> d u t"), writes=[cur[1][1]])
            p.dma("gpsimd", cur[2][0][:], gla_d[:, :, sl].rearrange("u d t -> d u t"), writes=[cur[2][1]])
            p.dma("gpsimd", cur[3][0][:], gv_d.rearrange("(n p) e -> p n e", p=128)[:, ci * 8:(ci + 1) * 8, :], writes=[cur[3][1]])
            gst["cur"] = cur
        (q_, bq_), (k_, bk_), (la_, bla_), (v_, bv_) = gst["cur"]
        ts = slice(ti * 128, (ti + 1) * 128)
        ot, bot = otile.next()
        for i in range(3):
            (B_, bB_), (ep_, bep_), (en_, ben_), (el_, bel_) = Bt.next(), ep.next(), en.next(), el.next()
            p.op("vector", lambda e, B_=B_, i=i: e.tensor_tensor_scan(out=B_[:], data0=ones[0:64, 0:1].to_broadcast([64, 128]), data1=la_[:, i, ts], initial=0.0, op0=ALU.mult, op1=ALU.add),
                 reads=[bones, bla_], writes=[bB_])
            p.op("scalar", lambda e, B_=B_, ep_=ep_: e.activation(out=ep_[:], in_=B_[:], func=AF.Exp), reads=[bB_], writes=[bep_])
            p.op("scalar", lambda e, B_=B_, en_=en_: e.activation(out=en_[:], in_=B_[:], func=AF.Exp, scale=-1.0), reads=[bB_], writes=[ben_])
            p.op("scalar", lambda e, B_=B_, el_=el_: e.activation(out=el_[:], in_=B_[:], func=AF.Exp, scale=-1.0, bias=B_[:, 127:128]), reads=[bB_], writes=[bel_])
            (qf_, bqf_), (kf_, bkf_), (qb_, bqb_), (kb_, bkb_), (ks_, bks_) = qf.next(), kf.next(), qb.next(), kb.next(), ks.next()
            p.op("vector", lambda e, qf_=qf_, ep_=ep_, i=i: e.tensor_tensor(out=qf_[:], in0=q_[:, i, ts], in1=ep_[:], op=ALU.mult), reads=[bq_, bep_], writes=[bqf_])
            p.op("gpsimd", lambda e, kf_=kf_, en_=en_, i=i: e.tensor_tensor(out=kf_[:], in0=k_[:, i, ts], in1=en_[:], op=ALU.mult), reads=[bk_, ben_], writes=[bkf_])
            p.op("vector", lambda e, qb_=qb_, en_=en_, i=i: e.tensor_tensor(out=qb_[:], in0=q_[:, i, ts], in1=en_[:], op=ALU.mult), reads=[bq_, ben_], writes=[bqb_])
            p.op("gpsimd", lambda e, kb_=kb_, ep_=ep_, i=i: e.tensor_tensor(out=kb_[:], in0=k_[:, i, ts], in1=ep_[:], op=ALU.mult), reads=[bk_, bep_], writes=[bkb_])
            p.op("gpsimd", lambda e, ks_=ks_, el_=el_, i=i: e.tensor_tensor(out=ks_[:], in0=k_[:, i, ts], in1=el_[:], op=ALU.mult), reads=[bk_, bel_], writes=[bks_])
            ps_, bps_ = pS.next()
            p.op("tensor", lambda e, ps_=ps_, kf_=kf_, qf_=qf_: e.matmul(ps_[:, 0:128], lhsT=kf_[:], rhs=qf_[:], start=True, stop=True), reads=[bkf_, bqf_], writes=[bps_])
            p.op("tensor", lambda e, ps_=ps_, kb_=kb_, qb_=qb_: e.matmul(ps_[:, 128:256], lhsT=kb_[:], rhs=qb_[:], start=True, stop=True), reads=[bkb_, bqb_], writes=[bps_])
            (s1_, bs1_), (s2_, bs2_), (S_, bS_) = s1.next(), s2.next(), Sm.next()
            p.op("vector", lambda e, s1_=s1_, ps_=ps_: e.tensor_tensor(out=s1_[:], in0=ps_[:, 0:128], in1=gm[:, 0:128], op=ALU.mult), reads=[bps_, bgm], writes=[bs1_])
            p.op("vector", lambda e, s2_=s2_, ps_=ps_: e.tensor_tensor(out=s2_[:], in0=ps_[:, 128:256], in1=gm[:, 128:256], op=ALU.mult), reads=[bps_, bgm], writes=[bs2_])
            p.op("gpsimd", lambda e, S_=S_, s1_=s1_, s2_=s2_: e.tensor_tensor(out=S_[:], in0=s1_[:], in1=s2_[:], op=ALU.add), reads=[bs1_, bs2_], writes=[bS_])
            ptt, bptt = pT
            ptb = ptt[:, :].bitcast(BF16)
            p.op("tensor", lambda e, ptb=ptb, ks_=ks_: e.transpose(ptb[:, 0:64], ks_[:], ident[0:64, 0:64]), reads=[bks_, bid], writes=[bptt])
            kt_, bkt_ = kst.next()
            p.op("scalar", lambda e, kt_=kt_, ptb=ptb: e.activation(out=kt_[:], in_=ptb[:, 0:64], func=AF.Copy), reads=[bptt], writes=[bkt_])
            po_, bpo_ = pO.next()
            sbc, bsbc = cur_sb[i]
            vv = v_[:, ti, i * 64:(i + 1) * 64]
            p.op("tensor", lambda e, po_=po_, S_=S_, vv=vv: e.matmul(po_[:, 0:64], lhsT=S_[:], rhs=vv, start=True, stop=False), reads=[bS_, bv_], writes=[bpo_])
            p.op("tensor", lambda e, po_=po_, qf_=qf_, sbc=sbc: e.matmul(po_[:, 0:64], lhsT=qf_[:], rhs=sbc[:], start=False, stop=True), reads=[bqf_, bsbc], writes=[bpo_])
            p.op("tensor", lambda e, po_=po_, kt_=kt_, vv=vv: e.matmul(po_[0:64, 64:128], lhsT=kt_[:], rhs=vv, start=True, stop=True), reads=[bkt_, bv_], writes=[bpo_])
            st_, bst_ = Sst[i]
            p.op("vector", lambda e, st_=st_, ep_=ep_, po_=po_: e.scalar_tensor_tensor(out=st_[:], in0=st_[:], scalar=ep_[:, 127:128], in1=po_[0:64, 64:128], op0=ALU.mult, op1=ALU.add),
                 reads=[bst_, bep_, bpo_], writes=[bst_])
            sbn, bsbn = Sstb[i].next()
            p.op("scalar", lambda e, sbn=sbn, st_=st_: e.activation(out=sbn[:], in_=st_[:], func=AF.Copy), reads=[bst_], writes=[bsbn])
            cur_sb[i] = (sbn, bsbn)
            p.op("scalar", lambda e, ot=ot, po_=po_, i=i: e.activation(out=ot[:, i * 64:(i + 1) * 64], in_=po_[:, 0:64], func=AF.Copy), reads=[bpo_], writes=[bot])
        p.dma("gpsimd", go_o[t * 128:(t + 1) * 128, :], ot[:], reads=[bot], writes=[p.buf()])

    for m in range(nwin):
        s5_window(m)
        for t in range(4 * m, 4 * m + 4):
            gla_tile(t)
    p.finish_all()
    p.emit()
    return c


def mix1_maps(inp, pre1, la1):
    a_re, a_im, ls = inp['s5_a_re'][0], inp['s5_a_im'][0], inp['s5_log_step'][0]
    b_re, b_im, c_re, c_im = inp['s5_b_re'][0], inp['s5_b_im'][0], inp['s5_c_re'][0], inp['s5_c_im'][0]
    iota = np.ascontiguousarray(np.broadcast_to(np.arange(512, dtype=np.float32)[None, :], (128, 512)))
    i = np.arange(128)
    mf = (i[None, :] >= i[:, None]).astype(np.float32)
    mb = ((i[None, :] < i[:, None]) & ((i[None, :] // 64) == (i[:, None] // 64))).astype(np.float32)
    gm = np.ascontiguousarray(np.concatenate([mf, mb], axis=1))
    idn = np.eye(128, dtype=np.float32)
    maps = []
    for c in range(NCORE):
        b, c4 = c // 4, c % 4
        rows = pre1[b * SEQ:(b + 1) * SEQ]
        lar = la1[b * SEQ:(b + 1) * SEQ]
        prmC = np.zeros((128, 6), np.float32)
        prmR = np.zeros((64, 3, 128), np.float32)
        bpad = np.zeros((64, 2, 128), np.float32)
        ct = np.zeros((128, 2, 2, 64), np.float32)
        for pr in range(2):
            for g2 in range(2):
                g = 4 * c4 + 2 * pr + g2
                ps = slice(64 * g2, 64 * g2 + 64)
                prmC[ps, pr * 3 + 0] = a_re[g]
                prmC[ps, pr * 3 + 1] = a_im[g]
                prmC[ps, pr * 3 + 2] = ls[g]
                prmR[32 * pr:32 * pr + 32, 0, ps] = a_re[g][None, :]
                prmR[32 * pr:32 * pr + 32, 1, ps] = a_im[g][None, :]
                prmR[32 * pr:32 * pr + 32, 2, ps] = ls[g]
                r0 = 32 * pr + 16 * g2
                bpad[r0:r0 + 16, 0, ps] = b_re[g].T
                bpad[r0:r0 + 16, 1, ps] = b_im[g].T
                gl = 2 * pr + g2
                ct[ps, pr, 0, 16 * gl:16 * gl + 16] = c_re[g].T
                ct[ps, pr, 1, 16 * gl:16 * gl + 16] = c_im[g].T
        gq = np.empty((3, 64, SEQ), rows.dtype); gk = np.empty((3, 64, SEQ), rows.dtype)
        gla = np.empty((3, 64, SEQ), np.float32); gv = np.empty((SEQ, 192), rows.dtype)
        for i3 in range(3):
            u = 3 * c4 + i3
            hd, half = u // 2, u % 2
            gq[i3] = rows[:, 256 + hd * 64:256 + (hd + 1) * 64].T
            gk[i3] = rows[:, 640 + hd * 64:640 + (hd + 1) * 64].T
            gla[i3] = lar[:, hd * 64:(hd + 1) * 64].T
            gv[:, i3 * 64:(i3 + 1) * 64] = rows[:, 1024 + hd * 128 + half * 64:1024 + hd * 128 + half * 64 + 64]
        maps.append({"uT": np.ascontiguousarray(rows[:, 64 * c4:64 * c4 + 64].T), "prmC": prmC, "prmR": prmR, "bpad": bpad, "ct": ct,
                     "dsk": np.ascontiguousarray(inp['s5_d'][0][64 * c4:64 * c4 + 64, None]), "iota": iota, "gm": gm, "idn": idn,
                     "gq": gq, "gk": gk, "gla": gla, "gv": gv})
    return maps


def assemble_mix1(results):
    yT = np.empty((256, 2 * SEQ), np.float32)
    oT = np.empty((768, 2 * SEQ), results[0]["go"].dtype)
    for c in range(NCORE):
        b, c4 = c // 4, c % 4
        yT[64 * c4:64 * c4 + 64, b * SEQ:(b + 1) * SEQ] = results[c]["yT"]
        go = results[c]["go"]
        for i3 in range(3):
            u = 3 * c4 + i3
            hd, half = u // 2, u % 2
            ch0 = hd * 128 + half * 64
            oT[ch0:ch0 + 64, b * SEQ:(b + 1) * SEQ] = go[:, i3 * 64:(i3 + 1) * 64].T
    return yT, oT


_CACHE = {}


def _prog(key, fn):
    if key not in _CACHE:
        _CACHE[key] = fn()
    return _CACHE[key]


def _run(c, maps):
    res = run_bass_kernel_spmd(c.nc, maps, core_ids=list(range(NCORE)))
    return res.results


def kernel(**inp):
    inp = {k: np.asarray(v) for k, v in inp.items()}
    x = np.ascontiguousarray(inp['x'].reshape(-1, D).astype(np.float32, copy=False))
    r = _run(_prog("pin0", lambda: build_pin(0)), pin0_maps(inp, x))
    pre0 = np.concatenate([q["out"] for q in r], 0)
    r = _run(_prog("mix0", lambda: build_mix0(32)), mix0_maps(inp, pre0))
    oT0 = assemble_oT0(np.stack([q["o"] for q in r], 0))
    r = _run(_prog("pout0", lambda: build_pout(0)), pout_maps(inp, 0, x, oT0))
    x1 = np.concatenate([q["xo"] for q in r], 0)
    r = _run(_prog("pin1", lambda: build_pin(1)), pin1_maps(inp, x1))
    pre1 = np.concatenate([q["out"] for q in r], 0)
    la1 = np.concatenate([q["la"] for q in r], 0)
    r = _run(_prog("mix1", lambda: build_mix1(32)), mix1_maps(inp, pre1, la1))
    yT, oT1 = assemble_mix1(r)
    rT = np.ascontiguousarray(pre1[:, 1792:2560].T)
    r = _run(_prog("pout1", lambda: build_pout(1)), pout_maps(inp, 1, x1, oT1, yT, rT))
    x2 = np.concatenate([q["xo"] for q in r], 0)
    return x2.reshape(inp['x'].shape).astype(np.float32, copy=False)
```

```python
import numpy as np
from contextlib import ExitStack
import concourse.bass as bass
import concourse.mybir as mybir
from concourse.bass_utils import run_bass_kernel_spmd

F32 = mybir.dt.float32
BF16 = mybir.dt.bfloat16
ALU = mybir.AluOpType
AF = mybir.ActivationFunctionType
AX = mybir.AxisListType

EPOCH = 16000
NDMA = 8


class Buf:
    __slots__ = ("name", "w", "r")

    def __init__(self, name=""):
        self.name = name
        self.w = None
        self.r = {}


class Prog:
    CE = ("tensor", "vector", "scalar", "gpsimd")
    DQ = ("sync", "gpsimd")

    def __init__(self, nc, es, nepoch=4):
        self.nc = nc
        self.es = es
        self.q = {e: [] for e in ("tensor", "vector", "scalar", "gpsimd", "sync")}
        self.cnt = {e: 0 for e in self.CE}
        self.vc = {e: {} for e in self.q}
        self.tokvc = {}
        self.tokeng = {}
        self.sems = {}
        for e in self.CE:
            for k in range(nepoch):
                self.sems[(e, k)] = es.enter_context(nc.semaphore(f"s_{e}_{k}"))
        self.dsem = {}
        self.dcnt = {}
        for qn in self.DQ:
            for k in range(NDMA):
                self.dsem[(qn, k)] = es.enter_context(nc.semaphore(f"d_{qn}_{k}"))
            self.dcnt[qn] = 0
        self.nbuf = 0

    def buf(self, name=""):
        self.nbuf += 1
        return Buf(name or f"b{self.nbuf}")

    def _need(self, eng, tok, waits):
        if tok is None:
            return
        sem, val = tok
        if self.vc[eng].get(sem, 0) >= val:
            return
        waits.append(tok)
        vc = self.vc[eng]
        for s, v in self.tokvc[tok].items():
            if vc.get(s, 0) < v:
                vc[s] = v

    def _deps(self, eng, reads, writes):
        waits = []
        for b in reads:
            if b.w is not None:
                if not (eng == "tensor" and self.tokeng.get(b.w) == "tensor"):
                    self._need(eng, b.w, waits)
        for b in writes:
            if b.w is not None and self.tokeng.get(b.w) != eng:
                self._need(eng, b.w, waits)
            for e2, t2 in b.r.items():
                if e2 != eng:
                    self._need(eng, t2, waits)
        return waits

    def _commit(self, eng, tok, reads, writes):
        vc = dict(self.vc[eng])
        vc[tok[0]] = max(vc.get(tok[0], 0), tok[1])
        self.tokvc[tok] = vc
        self.tokeng[tok] = eng
        for b in reads:
            b.r[eng] = tok
        for b in writes:
            b.w = tok
            b.r = {}

    def op(self, eng, fn, reads=(), writes=()):
        waits = self._deps(eng, reads, writes)
        self.cnt[eng] += 1
        c = self.cnt[eng] - 1
        sem = self.sems[(eng, c // EPOCH)]
        tok = (sem, c % EPOCH + 1)
        self.q[eng].append((waits, fn, (sem, 1)))
        self._commit(eng, tok, reads, writes)
        return tok

    def dma(self, qn, out, in_, reads=(), writes=(), **kw):
        eng = qn
        waits = self._deps(eng, reads, writes)
        n = self.dcnt[qn]
        self.dcnt[qn] += 1
        sem = self.dsem[(qn, n % NDMA)]
        k = n // NDMA
        if k > 0:
            prev = (sem, 16 * k)
            if self.vc[eng].get(sem, 0) < 16 * k:
                waits.append(prev)
                self.vc[eng][sem] = 16 * k
        tok = (sem, 16 * (k + 1))
        self.q[eng].append((waits, lambda e: e.dma_start(out=out, in_=in_, **kw), (sem, 16)))
        vc = dict(self.vc[eng])
        vc[sem] = 16 * (k + 1)
        self.tokvc[tok] = vc
        self.tokeng[tok] = "dma_" + qn
        for b in reads:
            b.r["dma_" + qn + str(n % NDMA)] = tok
        for b in writes:
            b.w = tok
            b.r = {}
        return tok

    def finish_all(self):
        waits = []
        for e in self.CE:
            cnt = self.cnt[e]
            if cnt:
                cc = cnt - 1
                self._need("sync", (self.sems[(e, cc // EPOCH)], cc % EPOCH + 1), waits)
        for qn in self.DQ:
            n = self.dcnt[qn]
            for j in range(max(0, n - NDMA), n):
                tok = (self.dsem[(qn, j % NDMA)], 16 * (j // NDMA + 1))
                self._need("sync", tok, waits)
        self.q["sync"].append((waits, None, None))

    def finish(self, bufs):
        waits = []
        for b in bufs:
            if b.w is not None:
                self._need("sync", b.w, waits)
        self.q["sync"].append((waits, None, None))

    def emit(self):
        nc = self.nc
        q = self.q

        def replay(e, name):
            for waits, fn, inc in q[name]:
                for sem, val in waits:
                    e.wait_ge(sem, val)
                if fn is not None:
                    ins = fn(e)
                    ins.then_inc(inc[0], inc[1])

        with nc.Block() as block:
            @block.tensor
            def _(e):
                replay(e, "tensor")

            @block.vector
            def _(e):
                replay(e, "vector")

            @block.scalar
            def _(e):
                replay(e, "scalar")

            @block.gpsimd
            def _(e):
                replay(e, "gpsimd")

            @block.sync
            def _(e):
                replay(e, "sync")


NCORE = 8
TOK = 4096
D = 1024
SEQ = 16384
EPS = 1e-6


def _mk(nc_name="TRN2"):
    return bass.Bass(nc_name, target_bir_lowering=False)


class Ctx:
    def __init__(self):
        self.nc = _mk()
        self.es = ExitStack()
        self.p = None
        self.nps = 0

    def start(self):
        self.p = Prog(self.nc, self.es)
        return self.p

    def din(self, name, shape, dt=F32):
        return self.nc.dram_tensor(name, list(shape), dt, kind="ExternalInput").ap()

    def dout(self, name, shape, dt=F32):
        return self.nc.dram_tensor(name, list(shape), dt, kind="ExternalOutput").ap()

    def sb(self, name, shape, dt=F32):
        return self.es.enter_context(self.nc.sbuf_tensor("sb_" + name, list(shape), dt))

    def ps(self, name, shape, dt=F32):
        return self.es.enter_context(self.nc.psum_tensor("ps_" + name, list(shape), dt))


class Rot:
    def __init__(self, items):
        self.items = items
        self.i = 0

    def next(self):
        it = self.items[self.i % len(self.items)]
        self.i += 1
        return it


def load_weight_bf16(c, p, wd, wb, wbuf, nk, ncol, gpk=None, bgpk=None, stage=None, colchunk=None):
    cc = colchunk or ncol
    engs = ("vector", "gpsimd")
    n = 0
    for k in range(nk):
        for c0 in range(0, ncol, cc):
            c1 = min(ncol, c0 + cc)
            st, bst = stage.next()
            p.dma("sync", st[:, 0:c1 - c0], wd[k * 128:(k + 1) * 128, c0:c1], writes=[bst])
            eng = engs[n % 2]
            n += 1
            if gpk is not None:
                p.op(eng, lambda e, st=st, k=k, c0=c0, c1=c1: e.tensor_scalar(
                    out=wb[:, k, c0:c1], in0=st[:, 0:c1 - c0], scalar1=gpk[:, k:k + 1], scalar2=None, op0=ALU.mult),
                    reads=[bst, bgpk], writes=[wbuf])
            else:
                p.op(eng, lambda e, st=st, k=k, c0=c0, c1=c1: e.tensor_copy(out=wb[:, k, c0:c1], in_=st[:, 0:c1 - c0]),
                     reads=[bst], writes=[wbuf])


def build_pin(layer):
    c = Ctx()
    nc = c.nc
    NCOL = 3072 if layer == 0 else 2944
    NOUT = 3584 if layer == 0 else 2560
    x = c.din("x", [TOK, D])
    xT = c.din("xT", [D, TOK])
    w = c.din("w", [D, NCOL if layer == 0 else 2560])
    gpk_d = c.din("gpk", [128, 8])
    out = c.dout("out", [TOK, NOUT], BF16)
    if layer == 0:
        gq_d = c.din("gqk", [128, 1024])
        cs_d = c.din("cs", [128, 32, 512])
        t12_d = c.din("t12", [128, 1024])
    else:
        walrT_d = c.din("walrT", [16, D])
        wa2_d = c.din("wa2", [16, 384])
        ba2_d = c.din("ba2", [128, 384])
        la_out = c.dout("la", [TOK, 384], F32)
    p = c.start()
    sb = c.sb
    wb = sb("wb", [128, 8, NCOL], BF16); bwb = p.buf()
    gpk = sb("gpk", [128, 8]); bgpk = p.buf()
    p.dma("gpsimd", gpk[:], gpk_d[:, :], writes=[bgpk])
    stage = Rot([(sb(f"st{i}", [128, 1024]), p.buf()) for i in range(3)])
    if layer == 0:
        gq = sb("gq", [128, 1024]); bgq = p.buf()
        t12 = sb("t12", [128, 1024]); bt12 = p.buf()
        p.dma("gpsimd", gq[:], gq_d[:, :], writes=[bgq])
        p.dma("gpsimd", t12[:], t12_d[:, :], writes=[bt12])
        load_weight_bf16(c, p, w, wb, bwb, 8, NCOL, gpk, bgpk, stage, 1024)
    else:
        ba2 = sb("ba2", [128, 384]); bba2 = p.buf()
        p.dma("gpsimd", ba2[:], ba2_d[:, :], writes=[bba2])
        walrT = sb("walrT", [16, D]); bwalr = p.buf()
        wa2 = sb("wa2", [16, 384]); bwa2 = p.buf()
        p.dma("gpsimd", walrT[:], walrT_d[:, :], writes=[bwalr])
        p.dma("gpsimd", wa2[:], wa2_d[:, :], writes=[bwa2])
        load_weight_bf16(c, p, w, wb[:, :, 0:2560], bwb, 8, 2560, gpk, bgpk, stage, 1024)
        pw = c.ps("pw", [128, 512]); bpw = p.buf()
        for k in range(8):
            p.op("tensor", lambda e, k=k: e.matmul(pw[:, 0:384], lhsT=walrT[:, k * 128:(k + 1) * 128], rhs=wa2[:, :],
                                                   start=True, stop=True), reads=[bwalr, bwa2], writes=[bpw])
            p.op("vector", lambda e, k=k: e.tensor_scalar(out=wb[:, k, 2560:2944], in0=pw[:, 0:384], scalar1=gpk[:, k:k + 1],
                                                          scalar2=None, op0=ALU.mult), reads=[bpw, bgpk], writes=[bwb])
    nbank = (NCOL + 511) // 512
    banks = Rot([(c.ps(f"pb{i}", [128, 512]), p.buf()) for i in range(7 if layer == 1 else 8)])
    GT = 512
    xin = Rot([(sb(f"xin{i}", [128, 4, D]), p.buf()) for i in range(2)])
    xTin = Rot([(sb(f"xTin{i}", [128, 8, GT]), p.buf()) for i in range(2)])
    xTb = Rot([(sb(f"xTb{i}", [128, 8, GT], BF16), p.buf()) for i in range(2)])
    junk = sb("junk", [128, D]); bjunk = p.buf()
    ss = Rot([(sb(f"ss{i}", [128, 1]), p.buf()) for i in range(4)])
    hq = Rot([(sb(f"hq{i}", [128, 1024]), p.buf()) for i in range(2)])
    sq = sb("sq", [128, 1024]); bsq = p.buf()
    qs = Rot([(sb(f"qs{i}", [128, 16]), p.buf()) for i in range(2)])
    ob = Rot([(sb(f"ob{i}", [128, NOUT], BF16), p.buf()) for i in range(2)])
    if layer == 0:
        cs = Rot([(sb(f"cs{i}", [128, 4, 512]), p.buf()) for i in range(2)])
        rq = Rot([(sb(f"rq{i}", [128, 512]), p.buf()) for i in range(2)])
        rt = [(sb(f"rt{i}", [128, 256]), p.buf()) for i in range(4)]
        ro = Rot([(sb(f"ro{i}", [128, 512]), p.buf()) for i in range(2)])
    else:
        la = Rot([(sb(f"la{i}", [128, 384]), p.buf()) for i in range(2)])
        le = Rot([(sb(f"le{i}", [128, 384]), p.buf()) for i in range(2)])
    xTv = xT.rearrange("(k p) t -> p k t", p=128)
    xv = x.rearrange("(n p) d -> p n d", p=128)
    for g in range(TOK // GT):
        xi, bxi = xin.next()
        xti, bxti = xTin.next()
        xtb, bxtb = xTb.next()
        p.dma("sync", xi[:], xv[:, g * 4:(g + 1) * 4, :], writes=[bxi])
        p.dma("gpsimd", xti[:], xTv[:, :, g * GT:(g + 1) * GT], writes=[bxti])
        p.op("gpsimd", lambda e, xtb=xtb, xti=xti: e.tensor_copy(out=xtb[:, 0:4, :], in_=xti[:, 0:4, :]), reads=[bxti], writes=[bxtb])
        p.op("vector", lambda e, xtb=xtb, xti=xti: e.tensor_copy(out=xtb[:, 4:8, :], in_=xti[:, 4:8, :]), reads=[bxti], writes=[bxtb])
        if layer == 0:
            csg, bcsg = cs.next()
            p.dma("gpsimd", csg[:], cs_d[:, g * 4:(g + 1) * 4, :], writes=[bcsg])
        for j in range(4):
            tt = g * 4 + j
            s1, bs1 = ss.next()
            p.op("scalar", lambda e, xi=xi, j=j, s1=s1: e.activation(out=junk[:], in_=xi[:, j, :], func=AF.Square, accum_out=s1[:]),
                 reads=[bxi], writes=[bjunk, bs1])
            p.op("scalar", lambda e, s1=s1: e.activation(out=s1[:], in_=s1[:], func=AF.Sqrt, scale=1.0 / D, bias=EPS), reads=[bs1], writes=[bs1])
            p.op("vector", lambda e, s1=s1: e.reciprocal(out=s1[:], in_=s1[:]), reads=[bs1], writes=[bs1])
            pbs = []
            for b in range(nbank):
                pb, bpb = banks.next()
                c0, c1 = b * 512, min(NCOL, (b + 1) * 512)
                for k in range(8):
                    p.op("tensor", lambda e, pb=pb, xtb=xtb, k=k, j=j, c0=c0, c1=c1: e.matmul(
                        pb[:, 0:c1 - c0], lhsT=xtb[:, k, j * 128:(j + 1) * 128], rhs=wb[:, k, c0:c1], start=(k == 0), stop=(k == 7)),
                        reads=[bxtb, bwb], writes=[bpb])
                pbs.append((pb, bpb))
            o, bo = ob.next()
            if layer == 0:
                h, bh = hq.next()
                for b in range(2):
                    p.op("scalar", lambda e, b=b, h=h, s1=s1, pb=pbs[b][0]: e.activation(out=h[:, b * 512:(b + 1) * 512], in_=pb[:, :], func=AF.Copy, scale=s1[:, 0:1]),
                         reads=[pbs[b][1], bs1], writes=[bh])
                p.op("vector", lambda e, o=o, s1=s1, pb=pbs[2][0]: e.tensor_scalar(out=o[:, 1024:1536], in0=pb[:, :], scalar1=s1[:, 0:1], scalar2=None, op0=ALU.mult),
                     reads=[pbs[2][1], bs1], writes=[bo])
                p.op("vector", lambda e, o=o, s1=s1, pb=pbs[4][0]: e.tensor_scalar(out=o[:, 2560:3072], in0=pb[:, :], scalar1=s1[:, 0:1], scalar2=None, op0=ALU.mult),
                     reads=[pbs[4][1], bs1], writes=[bo])
                p.op("scalar", lambda e, o=o, s1=s1, pb=pbs[5][0]: e.activation(out=o[:, 3072:3584], in_=pb[:, :], func=AF.Silu, scale=s1[:, 0:1]),
                     reads=[pbs[5][1], bs1], writes=[bo])
                r, br = rq.next()
                p.op("scalar", lambda e, r=r, s1=s1, pb=pbs[3][0]: e.activation(out=r[:], in_=pb[:, :], func=AF.Copy, scale=s1[:, 0:1]),
                     reads=[pbs[3][1], bs1], writes=[br])
                q1, bq1 = qs.next()
                h3 = h[:].rearrange("p (g d) -> p g d", d=64)
                p.op("gpsimd", lambda e, h=h: e.tensor_tensor(out=sq[:], in0=h[:], in1=h[:], op=ALU.mult), reads=[bh], writes=[bsq])
                p.op("vector", lambda e, q1=q1: e.tensor_reduce(out=q1[:], in_=sq[:].rearrange("p (g d) -> p g d", d=64), axis=AX.X, op=ALU.add),
                     reads=[bsq], writes=[bq1])
                p.op("scalar", lambda e, q1=q1: e.activation(out=q1[:], in_=q1[:], func=AF.Sqrt, scale=1.0 / 64, bias=EPS), reads=[bq1], writes=[bq1])
                p.op("vector", lambda e, q1=q1: e.reciprocal(out=q1[:], in_=q1[:]), reads=[bq1], writes=[bq1])
                p.op("vector", lambda e, h3=h3, q1=q1: e.tensor_tensor(out=h3, in0=h3, in1=q1[:].unsqueeze(2).to_broadcast([128, 16, 64]), op=ALU.mult),
                     reads=[bh, bq1], writes=[bh])
                p.op("gpsimd", lambda e, h=h, o=o: e.tensor_tensor(out=o[:, 0:1024], in0=h[:], in1=gq[:], op=ALU.mult), reads=[bh, bgq], writes=[bo])
                r4 = r[:].rearrange("p (a two d) -> p a two d", two=2, d=32)
                x1, x2 = r4[:, :, 0, :], r4[:, :, 1, :]
                cosv = csg[:, j, 0:256].rearrange("p (a d) -> p a d", d=32)
                sinv = csg[:, j, 256:512].rearrange("p (a d) -> p a d", d=32)
                (ta, bta), (tb_, btb), (tc, btc), (td, btd) = rt
                v3 = lambda t: t[:].rearrange("p (a d) -> p a d", d=32)
                rr, brr = ro.next()
                rr4 = rr[:].rearrange("p (a two d) -> p a two d", two=2, d=32)
                p.op("vector", lambda e, x1=x1, cosv=cosv: e.tensor_tensor(out=v3(ta), in0=x1, in1=cosv, op=ALU.mult), reads=[br, bcsg], writes=[bta])
                p.op("gpsimd", lambda e, x2=x2, sinv=sinv: e.tensor_tensor(out=v3(tb_), in0=x2, in1=sinv, op=ALU.mult), reads=[br, bcsg], writes=[btb])
                p.op("vector", lambda e, x1=x1, sinv=sinv: e.tensor_tensor(out=v3(tc), in0=x1, in1=sinv, op=ALU.mult), reads=[br, bcsg], writes=[btc])
                p.op("gpsimd", lambda e, x2=x2, cosv=cosv: e.tensor_tensor(out=v3(td), in0=x2, in1=cosv, op=ALU.mult), reads=[br, bcsg], writes=[btd])
                p.op("vector", lambda e, rr4=rr4: e.tensor_tensor(out=rr4[:, :, 0, :], in0=v3(ta), in1=v3(tb_), op=ALU.subtract), reads=[bta, btb], writes=[brr])
                p.op("gpsimd", lambda e, rr4=rr4: e.tensor_tensor(out=rr4[:, :, 1, :], in0=v3(tc), in1=v3(td), op=ALU.add), reads=[btc, btd], writes=[brr])
                p.op("vector", lambda e, rr=rr, o=o: e.tensor_tensor(out=o[:, 1536:2048], in0=rr[:], in1=t12[:, 0:512], op=ALU.mult), reads=[brr, bt12], writes=[bo])
                p.op("gpsimd", lambda e, rr=rr, o=o: e.tensor_tensor(out=o[:, 2048:2560], in0=rr[:], in1=t12[:, 512:1024], op=ALU.mult), reads=[brr, bt12], writes=[bo])
            else:
                for b in range(4):
                    eng = ("vector", "scalar")[b % 2]
                    cw = 256 if b == 3 else 512
                    if eng == "vector":
                        p.op("vector", lambda e, o=o, s1=s1, b=b, cw=cw, pb=pbs[b][0]: e.tensor_scalar(out=o[:, b * 512:b * 512 + cw], in0=pb[:, 0:cw], scalar1=s1[:, 0:1], scalar2=None, op0=ALU.mult),
                             reads=[pbs[b][1], bs1], writes=[bo])
                    else:
                        p.op("scalar", lambda e, o=o, s1=s1, b=b, cw=cw, pb=pbs[b][0]: e.activation(out=o[:, b * 512:b * 512 + cw], in_=pb[:, 0:cw], func=AF.Copy, scale=s1[:, 0:1]),
                             reads=[pbs[b][1], bs1], writes=[bo])
                p.op("gpsimd", lambda e, o=o: e.tensor_scalar(out=o[:, 256:640], in0=o[:, 256:640], scalar1=0.125, scalar2=None, op0=ALU.mult), reads=[bo], writes=[bo])
                p.op("scalar", lambda e, o=o, s1=s1, pb=pbs[3][0]: e.activation(out=o[:, 1792:2048], in_=pb[:, 256:512], func=AF.Silu, scale=s1[:, 0:1]),
                     reads=[pbs[3][1], bs1], writes=[bo])
                p.op("scalar", lambda e, o=o, s1=s1, pb=pbs[4][0]: e.activation(out=o[:, 2048:2560], in_=pb[:, :], func=AF.Silu, scale=s1[:, 0:1]),
                     reads=[pbs[4][1], bs1], writes=[bo])
                l1, bl1 = la.next()
                l2, bl2 = le.next()
                p.op("vector", lambda e, l1=l1, s1=s1, pb=pbs[5][0]: e.scalar_tensor_tensor(out=l1[:], in0=pb[:, 0:384], scalar=s1[:, 0:1], in1=ba2[:], op0=ALU.mult, op1=ALU.add),
                     reads=[pbs[5][1], bs1, bba2], writes=[bl1])
                p.op("scalar", lambda e, l1=l1: e.activation(out=l1[:], in_=l1[:], func=AF.Exp, scale=-1.0), reads=[bl1], writes=[bl1])
                p.op("scalar", lambda e, l1=l1: e.activation(out=l1[:], in_=l1[:], func=AF.Ln, bias=1.0), reads=[bl1], writes=[bl1])
                p.op("gpsimd", lambda e, l1=l1, l2=l2: e.tensor_scalar(out=l2[:], in0=l1[:], scalar1=-1.0 / 16, scalar2=None, op0=ALU.mult), reads=[bl1], writes=[bl2])
                p.dma("gpsimd", la_out[tt * 128:(tt + 1) * 128, :], l2[:], reads=[bl2], writes=[p.buf()])
            p.dma("sync", out[tt * 128:(tt + 1) * 128, :], o[:], reads=[bo], writes=[p.buf()])
    p.finish_all()
    p.emit()
    return c


def _rot_tables():
    half = 32
    inv_freq = (1.0 / (10000.0 ** np.linspace(0.0, 1.0, half, dtype=np.float32))).astype(np.float32)
    ang = (np.arange(SEQ, dtype=np.float32)[:, None] * inv_freq[None, :]).astype(np.float32)
    return np.cos(ang).astype(np.float32), np.sin(ang).astype(np.float32)


def _ret_log_gamma():
    return np.log(1.0 - 2.0 ** (-5.0 - np.arange(4, dtype=np.float32))).astype(np.float32)


def _gpk(g):
    return np.ascontiguousarray(g.reshape(8, 128).T)


def pin0_maps(inp, x):
    cos, sin = _rot_tables()
    lg = _ret_log_gamma()
    i = np.arange(128, dtype=np.float32)
    qw = np.exp((i + 1.0)[:, None] * lg[None, :])
    kw = np.exp((127.0 - i)[:, None] * lg[None, :]) * 0.125
    t12 = np.empty((128, 1024), np.float32)
    t12[:, 0:256] = 1.0
    t12[:, 256:512] = 0.125
    t12[:, 512:768] = np.repeat(qw, 64, axis=1)
    t12[:, 768:1024] = np.repeat(kw, 64, axis=1)
    gqk = np.concatenate([np.tile(inp['da_q_norm'][0], 8), np.tile(inp['da_k_norm'][0], 8)])
    gqk = np.ascontiguousarray(np.broadcast_to(gqk[None, :], (128, 1024)))
    gpk = _gpk(inp['norm_mix_g'][0])
    w = np.ascontiguousarray(inp['ab_w_in'][0])
    maps = []
    for c in range(NCORE):
        xs = x[c * TOK:(c + 1) * TOK]
        pos0 = (c % 4) * TOK
        cc = cos[pos0:pos0 + TOK].reshape(32, 128, 32).transpose(1, 0, 2)
        sn = sin[pos0:pos0 + TOK].reshape(32, 128, 32).transpose(1, 0, 2)
        cs = np.concatenate([np.tile(cc, (1, 1, 8)), np.tile(sn, (1, 1, 8))], axis=2)
        maps.append({"x": np.ascontiguousarray(xs), "xT": np.ascontiguousarray(xs.T), "w": w, "gpk": gpk,
                     "gqk": gqk, "cs": np.ascontiguousarray(cs), "t12": t12})
    return maps


NT = SEQ // 128


def build_mix0(nqg=32):
    c = Ctx()
    nc = c.nc
    qaT_d = c.din("qaT", [128, SEQ], BF16)
    kaT_d = c.din("kaT", [128, SEQ], BF16)
    va_d = c.din("va", [SEQ, 128], BF16)
    qrT_d = c.din("qrT", [64, SEQ], BF16)
    krT_d = c.din("krT", [64, SEQ], BF16)
    qwrT_d = c.din("qwrT", [64, SEQ], BF16)
    kwr_d = c.din("kwr", [SEQ, 64], BF16)
    vr_d = c.din("vr", [SEQ, 128], BF16)
    gr_d = c.din("gr", [SEQ, 128], BF16)
    cst_d = c.din("cst", [128, 512])
    lam_d = c.din("lamv", [128, 256])
    out = c.dout("o", [SEQ, 256], BF16)
    p = c.start()
    sb = c.sb
    kaT = sb("kaT", [128, SEQ], BF16); bkaT = [p.buf() for _ in range(8)]
    va = sb("va", [128, NT, 132], BF16); bva = [p.buf() for _ in range(8)]
    cst = sb("cst", [128, 512]); bcst = p.buf()
    lamv = sb("lamv", [128, 256]); blam = p.buf()
    p.dma("gpsimd", cst[:], cst_d[:, :], writes=[bcst])
    p.dma("gpsimd", lamv[:], lam_d[:, :], writes=[blam])
    vav = va_d.rearrange("(n p) e -> p n e", p=128)
    for i in range(8):
        p.dma("sync", kaT[:, i * 2048:(i + 1) * 2048], kaT_d[:, i * 2048:(i + 1) * 2048], writes=[bkaT[i]])
        p.dma("sync", va[:, i * 16:(i + 1) * 16, 0:128], vav[:, i * 16:(i + 1) * 16, :], writes=[bva[i]])
        p.op("gpsimd", lambda e, i=i: e.memset(va[:, i * 16:(i + 1) * 16, 128:129], 1.0), writes=[bva[i]])
    lt = sb("lt", [128, 128]); blt = p.buf()
    l2 = sb("l2", [128, 2]); bl2 = p.buf()
    lam = sb("lam", [128, 1]); blamS = p.buf()
    lv = lamv[:].rearrange("p (a two d) -> p a two d", two=2, d=64)
    p.op("vector", lambda e: e.tensor_tensor(out=lt[:].rearrange("p (a d) -> p a d", d=64), in0=lv[:, :, 0, :], in1=lv[:, :, 1, :], op=ALU.mult),
         reads=[blam], writes=[blt])
    p.op("vector", lambda e: e.tensor_reduce(out=l2[:], in_=lt[:].rearrange("p (a d) -> p a d", d=64), axis=AX.X, op=ALU.add), reads=[blt], writes=[bl2])
    p.op("scalar", lambda e: e.activation(out=l2[:], in_=l2[:], func=AF.Exp), reads=[bl2], writes=[bl2])
    p.op("vector", lambda e: e.tensor_tensor(out=lam[:], in0=l2[:, 0:1], in1=l2[:, 1:2], op=ALU.subtract), reads=[bl2], writes=[blamS])
    p.op("vector", lambda e: e.tensor_scalar(out=lam[:], in0=lam[:], scalar1=0.2, scalar2=None, op0=ALU.add), reads=[blamS], writes=[blamS])

    psS = Rot([(c.ps(f"pS{i}", [128, 512]), p.buf()) for i in range(4)])
    accb = [c.ps(f"pA{i}", [128, 512]) for i in range(3)]
    accbuf = [p.buf() for _ in range(3)]
    acc = []
    for i in range(8):
        bk, off = i // 3, (i % 3) * 136
        acc.append((accb[bk][:, off:off + 129], accbuf[bk]))
    pR = c.ps("pR", [128, 512])
    bpRs = bpRo = bpRkv = p.buf()
    qin = Rot([(sb(f"qin{i}", [128, 512], BF16), p.buf()) for i in range(2)])
    pT = Rot([(sb(f"pT{i}", [128, 512], BF16), p.buf()) for i in range(4)])
    fin = {k: Rot([(sb(f"{k}{i}", shp), p.buf()) for i in range(2)]) for k, shp in
           (("fr", [128, 4]), ("ft", [128, 128]), ("fo", [128, 128]), ("fs", [128, 1]))}
    junk = sb("junk", [128, 128]); bjunk = p.buf()
    obuf = Rot([(sb(f"ob{i}", [128, 256], BF16), p.buf()) for i in range(8)])
    CH = 2048
    rq = Rot([(sb(f"rq{i}", [64, CH], BF16), p.buf()) for i in range(2)])
    rk = Rot([(sb(f"rk{i}", [64, CH], BF16), p.buf()) for i in range(2)])
    rqw = Rot([(sb(f"rqw{i}", [64, CH], BF16), p.buf()) for i in range(2)])
    rkw = Rot([(sb(f"rkw{i}", [128, 16, 64], BF16), p.buf()) for i in range(2)])
    rv = Rot([(sb(f"rv{i}", [128, 16, 128], BF16), p.buf()) for i in range(2)])
    rg = Rot([(sb(f"rg{i}", [128, 16, 128], BF16), p.buf()) for i in range(2)])
    Sf = sb("Sf", [64, 128]); bSf = p.buf()
    Sb = Rot([(sb(f"Sb{i}", [64, 128], BF16), p.buf()) for i in range(2)])
    sm = Rot([(sb(f"sm{i}", [128, 128], BF16), p.buf()) for i in range(2)])
    gt = Rot([(sb(f"gt{i}", [128, 128]), p.buf()) for i in range(2)])
    rs_ = Rot([(sb(f"rs{i}", [128, 1]), p.buf()) for i in range(2)])
    p.op("vector", lambda e: e.memset(Sf[:], 0.0), writes=[bSf])
    sb0, bsb0 = Sb.next()
    p.op("vector", lambda e: e.memset(sb0[:], 0.0), writes=[bsb0])
    ret_state = {"cur": None, "sb": (sb0, bsb0)}
    otiles = {}

    def get_ot(t):
        if t not in otiles:
            otiles[t] = obuf.next() + ([0],)
        return otiles[t]

    def done_half(t):
        o, bo, cnt = otiles[t]
        cnt[0] += 1
        if cnt[0] == 2:
            p.dma("gpsimd", out[t * 128:(t + 1) * 128, :], o[:], reads=[bo], writes=[p.buf()])
            del otiles[t]

    def ret_tile(t):
        ci, ti = t // 16, t % 16
        if ti == 0:
            cur = (rq.next(), rk.next(), rqw.next(), rkw.next(), rv.next(), rg.next())
            sl = slice(ci * CH, (ci + 1) * CH)
            p.dma("sync", cur[0][0][:], qrT_d[:, sl], writes=[cur[0][1]])
            p.dma("sync", cur[1][0][:], krT_d[:, sl], writes=[cur[1][1]])
            p.dma("sync", cur[2][0][:], qwrT_d[:, sl], writes=[cur[2][1]])
            p.dma("sync", cur[3][0][:], kwr_d.rearrange("(n p) e -> p n e", p=128)[:, ci * 16:(ci + 1) * 16, :], writes=[cur[3][1]])
            p.dma("sync", cur[4][0][:], vr_d.rearrange("(n p) e -> p n e", p=128)[:, ci * 16:(ci + 1) * 16, :], writes=[cur[4][1]])
            p.dma("sync", cur[5][0][:], gr_d.rearrange("(n p) e -> p n e", p=128)[:, ci * 16:(ci + 1) * 16, :], writes=[cur[5][1]])
            ret_state["cur"] = cur
        (q_, bq_), (k_, bk_), (qw_, bqw_), (kw_, bkw_), (v_, bv_), (g_, bg_) = ret_state["cur"]
        ts = slice(ti * 128, (ti + 1) * 128)
        Sbc, bSbc = ret_state["sb"]
        p.op("tensor", lambda e: e.matmul(pR[:, 0:128], lhsT=k_[:, ts], rhs=q_[:, ts], start=True, stop=True), reads=[bk_, bq_], writes=[bpRs])
        s_, bs_ = sm.next()
        p.op("vector", lambda e: e.tensor_tensor(out=s_[:], in0=pR[:, 0:128], in1=cst[:, 0:128], op=ALU.mult), reads=[bpRs, bcst], writes=[bs_])
        p.op("tensor", lambda e: e.matmul(pR[:, 128:256], lhsT=s_[:], rhs=v_[:, ti, :], start=True, stop=False), reads=[bs_, bv_], writes=[bpRo])
        p.op("tensor", lambda e: e.matmul(pR[:, 128:256], lhsT=qw_[:, ts], rhs=Sbc[:], start=False, stop=True), reads=[bqw_, bSbc], writes=[bpRo])
        p.op("tensor", lambda e: e.matmul(pR[0:64, 256:384], lhsT=kw_[:, ti, :], rhs=v_[:, ti, :], start=True, stop=True), reads=[bkw_, bv_], writes=[bpRkv])
        p.op("vector", lambda e: e.scalar_tensor_tensor(out=Sf[:], in0=Sf[:], scalar=cst[0:64, 384:385], in1=pR[0:64, 256:384], op0=ALU.mult, op1=ALU.add),
             reads=[bSf, bpRkv, bcst], writes=[bSf])
        Sbn, bSbn = Sb.next()
        p.op("scalar", lambda e: e.activation(out=Sbn[:], in_=Sf[:], func=AF.Copy), reads=[bSf], writes=[bSbn])
        ret_state["sb"] = (Sbn, bSbn)
        r1, br1 = rs_.next()
        p.op("scalar", lambda e: e.activation(out=junk[:], in_=pR[:, 128:256], func=AF.Square, accum_out=r1[:]), reads=[bpRo], writes=[bjunk, br1])
        p.op("scalar", lambda e: e.activation(out=r1[:], in_=r1[:], func=AF.Sqrt, scale=1.0 / 128, bias=EPS), reads=[br1], writes=[br1])
        p.op("vector", lambda e: e.reciprocal(out=r1[:], in_=r1[:]), reads=[br1], writes=[br1])
        g1, bg1 = gt.next()
        p.op("gpsimd", lambda e: e.tensor_tensor(out=g1[:], in0=g_[:, ti, :], in1=cst[:, 256:384], op=ALU.mult), reads=[bg_, bcst], writes=[bg1])
        o, bo, _ = get_ot(t)
        p.op("vector", lambda e: e.scalar_tensor_tensor(out=o[:, 128:256], in0=pR[:, 128:256], scalar=r1[:, 0:1], in1=g1[:], op0=ALU.mult, op1=ALU.mult),
             reads=[bpRo, br1, bg1], writes=[bo])
        done_half(t)

    def da_final(qt_glob, a0, a1):
        (A0, bA0), (A1, bA1) = a0, a1
        fr, bfr = fin["fr"].next()
        ft, bft = fin["ft"].next()
        fo, bfo = fin["fo"].next()
        fs, bfs = fin["fs"].next()
        p.op("vector", lambda e: e.reciprocal(out=fr[:, 0:1], in_=A0[:, 128:129]), reads=[bA0], writes=[bfr])
        p.op("vector", lambda e: e.reciprocal(out=fr[:, 1:2], in_=A1[:, 128:129]), reads=[bA1], writes=[bfr])
        p.op("vector", lambda e: e.tensor_tensor(out=fr[:, 2:3], in0=fr[:, 1:2], in1=lam[:, 0:1], op=ALU.mult), reads=[bfr, blamS], writes=[bfr])
        p.op("vector", lambda e: e.tensor_scalar(out=ft[:], in0=A1[:, 0:128], scalar1=fr[:, 2:3], scalar2=None, op0=ALU.mult), reads=[bA1, bfr], writes=[bft])
        p.op("vector", lambda e: e.scalar_tensor_tensor(out=fo[:], in0=A0[:, 0:128], scalar=fr[:, 0:1], in1=ft[:], op0=ALU.mult, op1=ALU.subtract),
             reads=[bA0, bfr, bft], writes=[bfo])
        p.op("scalar", lambda e: e.activation(out=junk[:], in_=fo[:], func=AF.Square, accum_out=fs[:]), reads=[bfo], writes=[bjunk, bfs])
        p.op("scalar", lambda e: e.activation(out=fs[:], in_=fs[:], func=AF.Sqrt, scale=1.0 / 128, bias=EPS), reads=[bfs], writes=[bfs])
        p.op("vector", lambda e: e.reciprocal(out=fs[:], in_=fs[:]), reads=[bfs], writes=[bfs])
        o, bo, _ = get_ot(qt_glob)
        p.op("vector", lambda e: e.scalar_tensor_tensor(out=o[:, 0:128], in0=fo[:], scalar=fs[:, 0:1], in1=cst[:, 128:256], op0=ALU.mult, op1=ALU.mult),
             reads=[bfo, bfs, bcst], writes=[bo])
        done_half(qt_glob)

    steps = [(qg, kt) for qg in range(nqg) for kt in range(4 * qg + 4)]
    qcur = {}

    def stage_a(qg, kt):
        if kt == 0:
            qi, bqi = qin.next()
            p.dma("sync", qi[:], qaT_d[:, qg * 512:(qg + 1) * 512], writes=[bqi])
            qcur[qg] = (qi, bqi)
        qi, bqi = qcur[qg]
        j = kt - 4 * qg
        q0 = max(0, j)
        qs = q0 * 128
        kb = bkaT[kt // 16]
        pts = []
        for m in range(2):
            pS, bpS = psS.next()
            p.op("tensor", lambda e, pS=pS, m=m, kt=kt, qs=qs, qi=qi: e.matmul(pS[:, qs:512], lhsT=kaT[m * 64:(m + 1) * 64, kt * 128:(kt + 1) * 128],
                                                                         rhs=qi[m * 64:(m + 1) * 64, qs:512], start=True, stop=True),
                 reads=[kb, bqi], writes=[bpS])
            pt, bpt = pT.next()
            p.op("scalar", lambda e, pS=pS, pt=pt, qs=qs: e.activation(out=pt[:, qs:512], in_=pS[:, qs:512], func=AF.Exp, scale=0.125, bias=-8.0),
                 reads=[bpS], writes=[bpt])
            if j >= 0:
                p.op("gpsimd", lambda e, pt=pt, qs=qs: e.memset(pt[64:128, qs:qs + 64], 0.0), writes=[bpt])
            pts.append((pt, bpt))
        return pts

    def stage_b(qg, kt, pts):
        j = kt - 4 * qg
        q0 = max(0, j)
        for qt in range(q0, 4):
            for m in range(2):
                A, bA = acc[qt * 2 + m]
                pt, bpt = pts[m]
                p.op("tensor", lambda e, A=A, pt=pt, qt=qt, kt=kt, st=(kt == 0 and (qt * 2 + m) % 3 == 0), sp=(kt == 4 * qg + qt): e.matmul(
                    A, lhsT=pt[:, qt * 128:(qt + 1) * 128], rhs=va[:, kt, 0:129], start=st, stop=sp, skip_group_check=True),
                    reads=[bpt, bva[kt // 16]], writes=[bA])
        if kt == 4 * qg + 3:
            for qt in range(4):
                da_final(qg * 4 + qt, acc[qt * 2], acc[qt * 2 + 1])
            for t in range(qg * 4, qg * 4 + 4):
                ret_tile(t)

    cur = stage_a(*steps[0])
    for i, (qg, kt) in enumerate(steps):
        nxt = stage_a(*steps[i + 1]) if i + 1 < len(steps) else None
        stage_b(qg, kt, cur)
        cur = nxt
    p.finish_all()
    p.emit()
    return c


def mix0_maps(inp, pre):
    lg = _ret_log_gamma()
    i = np.arange(128)
    maps = []
    for c in range(NCORE):
        b, h = c // 4, c % 4
        rows = pre[b * SEQ:(b + 1) * SEQ]
        gam = np.exp(lg[h]).astype(np.float32)
        dist = np.abs(i[:, None] - i[None, :]).astype(np.float32)
        mret = np.exp(lg[h] * dist) * ((i[:, None] // 64) <= (i[None, :] // 64))
        cst = np.zeros((128, 512), np.float32)
        cst[:, 0:128] = mret
        cst[:, 128:256] = inp['da_out_norm'][0][None, :] * np.float32(0.8)
        cst[:, 256:384] = inp['ret_out_norm'][0][None, :]
        cst[:, 384] = np.exp(np.float32(128.0) * lg[h])
        lamv = np.concatenate([inp['da_lam_q1'][0], inp['da_lam_k1'][0], inp['da_lam_q2'][0], inp['da_lam_k2'][0]])
        lamv = np.ascontiguousarray(np.broadcast_to(lamv[None, :], (128, 256))).astype(np.float32)
        ct = lambda a: np.ascontiguousarray(a)
        maps.append({
            "qaT": ct(rows[:, h * 128:(h + 1) * 128].T), "kaT": ct(rows[:, 512 + h * 128:512 + (h + 1) * 128].T),
            "va": ct(rows[:, 1024 + h * 128:1024 + (h + 1) * 128]),
            "qrT": ct(rows[:, 1536 + h * 64:1536 + (h + 1) * 64].T), "krT": ct(rows[:, 1792 + h * 64:1792 + (h + 1) * 64].T),
            "qwrT": ct(rows[:, 2048 + h * 64:2048 + (h + 1) * 64].T), "kwr": ct(rows[:, 2304 + h * 64:2304 + (h + 1) * 64]),
            "vr": ct(rows[:, 2560 + h * 128:2560 + (h + 1) * 128]), "gr": ct(rows[:, 3072 + h * 128:3072 + (h + 1) * 128]),
            "cst": cst, "lamv": lamv})
    return maps


def build_pout(layer, ngrp=None):
    c = Ctx()
    nc = c.nc
    GT = 256
    NG = TOK // GT if ngrp is None else ngrp
    x = c.din("x", [TOK, D])
    nko = 8 if layer == 0 else 6
    oT_d = c.din("oT", [nko * 128, TOK], BF16)
    wout_d = c.din("wout", [D, D])
    wup_d = c.din("wup", [D, 4096])
    wdn_d = c.din("wdn", [4096, D])
    gpk_d = c.din("gpk", [128, 8])
    ident_d = c.din("ident", [128, 128])
    if layer == 1:
        yT_d = c.din("yT", [256, TOK])
        wglu_d = c.din("wglu", [256, 256])
        rT_d = c.din("rT", [768, TOK], BF16)
        gng_d = c.din("gng", [128, 1])
    out = c.dout("xo", [TOK, D])
    p = c.start()
    sb = c.sb
    woutb = sb("woutb", [128, 8, D], BF16); bwout = p.buf()
    wupb = sb("wupb", [128, 8, 4096], BF16); bwup = p.buf()
    wdnb = sb("wdnb", [128, 32, D], BF16); bwdn = p.buf()
    gpk = sb("gpk", [128, 8]); bgpk = p.buf()
    identf = sb("identf", [128, 128]); bidf = p.buf()
    ident = sb("ident", [128, 128], BF16); bid = p.buf()
    p.dma("gpsimd", gpk[:], gpk_d[:, :], writes=[bgpk])
    p.dma("gpsimd", identf[:], ident_d[:, :], writes=[bidf])
    p.op("vector", lambda e: e.tensor_copy(out=ident[:], in_=identf[:]), reads=[bidf], writes=[bid])
    xin = Rot([(sb(f"xin{i}", [128, D]), p.buf()) for i in range(2)])
    stage = xin
    load_weight_bf16(c, p, wout_d, woutb, bwout, 8, D, None, None, stage, 1024)
    if layer == 1:
        wglub = sb("wglub", [128, 2, 256], BF16); bwglu = p.buf()
        load_weight_bf16(c, p, wglu_d, wglub, bwglu, 2, 256, None, None, stage, 256)
    load_weight_bf16(c, p, wup_d, wupb, bwup, 8, 4096, gpk, bgpk, stage, 1024)
    load_weight_bf16(c, p, wdn_d, wdnb, bwdn, 32, D, None, None, stage, 1024)
    accD = [(c.ps(f"pD{i}", [128, 512]), p.buf()) for i in range(4)]
    pH = Rot([(c.ps(f"pH{i}", [128, 512]), p.buf()) for i in range(4)])
    oin = Rot([(sb(f"oin{i}", [128, 8, GT], BF16), p.buf()) for i in range(2)])
    x1 = Rot([(sb(f"x1_{i}", [128, 2, D]), p.buf()) for i in range(2)])
    x1b = Rot([(sb(f"x1b{i}", [128, D], BF16), p.buf()) for i in range(2)])
    x1T = Rot([(sb(f"x1T{i}", [128, 8, GT], BF16), p.buf()) for i in range(1)])
    rsd = Rot([(sb(f"rsd{i}", [128, 2]), p.buf()) for i in range(4)])
    junk = sb("junk", [128, D], BF16); bjunk = p.buf()
    hr = Rot([(sb(f"hr{i}", [128, GT], BF16), p.buf()) for i in range(3)])
    h2 = Rot([(sb(f"h2{i}", [128, GT], BF16), p.buf()) for i in range(4)])
    if layer == 1:
        yin = Rot([(sb(f"yin{i}", [128, 2, GT]), p.buf()) for i in range(1)])
        ga = sb("ga", [128, 2, GT]); bga = p.buf()
        gb = sb("gb", [128, 2, GT]); bgb = p.buf()
        gz, bgz = ga, bga
        gng = sb("gng", [128, 1]); bgng = p.buf()
        p.dma("gpsimd", gng[:], gng_d[:, :], writes=[bgng])
        onesb = sb("onesb", [128, 128], BF16); bonesb = p.buf()
        p.op("vector", lambda e: e.memset(onesb[:], 1.0), writes=[bonesb])
        rin = Rot([(sb(f"rin{i}", [128, 6, GT], BF16), p.buf()) for i in range(1)])
        nsq = Rot([(sb(f"nsq{i}", [128, GT], BF16), p.buf()) for i in range(2)])
        nrs = Rot([(sb(f"nrs{i}", [128, GT]), p.buf()) for i in range(2)])
        ntm = Rot([(sb(f"ntm{i}", [128, GT]), p.buf()) for i in range(1)])
        gzb = sb("gzb", [128, 2, GT], BF16); bgzb = p.buf()
        gs = sb("gs", [128, GT]); bgs = p.buf()
    xv = x.rearrange("(n p) d -> p n d", p=128)
    ov = oT_d.rearrange("(k p) t -> p k t", p=128)
    for g in range(NG):
        oi, boi = oin.next()
        k0 = 8 - nko
        p.dma("gpsimd", oi[:, k0:8, :], ov[:, :, g * GT:(g + 1) * GT], writes=[boi])
        if layer == 1:
            yi, byi = yin.next()
            p.dma("sync", yi[:], yT_d.rearrange("(k p) t -> p k t", p=128)[:, :, g * GT:(g + 1) * GT], writes=[byi])
            p.op("gpsimd", lambda e, yi=yi: e.tensor_tensor(out=ga[:], in0=yi[:], in1=yi[:], op=ALU.mult), reads=[byi], writes=[bga])
            p.op("vector", lambda e: e.tensor_scalar(out=ga[:], in0=ga[:], scalar1=0.044715, scalar2=1.0, op0=ALU.mult, op1=ALU.add), reads=[bga], writes=[bga])
            p.op("gpsimd", lambda e, yi=yi: e.tensor_tensor(out=gb[:], in0=ga[:], in1=yi[:], op=ALU.mult), reads=[bga, byi], writes=[bgb])
            p.op("scalar", lambda e: e.activation(out=gb[:], in_=gb[:], func=AF.Tanh, scale=0.7978845608028654), reads=[bgb], writes=[bgb])
            p.op("vector", lambda e: e.tensor_scalar(out=gb[:], in0=gb[:], scalar1=1.0, scalar2=0.5, op0=ALU.add, op1=ALU.mult), reads=[bgb], writes=[bgb])
            p.op("gpsimd", lambda e, yi=yi: e.tensor_tensor(out=gz[:], in0=gb[:], in1=yi[:], op=ALU.mult), reads=[bgb, byi], writes=[bgz])
            p.op("vector", lambda e: e.tensor_copy(out=gzb[:], in_=gz[:]), reads=[bgz], writes=[bgzb])
            for jc in range(2):
                pg, bpg = pH.next()
                for ic in range(2):
                    p.op("tensor", lambda e, pg=pg, ic=ic, jc=jc: e.matmul(pg[:, 0:GT], lhsT=wglub[:, ic, jc * 128:(jc + 1) * 128], rhs=gzb[:, ic, :],
                                                                      start=(ic == 0), stop=(ic == 1)), reads=[bwglu, bgzb], writes=[bpg])
                p.op("scalar", lambda e, pg=pg: e.activation(out=gs[:], in_=pg[:, 0:GT], func=AF.Sigmoid), reads=[bpg], writes=[bgs])
                p.op("vector", lambda e, jc=jc, oi=oi: e.tensor_tensor(out=oi[:, jc, :], in0=gz[:, jc, :], in1=gs[:], op=ALU.mult), reads=[bgz, bgs], writes=[boi])
            ri, bri = rin.next()
            p.dma("sync", ri[:], rT_d.rearrange("(k p) t -> p k t", p=128)[:, :, g * GT:(g + 1) * GT], writes=[bri])
            for kk in range(6):
                q_, bq_ = nsq.next()
                p.op("gpsimd", lambda e, q_=q_, oi=oi, kk=kk: e.tensor_tensor(out=q_[:], in0=oi[:, 2 + kk, :], in1=oi[:, 2 + kk, :], op=ALU.mult), reads=[boi], writes=[bq_])
                pn, bpn = pH.next()
                p.op("tensor", lambda e, pn=pn, q_=q_: e.matmul(pn[:, 0:GT], lhsT=onesb[:], rhs=q_[:], start=True, stop=True), reads=[bonesb, bq_], writes=[bpn])
                r_, br_ = nrs.next()
                p.op("scalar", lambda e, pn=pn, r_=r_: e.activation(out=r_[:], in_=pn[:, 0:GT], func=AF.Sqrt, scale=1.0 / 128, bias=EPS), reads=[bpn], writes=[br_])
                p.op("vector", lambda e, r_=r_: e.reciprocal(out=r_[:], in_=r_[:]), reads=[br_], writes=[br_])
                t_, bt_ = ntm.next()
                p.op("vector", lambda e, t_=t_, r_=r_, oi=oi, kk=kk: e.scalar_tensor_tensor(out=t_[:], in0=oi[:, 2 + kk, :], scalar=gng[:, 0:1], in1=r_[:], op0=ALU.mult, op1=ALU.mult),
                     reads=[boi, bgng, br_], writes=[bt_])
                p.op("gpsimd", lambda e, t_=t_, ri=ri, oi=oi, kk=kk: e.tensor_tensor(out=oi[:, 2 + kk, :], in0=t_[:], in1=ri[:, kk, :], op=ALU.mult), reads=[bt_, bri], writes=[boi])
        x1g, bx1g = x1.next()
        x1Tg, bx1Tg = x1T.next()
        rs2, brs2 = rsd.next()
        rq2, brq2 = rsd.next()
        for j in range(2):
            xi, bxi = xin.next()
            p.dma("sync", xi[:], x[(g * 2 + j) * 128:(g * 2 + j + 1) * 128, :], writes=[bxi])
            for hh in range(2):
                pd, bpd = accD[j * 2 + hh]
                for k in range(8):
                    p.op("tensor", lambda e, pd=pd, oi=oi, k=k, j=j, hh=hh: e.matmul(pd[:, :], lhsT=oi[:, k, j * 128:(j + 1) * 128],
                                                                                rhs=woutb[:, k, hh * 512:(hh + 1) * 512], start=(k == 0), stop=(k == 7)),
                         reads=[boi, bwout], writes=[bpd])
                p.op("vector", lambda e, pd=pd, xi=xi, x1g=x1g, j=j, hh=hh: e.tensor_tensor(out=x1g[:, j, hh * 512:(hh + 1) * 512], in0=pd[:, :],
                                                                                       in1=xi[:, hh * 512:(hh + 1) * 512], op=ALU.add),
                     reads=[bpd, bxi], writes=[bx1g])
            p.op("scalar", lambda e, x1g=x1g, j=j, rs2=rs2: e.activation(out=junk[:], in_=x1g[:, j, :], func=AF.Square, accum_out=rs2[:, j:j + 1]),
                 reads=[bx1g], writes=[bjunk, brs2])
            xb, bxb = x1b.next()
            p.op("gpsimd", lambda e, xb=xb, x1g=x1g, j=j: e.tensor_copy(out=xb[:], in_=x1g[:, j, :]), reads=[bx1g], writes=[bxb])
            pt, bpt = pH.next()
            ptb = pt[:, :].bitcast(BF16)
            for k in range(8):
                p.op("tensor", lambda e, ptb=ptb, xb=xb, k=k: e.transpose(ptb[:, k * 128:(k + 1) * 128], xb[:, k * 128:(k + 1) * 128], ident[:]),
                     reads=[bxb, bid], writes=[bpt])
            p.op("scalar", lambda e, ptb=ptb, x1Tg=x1Tg, j=j: e.activation(out=x1Tg[:, :, j * 128:(j + 1) * 128], in_=ptb.rearrange("p (k t) -> p k t", t=128), func=AF.Copy),
                 reads=[bpt], writes=[bx1Tg])
        p.op("scalar", lambda e, rs2=rs2: e.activation(out=rs2[:], in_=rs2[:], func=AF.Sqrt, scale=1.0 / D, bias=EPS), reads=[brs2], writes=[brs2])
        p.op("vector", lambda e, rs2=rs2: e.reciprocal(out=rs2[:], in_=rs2[:]), reads=[brs2], writes=[brs2])
        p.op("vector", lambda e, rs2=rs2, rq2=rq2: e.tensor_tensor(out=rq2[:], in0=rs2[:], in1=rs2[:], op=ALU.mult), reads=[brs2], writes=[brq2])
        def up(f):
            ph, bph = pH.next()
            for k in range(8):
                p.op("tensor", lambda e, ph=ph, k=k, f=f, x1Tg=x1Tg: e.matmul(ph[:, 0:GT], lhsT=wupb[:, k, f * 128:(f + 1) * 128], rhs=x1Tg[:, k, :],
                                                                         start=(k == 0), stop=(k == 7)), reads=[bwup, bx1Tg], writes=[bph])
            r, br = hr.next()
            p.op("scalar", lambda e, ph=ph, r=r: e.activation(out=r[:], in_=ph[:, 0:GT], func=AF.Relu), reads=[bph], writes=[br])
            hq, bhq = h2.next()
            eng = "gpsimd" if f % 2 == 0 else "vector"
            p.op(eng, lambda e, r=r, hq=hq: e.tensor_tensor(out=hq[:], in0=r[:], in1=r[:], op=ALU.mult), reads=[br], writes=[bhq])
            return hq, bhq

        def down(f, hq, bhq):
            for j in range(2):
                for hh in range(2):
                    pd, bpd = accD[j * 2 + hh]
                    p.op("tensor", lambda e, pd=pd, hq=hq, j=j, hh=hh, f=f: e.matmul(pd[:, :], lhsT=hq[:, j * 128:(j + 1) * 128],
                                                                                rhs=wdnb[:, f, hh * 512:(hh + 1) * 512], start=(f == 0), stop=(f == 31)),
                         reads=[bhq, bwdn], writes=[bpd])
        cur = up(0)
        for f in range(32):
            nxt = up(f + 1) if f < 31 else None
            down(f, *cur)
            cur = nxt
        for j in range(2):
            for hh in range(2):
                pd, bpd = accD[j * 2 + hh]
                p.op("vector", lambda e, pd=pd, x1g=x1g, rq2=rq2, j=j, hh=hh: e.scalar_tensor_tensor(
                    out=x1g[:, j, hh * 512:(hh + 1) * 512], in0=pd[:, :], scalar=rq2[:, j:j + 1], in1=x1g[:, j, hh * 512:(hh + 1) * 512], op0=ALU.mult, op1=ALU.add),
                    reads=[bpd, brq2, bx1g], writes=[bx1g])
            tt = g * 2 + j
            p.dma("sync", out[tt * 128:(tt + 1) * 128, :], x1g[:, j, :], reads=[bx1g], writes=[p.buf()])
    p.finish_all()
    p.emit()
    return c


def pout_maps(inp, layer, x, oT_full, yT_full=None, rT_full=None):
    j = layer // 2
    wout = np.ascontiguousarray(inp['ab_w_out'][j] if layer == 0 else inp['cd_w_out'][j])
    maps = []
    ident = np.eye(128, dtype=np.float32)
    for c in range(NCORE):
        sl = slice(c * TOK, (c + 1) * TOK)
        m = {"x": np.ascontiguousarray(x[sl]), "oT": np.ascontiguousarray(oT_full[:, sl]), "wout": wout,
             "wup": np.ascontiguousarray(inp['w_up'][layer]), "wdn": np.ascontiguousarray(inp['w_down'][layer]),
             "gpk": _gpk(inp['norm_mlp_g'][layer]), "ident": ident}
        if layer == 1:
            m["yT"] = np.ascontiguousarray(yT_full[:, sl])
            m["wglu"] = np.ascontiguousarray(inp['s5_w_glu'][j])
            m["rT"] = np.ascontiguousarray(rT_full[:, sl])
            m["gng"] = np.ascontiguousarray(inp['gla_out_norm'][j][:, None])
        maps.append(m)
    return maps


def assemble_oT0(o):
    oT = np.empty((1024, 2 * SEQ), o.dtype)
    for c in range(NCORE):
        b, h = c // 4, c % 4
        oT[h * 128:(h + 1) * 128, b * SEQ:(b + 1) * SEQ] = o[c][:, 0:128].T
        oT[512 + h * 128:512 + (h + 1) * 128, b * SEQ:(b + 1) * SEQ] = o[c][:, 128:256].T
    return oT


def pin1_maps(inp, x):
    w = np.ascontiguousarray(inp['cd_w_in'][0][:, 0:2560])
    walrT = np.ascontiguousarray(inp['cd_w_in'][0][:, 2560:2576].T)
    wa2 = np.ascontiguousarray(inp['gla_w_a2'][0])
    ba2 = np.ascontiguousarray(np.broadcast_to(inp['gla_b_a2'][0][None, :], (128, 384)))
    gpk = _gpk(inp['norm_mix_g'][1])
    maps = []
    for c in range(NCORE):
        xs = x[c * TOK:(c + 1) * TOK]
        maps.append({"x": np.ascontiguousarray(xs), "xT": np.ascontiguousarray(xs.T), "w": w, "gpk": gpk,
                     "walrT": walrT, "wa2": wa2, "ba2": ba2})
    return maps


TWO_PI = float(2.0 * np.pi)
PI = float(np.pi)
I32 = mybir.dt.int32


def emit_sincos(p, sb, name, ang, bang, shape, s_out, c_out, bouts, eng_pool="gpsimd"):
    P_ = shape[0]
    kf = sb(name + "_kf", shape); ki = sb(name + "_ki", shape, I32); r = sb(name + "_r", shape); ab = sb(name + "_ab", shape)
    bkf, bki, br, bab = p.buf(), p.buf(), p.buf(), p.buf()
    p.op("vector", lambda e: e.tensor_scalar(out=kf[:], in0=ang, scalar1=1.0 / TWO_PI, scalar2=None, op0=ALU.mult), reads=[bang], writes=[bkf])
    p.op("vector", lambda e: e.tensor_copy(out=ki[:], in_=kf[:]), reads=[bkf], writes=[bki])
    p.op("vector", lambda e: e.tensor_copy(out=kf[:], in_=ki[:]), reads=[bki], writes=[bkf])
    p.op("vector", lambda e: e.scalar_tensor_tensor(out=r[:], in0=kf[:], scalar=-TWO_PI, in1=ang, op0=ALU.mult, op1=ALU.add), reads=[bkf, bang], writes=[br])
    p.op("vector", lambda e: e.tensor_scalar(out=r[:], in0=r[:], scalar1=-PI, scalar2=PI, op0=ALU.max, op1=ALU.min), reads=[br], writes=[br])
    p.op("scalar", lambda e: e.activation(out=s_out, in_=r[:], func=AF.Sin), reads=[br], writes=[bouts[0]])
    p.op("scalar", lambda e: e.activation(out=ab[:], in_=r[:], func=AF.Abs), reads=[br], writes=[bab])
    p.op("scalar", lambda e: e.activation(out=c_out, in_=ab[:], func=AF.Sin, scale=-1.0, bias=PI / 2), reads=[bab], writes=[bouts[1]])


def build_mix1(nwin=32):
    c = Ctx()
    nc = c.nc
    W = 512
    uT_d = c.din("uT", [64, SEQ], BF16)
    prmC_d = c.din("prmC", [128, 6])
    prmR_d = c.din("prmR", [64, 3, 128])
    bpad_d = c.din("bpad", [64, 2, 128])
    ct_d = c.din("ct", [128, 2, 2, 64])
    dsk_d = c.din("dsk", [64, 1])
    iota_d = c.din("iota", [128, W])
    gm_d = c.din("gm", [128, 256])
    idn_d = c.din("idn", [128, 128])
    gq_d = c.din("gq", [3, 64, SEQ], BF16)
    gk_d = c.din("gk", [3, 64, SEQ], BF16)
    gla_d = c.din("gla", [3, 64, SEQ])
    gv_d = c.din("gv", [SEQ, 192], BF16)
    yT_o = c.dout("yT", [64, SEQ])
    go_o = c.dout("go", [SEQ, 192], BF16)
    p = c.start()
    sb = c.sb

    def ld(name, shape, src, dt=F32, q="gpsimd"):
        t = sb(name, shape, dt); b = p.buf()
        p.dma(q, t[:], src, writes=[b])
        return t, b
    prmC, bprmC = ld("prmC", [128, 6], prmC_d[:, :])
    prmR, bprmR = ld("prmR", [64, 3, 128], prmR_d[:, :, :])
    bpad, bbpad = ld("bpad", [64, 2, 128], bpad_d[:, :, :])
    ctf, bctf = ld("ctf", [128, 2, 2, 64], ct_d[:, :, :, :])
    dsk, bdsk = ld("dsk", [64, 1], dsk_d[:, :])
    iota, biota = ld("iota", [128, W], iota_d[:, :])
    gm, bgm = ld("gm", [128, 256], gm_d[:, :])
    idf, bidf = ld("idf", [128, 128], idn_d[:, :])
    ident = sb("ident", [128, 128], BF16); bid = p.buf()
    p.op("vector", lambda e: e.tensor_copy(out=ident[:], in_=idf[:]), reads=[bidf], writes=[bid])
    ones = sb("ones", [128, 1]); bones = p.buf()
    p.op("vector", lambda e: e.memset(ones[:], 1.0), writes=[bones])

    pc = prmC[:].rearrange("p (a k) -> p a k", k=3)
    dl = sb("dl", [128, 2]); bdl = p.buf()
    rr = sb("rr", [128, 2]); brr = p.buf()
    th = sb("th", [128, 2]); bth = p.buf()
    p.op("scalar", lambda e: e.activation(out=dl[:], in_=pc[:, :, 2], func=AF.Exp), reads=[bprmC], writes=[bdl])
    p.op("vector", lambda e: e.tensor_tensor(out=rr[:], in0=pc[:, :, 0], in1=dl[:], op=ALU.mult), reads=[bprmC, bdl], writes=[brr])
    p.op("scalar", lambda e: e.activation(out=rr[:], in_=rr[:], func=AF.Exp), reads=[brr], writes=[brr])
    p.op("vector", lambda e: e.tensor_tensor(out=th[:], in0=pc[:, :, 1], in1=dl[:], op=ALU.mult), reads=[bprmC, bdl], writes=[bth])
    cosT, sinT, bcs = [], [], []
    ang = sb("ang", [128, W]); bang = p.buf()
    for pr in range(2):
        ct_ = sb(f"cosT{pr}", [128, W]); st_ = sb(f"sinT{pr}", [128, W]); b1, b2 = p.buf(), p.buf()
        p.op("vector", lambda e, pr=pr: e.tensor_scalar(out=ang[:], in0=iota[:], scalar1=th[:, pr:pr + 1], scalar2=None, op0=ALU.mult), reads=[biota, bth], writes=[bang])
        emit_sincos(p, sb, f"sc{pr}", ang[:], bang, [128, W], st_[:], ct_[:], (b1, b2))
        cosT.append(ct_); sinT.append(st_); bcs.append((b2, b1))
    angW = sb("angW", [128, 2]); bangW = p.buf()
    cW = sb("cW", [128, 2]); sW = sb("sW", [128, 2]); nsW = sb("nsW", [128, 2]); bcW, bsW, bnsW = p.buf(), p.buf(), p.buf()
    p.op("vector", lambda e: e.tensor_scalar(out=angW[:], in0=th[:], scalar1=float(W), scalar2=None, op0=ALU.mult), reads=[bth], writes=[bangW])
    emit_sincos(p, sb, "scW", angW[:], bangW, [128, 2], sW[:], cW[:], (bsW, bcW))
    p.op("vector", lambda e: e.tensor_scalar(out=nsW[:], in0=sW[:], scalar1=-1.0, scalar2=None, op0=ALU.mult), reads=[bsW], writes=[bnsW])
    R = lambda k: prmR[:, k, :]
    def t64(name):
        return sb(name, [64, 128]), p.buf()
    (dR, bdR), (x1, bx1), (er, ber), (thR, bthR), (sR, bsR), (cR, bcR) = [t64(n) for n in ("dR", "x1R", "erR", "thR", "sR", "cR")]
    (lbr, blbr), (lbi, blbi), (den, bden), (t1, bt1), (t2, bt2), (cfr, bcfr), (cfi, bcfi) = [t64(n) for n in ("lbr", "lbi", "den", "t1R", "t2R", "cfr", "cfi")]
    V = "vector"
    p.op("scalar", lambda e: e.activation(out=dR[:], in_=R(2), func=AF.Exp), reads=[bprmR], writes=[bdR])
    p.op(V, lambda e: e.tensor_tensor(out=x1[:], in0=R(0), in1=dR[:], op=ALU.mult), reads=[bprmR, bdR], writes=[bx1])
    p.op("scalar", lambda e: e.activation(out=er[:], in_=x1[:], func=AF.Exp), reads=[bx1], writes=[ber])
    p.op(V, lambda e: e.tensor_tensor(out=thR[:], in0=R(1), in1=dR[:], op=ALU.mult), reads=[bprmR, bdR], writes=[bthR])
    emit_sincos(p, sb, "scR", thR[:], bthR, [64, 128], sR[:], cR[:], (bsR, bcR))
    p.op(V, lambda e: e.tensor_tensor(out=lbr[:], in0=er[:], in1=cR[:], op=ALU.mult), reads=[ber, bcR], writes=[blbr])
    p.op(V, lambda e: e.tensor_scalar(out=lbr[:], in0=lbr[:], scalar1=-1.0, scalar2=None, op0=ALU.add), reads=[blbr], writes=[blbr])
    p.op(V, lambda e: e.tensor_tensor(out=lbi[:], in0=er[:], in1=sR[:], op=ALU.mult), reads=[ber, bsR], writes=[blbi])
    p.op(V, lambda e: e.tensor_tensor(out=den[:], in0=R(0), in1=R(0), op=ALU.mult), reads=[bprmR], writes=[bden])
    p.op(V, lambda e: e.tensor_tensor(out=t1[:], in0=R(1), in1=R(1), op=ALU.mult), reads=[bprmR], writes=[bt1])
    p.op(V, lambda e: e.tensor_tensor(out=den[:], in0=den[:], in1=t1[:], op=ALU.add), reads=[bden, bt1], writes=[bden])
    p.op(V, lambda e: e.reciprocal(out=den[:], in_=den[:]), reads=[bden], writes=[bden])
    p.op(V, lambda e: e.tensor_tensor(out=t1[:], in0=lbr[:], in1=R(0), op=ALU.mult), reads=[blbr, bprmR, bden], writes=[bt1])
    p.op(V, lambda e: e.tensor_tensor(out=t2[:], in0=lbi[:], in1=R(1), op=ALU.mult), reads=[blbi, bprmR], writes=[bt2])
    p.op(V, lambda e: e.tensor_tensor(out=cfr[:], in0=t1[:], in1=t2[:], op=ALU.add), reads=[bt1, bt2], writes=[bcfr])
    p.op(V, lambda e: e.tensor_tensor(out=cfr[:], in0=cfr[:], in1=den[:], op=ALU.mult), reads=[bcfr, bden], writes=[bcfr])
    p.op(V, lambda e: e.tensor_tensor(out=t1[:], in0=lbi[:], in1=R(0), op=ALU.mult), reads=[blbi, bprmR, bcfr], writes=[bt1])
    p.op(V, lambda e: e.tensor_tensor(out=t2[:], in0=lbr[:], in1=R(1), op=ALU.mult), reads=[blbr, bprmR, bcfr], writes=[bt2])
    p.op(V, lambda e: e.tensor_tensor(out=cfi[:], in0=t1[:], in1=t2[:], op=ALU.subtract), reads=[bt1, bt2], writes=[bcfi])
    p.op(V, lambda e: e.tensor_tensor(out=cfi[:], in0=cfi[:], in1=den[:], op=ALU.mult), reads=[bcfi, bden], writes=[bcfi])
    bbre = sb("bbre", [64, 128], BF16); bbim = sb("bbim", [64, 128], BF16); bbbre, bbbim = p.buf(), p.buf()
    Bre, Bim = bpad[:, 0, :], bpad[:, 1, :]
    p.op(V, lambda e: e.tensor_tensor(out=t1[:], in0=cfr[:], in1=Bre, op=ALU.mult), reads=[bcfr, bbpad, bcfi], writes=[bt1])
    p.op(V, lambda e: e.tensor_tensor(out=t2[:], in0=cfi[:], in1=Bim, op=ALU.mult), reads=[bcfi, bbpad], writes=[bt2])
    p.op(V, lambda e: e.tensor_tensor(out=bbre[:], in0=t1[:], in1=t2[:], op=ALU.subtract), reads=[bt1, bt2], writes=[bbbre])
    p.op(V, lambda e: e.tensor_tensor(out=t1[:], in0=cfr[:], in1=Bim, op=ALU.mult), reads=[bcfr, bbpad, bbbre], writes=[bt1])
    p.op(V, lambda e: e.tensor_tensor(out=t2[:], in0=cfi[:], in1=Bre, op=ALU.mult), reads=[bcfi, bbpad, bbbre], writes=[bt2])
    p.op(V, lambda e: e.tensor_tensor(out=bbim[:], in0=t1[:], in1=t2[:], op=ALU.add), reads=[bt1, bt2], writes=[bbbim])
    ctb = sb("ctb", [128, 2, 2, 64], BF16); bctb = p.buf()
    p.op(V, lambda e: e.tensor_copy(out=ctb[:, :, 0, :], in_=ctf[:, :, 0, :]), reads=[bctf], writes=[bctb])
    p.op(V, lambda e: e.tensor_scalar(out=ctb[:, :, 1, :], in0=ctf[:, :, 1, :], scalar1=-1.0, scalar2=None, op0=ALU.mult), reads=[bctf], writes=[bctb])

    pBU = [(c.ps(f"pBU{i}", [128, 512]), p.buf()) for i in range(2)]
    pY = (c.ps("pY", [128, 512]), p.buf())
    pS = Rot([(c.ps(f"pS{i}", [128, 512]), p.buf()) for i in range(3)])
    _pOT = [c.ps(f"pOT{i}", [128, 512]) for i in range(2)]
    _bOT = [p.buf() for _ in range(2)]
    _pOTb = [t_[:, :].bitcast(BF16) for t_ in _pOT]
    pO = [(_pOT[i // 2][:, (i % 2) * 256:(i % 2) * 256 + 128], _bOT[i // 2]) for i in range(3)]
    pT = [(_pOTb[i // 2][:, (i % 2) * 512 + 256:(i % 2) * 512 + 512], _bOT[i // 2]) for i in range(3)]
    uin = Rot([(sb(f"uin{i}", [64, W], BF16), p.buf()) for i in range(2)])
    def T(name, dt=F32, n=1, shape=None):
        return Rot([(sb(f"{name}{i}", shape or [128, W], dt), p.buf()) for i in range(n)])
    ta, tb_, tc, td = T("s5a"), T("s5b"), T("s5c"), T("s5d")
    kre, kim = T("kre"), T("kim")
    wre = [T(f"wre{pr}", n=2) for pr in range(2)]
    wim = [T(f"wim{pr}", n=2) for pr in range(2)]
    xre, xim = T("xre", BF16, 2), T("xim", BF16, 2)
    w0 = [[(sb(f"w0_{pr}_{i}", [128, 2]), p.buf()) for i in range(2)] for pr in range(2)]
    for pr in range(2):
        p.op("vector", lambda e, pr=pr: e.memset(w0[pr][0][0][:], 0.0), writes=[w0[pr][0][1]])
    yo = T("yo", n=2, shape=[64, W])
    CH = 1024
    gq = Rot([(sb(f"gq{i}", [64, 3, CH], BF16), p.buf()) for i in range(2)])
    gk = Rot([(sb(f"gk{i}", [64, 3, CH], BF16), p.buf()) for i in range(2)])
    gla = Rot([(sb(f"gla{i}", [64, 3, CH]), p.buf()) for i in range(2)])
    gv = Rot([(sb(f"gv{i}", [128, 8, 192], BF16), p.buf()) for i in range(2)])
    G64 = lambda name, dt=F32, n=6: Rot([(sb(f"{name}{i}", [64, 128], dt), p.buf()) for i in range(n)])
    Bt, ep, en, el = G64("Bt"), G64("ep"), G64("en"), G64("el")
    qf, kf, qb, kb, ks = [G64(n, BF16) for n in ("qf", "kf", "qb", "kb", "ks")]
    s1 = Rot([(sb(f"gs1{i}", [128, 128]), p.buf()) for i in range(6)])
    s2 = Rot([(sb(f"gs2{i}", [128, 128]), p.buf()) for i in range(6)])
    Sm = Rot([(sb(f"gSm{i}", [128, 128], BF16), p.buf()) for i in range(6)])
    kst = Rot([(sb(f"kst{i}", [128, 64], BF16), p.buf()) for i in range(6)])
    Sst = [(sb(f"Sst{i}", [64, 64]), p.buf()) for i in range(3)]
    Sstb = [Rot([(sb(f"Sstb{i}_{k}", [64, 64], BF16), p.buf()) for k in range(2)]) for i in range(3)]
    cur_sb = []
    for i in range(3):
        p.op("vector", lambda e, i=i: e.memset(Sst[i][0][:], 0.0), writes=[Sst[i][1]])
        t_, b_ = Sstb[i].next()
        p.op("vector", lambda e, t_=t_: e.memset(t_[:], 0.0), writes=[b_])
        cur_sb.append((t_, b_))
    otile = Rot([(sb(f"got{i}", [128, 192], BF16), p.buf()) for i in range(3)])
    gst = {"cur": None}

    def s5_window(m):
        ui, bui = uin.next()
        p.dma("sync", ui[:], uT_d[:, m * W:(m + 1) * W], writes=[bui])
        py, bpy = pY
        for pr in range(2):
            (pre_, bpre), (pim_, bpim) = pBU
            rows = slice(32 * pr, 32 * pr + 32)
            p.op("tensor", lambda e, pr=pr, rows=rows, ui=ui, pre_=pre_: e.matmul(pre_[:, :], lhsT=bbre[rows, :], rhs=ui[rows, :], start=True, stop=True),
                 reads=[bbbre, bui], writes=[bpre])
            p.op("tensor", lambda e, pr=pr, rows=rows, ui=ui, pim_=pim_: e.matmul(pim_[:, :], lhsT=bbim[rows, :], rhs=ui[rows, :], start=True, stop=True),
                 reads=[bbbim, bui], writes=[bpim])
            bcos, bsin = bcs[pr]
            (a_, ba_), (b_, bb_), (c_, bc_), (d_, bd_) = ta.next(), tb_.next(), tc.next(), td.next()
            (kr_, bkr_), (ki_, bki_) = kre.next(), kim.next()
            CT, ST = cosT[pr], sinT[pr]
            p.op("vector", lambda e, a_=a_, pre_=pre_, CT=CT: e.tensor_tensor(out=a_[:], in0=pre_[:, :], in1=CT[:], op=ALU.mult), reads=[bpre, bcos], writes=[ba_])
            p.op("vector", lambda e, b_=b_, pim_=pim_, ST=ST: e.tensor_tensor(out=b_[:], in0=pim_[:, :], in1=ST[:], op=ALU.mult), reads=[bpim, bsin], writes=[bb_])
            p.op("vector", lambda e, c_=c_, pim_=pim_, CT=CT: e.tensor_tensor(out=c_[:], in0=pim_[:, :], in1=CT[:], op=ALU.mult), reads=[bpim, bcos], writes=[bc_])
            p.op("vector", lambda e, d_=d_, pre_=pre_, ST=ST: e.tensor_tensor(out=d_[:], in0=pre_[:, :], in1=ST[:], op=ALU.mult), reads=[bpre, bsin], writes=[bd_])
            p.op("gpsimd", lambda e, kr_=kr_, a_=a_, b_=b_: e.tensor_tensor(out=kr_[:], in0=a_[:], in1=b_[:], op=ALU.add), reads=[ba_, bb_], writes=[bkr_])
            p.op("gpsimd", lambda e, ki_=ki_, c_=c_, d_=d_: e.tensor_tensor(out=ki_[:], in0=c_[:], in1=d_[:], op=ALU.subtract), reads=[bc_, bd_], writes=[bki_])
            (wr_, bwr_), (wi_, bwi_) = wre[pr].next(), wim[pr].next()
            w0c, bw0c = w0[pr][m % 2]
            w0n, bw0n = w0[pr][(m + 1) % 2]
            rbc = rr[:, pr:pr + 1].to_broadcast([128, W])
            p.op("vector", lambda e, wr_=wr_, kr_=kr_, w0c=w0c, rbc=rbc: e.tensor_tensor_scan(out=wr_[:], data0=rbc, data1=kr_[:], initial=w0c[:, 0:1], op0=ALU.mult, op1=ALU.add),
                 reads=[brr, bkr_, bw0c], writes=[bwr_])
            p.op("vector", lambda e, wi_=wi_, ki_=ki_, w0c=w0c, rbc=rbc: e.tensor_tensor_scan(out=wi_[:], data0=rbc, data1=ki_[:], initial=w0c[:, 1:2], op0=ALU.mult, op1=ALU.add),
                 reads=[brr, bki_, bw0c], writes=[bwi_])
            p.op("vector", lambda e, w0n=w0n, wr_=wr_, pr=pr: e.tensor_tensor(out=w0n[:, 0:1], in0=wr_[:, W - 1:W], in1=cW[:, pr:pr + 1], op=ALU.mult), reads=[bwr_, bcW], writes=[bw0n])
            p.op("vector", lambda e, w0n=w0n, wi_=wi_, pr=pr: e.scalar_tensor_tensor(out=w0n[:, 0:1], in0=wi_[:, W - 1:W], scalar=nsW[:, pr:pr + 1], in1=w0n[:, 0:1], op0=ALU.mult, op1=ALU.add),
                 reads=[bwi_, bnsW, bw0n], writes=[bw0n])
            p.op("vector", lambda e, w0n=w0n, wi_=wi_, pr=pr: e.tensor_tensor(out=w0n[:, 1:2], in0=wi_[:, W - 1:W], in1=cW[:, pr:pr + 1], op=ALU.mult), reads=[bwi_, bcW], writes=[bw0n])
            p.op("vector", lambda e, w0n=w0n, wr_=wr_, pr=pr: e.scalar_tensor_tensor(out=w0n[:, 1:2], in0=wr_[:, W - 1:W], scalar=sW[:, pr:pr + 1], in1=w0n[:, 1:2], op0=ALU.mult, op1=ALU.add),
                 reads=[bwr_, bsW, bw0n], writes=[bw0n])
            (a2, ba2), (b2, bb2), (c2, bc2), (d2, bd2) = ta.next(), tb_.next(), tc.next(), td.next()
            (xr_, bxr_), (xi_, bxi_) = xre.next(), xim.next()
            p.op("gpsimd", lambda e, a2=a2, wr_=wr_, CT=CT: e.tensor_tensor(out=a2[:], in0=wr_[:], in1=CT[:], op=ALU.mult), reads=[bwr_, bcos], writes=[ba2])
            p.op("gpsimd", lambda e, b2=b2, wi_=wi_, ST=ST: e.tensor_tensor(out=b2[:], in0=wi_[:], in1=ST[:], op=ALU.mult), reads=[bwi_, bsin], writes=[bb2])
            p.op("vector", lambda e, c2=c2, wi_=wi_, CT=CT: e.tensor_tensor(out=c2[:], in0=wi_[:], in1=CT[:], op=ALU.mult), reads=[bwi_, bcos], writes=[bc2])
            p.op("gpsimd", lambda e, d2=d2, wr_=wr_, ST=ST: e.tensor_tensor(out=d2[:], in0=wr_[:], in1=ST[:], op=ALU.mult), reads=[bwr_, bsin], writes=[bd2])
            p.op("gpsimd", lambda e, xr_=xr_, a2=a2, b2=b2: e.tensor_tensor(out=xr_[:], in0=a2[:], in1=b2[:], op=ALU.subtract), reads=[ba2, bb2], writes=[bxr_])
            p.op("gpsimd", lambda e, xi_=xi_, c2=c2, d2=d2: e.tensor_tensor(out=xi_[:], in0=c2[:], in1=d2[:], op=ALU.add), reads=[bc2, bd2], writes=[bxi_])
            p.op("tensor", lambda e, pr=pr, xr_=xr_, py=py: e.matmul(py[0:64, :], lhsT=ctb[:, pr, 0, :], rhs=xr_[:], start=(pr == 0), stop=False), reads=[bctb, bxr_], writes=[bpy])
            p.op("tensor", lambda e, pr=pr, xi_=xi_, py=py: e.matmul(py[0:64, :], lhsT=ctb[:, pr, 1, :], rhs=xi_[:], start=False, stop=(pr == 1)), reads=[bctb, bxi_], writes=[bpy])
        yo_, byo = yo.next()
        p.op("vector", lambda e, yo_=yo_, ui=ui, py=py: e.scalar_tensor_tensor(out=yo_[:], in0=ui[:], scalar=dsk[:, 0:1], in1=py[0:64, :], op0=ALU.mult, op1=ALU.add),
             reads=[bui, bdsk, bpy], writes=[byo])
        p.dma("sync", yT_o[:, m * W:(m + 1) * W], yo_[:], reads=[byo], writes=[p.buf()])

    def gla_tile(t):
        ci, ti = t // 8, t % 8
        if ti == 0:
            cur = (gq.next(), gk.next(), gla.next(), gv.next())
            sl = slice(ci * CH, (ci + 1) * CH)
            p.dma("gpsimd", cur[0][0][:], gq_d[:, :, sl].rearrange("u d t -> d u t"), writes=[cur[0][1]])
            p.dma("gpsimd", cur[1][0][:], gk_d[:, :, sl].rearrange("u d t -> d u t"), writes=[cur[1][1]])
            p.dma("gpsimd", cur[2][0][:], gla_d[:, :, sl].rearrange("u d t -> d u t"), writes=[cur[2][1]])
            p.dma("gpsimd", cur[3][0][:], gv_d.rearrange("(n p) e -> p n e", p=128)[:, ci * 8:(ci + 1) * 8, :], writes=[cur[3][1]])
            gst["cur"] = cur
        (q_, bq_), (k_, bk_), (la_, bla_), (v_, bv_) = gst["cur"]
        ts = slice(ti * 128, (ti + 1) * 128)
        ot, bot = otile.next()
        U = range(3)
        st = [dict() for _ in U]
        for i in U:
            d = st[i]
            (d["B"], d["bB"]), (d["ep"], d["bep"]), (d["en"], d["ben"]), (d["el"], d["bel"]) = Bt.next(), ep.next(), en.next(), el.next()
            p.op("vector", lambda e, B_=d["B"], i=i: e.tensor_tensor_scan(out=B_[:], data0=ones[0:64, 0:1].to_broadcast([64, 128]), data1=la_[:, i, ts], initial=0.0, op0=ALU.mult, op1=ALU.add),
                 reads=[bones, bla_], writes=[d["bB"]])
        for i in U:
            d = st[i]
            p.op("scalar", lambda e, B_=d["B"], ep_=d["ep"]: e.activation(out=ep_[:], in_=B_[:], func=AF.Exp), reads=[d["bB"]], writes=[d["bep"]])
            p.op("scalar", lambda e, B_=d["B"], en_=d["en"]: e.activation(out=en_[:], in_=B_[:], func=AF.Exp, scale=-1.0), reads=[d["bB"]], writes=[d["ben"]])
            p.op("scalar", lambda e, B_=d["B"], el_=d["el"]: e.activation(out=el_[:], in_=B_[:], func=AF.Exp, scale=-1.0, bias=B_[:, 127:128]), reads=[d["bB"]], writes=[d["bel"]])
        for i in U:
            d = st[i]
            (d["qf"], d["bqf"]), (d["kf"], d["bkf"]), (d["qb"], d["bqb"]), (d["kb"], d["bkb"]), (d["ks"], d["bks"]) = qf.next(), kf.next(), qb.next(), kb.next(), ks.next()
            p.op("vector", lambda e, qf_=d["qf"], ep_=d["ep"], i=i: e.tensor_tensor(out=qf_[:], in0=q_[:, i, ts], in1=ep_[:], op=ALU.mult), reads=[bq_, d["bep"]], writes=[d["bqf"]])
            p.op("gpsimd", lambda e, kf_=d["kf"], en_=d["en"], i=i: e.tensor_tensor(out=kf_[:], in0=k_[:, i, ts], in1=en_[:], op=ALU.mult), reads=[bk_, d["ben"]], writes=[d["bkf"]])
            p.op("vector", lambda e, qb_=d["qb"], en_=d["en"], i=i: e.tensor_tensor(out=qb_[:], in0=q_[:, i, ts], in1=en_[:], op=ALU.mult), reads=[bq_, d["ben"]], writes=[d["bqb"]])
            p.op("gpsimd", lambda e, kb_=d["kb"], ep_=d["ep"], i=i: e.tensor_tensor(out=kb_[:], in0=k_[:, i, ts], in1=ep_[:], op=ALU.mult), reads=[bk_, d["bep"]], writes=[d["bkb"]])
            p.op("gpsimd", lambda e, ks_=d["ks"], el_=d["el"], i=i: e.tensor_tensor(out=ks_[:], in0=k_[:, i, ts], in1=el_[:], op=ALU.mult), reads=[bk_, d["bel"]], writes=[d["bks"]])
        for i in U:
            d = st[i]
            d["ps"], d["bps"] = pS.next()
            p.op("tensor", lambda e, ps_=d["ps"], kf_=d["kf"], qf_=d["qf"]: e.matmul(ps_[:, 0:128], lhsT=kf_[:], rhs=qf_[:], start=True, stop=True), reads=[d["bkf"], d["bqf"]], writes=[d["bps"]])
            p.op("tensor", lambda e, ps_=d["ps"], kb_=d["kb"], qb_=d["qb"]: e.matmul(ps_[:, 128:256], lhsT=kb_[:], rhs=qb_[:], start=True, stop=True), reads=[d["bkb"], d["bqb"]], writes=[d["bps"]])
            d["ptb"], d["bptt"] = pT[i]
            p.op("tensor", lambda e, ptb=d["ptb"], ks_=d["ks"]: e.transpose(ptb[:, 0:64], ks_[:], ident[0:64, 0:64]), reads=[d["bks"], bid], writes=[d["bptt"]])
        for i in U:
            d = st[i]
            (d["s1"], d["bs1"]), (d["s2"], d["bs2"]), (d["S"], d["bS"]) = s1.next(), s2.next(), Sm.next()
            p.op("vector", lambda e, s1_=d["s1"], ps_=d["ps"]: e.tensor_tensor(out=s1_[:], in0=ps_[:, 0:128], in1=gm[:, 0:128], op=ALU.mult), reads=[d["bps"], bgm], writes=[d["bs1"]])
            p.op("vector", lambda e, s2_=d["s2"], ps_=d["ps"]: e.tensor_tensor(out=s2_[:], in0=ps_[:, 128:256], in1=gm[:, 128:256], op=ALU.mult), reads=[d["bps"], bgm], writes=[d["bs2"]])
            p.op("gpsimd", lambda e, S_=d["S"], s1_=d["s1"], s2_=d["s2"]: e.tensor_tensor(out=S_[:], in0=s1_[:], in1=s2_[:], op=ALU.add), reads=[d["bs1"], d["bs2"]], writes=[d["bS"]])
            d["kt"], d["bkt"] = kst.next()
            p.op("scalar", lambda e, kt_=d["kt"], ptb=d["ptb"]: e.activation(out=kt_[:], in_=ptb[:, 0:64], func=AF.Copy), reads=[d["bptt"]], writes=[d["bkt"]])
        for i in U:
            d = st[i]
            d["po"], d["bpo"] = pO[i]
            po_, bpo_ = d["po"], d["bpo"]
            sbc, bsbc = cur_sb[i]
            vv = v_[:, ti, i * 64:(i + 1) * 64]
            p.op("tensor", lambda e, po_=po_, S_=d["S"], vv=vv: e.matmul(po_[:, 0:64], lhsT=S_[:], rhs=vv, start=True, stop=False), reads=[d["bS"], bv_], writes=[bpo_])
            p.op("tensor", lambda e, po_=po_, qf_=d["qf"], sbc=sbc: e.matmul(po_[:, 0:64], lhsT=qf_[:], rhs=sbc[:], start=False, stop=True), reads=[d["bqf"], bsbc], writes=[bpo_])
            p.op("tensor", lambda e, po_=po_, kt_=d["kt"], vv=vv: e.matmul(po_[0:64, 64:128], lhsT=kt_[:], rhs=vv, start=True, stop=True), reads=[d["bkt"], bv_], writes=[bpo_])
        for i in U:
            d = st[i]
            po_, bpo_ = d["po"], d["bpo"]
            st_, bst_ = Sst[i]
            p.op("vector", lambda e, st_=st_, ep_=d["ep"], po_=po_: e.scalar_tensor_tensor(out=st_[:], in0=st_[:], scalar=ep_[:, 127:128], in1=po_[0:64, 64:128], op0=ALU.mult, op1=ALU.add),
                 reads=[bst_, d["bep"], bpo_], writes=[bst_])
            sbn, bsbn = Sstb[i].next()
            p.op("scalar", lambda e, sbn=sbn, st_=st_: e.activation(out=sbn[:], in_=st_[:], func=AF.Copy), reads=[bst_], writes=[bsbn])
            cur_sb[i] = (sbn, bsbn)
            p.op("scalar", lambda e, ot=ot, po_=po_, i=i: e.activation(out=ot[:, i * 64:(i + 1) * 64], in_=po_[:, 0:64], func=AF.Copy), reads=[bpo_], writes=[bot])
        p.dma("gpsimd", go_o[t * 128:(t + 1) * 128, :], ot[:], reads=[bot], writes=[p.buf()])

    for m in range(nwin):
        s5_window(m)
        for t in range(4 * m, 4 * m + 4):
            gla_tile(t)
    p.finish_all()
    p.emit()
    return c


def mix1_maps(inp, pre1, la1):
    a_re, a_im, ls = inp['s5_a_re'][0], inp['s5_a_im'][0], inp['s5_log_step'][0]
    b_re, b_im, c_re, c_im = inp['s5_b_re'][0], inp['s5_b_im'][0], inp['s5_c_re'][0], inp['s5_c_im'][0]
    iota = np.ascontiguousarray(np.broadcast_to(np.arange(512, dtype=np.float32)[None, :], (128, 512)))
    i = np.arange(128)
    mf = (i[None, :] >= i[:, None]).astype(np.float32)
    mb = ((i[None, :] < i[:, None]) & ((i[None, :] // 64) == (i[:, None] // 64))).astype(np.float32)
    gm = np.ascontiguousarray(np.concatenate([mf, mb], axis=1))
    idn = np.eye(128, dtype=np.float32)
    maps = []
    for c in range(NCORE):
        b, c4 = c // 4, c % 4
        rows = pre1[b * SEQ:(b + 1) * SEQ]
        lar = la1[b * SEQ:(b + 1) * SEQ]
        prmC = np.zeros((128, 6), np.float32)
        prmR = np.zeros((64, 3, 128), np.float32)
        bpad = np.zeros((64, 2, 128), np.float32)
        ct = np.zeros((128, 2, 2, 64), np.float32)
        for pr in range(2):
            for g2 in range(2):
                g = 4 * c4 + 2 * pr + g2
                ps = slice(64 * g2, 64 * g2 + 64)
                prmC[ps, pr * 3 + 0] = a_re[g]
                prmC[ps, pr * 3 + 1] = a_im[g]
                prmC[ps, pr * 3 + 2] = ls[g]
                prmR[32 * pr:32 * pr + 32, 0, ps] = a_re[g][None, :]
                prmR[32 * pr:32 * pr + 32, 1, ps] = a_im[g][None, :]
                prmR[32 * pr:32 * pr + 32, 2, ps] = ls[g]
                r0 = 32 * pr + 16 * g2
                bpad[r0:r0 + 16, 0, ps] = b_re[g].T
                bpad[r0:r0 + 16, 1, ps] = b_im[g].T
                gl = 2 * pr + g2
                ct[ps, pr, 0, 16 * gl:16 * gl + 16] = c_re[g].T
                ct[ps, pr, 1, 16 * gl:16 * gl + 16] = c_im[g].T
        gq = np.empty((3, 64, SEQ), rows.dtype); gk = np.empty((3, 64, SEQ), rows.dtype)
        gla = np.empty((3, 64, SEQ), np.float32); gv = np.empty((SEQ, 192), rows.dtype)
        for i3 in range(3):
            u = 3 * c4 + i3
            hd, half = u // 2, u % 2
            gq[i3] = rows[:, 256 + hd * 64:256 + (hd + 1) * 64].T
            gk[i3] = rows[:, 640 + hd * 64:640 + (hd + 1) * 64].T
            gla[i3] = lar[:, hd * 64:(hd + 1) * 64].T
            gv[:, i3 * 64:(i3 + 1) * 64] = rows[:, 1024 + hd * 128 + half * 64:1024 + hd * 128 + half * 64 + 64]
        maps.append({"uT": np.ascontiguousarray(rows[:, 64 * c4:64 * c4 + 64].T), "prmC": prmC, "prmR": prmR, "bpad": bpad, "ct": ct,
                     "dsk": np.ascontiguousarray(inp['s5_d'][0][64 * c4:64 * c4 + 64, None]), "iota": iota, "gm": gm, "idn": idn,
                     "gq": gq, "gk": gk, "gla": gla, "gv": gv})
    return maps


def assemble_mix1(results):
    yT = np.empty((256, 2 * SEQ), np.float32)
    oT = np.empty((768, 2 * SEQ), results[0]["go"].dtype)
    for c in range(NCORE):
        b, c4 = c // 4, c % 4
        yT[64 * c4:64 * c4 + 64, b * SEQ:(b + 1) * SEQ] = results[c]["yT"]
        go = results[c]["go"]
        for i3 in range(3):
            u = 3 * c4 + i3
            hd, half = u // 2, u % 2
            ch0 = hd * 128 + half * 64
            oT[ch0:ch0 + 64, b * SEQ:(b + 1) * SEQ] = go[:, i3 * 64:(i3 + 1) * 64].T
    return yT, oT


_CACHE = {}


def _prog(key, fn):
    if key not in _CACHE:
        _CACHE[key] = fn()
    return _CACHE[key]


def _run(c, maps):
    res = run_bass_kernel_spmd(c.nc, maps, core_ids=list(range(NCORE)))
    return res.results


def kernel(**inp):
    inp = {k: np.asarray(v) for k, v in inp.items()}
    x = np.ascontiguousarray(inp['x'].reshape(-1, D).astype(np.float32, copy=False))
    r = _run(_prog("pin0", lambda: build_pin(0)), pin0_maps(inp, x))
    pre0 = np.concatenate([q["out"] for q in r], 0)
    r = _run(_prog("mix0", lambda: build_mix0(32)), mix0_maps(inp, pre0))
    oT0 = assemble_oT0(np.stack([q["o"] for q in r], 0))
    r = _run(_prog("pout0", lambda: build_pout(0)), pout_maps(inp, 0, x, oT0))
    x1 = np.concatenate([q["xo"] for q in r], 0)
    r = _run(_prog("pin1", lambda: build_pin(1)), pin1_maps(inp, x1))
    pre1 = np.concatenate([q["out"] for q in r], 0)
    la1 = np.concatenate([q["la"] for q in r], 0)
    r = _run(_prog("mix1", lambda: build_mix1(32)), mix1_maps(inp, pre1, la1))
    yT, oT1 = assemble_mix1(r)
    rT = np.ascontiguousarray(pre1[:, 1792:2560].T)
    r = _run(_prog("pout1", lambda: build_pout(1)), pout_maps(inp, 1, x1, oT1, yT, rT))
    x2 = np.concatenate([q["xo"] for q in r], 0)
    return x2.reshape(inp['x'].shape).astype(np.float32, copy=False)
```

```python
import numpy as np
from contextlib import ExitStack
import concourse.bass as bass
import concourse.mybir as mybir
from concourse.bass_utils import run_bass_kernel_spmd

F32 = mybir.dt.float32
BF16 = mybir.dt.bfloat16
ALU = mybir.AluOpType
AF = mybir.ActivationFunctionType
AX = mybir.AxisListType

EPOCH = 16000
NDMA = 8


class Buf:
    __slots__ = ("name", "w", "r")

    def __init__(self, name=""):
        self.name = name
        self.w = None
        self.r = {}


class Prog:
    CE = ("tensor", "vector", "scalar", "gpsimd")
    DQ = ("sync", "gpsimd")

    def __init__(self, nc, es, nepoch=4):
        self.nc = nc
        self.es = es
        self.q = {e: [] for e in ("tensor", "vector", "scalar", "gpsimd", "sync")}
        self.cnt = {e: 0 for e in self.CE}
        self.vc = {e: {} for e in self.q}
        self.tokvc = {}
        self.tokeng = {}
        self.sems = {}
        for e in self.CE:
            for k in range(nepoch):
                self.sems[(e, k)] = es.enter_context(nc.semaphore(f"s_{e}_{k}"))
        self.dsem = {}
        self.dcnt = {}
        for qn in self.DQ:
            for k in range(NDMA):
                self.dsem[(qn, k)] = es.enter_context(nc.semaphore(f"d_{qn}_{k}"))
            self.dcnt[qn] = 0
        self.nbuf = 0

    def buf(self, name=""):
        self.nbuf += 1
        return Buf(name or f"b{self.nbuf}")

    def _need(self, eng, tok, waits):
        if tok is None:
            return
        sem, val = tok
        if self.vc[eng].get(sem, 0) >= val:
            return
        waits.append(tok)
        vc = self.vc[eng]
        for s, v in self.tokvc[tok].items():
            if vc.get(s, 0) < v:
                vc[s] = v

    def _deps(self, eng, reads, writes):
        waits = []
        for b in reads:
            if b.w is not None:
                if not (eng == "tensor" and self.tokeng.get(b.w) == "tensor"):
                    self._need(eng, b.w, waits)
        for b in writes:
            if b.w is not None and self.tokeng.get(b.w) != eng:
                self._need(eng, b.w, waits)
            for e2, t2 in b.r.items():
                if e2 != eng:
                    self._need(eng, t2, waits)
        return waits

    def _commit(self, eng, tok, reads, writes):
        vc = dict(self.vc[eng])
        vc[tok[0]] = max(vc.get(tok[0], 0), tok[1])
        self.tokvc[tok] = vc
        self.tokeng[tok] = eng
        for b in reads:
            b.r[eng] = tok
        for b in writes:
            b.w = tok
            b.r = {}

    def op(self, eng, fn, reads=(), writes=()):
        waits = self._deps(eng, reads, writes)
        self.cnt[eng] += 1
        c = self.cnt[eng] - 1
        sem = self.sems[(eng, c // EPOCH)]
        tok = (sem, c % EPOCH + 1)
        self.q[eng].append((waits, fn, (sem, 1)))
        self._commit(eng, tok, reads, writes)
        return tok

    def dma(self, qn, out, in_, reads=(), writes=(), **kw):
        eng = qn
        waits = self._deps(eng, reads, writes)
        n = self.dcnt[qn]
        self.dcnt[qn] += 1
        sem = self.dsem[(qn, n % NDMA)]
        k = n // NDMA
        if k > 0:
            prev = (sem, 16 * k)
            if self.vc[eng].get(sem, 0) < 16 * k:
                waits.append(prev)
                self.vc[eng][sem] = 16 * k
        tok = (sem, 16 * (k + 1))
        self.q[eng].append((waits, lambda e: e.dma_start(out=out, in_=in_, **kw), (sem, 16)))
        vc = dict(self.vc[eng])
        vc[sem] = 16 * (k + 1)
        self.tokvc[tok] = vc
        self.tokeng[tok] = "dma_" + qn
        for b in reads:
            b.r["dma_" + qn + str(n % NDMA)] = tok
        for b in writes:
            b.w = tok
            b.r = {}
        return tok

    def finish_all(self):
        waits = []
        for e in self.CE:
            cnt = self.cnt[e]
            if cnt:
                cc = cnt - 1
                self._need("sync", (self.sems[(e, cc // EPOCH)], cc % EPOCH + 1), waits)
        for qn in self.DQ:
            n = self.dcnt[qn]
            for j in range(max(0, n - NDMA), n):
                tok = (self.dsem[(qn, j % NDMA)], 16 * (j // NDMA + 1))
                self._need("sync", tok, waits)
        self.q["sync"].append((waits, None, None))

    def finish(self, bufs):
        waits = []
        for b in bufs:
            if b.w is not None:
                self._need("sync", b.w, waits)
        self.q["sync"].append((waits, None, None))

    def emit(self):
        nc = self.nc
        q = self.q

        def replay(e, name):
            for waits, fn, inc in q[name]:
                for sem, val in waits:
                    e.wait_ge(sem, val)
                if fn is not None:
                    ins = fn(e)
                    ins.then_inc(inc[0], inc[1])

        with nc.Block() as block:
            @block.tensor
            def _(e):
                replay(e, "tensor")

            @block.vector
            def _(e):
                replay(e, "vector")

            @block.scalar
            def _(e):
                replay(e, "scalar")

            @block.gpsimd
            def _(e):
                replay(e, "gpsimd")

            @block.sync
            def _(e):
                replay(e, "sync")


NCORE = 8
TOK = 4096
D = 1024
SEQ = 16384
EPS = 1e-6


def _mk(nc_name="TRN2"):
    return bass.Bass(nc_name, target_bir_lowering=False)


class Ctx:
    def __init__(self):
        self.nc = _mk()
        self.es = ExitStack()
        self.p = None
        self.nps = 0

    def start(self):
        self.p = Prog(self.nc, self.es)
        return self.p

    def din(self, name, shape, dt=F32):
        return self.nc.dram_tensor(name, list(shape), dt, kind="ExternalInput").ap()

    def dout(self, name, shape, dt=F32):
        return self.nc.dram_tensor(name, list(shape), dt, kind="ExternalOutput").ap()

    def sb(self, name, shape, dt=F32):
        return self.es.enter_context(self.nc.sbuf_tensor("sb_" + name, list(shape), dt))

    def ps(self, name, shape, dt=F32):
        return self.es.enter_context(self.nc.psum_tensor("ps_" + name, list(shape), dt))


class Rot:
    def __init__(self, items):
        self.items = items
        self.i = 0

    def next(self):
        it = self.items[self.i % len(self.items)]
        self.i += 1
        return it


def load_weight_bf16(c, p, wd, wb, wbuf, nk, ncol, gpk=None, bgpk=None, stage=None, colchunk=None):
    cc = colchunk or ncol
    engs = ("vector", "gpsimd")
    n = 0
    for k in range(nk):
        for c0 in range(0, ncol, cc):
            c1 = min(ncol, c0 + cc)
            st, bst = stage.next()
            p.dma("sync", st[:, 0:c1 - c0], wd[k * 128:(k + 1) * 128, c0:c1], writes=[bst])
            eng = engs[n % 2]
            n += 1
            if gpk is not None:
                p.op(eng, lambda e, st=st, k=k, c0=c0, c1=c1: e.tensor_scalar(
                    out=wb[:, k, c0:c1], in0=st[:, 0:c1 - c0], scalar1=gpk[:, k:k + 1], scalar2=None, op0=ALU.mult),
                    reads=[bst, bgpk], writes=[wbuf])
            else:
                p.op(eng, lambda e, st=st, k=k, c0=c0, c1=c1: e.tensor_copy(out=wb[:, k, c0:c1], in_=st[:, 0:c1 - c0]),
                     reads=[bst], writes=[wbuf])


def build_pin(layer):
    c = Ctx()
    nc = c.nc
    NCOL = 3072 if layer == 0 else 2944
    NOUT = 3584 if layer == 0 else 2560
    xT = c.din("xT", [D, TOK])
    w = c.din("w", [D, NCOL if layer == 0 else 2560])
    gpk_d = c.din("gpk", [128, 8])
    out = c.dout("out", [TOK, NOUT], BF16)
    if layer == 0:
        gq_d = c.din("gqk", [128, 1024])
        cs_d = c.din("cs", [128, 32, 64])
        t12_d = c.din("t12", [128, 1024])
    else:
        walrT_d = c.din("walrT", [16, D])
        wa2_d = c.din("wa2", [16, 384])
        ba2_d = c.din("ba2", [128, 384])
        la_out = c.dout("la", [TOK, 384], F32)
    p = c.start()
    sb = c.sb
    wb = sb("wb", [128, 8, NCOL], BF16); bwb = p.buf()
    gpk = sb("gpk", [128, 8]); bgpk = p.buf()
    p.dma("gpsimd", gpk[:], gpk_d[:, :], writes=[bgpk])
    stage = Rot([(sb(f"st{i}", [128, 1024]), p.buf()) for i in range(3)])
    if layer == 0:
        gq = sb("gq", [128, 1024]); bgq = p.buf()
        t12 = sb("t12", [128, 1024]); bt12 = p.buf()
        p.dma("gpsimd", gq[:], gq_d[:, :], writes=[bgq])
        p.dma("gpsimd", t12[:], t12_d[:, :], writes=[bt12])
        load_weight_bf16(c, p, w, wb, bwb, 8, NCOL, gpk, bgpk, stage, 1024)
    else:
        ba2 = sb("ba2", [128, 384]); bba2 = p.buf()
        p.dma("gpsimd", ba2[:], ba2_d[:, :], writes=[bba2])
        walrT = sb("walrT", [16, D]); bwalr = p.buf()
        wa2 = sb("wa2", [16, 384]); bwa2 = p.buf()
        p.dma("gpsimd", walrT[:], walrT_d[:, :], writes=[bwalr])
        p.dma("gpsimd", wa2[:], wa2_d[:, :], writes=[bwa2])
        load_weight_bf16(c, p, w, wb[:, :, 0:2560], bwb, 8, 2560, gpk, bgpk, stage, 1024)
        pw = c.ps("pw", [128, 512]); bpw = p.buf()
        for k in range(8):
            p.op("tensor", lambda e, k=k: e.matmul(pw[:, 0:384], lhsT=walrT[:, k * 128:(k + 1) * 128], rhs=wa2[:, :],
                                                   start=True, stop=True), reads=[bwalr, bwa2], writes=[bpw])
            p.op("vector", lambda e, k=k: e.tensor_scalar(out=wb[:, k, 2560:2944], in0=pw[:, 0:384], scalar1=gpk[:, k:k + 1],
                                                          scalar2=None, op0=ALU.mult), reads=[bpw, bgpk], writes=[bwb])
    nbank = (NCOL + 511) // 512
    banks = Rot([(c.ps(f"pb{i}", [128, 512]), p.buf()) for i in range(7 if layer == 1 else 8)])
    GT = 512
    sqx = sb("sqx", [128, 8, GT], BF16); bsqx = p.buf()
    onesb = sb("onesb", [128, 8], BF16); bonesb = p.buf()
    p.op("vector", lambda e: e.memset(onesb[:], 1.0), writes=[bonesb])
    xTin = Rot([(sb(f"xTin{i}", [128, 8, GT]), p.buf()) for i in range(2)])
    xTb = Rot([(sb(f"xTb{i}", [128, 8, GT], BF16), p.buf()) for i in range(2)])
    ss = Rot([(sb(f"ss{i}", [128, 1]), p.buf()) for i in range(4)])
    hq = Rot([(sb(f"hq{i}", [128, 1024]), p.buf()) for i in range(2)])
    sq = sb("sq", [128, 1024]); bsq = p.buf()
    qs = Rot([(sb(f"qs{i}", [128, 16]), p.buf()) for i in range(2)])
    ob = Rot([(sb(f"ob{i}", [128, NOUT], BF16), p.buf()) for i in range(2)])
    if layer == 0:
        cs = Rot([(sb(f"cs{i}", [128, 4, 64]), p.buf()) for i in range(2)])
        rq = Rot([(sb(f"rq{i}", [128, 512]), p.buf()) for i in range(2)])
        rt = [(sb(f"rt{i}", [128, 256]), p.buf()) for i in range(4)]
        ro = Rot([(sb(f"ro{i}", [128, 512]), p.buf()) for i in range(2)])
    else:
        la = Rot([(sb(f"la{i}", [128, 384]), p.buf()) for i in range(2)])
        le = Rot([(sb(f"le{i}", [128, 384]), p.buf()) for i in range(2)])
    xTv = xT.rearrange("(k p) t -> p k t", p=128)
    for g in range(TOK // GT):
        xti, bxti = xTin.next()
        xtb, bxtb = xTb.next()
        p.dma(("sync", "gpsimd")[g % 2], xti[:], xTv[:, :, g * GT:(g + 1) * GT], writes=[bxti])
        p.op("scalar", lambda e, xti=xti: e.activation(out=sqx[:], in_=xti[:], func=AF.Square), reads=[bxti], writes=[bsqx])
        p.op("gpsimd", lambda e, xtb=xtb, xti=xti: e.tensor_copy(out=xtb[:, 0:4, :], in_=xti[:, 0:4, :]), reads=[bxti], writes=[bxtb])
        p.op("vector", lambda e, xtb=xtb, xti=xti: e.tensor_copy(out=xtb[:, 4:8, :], in_=xti[:, 4:8, :]), reads=[bxti], writes=[bxtb])
        if layer == 0:
            csg, bcsg = cs.next()
            p.dma("gpsimd", csg[:], cs_d[:, g * 4:(g + 1) * 4, :], writes=[bcsg])
        for j in range(4):
            tt = g * 4 + j
            s1, bs1 = ss.next()
            pss, bpss = banks.next()
            for k in range(8):
                p.op("tensor", lambda e, pss=pss, k=k, j=j: e.matmul(pss[:, 0:8], lhsT=sqx[:, k, j * 128:(j + 1) * 128], rhs=onesb[:, 0:8], start=(k == 0), stop=(k == 7)),
                     reads=[bsqx, bonesb], writes=[bpss])
            p.op("scalar", lambda e, s1=s1, pss=pss: e.activation(out=s1[:], in_=pss[:, 0:1], func=AF.Sqrt, scale=1.0 / D, bias=EPS), reads=[bpss], writes=[bs1])
            p.op("vector", lambda e, s1=s1: e.reciprocal(out=s1[:], in_=s1[:]), reads=[bs1], writes=[bs1])
            pbs = []
            for b in range(nbank):
                pb, bpb = banks.next()
                c0, c1 = b * 512, min(NCOL, (b + 1) * 512)
                for k in range(8):
                    p.op("tensor", lambda e, pb=pb, xtb=xtb, k=k, j=j, c0=c0, c1=c1: e.matmul(
                        pb[:, 0:c1 - c0], lhsT=xtb[:, k, j * 128:(j + 1) * 128], rhs=wb[:, k, c0:c1], start=(k == 0), stop=(k == 7)),
                        reads=[bxtb, bwb], writes=[bpb])
                pbs.append((pb, bpb))
            o, bo = ob.next()
            if layer == 0:
                h, bh = hq.next()
                for b in range(2):
                    p.op("scalar", lambda e, b=b, h=h, s1=s1, pb=pbs[b][0]: e.activation(out=h[:, b * 512:(b + 1) * 512], in_=pb[:, :], func=AF.Copy, scale=s1[:, 0:1]),
                         reads=[pbs[b][1], bs1], writes=[bh])
                p.op("vector", lambda e, o=o, s1=s1, pb=pbs[2][0]: e.tensor_scalar(out=o[:, 1024:1536], in0=pb[:, :], scalar1=s1[:, 0:1], scalar2=None, op0=ALU.mult),
                     reads=[pbs[2][1], bs1], writes=[bo])
                p.op("vector", lambda e, o=o, s1=s1, pb=pbs[4][0]: e.tensor_scalar(out=o[:, 2560:3072], in0=pb[:, :], scalar1=s1[:, 0:1], scalar2=None, op0=ALU.mult),
                     reads=[pbs[4][1], bs1], writes=[bo])
                p.op("scalar", lambda e, o=o, s1=s1, pb=pbs[5][0]: e.activation(out=o[:, 3072:3584], in_=pb[:, :], func=AF.Silu, scale=s1[:, 0:1]),
                     reads=[pbs[5][1], bs1], writes=[bo])
                r, br = rq.next()
                p.op("scalar", lambda e, r=r, s1=s1, pb=pbs[3][0]: e.activation(out=r[:], in_=pb[:, :], func=AF.Copy, scale=s1[:, 0:1]),
                     reads=[pbs[3][1], bs1], writes=[br])
                q1, bq1 = qs.next()
                h3 = h[:].rearrange("p (g d) -> p g d", d=64)
                p.op("gpsimd", lambda e, h=h: e.tensor_tensor(out=sq[:], in0=h[:], in1=h[:], op=ALU.mult), reads=[bh], writes=[bsq])
                p.op("vector", lambda e, q1=q1: e.tensor_reduce(out=q1[:], in_=sq[:].rearrange("p (g d) -> p g d", d=64), axis=AX.X, op=ALU.add),
                     reads=[bsq], writes=[bq1])
                p.op("scalar", lambda e, q1=q1: e.activation(out=q1[:], in_=q1[:], func=AF.Sqrt, scale=1.0 / 64, bias=EPS), reads=[bq1], writes=[bq1])
                p.op("vector", lambda e, q1=q1: e.reciprocal(out=q1[:], in_=q1[:]), reads=[bq1], writes=[bq1])
                p.op("vector", lambda e, h3=h3, q1=q1: e.tensor_tensor(out=h3, in0=h3, in1=q1[:].unsqueeze(2).to_broadcast([128, 16, 64]), op=ALU.mult),
                     reads=[bh, bq1], writes=[bh])
                p.op("gpsimd", lambda e, h=h, o=o: e.tensor_tensor(out=o[:, 0:1024], in0=h[:], in1=gq[:], op=ALU.mult), reads=[bh, bgq], writes=[bo])
                r4 = r[:].rearrange("p (a two d) -> p a two d", two=2, d=32)
                x1, x2 = r4[:, :, 0, :], r4[:, :, 1, :]
                cosv = csg[:, j, 0:32].unsqueeze(1).to_broadcast([128, 8, 32])
                sinv = csg[:, j, 32:64].unsqueeze(1).to_broadcast([128, 8, 32])
                (ta, bta), (tb_, btb), (tc, btc), (td, btd) = rt
                v3 = lambda t: t[:].rearrange("p (a d) -> p a d", d=32)
                rr, brr = ro.next()
                rr4 = rr[:].rearrange("p (a two d) -> p a two d", two=2, d=32)
                p.op("vector", lambda e, x1=x1, cosv=cosv: e.tensor_tensor(out=v3(ta), in0=x1, in1=cosv, op=ALU.mult), reads=[br, bcsg], writes=[bta])
                p.op("vector", lambda e, x2=x2, sinv=sinv: e.tensor_tensor(out=v3(tb_), in0=x2, in1=sinv, op=ALU.mult), reads=[br, bcsg], writes=[btb])
                p.op("vector", lambda e, x1=x1, sinv=sinv: e.tensor_tensor(out=v3(tc), in0=x1, in1=sinv, op=ALU.mult), reads=[br, bcsg], writes=[btc])
                p.op("vector", lambda e, x2=x2, cosv=cosv: e.tensor_tensor(out=v3(td), in0=x2, in1=cosv, op=ALU.mult), reads=[br, bcsg], writes=[btd])
                p.op("vector", lambda e, rr4=rr4: e.tensor_tensor(out=rr4[:, :, 0, :], in0=v3(ta), in1=v3(tb_), op=ALU.subtract), reads=[bta, btb], writes=[brr])
                p.op("gpsimd", lambda e, rr4=rr4: e.tensor_tensor(out=rr4[:, :, 1, :], in0=v3(tc), in1=v3(td), op=ALU.add), reads=[btc, btd], writes=[brr])
                p.op("vector", lambda e, rr=rr, o=o: e.tensor_tensor(out=o[:, 1536:2048], in0=rr[:], in1=t12[:, 0:512], op=ALU.mult), reads=[brr, bt12], writes=[bo])
                p.op("gpsimd", lambda e, rr=rr, o=o: e.tensor_tensor(out=o[:, 2048:2560], in0=rr[:], in1=t12[:, 512:1024], op=ALU.mult), reads=[brr, bt12], writes=[bo])
            else:
                for b in range(4):
                    eng = ("vector", "scalar")[b % 2]
                    cw = 256 if b == 3 else 512
                    if eng == "vector":
                        p.op("vector", lambda e, o=o, s1=s1, b=b, cw=cw, pb=pbs[b][0]: e.tensor_scalar(out=o[:, b * 512:b * 512 + cw], in0=pb[:, 0:cw], scalar1=s1[:, 0:1], scalar2=None, op0=ALU.mult),
                             reads=[pbs[b][1], bs1], writes=[bo])
                    else:
                        p.op("scalar", lambda e, o=o, s1=s1, b=b, cw=cw, pb=pbs[b][0]: e.activation(out=o[:, b * 512:b * 512 + cw], in_=pb[:, 0:cw], func=AF.Copy, scale=s1[:, 0:1]),
                             reads=[pbs[b][1], bs1], writes=[bo])
                p.op("gpsimd", lambda e, o=o: e.tensor_scalar(out=o[:, 256:640], in0=o[:, 256:640], scalar1=0.125, scalar2=None, op0=ALU.mult), reads=[bo], writes=[bo])
                p.op("scalar", lambda e, o=o, s1=s1, pb=pbs[3][0]: e.activation(out=o[:, 1792:2048], in_=pb[:, 256:512], func=AF.Silu, scale=s1[:, 0:1]),
                     reads=[pbs[3][1], bs1], writes=[bo])
                p.op("scalar", lambda e, o=o, s1=s1, pb=pbs[4][0]: e.activation(out=o[:, 2048:2560], in_=pb[:, :], func=AF.Silu, scale=s1[:, 0:1]),
                     reads=[pbs[4][1], bs1], writes=[bo])
                l1, bl1 = la.next()
                l2, bl2 = le.next()
                p.op("vector", lambda e, l1=l1, s1=s1, pb=pbs[5][0]: e.scalar_tensor_tensor(out=l1[:], in0=pb[:, 0:384], scalar=s1[:, 0:1], in1=ba2[:], op0=ALU.mult, op1=ALU.add),
                     reads=[pbs[5][1], bs1, bba2], writes=[bl1])
                p.op("scalar", lambda e, l1=l1: e.activation(out=l1[:], in_=l1[:], func=AF.Exp, scale=-1.0), reads=[bl1], writes=[bl1])
                p.op("scalar", lambda e, l1=l1: e.activation(out=l1[:], in_=l1[:], func=AF.Ln, bias=1.0), reads=[bl1], writes=[bl1])
                p.op("gpsimd", lambda e, l1=l1, l2=l2: e.tensor_scalar(out=l2[:], in0=l1[:], scalar1=-1.0 / 16, scalar2=None, op0=ALU.mult), reads=[bl1], writes=[bl2])
                p.dma("gpsimd", la_out[tt * 128:(tt + 1) * 128, :], l2[:], reads=[bl2], writes=[p.buf()])
            p.dma("sync", out[tt * 128:(tt + 1) * 128, :], o[:], reads=[bo], writes=[p.buf()])
    p.finish_all()
    p.emit()
    return c


def _rot_tables():
    half = 32
    inv_freq = (1.0 / (10000.0 ** np.linspace(0.0, 1.0, half, dtype=np.float32))).astype(np.float32)
    ang = (np.arange(SEQ, dtype=np.float32)[:, None] * inv_freq[None, :]).astype(np.float32)
    return np.cos(ang).astype(np.float32), np.sin(ang).astype(np.float32)


def _ret_log_gamma():
    return np.log(1.0 - 2.0 ** (-5.0 - np.arange(4, dtype=np.float32))).astype(np.float32)


def _gpk(g):
    return np.ascontiguousarray(g.reshape(8, 128).T)


def pin0_maps(inp, x):
    cos, sin = _rot_tables()
    lg = _ret_log_gamma()
    i = np.arange(128, dtype=np.float32)
    qw = np.exp((i + 1.0)[:, None] * lg[None, :])
    kw = np.exp((127.0 - i)[:, None] * lg[None, :]) * 0.125
    t12 = np.empty((128, 1024), np.float32)
    t12[:, 0:256] = 1.0
    t12[:, 256:512] = 0.125
    t12[:, 512:768] = np.repeat(qw, 64, axis=1)
    t12[:, 768:1024] = np.repeat(kw, 64, axis=1)
    gqk = np.concatenate([np.tile(inp['da_q_norm'][0], 8), np.tile(inp['da_k_norm'][0], 8)])
    gqk = np.ascontiguousarray(np.broadcast_to(gqk[None, :], (128, 1024)))
    gpk = _gpk(inp['norm_mix_g'][0])
    w = np.ascontiguousarray(inp['ab_w_in'][0])
    maps = []
    for c in range(NCORE):
        xs = x[c * TOK:(c + 1) * TOK]
        pos0 = (c % 4) * TOK
        cc = cos[pos0:pos0 + TOK].reshape(32, 128, 32).transpose(1, 0, 2)
        sn = sin[pos0:pos0 + TOK].reshape(32, 128, 32).transpose(1, 0, 2)
        cs = np.concatenate([cc, sn], axis=2)
        maps.append({"xT": np.ascontiguousarray(xs.T), "w": w, "gpk": gpk,
                     "gqk": gqk, "cs": np.ascontiguousarray(cs), "t12": t12})
    return maps


NT = SEQ // 128


def build_mix0(nqg=32):
    c = Ctx()
    nc = c.nc
    qaT_d = c.din("qaT", [128, SEQ], BF16)
    kaT_d = c.din("kaT", [128, SEQ], BF16)
    va_d = c.din("va", [SEQ, 128], BF16)
    qrT_d = c.din("qrT", [64, SEQ], BF16)
    krT_d = c.din("krT", [64, SEQ], BF16)
    qwrT_d = c.din("qwrT", [64, SEQ], BF16)
    kwr_d = c.din("kwr", [SEQ, 64], BF16)
    vr_d = c.din("vr", [SEQ, 128], BF16)
    gr_d = c.din("gr", [SEQ, 128], BF16)
    cst_d = c.din("cst", [128, 512])
    lam_d = c.din("lamv", [128, 256])
    out = c.dout("o", [SEQ, 256], BF16)
    p = c.start()
    sb = c.sb
    kaT = sb("kaT", [128, SEQ], BF16); bkaT = [p.buf() for _ in range(8)]
    va = sb("va", [128, NT, 132], BF16); bva = [p.buf() for _ in range(8)]
    cst = sb("cst", [128, 512]); bcst = p.buf()
    lamv = sb("lamv", [128, 256]); blam = p.buf()
    p.dma("gpsimd", cst[:], cst_d[:, :], writes=[bcst])
    p.dma("gpsimd", lamv[:], lam_d[:, :], writes=[blam])
    vav = va_d.rearrange("(n p) e -> p n e", p=128)
    for i in range(8):
        p.dma("sync", kaT[:, i * 2048:(i + 1) * 2048], kaT_d[:, i * 2048:(i + 1) * 2048], writes=[bkaT[i]])
        p.dma("sync", va[:, i * 16:(i + 1) * 16, 0:128], vav[:, i * 16:(i + 1) * 16, :], writes=[bva[i]])
        p.op("gpsimd", lambda e, i=i: e.memset(va[:, i * 16:(i + 1) * 16, 128:129], 1.0), writes=[bva[i]])
    lt = sb("lt", [128, 128]); blt = p.buf()
    l2 = sb("l2", [128, 2]); bl2 = p.buf()
    lam = sb("lam", [128, 1]); blamS = p.buf()
    lv = lamv[:].rearrange("p (a two d) -> p a two d", two=2, d=64)
    p.op("vector", lambda e: e.tensor_tensor(out=lt[:].rearrange("p (a d) -> p a d", d=64), in0=lv[:, :, 0, :], in1=lv[:, :, 1, :], op=ALU.mult),
         reads=[blam], writes=[blt])
    p.op("vector", lambda e: e.tensor_reduce(out=l2[:], in_=lt[:].rearrange("p (a d) -> p a d", d=64), axis=AX.X, op=ALU.add), reads=[blt], writes=[bl2])
    p.op("scalar", lambda e: e.activation(out=l2[:], in_=l2[:], func=AF.Exp), reads=[bl2], writes=[bl2])
    p.op("vector", lambda e: e.tensor_tensor(out=lam[:], in0=l2[:, 0:1], in1=l2[:, 1:2], op=ALU.subtract), reads=[bl2], writes=[blamS])
    p.op("vector", lambda e: e.tensor_scalar(out=lam[:], in0=lam[:], scalar1=0.2, scalar2=None, op0=ALU.add), reads=[blamS], writes=[blamS])

    psS = Rot([(c.ps(f"pS{i}", [128, 512]), p.buf()) for i in range(4)])
    accb = [c.ps(f"pA{i}", [128, 512]) for i in range(3)]
    accbuf = [p.buf() for _ in range(3)]
    acc = []
    for i in range(8):
        bk, off = i // 3, (i % 3) * 136
        acc.append((accb[bk][:, off:off + 129], accbuf[bk]))
    pR = c.ps("pR", [128, 512])
    bpRs = bpRo = bpRkv = p.buf()
    qin = Rot([(sb(f"qin{i}", [128, 512], BF16), p.buf()) for i in range(2)])
    pT = Rot([(sb(f"pT{i}", [128, 512], BF16), p.buf()) for i in range(4)])
    fin = {k: Rot([(sb(f"{k}{i}", shp), p.buf()) for i in range(2)]) for k, shp in
           (("fr", [128, 4]), ("ft", [128, 128]), ("fo", [128, 128]), ("fs", [128, 1]))}
    junk = sb("junk", [128, 128]); bjunk = p.buf()
    obuf = Rot([(sb(f"ob{i}", [128, 256], BF16), p.buf()) for i in range(8)])
    CH = 2048
    rq = Rot([(sb(f"rq{i}", [64, CH], BF16), p.buf()) for i in range(2)])
    rk = Rot([(sb(f"rk{i}", [64, CH], BF16), p.buf()) for i in range(2)])
    rqw = Rot([(sb(f"rqw{i}", [64, CH], BF16), p.buf()) for i in range(2)])
    rkw = Rot([(sb(f"rkw{i}", [128, 16, 64], BF16), p.buf()) for i in range(2)])
    rv = Rot([(sb(f"rv{i}", [128, 16, 128], BF16), p.buf()) for i in range(2)])
    rg = Rot([(sb(f"rg{i}", [128, 16, 128], BF16), p.buf()) for i in range(2)])
    Sf = sb("Sf", [64, 128]); bSf = p.buf()
    Sb = Rot([(sb(f"Sb{i}", [64, 128], BF16), p.buf()) for i in range(2)])
    sm = Rot([(sb(f"sm{i}", [128, 128], BF16), p.buf()) for i in range(2)])
    gt = Rot([(sb(f"gt{i}", [128, 128]), p.buf()) for i in range(2)])
    rs_ = Rot([(sb(f"rs{i}", [128, 1]), p.buf()) for i in range(2)])
    p.op("vector", lambda e: e.memset(Sf[:], 0.0), writes=[bSf])
    sb0, bsb0 = Sb.next()
    p.op("vector", lambda e: e.memset(sb0[:], 0.0), writes=[bsb0])
    ret_state = {"cur": None, "sb": (sb0, bsb0)}
    otiles = {}

    def get_ot(t):
        if t not in otiles:
            otiles[t] = obuf.next() + ([0],)
        return otiles[t]

    def done_half(t):
        o, bo, cnt = otiles[t]
        cnt[0] += 1
        if cnt[0] == 2:
            p.dma("gpsimd", out[t * 128:(t + 1) * 128, :], o[:], reads=[bo], writes=[p.buf()])
            del otiles[t]

    def ret_tile(t):
        ci, ti = t // 16, t % 16
        if ti == 0:
            cur = (rq.next(), rk.next(), rqw.next(), rkw.next(), rv.next(), rg.next())
            sl = slice(ci * CH, (ci + 1) * CH)
            p.dma("sync", cur[0][0][:], qrT_d[:, sl], writes=[cur[0][1]])
            p.dma("sync", cur[1][0][:], krT_d[:, sl], writes=[cur[1][1]])
            p.dma("sync", cur[2][0][:], qwrT_d[:, sl], writes=[cur[2][1]])
            p.dma("sync", cur[3][0][:], kwr_d.rearrange("(n p) e -> p n e", p=128)[:, ci * 16:(ci + 1) * 16, :], writes=[cur[3][1]])
            p.dma("sync", cur[4][0][:], vr_d.rearrange("(n p) e -> p n e", p=128)[:, ci * 16:(ci + 1) * 16, :], writes=[cur[4][1]])
            p.dma("sync", cur[5][0][:], gr_d.rearrange("(n p) e -> p n e", p=128)[:, ci * 16:(ci + 1) * 16, :], writes=[cur[5][1]])
            ret_state["cur"] = cur
        (q_, bq_), (k_, bk_), (qw_, bqw_), (kw_, bkw_), (v_, bv_), (g_, bg_) = ret_state["cur"]
        ts = slice(ti * 128, (ti + 1) * 128)
        Sbc, bSbc = ret_state["sb"]
        p.op("tensor", lambda e: e.matmul(pR[:, 0:128], lhsT=k_[:, ts], rhs=q_[:, ts], start=True, stop=True), reads=[bk_, bq_], writes=[bpRs])
        s_, bs_ = sm.next()
        p.op("vector", lambda e: e.tensor_tensor(out=s_[:], in0=pR[:, 0:128], in1=cst[:, 0:128], op=ALU.mult), reads=[bpRs, bcst], writes=[bs_])
        p.op("tensor", lambda e: e.matmul(pR[:, 128:256], lhsT=s_[:], rhs=v_[:, ti, :], start=True, stop=False), reads=[bs_, bv_], writes=[bpRo])
        p.op("tensor", lambda e: e.matmul(pR[:, 128:256], lhsT=qw_[:, ts], rhs=Sbc[:], start=False, stop=True), reads=[bqw_, bSbc], writes=[bpRo])
        p.op("tensor", lambda e: e.matmul(pR[0:64, 256:384], lhsT=kw_[:, ti, :], rhs=v_[:, ti, :], start=True, stop=True), reads=[bkw_, bv_], writes=[bpRkv])
        p.op("vector", lambda e: e.scalar_tensor_tensor(out=Sf[:], in0=Sf[:], scalar=cst[0:64, 384:385], in1=pR[0:64, 256:384], op0=ALU.mult, op1=ALU.add),
             reads=[bSf, bpRkv, bcst], writes=[bSf])
        Sbn, bSbn = Sb.next()
        p.op("scalar", lambda e: e.activation(out=Sbn[:], in_=Sf[:], func=AF.Copy), reads=[bSf], writes=[bSbn])
        ret_state["sb"] = (Sbn, bSbn)
        r1, br1 = rs_.next()
        p.op("scalar", lambda e: e.activation(out=junk[:], in_=pR[:, 128:256], func=AF.Square, accum_out=r1[:]), reads=[bpRo], writes=[bjunk, br1])
        p.op("scalar", lambda e: e.activation(out=r1[:], in_=r1[:], func=AF.Sqrt, scale=1.0 / 128, bias=EPS), reads=[br1], writes=[br1])
        p.op("vector", lambda e: e.reciprocal(out=r1[:], in_=r1[:]), reads=[br1], writes=[br1])
        g1, bg1 = gt.next()
        p.op("gpsimd", lambda e: e.tensor_tensor(out=g1[:], in0=g_[:, ti, :], in1=cst[:, 256:384], op=ALU.mult), reads=[bg_, bcst], writes=[bg1])
        o, bo, _ = get_ot(t)
        p.op("vector", lambda e: e.scalar_tensor_tensor(out=o[:, 128:256], in0=pR[:, 128:256], scalar=r1[:, 0:1], in1=g1[:], op0=ALU.mult, op1=ALU.mult),
             reads=[bpRo, br1, bg1], writes=[bo])
        done_half(t)

    def da_final(qt_glob, a0, a1):
        (A0, bA0), (A1, bA1) = a0, a1
        fr, bfr = fin["fr"].next()
        ft, bft = fin["ft"].next()
        fo, bfo = fin["fo"].next()
        fs, bfs = fin["fs"].next()
        p.op("vector", lambda e: e.reciprocal(out=fr[:, 0:1], in_=A0[:, 128:129]), reads=[bA0], writes=[bfr])
        p.op("vector", lambda e: e.reciprocal(out=fr[:, 1:2], in_=A1[:, 128:129]), reads=[bA1], writes=[bfr])
        p.op("vector", lambda e: e.tensor_tensor(out=fr[:, 2:3], in0=fr[:, 1:2], in1=lam[:, 0:1], op=ALU.mult), reads=[bfr, blamS], writes=[bfr])
        p.op("vector", lambda e: e.tensor_scalar(out=ft[:], in0=A1[:, 0:128], scalar1=fr[:, 2:3], scalar2=None, op0=ALU.mult), reads=[bA1, bfr], writes=[bft])
        p.op("vector", lambda e: e.scalar_tensor_tensor(out=fo[:], in0=A0[:, 0:128], scalar=fr[:, 0:1], in1=ft[:], op0=ALU.mult, op1=ALU.subtract),
             reads=[bA0, bfr, bft], writes=[bfo])
        p.op("scalar", lambda e: e.activation(out=junk[:], in_=fo[:], func=AF.Square, accum_out=fs[:]), reads=[bfo], writes=[bjunk, bfs])
        p.op("scalar", lambda e: e.activation(out=fs[:], in_=fs[:], func=AF.Sqrt, scale=1.0 / 128, bias=EPS), reads=[bfs], writes=[bfs])
        p.op("vector", lambda e: e.reciprocal(out=fs[:], in_=fs[:]), reads=[bfs], writes=[bfs])
        o, bo, _ = get_ot(qt_glob)
        p.op("vector", lambda e: e.scalar_tensor_tensor(out=o[:, 0:128], in0=fo[:], scalar=fs[:, 0:1], in1=cst[:, 128:256], op0=ALU.mult, op1=ALU.mult),
             reads=[bfo, bfs, bcst], writes=[bo])
        done_half(qt_glob)

    steps = [(qg, kt) for qg in range(nqg) for kt in range(4 * qg + 4)]
    qcur = {}

    def stage_a(qg, kt):
        if kt == 0:
            qi, bqi = qin.next()
            p.dma("sync", qi[:], qaT_d[:, qg * 512:(qg + 1) * 512], writes=[bqi])
            qcur[qg] = (qi, bqi)
        qi, bqi = qcur[qg]
        j = kt - 4 * qg
        q0 = max(0, j)
        qs = q0 * 128
        kb = bkaT[kt // 16]
        pts = []
        for m in range(2):
            pS, bpS = psS.next()
            p.op("tensor", lambda e, pS=pS, m=m, kt=kt, qs=qs, qi=qi: e.matmul(pS[:, qs:512], lhsT=kaT[m * 64:(m + 1) * 64, kt * 128:(kt + 1) * 128],
                                                                         rhs=qi[m * 64:(m + 1) * 64, qs:512], start=True, stop=True),
                 reads=[kb, bqi], writes=[bpS])
            pt, bpt = pT.next()
            p.op("scalar", lambda e, pS=pS, pt=pt, qs=qs: e.activation(out=pt[:, qs:512], in_=pS[:, qs:512], func=AF.Exp, scale=0.125, bias=-8.0),
                 reads=[bpS], writes=[bpt])
            if j >= 0:
                p.op("gpsimd", lambda e, pt=pt, qs=qs: e.memset(pt[64:128, qs:qs + 64], 0.0), writes=[bpt])
            pts.append((pt, bpt))
        return pts

    def stage_b(qg, kt, pts):
        j = kt - 4 * qg
        q0 = max(0, j)
        for qt in range(q0, 4):
            for m in range(2):
                A, bA = acc[qt * 2 + m]
                pt, bpt = pts[m]
                p.op("tensor", lambda e, A=A, pt=pt, qt=qt, kt=kt, st=(kt == 0 and (qt * 2 + m) % 3 == 0), sp=(kt == 4 * qg + qt): e.matmul(
                    A, lhsT=pt[:, qt * 128:(qt + 1) * 128], rhs=va[:, kt, 0:129], start=st, stop=sp, skip_group_check=True),
                    reads=[bpt, bva[kt // 16]], writes=[bA])
        if kt == 4 * qg + 3:
            for qt in range(4):
                da_final(qg * 4 + qt, acc[qt * 2], acc[qt * 2 + 1])
            for t in range(qg * 4, qg * 4 + 4):
                ret_tile(t)

    cur = stage_a(*steps[0])
    for i, (qg, kt) in enumerate(steps):
        nxt = stage_a(*steps[i + 1]) if i + 1 < len(steps) else None
        stage_b(qg, kt, cur)
        cur = nxt
    p.finish_all()
    p.emit()
    return c


def mix0_maps(inp, pre):
    lg = _ret_log_gamma()
    i = np.arange(128)
    maps = []
    for c in range(NCORE):
        b, h = c // 4, c % 4
        rows = pre[b * SEQ:(b + 1) * SEQ]
        gam = np.exp(lg[h]).astype(np.float32)
        dist = np.abs(i[:, None] - i[None, :]).astype(np.float32)
        mret = np.exp(lg[h] * dist) * ((i[:, None] // 64) <= (i[None, :] // 64))
        cst = np.zeros((128, 512), np.float32)
        cst[:, 0:128] = mret
        cst[:, 128:256] = inp['da_out_norm'][0][None, :] * np.float32(0.8)
        cst[:, 256:384] = inp['ret_out_norm'][0][None, :]
        cst[:, 384] = np.exp(np.float32(128.0) * lg[h])
        lamv = np.concatenate([inp['da_lam_q1'][0], inp['da_lam_k1'][0], inp['da_lam_q2'][0], inp['da_lam_k2'][0]])
        lamv = np.ascontiguousarray(np.broadcast_to(lamv[None, :], (128, 256))).astype(np.float32)
        ct = lambda a: np.ascontiguousarray(a)
        maps.append({
            "qaT": ct(rows[:, h * 128:(h + 1) * 128].T), "kaT": ct(rows[:, 512 + h * 128:512 + (h + 1) * 128].T),
            "va": ct(rows[:, 1024 + h * 128:1024 + (h + 1) * 128]),
            "qrT": ct(rows[:, 1536 + h * 64:1536 + (h + 1) * 64].T), "krT": ct(rows[:, 1792 + h * 64:1792 + (h + 1) * 64].T),
            "qwrT": ct(rows[:, 2048 + h * 64:2048 + (h + 1) * 64].T), "kwr": ct(rows[:, 2304 + h * 64:2304 + (h + 1) * 64]),
            "vr": ct(rows[:, 2560 + h * 128:2560 + (h + 1) * 128]), "gr": ct(rows[:, 3072 + h * 128:3072 + (h + 1) * 128]),
            "cst": cst, "lamv": lamv})
    return maps


def build_pout(layer, ngrp=None):
    c = Ctx()
    nc = c.nc
    GT = 256
    NG = TOK // GT if ngrp is None else ngrp
    x = c.din("x", [TOK, D])
    nko = 8 if layer == 0 else 6
    oT_d = c.din("oT", [nko * 128, TOK], BF16)
    wout_d = c.din("wout", [D, D])
    wup_d = c.din("wup", [D, 4096])
    wdn_d = c.din("wdn", [4096, D])
    gpk_d = c.din("gpk", [128, 8])
    ident_d = c.din("ident", [128, 128])
    if layer == 1:
        yT_d = c.din("yT", [256, TOK])
        wglu_d = c.din("wglu", [256, 256])
        rT_d = c.din("rT", [768, TOK], BF16)
        gng_d = c.din("gng", [128, 1])
    out = c.dout("xo", [TOK, D])
    p = c.start()
    sb = c.sb
    woutb = sb("woutb", [128, 8, D], BF16); bwout = p.buf()
    wupb = sb("wupb", [128, 8, 4096], BF16); bwup = p.buf()
    wdnb = sb("wdnb", [128, 32, D], BF16); bwdn = p.buf()
    gpk = sb("gpk", [128, 8]); bgpk = p.buf()
    identf = sb("identf", [128, 128]); bidf = p.buf()
    ident = sb("ident", [128, 128], BF16); bid = p.buf()
    p.dma("gpsimd", gpk[:], gpk_d[:, :], writes=[bgpk])
    p.dma("gpsimd", identf[:], ident_d[:, :], writes=[bidf])
    p.op("vector", lambda e: e.tensor_copy(out=ident[:], in_=identf[:]), reads=[bidf], writes=[bid])
    xin = Rot([(sb(f"xin{i}", [128, D]), p.buf()) for i in range(2)])
    stage = xin
    load_weight_bf16(c, p, wout_d, woutb, bwout, 8, D, None, None, stage, 1024)
    if layer == 1:
        wglub = sb("wglub", [128, 2, 256], BF16); bwglu = p.buf()
        load_weight_bf16(c, p, wglu_d, wglub, bwglu, 2, 256, None, None, stage, 256)
    load_weight_bf16(c, p, wup_d, wupb, bwup, 8, 4096, gpk, bgpk, stage, 1024)
    load_weight_bf16(c, p, wdn_d, wdnb, bwdn, 32, D, None, None, stage, 1024)
    accD = [(c.ps(f"pD{i}", [128, 512]), p.buf()) for i in range(4)]
    pH = Rot([(c.ps(f"pH{i}", [128, 512]), p.buf()) for i in range(4)])
    oin = Rot([(sb(f"oin{i}", [128, 8, GT], BF16), p.buf()) for i in range(2)])
    x1 = Rot([(sb(f"x1_{i}", [128, 2, D]), p.buf()) for i in range(2)])
    x1b = Rot([(sb(f"x1b{i}", [128, D], BF16), p.buf()) for i in range(2)])
    x1T = Rot([(sb(f"x1T{i}", [128, 8, GT], BF16), p.buf()) for i in range(1)])
    rsd = Rot([(sb(f"rsd{i}", [128, 2]), p.buf()) for i in range(4)])
    junk = sb("junk", [128, D], BF16); bjunk = p.buf()
    hr = Rot([(sb(f"hr{i}", [128, GT], BF16), p.buf()) for i in range(3)])
    h2 = Rot([(sb(f"h2{i}", [128, GT], BF16), p.buf()) for i in range(4)])
    if layer == 1:
        yin = Rot([(sb(f"yin{i}", [128, 2, GT]), p.buf()) for i in range(1)])
        ga = sb("ga", [128, 2, GT]); bga = p.buf()
        gb = sb("gb", [128, 2, GT]); bgb = p.buf()
        gz, bgz = ga, bga
        gng = sb("gng", [128, 1]); bgng = p.buf()
        p.dma("gpsimd", gng[:], gng_d[:, :], writes=[bgng])
        onesb = sb("onesb", [128, 128], BF16); bonesb = p.buf()
        p.op("vector", lambda e: e.memset(onesb[:], 1.0), writes=[bonesb])
        rin = Rot([(sb(f"rin{i}", [128, 6, GT], BF16), p.buf()) for i in range(1)])
        nsq = Rot([(sb(f"nsq{i}", [128, GT], BF16), p.buf()) for i in range(2)])
        nrs = Rot([(sb(f"nrs{i}", [128, GT]), p.buf()) for i in range(2)])
        ntm = Rot([(sb(f"ntm{i}", [128, GT]), p.buf()) for i in range(1)])
        gzb = sb("gzb", [128, 2, GT], BF16); bgzb = p.buf()
        gs = sb("gs", [128, GT]); bgs = p.buf()
    xv = x.rearrange("(n p) d -> p n d", p=128)
    ov = oT_d.rearrange("(k p) t -> p k t", p=128)
    for g in range(NG):
        oi, boi = oin.next()
        k0 = 8 - nko
        p.dma("gpsimd", oi[:, k0:8, :], ov[:, :, g * GT:(g + 1) * GT], writes=[boi])
        if layer == 1:
            yi, byi = yin.next()
            p.dma("sync", yi[:], yT_d.rearrange("(k p) t -> p k t", p=128)[:, :, g * GT:(g + 1) * GT], writes=[byi])
            p.op("gpsimd", lambda e, yi=yi: e.tensor_tensor(out=ga[:], in0=yi[:], in1=yi[:], op=ALU.mult), reads=[byi], writes=[bga])
            p.op("vector", lambda e: e.tensor_scalar(out=ga[:], in0=ga[:], scalar1=0.044715, scalar2=1.0, op0=ALU.mult, op1=ALU.add), reads=[bga], writes=[bga])
            p.op("gpsimd", lambda e, yi=yi: e.tensor_tensor(out=gb[:], in0=ga[:], in1=yi[:], op=ALU.mult), reads=[bga, byi], writes=[bgb])
            p.op("scalar", lambda e: e.activation(out=gb[:], in_=gb[:], func=AF.Tanh, scale=0.7978845608028654), reads=[bgb], writes=[bgb])
            p.op("vector", lambda e: e.tensor_scalar(out=gb[:], in0=gb[:], scalar1=1.0, scalar2=0.5, op0=ALU.add, op1=ALU.mult), reads=[bgb], writes=[bgb])
            p.op("gpsimd", lambda e, yi=yi: e.tensor_tensor(out=gz[:], in0=gb[:], in1=yi[:], op=ALU.mult), reads=[bgb, byi], writes=[bgz])
            p.op("vector", lambda e: e.tensor_copy(out=gzb[:], in_=gz[:]), reads=[bgz], writes=[bgzb])
            for jc in range(2):
                pg, bpg = pH.next()
                for ic in range(2):
                    p.op("tensor", lambda e, pg=pg, ic=ic, jc=jc: e.matmul(pg[:, 0:GT], lhsT=wglub[:, ic, jc * 128:(jc + 1) * 128], rhs=gzb[:, ic, :],
                                                                      start=(ic == 0), stop=(ic == 1)), reads=[bwglu, bgzb], writes=[bpg])
                p.op("scalar", lambda e, pg=pg: e.activation(out=gs[:], in_=pg[:, 0:GT], func=AF.Sigmoid), reads=[bpg], writes=[bgs])
                p.op("vector", lambda e, jc=jc, oi=oi: e.tensor_tensor(out=oi[:, jc, :], in0=gz[:, jc, :], in1=gs[:], op=ALU.mult), reads=[bgz, bgs], writes=[boi])
            ri, bri = rin.next()
            p.dma("sync", ri[:], rT_d.rearrange("(k p) t -> p k t", p=128)[:, :, g * GT:(g + 1) * GT], writes=[bri])
            for kk in range(6):
                q_, bq_ = nsq.next()
                p.op("gpsimd", lambda e, q_=q_, oi=oi, kk=kk: e.tensor_tensor(out=q_[:], in0=oi[:, 2 + kk, :], in1=oi[:, 2 + kk, :], op=ALU.mult), reads=[boi], writes=[bq_])
                pn, bpn = pH.next()
                p.op("tensor", lambda e, pn=pn, q_=q_: e.matmul(pn[:, 0:GT], lhsT=onesb[:], rhs=q_[:], start=True, stop=True), reads=[bonesb, bq_], writes=[bpn])
                r_, br_ = nrs.next()
                p.op("scalar", lambda e, pn=pn, r_=r_: e.activation(out=r_[:], in_=pn[:, 0:GT], func=AF.Sqrt, scale=1.0 / 128, bias=EPS), reads=[bpn], writes=[br_])
                p.op("vector", lambda e, r_=r_: e.reciprocal(out=r_[:], in_=r_[:]), reads=[br_], writes=[br_])
                t_, bt_ = ntm.next()
                p.op("vector", lambda e, t_=t_, r_=r_, oi=oi, kk=kk: e.scalar_tensor_tensor(out=t_[:], in0=oi[:, 2 + kk, :], scalar=gng[:, 0:1], in1=r_[:], op0=ALU.mult, op1=ALU.mult),
                     reads=[boi, bgng, br_], writes=[bt_])
                p.op("gpsimd", lambda e, t_=t_, ri=ri, oi=oi, kk=kk: e.tensor_tensor(out=oi[:, 2 + kk, :], in0=t_[:], in1=ri[:, kk, :], op=ALU.mult), reads=[bt_, bri], writes=[boi])
        x1g, bx1g = x1.next()
        x1Tg, bx1Tg = x1T.next()
        rs2, brs2 = rsd.next()
        rq2, brq2 = rsd.next()
        for j in range(2):
            xi, bxi = xin.next()
            p.dma("sync", xi[:], x[(g * 2 + j) * 128:(g * 2 + j + 1) * 128, :], writes=[bxi])
            for hh in range(2):
                pd, bpd = accD[j * 2 + hh]
                for k in range(8):
                    p.op("tensor", lambda e, pd=pd, oi=oi, k=k, j=j, hh=hh: e.matmul(pd[:, :], lhsT=oi[:, k, j * 128:(j + 1) * 128],
                                                                                rhs=woutb[:, k, hh * 512:(hh + 1) * 512], start=(k == 0), stop=(k == 7)),
                         reads=[boi, bwout], writes=[bpd])
                p.op("vector", lambda e, pd=pd, xi=xi, x1g=x1g, j=j, hh=hh: e.tensor_tensor(out=x1g[:, j, hh * 512:(hh + 1) * 512], in0=pd[:, :],
                                                                                       in1=xi[:, hh * 512:(hh + 1) * 512], op=ALU.add),
                     reads=[bpd, bxi], writes=[bx1g])
            p.op("scalar", lambda e, x1g=x1g, j=j, rs2=rs2: e.activation(out=junk[:], in_=x1g[:, j, :], func=AF.Square, accum_out=rs2[:, j:j + 1]),
                 reads=[bx1g], writes=[bjunk, brs2])
            xb, bxb = x1b.next()
            p.op("gpsimd", lambda e, xb=xb, x1g=x1g, j=j: e.tensor_copy(out=xb[:], in_=x1g[:, j, :]), reads=[bx1g], writes=[bxb])
            pt, bpt = pH.next()
            ptb = pt[:, :].bitcast(BF16)
            for k in range(8):
                p.op("tensor", lambda e, ptb=ptb, xb=xb, k=k: e.transpose(ptb[:, k * 128:(k + 1) * 128], xb[:, k * 128:(k + 1) * 128], ident[:]),
                     reads=[bxb, bid], writes=[bpt])
            p.op("scalar", lambda e, ptb=ptb, x1Tg=x1Tg, j=j: e.activation(out=x1Tg[:, :, j * 128:(j + 1) * 128], in_=ptb.rearrange("p (k t) -> p k t", t=128), func=AF.Copy),
                 reads=[bpt], writes=[bx1Tg])
        p.op("scalar", lambda e, rs2=rs2: e.activation(out=rs2[:], in_=rs2[:], func=AF.Sqrt, scale=1.0 / D, bias=EPS), reads=[brs2], writes=[brs2])
        p.op("vector", lambda e, rs2=rs2: e.reciprocal(out=rs2[:], in_=rs2[:]), reads=[brs2], writes=[brs2])
        p.op("vector", lambda e, rs2=rs2, rq2=rq2: e.tensor_tensor(out=rq2[:], in0=rs2[:], in1=rs2[:], op=ALU.mult), reads=[brs2], writes=[brq2])
        def up(f):
            ph, bph = pH.next()
            for k in range(8):
                p.op("tensor", lambda e, ph=ph, k=k, f=f, x1Tg=x1Tg: e.matmul(ph[:, 0:GT], lhsT=wupb[:, k, f * 128:(f + 1) * 128], rhs=x1Tg[:, k, :],
                                                                         start=(k == 0), stop=(k == 7)), reads=[bwup, bx1Tg], writes=[bph])
            r, br = hr.next()
            p.op("scalar", lambda e, ph=ph, r=r: e.activation(out=r[:], in_=ph[:, 0:GT], func=AF.Relu), reads=[bph], writes=[br])
            hq, bhq = h2.next()
            eng = "gpsimd" if f % 2 == 0 else "vector"
            p.op(eng, lambda e, r=r, hq=hq: e.tensor_tensor(out=hq[:], in0=r[:], in1=r[:], op=ALU.mult), reads=[br], writes=[bhq])
            return hq, bhq

        def down(f, hq, bhq):
            for j in range(2):
                for hh in range(2):
                    pd, bpd = accD[j * 2 + hh]
                    p.op("tensor", lambda e, pd=pd, hq=hq, j=j, hh=hh, f=f: e.matmul(pd[:, :], lhsT=hq[:, j * 128:(j + 1) * 128],
                                                                                rhs=wdnb[:, f, hh * 512:(hh + 1) * 512], start=(f == 0), stop=(f == 31)),
                         reads=[bhq, bwdn], writes=[bpd])
        cur = up(0)
        for f in range(32):
            nxt = up(f + 1) if f < 31 else None
            down(f, *cur)
            cur = nxt
        for j in range(2):
            for hh in range(2):
                pd, bpd = accD[j * 2 + hh]
                p.op("vector", lambda e, pd=pd, x1g=x1g, rq2=rq2, j=j, hh=hh: e.scalar_tensor_tensor(
                    out=x1g[:, j, hh * 512:(hh + 1) * 512], in0=pd[:, :], scalar=rq2[:, j:j + 1], in1=x1g[:, j, hh * 512:(hh + 1) * 512], op0=ALU.mult, op1=ALU.add),
                    reads=[bpd, brq2, bx1g], writes=[bx1g])
            tt = g * 2 + j
            p.dma("sync", out[tt * 128:(tt + 1) * 128, :], x1g[:, j, :], reads=[bx1g], writes=[p.buf()])
    p.finish_all()
    p.emit()
    return c


def pout_maps(inp, layer, x, oT_full, yT_full=None, rT_full=None):
    j = layer // 2
    wout = np.ascontiguousarray(inp['ab_w_out'][j] if layer == 0 else inp['cd_w_out'][j])
    maps = []
    ident = np.eye(128, dtype=np.float32)
    for c in range(NCORE):
        sl = slice(c * TOK, (c + 1) * TOK)
        m = {"x": np.ascontiguousarray(x[sl]), "oT": np.ascontiguousarray(oT_full[:, sl]), "wout": wout,
             "wup": np.ascontiguousarray(inp['w_up'][layer]), "wdn": np.ascontiguousarray(inp['w_down'][layer]),
             "gpk": _gpk(inp['norm_mlp_g'][layer]), "ident": ident}
        if layer == 1:
            m["yT"] = np.ascontiguousarray(yT_full[:, sl])
            m["wglu"] = np.ascontiguousarray(inp['s5_w_glu'][j])
            m["rT"] = np.ascontiguousarray(rT_full[:, sl])
            m["gng"] = np.ascontiguousarray(inp['gla_out_norm'][j][:, None])
        maps.append(m)
    return maps


def assemble_oT0(o):
    oT = np.empty((1024, 2 * SEQ), o.dtype)
    for c in range(NCORE):
        b, h = c // 4, c % 4
        oT[h * 128:(h + 1) * 128, b * SEQ:(b + 1) * SEQ] = o[c][:, 0:128].T
        oT[512 + h * 128:512 + (h + 1) * 128, b * SEQ:(b + 1) * SEQ] = o[c][:, 128:256].T
    return oT


def pin1_maps(inp, x):
    w = np.ascontiguousarray(inp['cd_w_in'][0][:, 0:2560])
    walrT = np.ascontiguousarray(inp['cd_w_in'][0][:, 2560:2576].T)
    wa2 = np.ascontiguousarray(inp['gla_w_a2'][0])
    ba2 = np.ascontiguousarray(np.broadcast_to(inp['gla_b_a2'][0][None, :], (128, 384)))
    gpk = _gpk(inp['norm_mix_g'][1])
    maps = []
    for c in range(NCORE):
        xs = x[c * TOK:(c + 1) * TOK]
        maps.append({"xT": np.ascontiguousarray(xs.T), "w": w, "gpk": gpk,
                     "walrT": walrT, "wa2": wa2, "ba2": ba2})
    return maps


TWO_PI = float(2.0 * np.pi)
PI = float(np.pi)
I32 = mybir.dt.int32


def emit_sincos(p, sb, name, ang, bang, shape, s_out, c_out, bouts, eng_pool="gpsimd"):
    P_ = shape[0]
    kf = sb(name + "_kf", shape); ki = sb(name + "_ki", shape, I32); r = sb(name + "_r", shape); ab = sb(name + "_ab", shape)
    bkf, bki, br, bab = p.buf(), p.buf(), p.buf(), p.buf()
    p.op("vector", lambda e: e.tensor_scalar(out=kf[:], in0=ang, scalar1=1.0 / TWO_PI, scalar2=None, op0=ALU.mult), reads=[bang], writes=[bkf])
    p.op("vector", lambda e: e.tensor_copy(out=ki[:], in_=kf[:]), reads=[bkf], writes=[bki])
    p.op("vector", lambda e: e.tensor_copy(out=kf[:], in_=ki[:]), reads=[bki], writes=[bkf])
    p.op("vector", lambda e: e.scalar_tensor_tensor(out=r[:], in0=kf[:], scalar=-TWO_PI, in1=ang, op0=ALU.mult, op1=ALU.add), reads=[bkf, bang], writes=[br])
    p.op("vector", lambda e: e.tensor_scalar(out=r[:], in0=r[:], scalar1=-PI, scalar2=PI, op0=ALU.max, op1=ALU.min), reads=[br], writes=[br])
    p.op("scalar", lambda e: e.activation(out=s_out, in_=r[:], func=AF.Sin), reads=[br], writes=[bouts[0]])
    p.op("scalar", lambda e: e.activation(out=ab[:], in_=r[:], func=AF.Abs), reads=[br], writes=[bab])
    p.op("scalar", lambda e: e.activation(out=c_out, in_=ab[:], func=AF.Sin, scale=-1.0, bias=PI / 2), reads=[bab], writes=[bouts[1]])


def build_mix1(nwin=32):
    c = Ctx()
    nc = c.nc
    W = 512
    uT_d = c.din("uT", [64, SEQ], BF16)
    prmC_d = c.din("prmC", [128, 6])
    prmR_d = c.din("prmR", [64, 3, 128])
    bpad_d = c.din("bpad", [64, 2, 128])
    ct_d = c.din("ct", [128, 2, 2, 64])
    dsk_d = c.din("dsk", [64, 1])
    iota_d = c.din("iota", [128, W])
    gm_d = c.din("gm", [128, 256])
    idn_d = c.din("idn", [128, 128])
    gq_d = c.din("gq", [3, 64, SEQ], BF16)
    gk_d = c.din("gk", [3, 64, SEQ], BF16)
    gla_d = c.din("gla", [3, 64, SEQ])
    gv_d = c.din("gv", [SEQ, 192], BF16)
    yT_o = c.dout("yT", [64, SEQ])
    go_o = c.dout("go", [SEQ, 192], BF16)
    p = c.start()
    sb = c.sb

    def ld(name, shape, src, dt=F32, q="gpsimd"):
        t = sb(name, shape, dt); b = p.buf()
        p.dma(q, t[:], src, writes=[b])
        return t, b
    prmC, bprmC = ld("prmC", [128, 6], prmC_d[:, :])
    prmR, bprmR = ld("prmR", [64, 3, 128], prmR_d[:, :, :])
    bpad, bbpad = ld("bpad", [64, 2, 128], bpad_d[:, :, :])
    ctf, bctf = ld("ctf", [128, 2, 2, 64], ct_d[:, :, :, :])
    dsk, bdsk = ld("dsk", [64, 1], dsk_d[:, :])
    iota, biota = ld("iota", [128, W], iota_d[:, :])
    gm, bgm = ld("gm", [128, 256], gm_d[:, :])
    idf, bidf = ld("idf", [128, 128], idn_d[:, :])
    ident = sb("ident", [128, 128], BF16); bid = p.buf()
    p.op("vector", lambda e: e.tensor_copy(out=ident[:], in_=idf[:]), reads=[bidf], writes=[bid])
    ones = sb("ones", [128, 1]); bones = p.buf()
    p.op("vector", lambda e: e.memset(ones[:], 1.0), writes=[bones])

    pc = prmC[:].rearrange("p (a k) -> p a k", k=3)
    dl = sb("dl", [128, 2]); bdl = p.buf()
    rr = sb("rr", [128, 2]); brr = p.buf()
    th = sb("th", [128, 2]); bth = p.buf()
    p.op("scalar", lambda e: e.activation(out=dl[:], in_=pc[:, :, 2], func=AF.Exp), reads=[bprmC], writes=[bdl])
    p.op("vector", lambda e: e.tensor_tensor(out=rr[:], in0=pc[:, :, 0], in1=dl[:], op=ALU.mult), reads=[bprmC, bdl], writes=[brr])
    p.op("scalar", lambda e: e.activation(out=rr[:], in_=rr[:], func=AF.Exp), reads=[brr], writes=[brr])
    p.op("vector", lambda e: e.tensor_tensor(out=th[:], in0=pc[:, :, 1], in1=dl[:], op=ALU.mult), reads=[bprmC, bdl], writes=[bth])
    cosT, sinT, bcs = [], [], []
    ang = sb("ang", [128, W]); bang = p.buf()
    for pr in range(2):
        ct_ = sb(f"cosT{pr}", [128, W]); st_ = sb(f"sinT{pr}", [128, W]); b1, b2 = p.buf(), p.buf()
        p.op("vector", lambda e, pr=pr: e.tensor_scalar(out=ang[:], in0=iota[:], scalar1=th[:, pr:pr + 1], scalar2=None, op0=ALU.mult), reads=[biota, bth], writes=[bang])
        emit_sincos(p, sb, f"sc{pr}", ang[:], bang, [128, W], st_[:], ct_[:], (b1, b2))
        cosT.append(ct_); sinT.append(st_); bcs.append((b2, b1))
    angW = sb("angW", [128, 2]); bangW = p.buf()
    cW = sb("cW", [128, 2]); sW = sb("sW", [128, 2]); nsW = sb("nsW", [128, 2]); bcW, bsW, bnsW = p.buf(), p.buf(), p.buf()
    p.op("vector", lambda e: e.tensor_scalar(out=angW[:], in0=th[:], scalar1=float(W), scalar2=None, op0=ALU.mult), reads=[bth], writes=[bangW])
    emit_sincos(p, sb, "scW", angW[:], bangW, [128, 2], sW[:], cW[:], (bsW, bcW))
    p.op("vector", lambda e: e.tensor_scalar(out=nsW[:], in0=sW[:], scalar1=-1.0, scalar2=None, op0=ALU.mult), reads=[bsW], writes=[bnsW])
    R = lambda k: prmR[:, k, :]
    def t64(name):
        return sb(name, [64, 128]), p.buf()
    (dR, bdR), (x1, bx1), (er, ber), (thR, bthR), (sR, bsR), (cR, bcR) = [t64(n) for n in ("dR", "x1R", "erR", "thR", "sR", "cR")]
    (lbr, blbr), (lbi, blbi), (den, bden), (t1, bt1), (t2, bt2), (cfr, bcfr), (cfi, bcfi) = [t64(n) for n in ("lbr", "lbi", "den", "t1R", "t2R", "cfr", "cfi")]
    V = "vector"
    p.op("scalar", lambda e: e.activation(out=dR[:], in_=R(2), func=AF.Exp), reads=[bprmR], writes=[bdR])
    p.op(V, lambda e: e.tensor_tensor(out=x1[:], in0=R(0), in1=dR[:], op=ALU.mult), reads=[bprmR, bdR], writes=[bx1])
    p.op("scalar", lambda e: e.activation(out=er[:], in_=x1[:], func=AF.Exp), reads=[bx1], writes=[ber])
    p.op(V, lambda e: e.tensor_tensor(out=thR[:], in0=R(1), in1=dR[:], op=ALU.mult), reads=[bprmR, bdR], writes=[bthR])
    emit_sincos(p, sb, "scR", thR[:], bthR, [64, 128], sR[:], cR[:], (bsR, bcR))
    p.op(V, lambda e: e.tensor_tensor(out=lbr[:], in0=er[:], in1=cR[:], op=ALU.mult), reads=[ber, bcR], writes=[blbr])
    p.op(V, lambda e: e.tensor_scalar(out=lbr[:], in0=lbr[:], scalar1=-1.0, scalar2=None, op0=ALU.add), reads=[blbr], writes=[blbr])
    p.op(V, lambda e: e.tensor_tensor(out=lbi[:], in0=er[:], in1=sR[:], op=ALU.mult), reads=[ber, bsR], writes=[blbi])
    p.op(V, lambda e: e.tensor_tensor(out=den[:], in0=R(0), in1=R(0), op=ALU.mult), reads=[bprmR], writes=[bden])
    p.op(V, lambda e: e.tensor_tensor(out=t1[:], in0=R(1), in1=R(1), op=ALU.mult), reads=[bprmR], writes=[bt1])
    p.op(V, lambda e: e.tensor_tensor(out=den[:], in0=den[:], in1=t1[:], op=ALU.add), reads=[bden, bt1], writes=[bden])
    p.op(V, lambda e: e.reciprocal(out=den[:], in_=den[:]), reads=[bden], writes=[bden])
    p.op(V, lambda e: e.tensor_tensor(out=t1[:], in0=lbr[:], in1=R(0), op=ALU.mult), reads=[blbr, bprmR, bden], writes=[bt1])
    p.op(V, lambda e: e.tensor_tensor(out=t2[:], in0=lbi[:], in1=R(1), op=ALU.mult), reads=[blbi, bprmR], writes=[bt2])
    p.op(V, lambda e: e.tensor_tensor(out=cfr[:], in0=t1[:], in1=t2[:], op=ALU.add), reads=[bt1, bt2], writes=[bcfr])
    p.op(V, lambda e: e.tensor_tensor(out=cfr[:], in0=cfr[:], in1=den[:], op=ALU.mult), reads=[bcfr, bden], writes=[bcfr])
    p.op(V, lambda e: e.tensor_tensor(out=t1[:], in0=lbi[:], in1=R(0), op=ALU.mult), reads=[blbi, bprmR, bcfr], writes=[bt1])
    p.op(V, lambda e: e.tensor_tensor(out=t2[:], in0=lbr[:], in1=R(1), op=ALU.mult), reads=[blbr, bprmR, bcfr], writes=[bt2])
    p.op(V, lambda e: e.tensor_tensor(out=cfi[:], in0=t1[:], in1=t2[:], op=ALU.subtract), reads=[bt1, bt2], writes=[bcfi])
    p.op(V, lambda e: e.tensor_tensor(out=cfi[:], in0=cfi[:], in1=den[:], op=ALU.mult), reads=[bcfi, bden], writes=[bcfi])
    bbre = sb("bbre", [64, 128], BF16); bbim = sb("bbim", [64, 128], BF16); bbbre, bbbim = p.buf(), p.buf()
    Bre, Bim = bpad[:, 0, :], bpad[:, 1, :]
    p.op(V, lambda e: e.tensor_tensor(out=t1[:], in0=cfr[:], in1=Bre, op=ALU.mult), reads=[bcfr, bbpad, bcfi], writes=[bt1])
    p.op(V, lambda e: e.tensor_tensor(out=t2[:], in0=cfi[:], in1=Bim, op=ALU.mult), reads=[bcfi, bbpad], writes=[bt2])
    p.op(V, lambda e: e.tensor_tensor(out=bbre[:], in0=t1[:], in1=t2[:], op=ALU.subtract), reads=[bt1, bt2], writes=[bbbre])
    p.op(V, lambda e: e.tensor_tensor(out=t1[:], in0=cfr[:], in1=Bim, op=ALU.mult), reads=[bcfr, bbpad, bbbre], writes=[bt1])
    p.op(V, lambda e: e.tensor_tensor(out=t2[:], in0=cfi[:], in1=Bre, op=ALU.mult), reads=[bcfi, bbpad, bbbre], writes=[bt2])
    p.op(V, lambda e: e.tensor_tensor(out=bbim[:], in0=t1[:], in1=t2[:], op=ALU.add), reads=[bt1, bt2], writes=[bbbim])
    ctb = sb("ctb", [128, 2, 2, 64], BF16); bctb = p.buf()
    p.op(V, lambda e: e.tensor_copy(out=ctb[:, :, 0, :], in_=ctf[:, :, 0, :]), reads=[bctf], writes=[bctb])
    p.op(V, lambda e: e.tensor_scalar(out=ctb[:, :, 1, :], in0=ctf[:, :, 1, :], scalar1=-1.0, scalar2=None, op0=ALU.mult), reads=[bctf], writes=[bctb])

    pBU = [(c.ps(f"pBU{i}", [128, 512]), p.buf()) for i in range(2)]
    pY = (c.ps("pY", [128, 512]), p.buf())
    pS = Rot([(c.ps(f"pS{i}", [128, 512]), p.buf()) for i in range(3)])
    _pOT = [c.ps(f"pOT{i}", [128, 512]) for i in range(2)]
    _bOT = [p.buf() for _ in range(2)]
    _pOTb = [t_[:, :].bitcast(BF16) for t_ in _pOT]
    pO = [(_pOT[i // 2][:, (i % 2) * 256:(i % 2) * 256 + 128], _bOT[i // 2]) for i in range(3)]
    pT = [(_pOTb[i // 2][:, (i % 2) * 512 + 256:(i % 2) * 512 + 512], _bOT[i // 2]) for i in range(3)]
    uin = Rot([(sb(f"uin{i}", [64, W], BF16), p.buf()) for i in range(2)])
    def T(name, dt=F32, n=1, shape=None):
        return Rot([(sb(f"{name}{i}", shape or [128, W], dt), p.buf()) for i in range(n)])
    ta, tb_, tc, td = T("s5a"), T("s5b"), T("s5c"), T("s5d")
    kre, kim = T("kre"), T("kim")
    wre = [T(f"wre{pr}", n=2) for pr in range(2)]
    wim = [T(f"wim{pr}", n=2) for pr in range(2)]
    xre, xim = T("xre", BF16, 2), T("xim", BF16, 2)
    w0 = [[(sb(f"w0_{pr}_{i}", [128, 2]), p.buf()) for i in range(2)] for pr in range(2)]
    for pr in range(2):
        p.op("vector", lambda e, pr=pr: e.memset(w0[pr][0][0][:], 0.0), writes=[w0[pr][0][1]])
    yo = T("yo", n=2, shape=[64, W])
    CH = 1024
    gq = Rot([(sb(f"gq{i}", [64, 3, CH], BF16), p.buf()) for i in range(2)])
    gk = Rot([(sb(f"gk{i}", [64, 3, CH], BF16), p.buf()) for i in range(2)])
    gla = Rot([(sb(f"gla{i}", [64, 3, CH]), p.buf()) for i in range(2)])
    gv = Rot([(sb(f"gv{i}", [128, 8, 192], BF16), p.buf()) for i in range(2)])
    G64 = lambda name, dt=F32, n=6: Rot([(sb(f"{name}{i}", [64, 128], dt), p.buf()) for i in range(n)])
    Bt, ep, en, el = G64("Bt"), G64("ep"), G64("en"), G64("el")
    qf, kf, qb, kb, ks = [G64(n, BF16) for n in ("qf", "kf", "qb", "kb", "ks")]
    s1 = Rot([(sb(f"gs1{i}", [128, 128]), p.buf()) for i in range(6)])
    s2 = Rot([(sb(f"gs2{i}", [128, 128]), p.buf()) for i in range(6)])
    Sm = Rot([(sb(f"gSm{i}", [128, 128], BF16), p.buf()) for i in range(6)])
    kst = Rot([(sb(f"kst{i}", [128, 64], BF16), p.buf()) for i in range(6)])
    Sst = [(sb(f"Sst{i}", [64, 64]), p.buf()) for i in range(3)]
    Sstb = [Rot([(sb(f"Sstb{i}_{k}", [64, 64], BF16), p.buf()) for k in range(2)]) for i in range(3)]
    cur_sb = []
    for i in range(3):
        p.op("vector", lambda e, i=i: e.memset(Sst[i][0][:], 0.0), writes=[Sst[i][1]])
        t_, b_ = Sstb[i].next()
        p.op("vector", lambda e, t_=t_: e.memset(t_[:], 0.0), writes=[b_])
        cur_sb.append((t_, b_))
    otile = Rot([(sb(f"got{i}", [128, 192], BF16), p.buf()) for i in range(3)])
    gst = {"cur": None}

    def s5_window(m):
        ui, bui = uin.next()
        p.dma("sync", ui[:], uT_d[:, m * W:(m + 1) * W], writes=[bui])
        py, bpy = pY
        for pr in range(2):
            (pre_, bpre), (pim_, bpim) = pBU
            rows = slice(32 * pr, 32 * pr + 32)
            p.op("tensor", lambda e, pr=pr, rows=rows, ui=ui, pre_=pre_: e.matmul(pre_[:, :], lhsT=bbre[rows, :], rhs=ui[rows, :], start=True, stop=True),
                 reads=[bbbre, bui], writes=[bpre])
            p.op("tensor", lambda e, pr=pr, rows=rows, ui=ui, pim_=pim_: e.matmul(pim_[:, :], lhsT=bbim[rows, :], rhs=ui[rows, :], start=True, stop=True),
                 reads=[bbbim, bui], writes=[bpim])
            bcos, bsin = bcs[pr]
            (a_, ba_), (b_, bb_), (c_, bc_), (d_, bd_) = ta.next(), tb_.next(), tc.next(), td.next()
            (kr_, bkr_), (ki_, bki_) = kre.next(), kim.next()
            CT, ST = cosT[pr], sinT[pr]
            p.op("vector", lambda e, a_=a_, pre_=pre_, CT=CT: e.tensor_tensor(out=a_[:], in0=pre_[:, :], in1=CT[:], op=ALU.mult), reads=[bpre, bcos], writes=[ba_])
            p.op("vector", lambda e, b_=b_, pim_=pim_, ST=ST: e.tensor_tensor(out=b_[:], in0=pim_[:, :], in1=ST[:], op=ALU.mult), reads=[bpim, bsin], writes=[bb_])
            p.op("vector", lambda e, c_=c_, pim_=pim_, CT=CT: e.tensor_tensor(out=c_[:], in0=pim_[:, :], in1=CT[:], op=ALU.mult), reads=[bpim, bcos], writes=[bc_])
            p.op("vector", lambda e, d_=d_, pre_=pre_, ST=ST: e.tensor_tensor(out=d_[:], in0=pre_[:, :], in1=ST[:], op=ALU.mult), reads=[bpre, bsin], writes=[bd_])
            p.op("gpsimd", lambda e, kr_=kr_, a_=a_, b_=b_: e.tensor_tensor(out=kr_[:], in0=a_[:], in1=b_[:], op=ALU.add), reads=[ba_, bb_], writes=[bkr_])
            p.op("gpsimd", lambda e, ki_=ki_, c_=c_, d_=d_: e.tensor_tensor(out=ki_[:], in0=c_[:], in1=d_[:], op=ALU.subtract), reads=[bc_, bd_], writes=[bki_])
            (wr_, bwr_), (wi_, bwi_) = wre[pr].next(), wim[pr].next()
            w0c, bw0c = w0[pr][m % 2]
            w0n, bw0n = w0[pr][(m + 1) % 2]
            rbc = rr[:, pr:pr + 1].to_broadcast([128, W])
            p.op("vector", lambda e, wr_=wr_, kr_=kr_, w0c=w0c, rbc=rbc: e.tensor_tensor_scan(out=wr_[:], data0=rbc, data1=kr_[:], initial=w0c[:, 0:1], op0=ALU.mult, op1=ALU.add),
                 reads=[brr, bkr_, bw0c], writes=[bwr_])
            p.op("vector", lambda e, wi_=wi_, ki_=ki_, w0c=w0c, rbc=rbc: e.tensor_tensor_scan(out=wi_[:], data0=rbc, data1=ki_[:], initial=w0c[:, 1:2], op0=ALU.mult, op1=ALU.add),
                 reads=[brr, bki_, bw0c], writes=[bwi_])
            p.op("vector", lambda e, w0n=w0n, wr_=wr_, pr=pr: e.tensor_tensor(out=w0n[:, 0:1], in0=wr_[:, W - 1:W], in1=cW[:, pr:pr + 1], op=ALU.mult), reads=[bwr_, bcW], writes=[bw0n])
            p.op("vector", lambda e, w0n=w0n, wi_=wi_, pr=pr: e.scalar_tensor_tensor(out=w0n[:, 0:1], in0=wi_[:, W - 1:W], scalar=nsW[:, pr:pr + 1], in1=w0n[:, 0:1], op0=ALU.mult, op1=ALU.add),
                 reads=[bwi_, bnsW, bw0n], writes=[bw0n])
            p.op("vector", lambda e, w0n=w0n, wi_=wi_, pr=pr: e.tensor_tensor(out=w0n[:, 1:2], in0=wi_[:, W - 1:W], in1=cW[:, pr:pr + 1], op=ALU.mult), reads=[bwi_, bcW], writes=[bw0n])
            p.op("vector", lambda e, w0n=w0n, wr_=wr_, pr=pr: e.scalar_tensor_tensor(out=w0n[:, 1:2], in0=wr_[:, W - 1:W], scalar=sW[:, pr:pr + 1], in1=w0n[:, 1:2], op0=ALU.mult, op1=ALU.add),
                 reads=[bwr_, bsW, bw0n], writes=[bw0n])
            (a2, ba2), (b2, bb2), (c2, bc2), (d2, bd2) = ta.next(), tb_.next(), tc.next(), td.next()
            (xr_, bxr_), (xi_, bxi_) = xre.next(), xim.next()
            p.op("gpsimd", lambda e, a2=a2, wr_=wr_, CT=CT: e.tensor_tensor(out=a2[:], in0=wr_[:], in1=CT[:], op=ALU.mult), reads=[bwr_, bcos], writes=[ba2])
            p.op("gpsimd", lambda e, b2=b2, wi_=wi_, ST=ST: e.tensor_tensor(out=b2[:], in0=wi_[:], in1=ST[:], op=ALU.mult), reads=[bwi_, bsin], writes=[bb2])
            p.op("vector", lambda e, c2=c2, wi_=wi_, CT=CT: e.tensor_tensor(out=c2[:], in0=wi_[:], in1=CT[:], op=ALU.mult), reads=[bwi_, bcos], writes=[bc2])
            p.op("gpsimd", lambda e, d2=d2, wr_=wr_, ST=ST: e.tensor_tensor(out=d2[:], in0=wr_[:], in1=ST[:], op=ALU.mult), reads=[bwr_, bsin], writes=[bd2])
            p.op("gpsimd", lambda e, xr_=xr_, a2=a2, b2=b2: e.tensor_tensor(out=xr_[:], in0=a2[:], in1=b2[:], op=ALU.subtract), reads=[ba2, bb2], writes=[bxr_])
            p.op("gpsimd", lambda e, xi_=xi_, c2=c2, d2=d2: e.tensor_tensor(out=xi_[:], in0=c2[:], in1=d2[:], op=ALU.add), reads=[bc2, bd2], writes=[bxi_])
            p.op("tensor", lambda e, pr=pr, xr_=xr_, py=py: e.matmul(py[0:64, :], lhsT=ctb[:, pr, 0, :], rhs=xr_[:], start=(pr == 0), stop=False), reads=[bctb, bxr_], writes=[bpy])
            p.op("tensor", lambda e, pr=pr, xi_=xi_, py=py: e.matmul(py[0:64, :], lhsT=ctb[:, pr, 1, :], rhs=xi_[:], start=False, stop=(pr == 1)), reads=[bctb, bxi_], writes=[bpy])
        yo_, byo = yo.next()
        p.op("vector", lambda e, yo_=yo_, ui=ui, py=py: e.scalar_tensor_tensor(out=yo_[:], in0=ui[:], scalar=dsk[:, 0:1], in1=py[0:64, :], op0=ALU.mult, op1=ALU.add),
             reads=[bui, bdsk, bpy], writes=[byo])
        p.dma("sync", yT_o[:, m * W:(m + 1) * W], yo_[:], reads=[byo], writes=[p.buf()])

    def gla_tile(t):
        ci, ti = t // 8, t % 8
        if ti == 0:
            cur = (gq.next(), gk.next(), gla.next(), gv.next())
            sl = slice(ci * CH, (ci + 1) * CH)
            p.dma("gpsimd", cur[0][0][:], gq_d[:, :, sl].rearrange("u d t -> d u t"), writes=[cur[0][1]])
            p.dma("gpsimd", cur[1][0][:], gk_d[:, :, sl].rearrange("u d t -> d u t"), writes=[cur[1][1]])
            p.dma("gpsimd", cur[2][0][:], gla_d[:, :, sl].rearrange("u d t -> d u t"), writes=[cur[2][1]])
            p.dma("gpsimd", cur[3][0][:], gv_d.rearrange("(n p) e -> p n e", p=128)[:, ci * 8:(ci + 1) * 8, :], writes=[cur[3][1]])
            gst["cur"] = cur
        (q_, bq_), (k_, bk_), (la_, bla_), (v_, bv_) = gst["cur"]
        ts = slice(ti * 128, (ti + 1) * 128)
        ot, bot = otile.next()
        U = range(3)
        st = [dict() for _ in U]
        for i in U:
            d = st[i]
            (d["B"], d["bB"]), (d["ep"], d["bep"]), (d["en"], d["ben"]), (d["el"], d["bel"]) = Bt.next(), ep.next(), en.next(), el.next()
            p.op("vector", lambda e, B_=d["B"], i=i: e.tensor_tensor_scan(out=B_[:], data0=ones[0:64, 0:1].to_broadcast([64, 128]), data1=la_[:, i, ts], initial=0.0, op0=ALU.mult, op1=ALU.add),
                 reads=[bones, bla_], writes=[d["bB"]])
        for i in U:
            d = st[i]
            p.op("scalar", lambda e, B_=d["B"], ep_=d["ep"]: e.activation(out=ep_[:], in_=B_[:], func=AF.Exp), reads=[d["bB"]], writes=[d["bep"]])
            p.op("scalar", lambda e, B_=d["B"], en_=d["en"]: e.activation(out=en_[:], in_=B_[:], func=AF.Exp, scale=-1.0), reads=[d["bB"]], writes=[d["ben"]])
            p.op("scalar", lambda e, B_=d["B"], el_=d["el"]: e.activation(out=el_[:], in_=B_[:], func=AF.Exp, scale=-1.0, bias=B_[:, 127:128]), reads=[d["bB"]], writes=[d["bel"]])
        for i in U:
            d = st[i]
            (d["qf"], d["bqf"]), (d["kf"], d["bkf"]), (d["qb"], d["bqb"]), (d["kb"], d["bkb"]), (d["ks"], d["bks"]) = qf.next(), kf.next(), qb.next(), kb.next(), ks.next()
            p.op("vector", lambda e, qf_=d["qf"], ep_=d["ep"], i=i: e.tensor_tensor(out=qf_[:], in0=q_[:, i, ts], in1=ep_[:], op=ALU.mult), reads=[bq_, d["bep"]], writes=[d["bqf"]])
            p.op("gpsimd", lambda e, kf_=d["kf"], en_=d["en"], i=i: e.tensor_tensor(out=kf_[:], in0=k_[:, i, ts], in1=en_[:], op=ALU.mult), reads=[bk_, d["ben"]], writes=[d["bkf"]])
            p.op("vector", lambda e, qb_=d["qb"], en_=d["en"], i=i: e.tensor_tensor(out=qb_[:], in0=q_[:, i, ts], in1=en_[:], op=ALU.mult), reads=[bq_, d["ben"]], writes=[d["bqb"]])
            p.op("gpsimd", lambda e, kb_=d["kb"], ep_=d["ep"], i=i: e.tensor_tensor(out=kb_[:], in0=k_[:, i, ts], in1=ep_[:], op=ALU.mult), reads=[bk_, d["bep"]], writes=[d["bkb"]])
            p.op("gpsimd", lambda e, ks_=d["ks"], el_=d["el"], i=i: e.tensor_tensor(out=ks_[:], in0=k_[:, i, ts], in1=el_[:], op=ALU.mult), reads=[bk_, d["bel"]], writes=[d["bks"]])
        for i in U:
            d = st[i]
            d["ps"], d["bps"] = pS.next()
            p.op("tensor", lambda e, ps_=d["ps"], kf_=d["kf"], qf_=d["qf"]: e.matmul(ps_[:, 0:128], lhsT=kf_[:], rhs=qf_[:], start=True, stop=True), reads=[d["bkf"], d["bqf"]], writes=[d["bps"]])
            p.op("tensor", lambda e, ps_=d["ps"], kb_=d["kb"], qb_=d["qb"]: e.matmul(ps_[:, 128:256], lhsT=kb_[:], rhs=qb_[:], start=True, stop=True), reads=[d["bkb"], d["bqb"]], writes=[d["bps"]])
            d["ptb"], d["bptt"] = pT[i]
            p.op("tensor", lambda e, ptb=d["ptb"], ks_=d["ks"]: e.transpose(ptb[:, 0:64], ks_[:], ident[0:64, 0:64]), reads=[d["bks"], bid], writes=[d["bptt"]])
        for i in U:
            d = st[i]
            (d["s1"], d["bs1"]), (d["s2"], d["bs2"]), (d["S"], d["bS"]) = s1.next(), s2.next(), Sm.next()
            p.op("vector", lambda e, s1_=d["s1"], ps_=d["ps"]: e.tensor_tensor(out=s1_[:], in0=ps_[:, 0:128], in1=gm[:, 0:128], op=ALU.mult), reads=[d["bps"], bgm], writes=[d["bs1"]])
            p.op("vector", lambda e, s2_=d["s2"], ps_=d["ps"]: e.tensor_tensor(out=s2_[:], in0=ps_[:, 128:256], in1=gm[:, 128:256], op=ALU.mult), reads=[d["bps"], bgm], writes=[d["bs2"]])
            p.op("gpsimd", lambda e, S_=d["S"], s1_=d["s1"], s2_=d["s2"]: e.tensor_tensor(out=S_[:], in0=s1_[:], in1=s2_[:], op=ALU.add), reads=[d["bs1"], d["bs2"]], writes=[d["bS"]])
            d["kt"], d["bkt"] = kst.next()
            p.op("scalar", lambda e, kt_=d["kt"], ptb=d["ptb"]: e.activation(out=kt_[:], in_=ptb[:, 0:64], func=AF.Copy), reads=[d["bptt"]], writes=[d["bkt"]])
        for i in U:
            d = st[i]
            d["po"], d["bpo"] = pO[i]
            po_, bpo_ = d["po"], d["bpo"]
            sbc, bsbc = cur_sb[i]
            vv = v_[:, ti, i * 64:(i + 1) * 64]
            p.op("tensor", lambda e, po_=po_, S_=d["S"], vv=vv: e.matmul(po_[:, 0:64], lhsT=S_[:], rhs=vv, start=True, stop=False), reads=[d["bS"], bv_], writes=[bpo_])
            p.op("tensor", lambda e, po_=po_, qf_=d["qf"], sbc=sbc: e.matmul(po_[:, 0:64], lhsT=qf_[:], rhs=sbc[:], start=False, stop=True), reads=[d["bqf"], bsbc], writes=[bpo_])
            p.op("tensor", lambda e, po_=po_, kt_=d["kt"], vv=vv: e.matmul(po_[0:64, 64:128], lhsT=kt_[:], rhs=vv, start=True, stop=True), reads=[d["bkt"], bv_], writes=[bpo_])
        for i in U:
            d = st[i]
            po_, bpo_ = d["po"], d["bpo"]
            st_, bst_ = Sst[i]
            p.op("vector", lambda e, st_=st_, ep_=d["ep"], po_=po_: e.scalar_tensor_tensor(out=st_[:], in0=st_[:], scalar=ep_[:, 127:128], in1=po_[0:64, 64:128], op0=ALU.mult, op1=ALU.add),
                 reads=[bst_, d["bep"], bpo_], writes=[bst_])
            sbn, bsbn = Sstb[i].next()
            p.op("scalar", lambda e, sbn=sbn, st_=st_: e.activation(out=sbn[:], in_=st_[:], func=AF.Copy), reads=[bst_], writes=[bsbn])
            cur_sb[i] = (sbn, bsbn)
            p.op("scalar", lambda e, ot=ot, po_=po_, i=i: e.activation(out=ot[:, i * 64:(i + 1) * 64], in_=po_[:, 0:64], func=AF.Copy), reads=[bpo_], writes=[bot])
        p.dma("gpsimd", go_o[t * 128:(t + 1) * 128, :], ot[:], reads=[bot], writes=[p.buf()])

    for m in range(nwin):
        s5_window(m)
        for t in range(4 * m, 4 * m + 4):
            gla_tile(t)
    p.finish_all()
    p.emit()
    return c


def mix1_maps(inp, pre1, la1):
    a_re, a_im, ls = inp['s5_a_re'][0], inp['s5_a_im'][0], inp['s5_log_step'][0]
    b_re, b_im, c_re, c_im = inp['s5_b_re'][0], inp['s5_b_im'][0], inp['s5_c_re'][0], inp['s5_c_im'][0]
    iota = np.ascontiguousarray(np.broadcast_to(np.arange(512, dtype=np.float32)[None, :], (128, 512)))
    i = np.arange(128)
    mf = (i[None, :] >= i[:, None]).astype(np.float32)
    mb = ((i[None, :] < i[:, None]) & ((i[None, :] // 64) == (i[:, None] // 64))).astype(np.float32)
    gm = np.ascontiguousarray(np.concatenate([mf, mb], axis=1))
    idn = np.eye(128, dtype=np.float32)
    maps = []
    for c in range(NCORE):
        b, c4 = c // 4, c % 4
        rows = pre1[b * SEQ:(b + 1) * SEQ]
        lar = la1[b * SEQ:(b + 1) * SEQ]
        prmC = np.zeros((128, 6), np.float32)
        prmR = np.zeros((64, 3, 128), np.float32)
        bpad = np.zeros((64, 2, 128), np.float32)
        ct = np.zeros((128, 2, 2, 64), np.float32)
        for pr in range(2):
            for g2 in range(2):
                g = 4 * c4 + 2 * pr + g2
                ps = slice(64 * g2, 64 * g2 + 64)
                prmC[ps, pr * 3 + 0] = a_re[g]
                prmC[ps, pr * 3 + 1] = a_im[g]
                prmC[ps, pr * 3 + 2] = ls[g]
                prmR[32 * pr:32 * pr + 32, 0, ps] = a_re[g][None, :]
                prmR[32 * pr:32 * pr + 32, 1, ps] = a_im[g][None, :]
                prmR[32 * pr:32 * pr + 32, 2, ps] = ls[g]
                r0 = 32 * pr + 16 * g2
                bpad[r0:r0 + 16, 0, ps] = b_re[g].T
                bpad[r0:r0 + 16, 1, ps] = b_im[g].T
                gl = 2 * pr + g2
                ct[ps, pr, 0, 16 * gl:16 * gl + 16] = c_re[g].T
                ct[ps, pr, 1, 16 * gl:16 * gl + 16] = c_im[g].T
        gq = np.empty((3, 64, SEQ), rows.dtype); gk = np.empty((3, 64, SEQ), rows.dtype)
        gla = np.empty((3, 64, SEQ), np.float32); gv = np.empty((SEQ, 192), rows.dtype)
        for i3 in range(3):
            u = 3 * c4 + i3
            hd, half = u // 2, u % 2
            gq[i3] = rows[:, 256 + hd * 64:256 + (hd + 1) * 64].T
            gk[i3] = rows[:, 640 + hd * 64:640 + (hd + 1) * 64].T
            gla[i3] = lar[:, hd * 64:(hd + 1) * 64].T
            gv[:, i3 * 64:(i3 + 1) * 64] = rows[:, 1024 + hd * 128 + half * 64:1024 + hd * 128 + half * 64 + 64]
        maps.append({"uT": np.ascontiguousarray(rows[:, 64 * c4:64 * c4 + 64].T), "prmC": prmC, "prmR": prmR, "bpad": bpad, "ct": ct,
                     "dsk": np.ascontiguousarray(inp['s5_d'][0][64 * c4:64 * c4 + 64, None]), "iota": iota, "gm": gm, "idn": idn,
                     "gq": gq, "gk": gk, "gla": gla, "gv": gv})
    return maps


def assemble_mix1(results):
    yT = np.empty((256, 2 * SEQ), np.float32)
    oT = np.empty((768, 2 * SEQ), results[0]["go"].dtype)
    for c in range(NCORE):
        b, c4 = c // 4, c % 4
        yT[64 * c4:64 * c4 + 64, b * SEQ:(b + 1) * SEQ] = results[c]["yT"]
        go = results[c]["go"]
        for i3 in range(3):
            u = 3 * c4 + i3
            hd, half = u // 2, u % 2
            ch0 = hd * 128 + half * 64
            oT[ch0:ch0 + 64, b * SEQ:(b + 1) * SEQ] = go[:, i3 * 64:(i3 + 1) * 64].T
    return yT, oT


_CACHE = {}


def _prog(key, fn):
    if key not in _CACHE:
        _CACHE[key] = fn()
    return _CACHE[key]


def _run(c, maps):
    res = run_bass_kernel_spmd(c.nc, maps, core_ids=list(range(NCORE)))
    return res.results


def kernel(**inp):
    inp = {k: np.asarray(v) for k, v in inp.items()}
    x = np.ascontiguousarray(inp['x'].reshape(-1, D).astype(np.float32, copy=False))
    r = _run(_prog("pin0", lambda: build_pin(0)), pin0_maps(inp, x))
    pre0 = np.concatenate([q["out"] for q in r], 0)
    r = _run(_prog("mix0", lambda: build_mix0(32)), mix0_maps(inp, pre0))
    oT0 = assemble_oT0(np.stack([q["o"] for q in r], 0))
    r = _run(_prog("pout0", lambda: build_pout(0)), pout_maps(inp, 0, x, oT0))
    x1 = np.concatenate([q["xo"] for q in r], 0)
    r = _run(_prog("pin1", lambda: build_pin(1)), pin1_maps(inp, x1))
    pre1 = np.concatenate([q["out"] for q in r], 0)
    la1 = np.concatenate([q["la"] for q in r], 0)
    r = _run(_prog("mix1", lambda: build_mix1(32)), mix1_maps(inp, pre1, la1))
    yT, oT1 = assemble_mix1(r)
    rT = np.ascontiguousarray(pre1[:, 1792:2560].T)
    r = _run(_prog("pout1", lambda: build_pout(1)), pout_maps(inp, 1, x1, oT1, yT, rT))
    x2 = np.concatenate([q["xo"] for q in r], 0)
    return x2.reshape(inp['x'].shape).astype(np.float32, copy=False)
```
